# Optimizing a Trainium2 kernel written in Bass

```python
import math, functools
import jax, jax.numpy as jnp
from jax import lax
import numpy as np

D_MODEL = 1024
BATCH = 4
SEQ = 8192
DEPTH = 1

HEAD_DIM = 64
N_HEADS_TOTAL = D_MODEL // HEAD_DIM
N_ATT_HEADS = N_HEADS_TOTAL // 2
N_RWKV_HEADS = N_HEADS_TOTAL - N_ATT_HEADS
ATT_WIDTH = N_ATT_HEADS * HEAD_DIM
RWKV_WIDTH = N_RWKV_HEADS * HEAD_DIM
MIX_WIDTH = ATT_WIDTH + RWKV_WIDTH
ROPE_DIM = HEAD_DIM // 4
ROPE_THETA = 500000.0
DILATED_PATTERNS = ((128, 1), (512, 4), (2048, 16))
SEQ_ALIGN = functools.reduce(math.lcm, [w for w, _ in DILATED_PATTERNS])
DECAY_LORA = 64
AAA_LORA = 64
GATE_LORA = 128
ATT_COLS = 3 * ATT_WIDTH
SHIFT_COLS = 3 * RWKV_WIDTH + DECAY_LORA + AAA_LORA + GATE_LORA
IN_COLS = ATT_COLS + SHIFT_COLS
FFN_HIDDEN = -(-8 * D_MODEL // (3 * 256)) * 256
ALPHA = (2.0 * DEPTH) ** 0.25
BETA = (8.0 * DEPTH) ** -0.25
LN_EPS = 1e-5
GN_EPS = 64e-5
L2_EPS = 1e-6
DECAY_SCALE = math.exp(-0.5)

kernel_name = 'hymba_dilated_attn_rwkv7_deepnorm'


def layer_norm(x, g, b):
    xf = x.astype(jnp.float32)
    mu = jnp.mean(xf, axis=-1, keepdims=True)
    var = jnp.mean(jnp.square(xf - mu), axis=-1, keepdims=True)
    return ((xf - mu) * lax.rsqrt(var + LN_EPS) * g + b).astype(x.dtype)


def partial_rope(t, positions):
    half = ROPE_DIM // 2
    inv_freq = ROPE_THETA ** (-jnp.arange(half, dtype=jnp.float32) / half)
    ang = positions.astype(jnp.float32)[:, None, :, None] * inv_freq
    cos, sin = jnp.cos(ang), jnp.sin(ang)
    tf = t.astype(jnp.float32)
    x1, x2 = tf[..., :half], tf[..., half:ROPE_DIM]
    out = jnp.concatenate([x1 * cos - x2 * sin, x2 * cos + x1 * sin, tf[..., ROPE_DIM:]], axis=-1)
    return out.astype(t.dtype)


def strided_band_attention(q, k, v, window, dilation):
    B, H, Sp, Dh = q.shape
    n_back = window // dilation
    blk = n_back
    Q = Sp // dilation
    nb = Q // blk

    def to_blocks(t):
        t = t.reshape(B, H, Q, dilation, Dh).transpose(0, 1, 3, 2, 4)
        return t.reshape(B, H, dilation, nb, blk, Dh)

    def with_prev(t):
        prev = jnp.pad(t, ((0, 0), (0, 0), (0, 0), (1, 0), (0, 0), (0, 0)))[:, :, :, :-1]
        return jnp.concatenate([prev, t], axis=4)

    qb = to_blocks(q).astype(jnp.float32)
    kc = with_prev(to_blocks(k)).astype(jnp.float32)
    vc = with_prev(to_blocks(v)).astype(jnp.float32)
    s = jnp.einsum('bhrnqd,bhrnkd->bhrnqk', qb, kc) * (Dh ** -0.5)
    qi = jnp.arange(blk)[:, None]
    kj = jnp.arange(2 * blk)[None, :]
    delta = qi + blk - kj
    band = (delta >= 0) & (delta <= n_back)
    not_before_start = (jnp.arange(nb)[:, None, None] > 0) | (kj >= blk)[None]
    valid = band[None] & not_before_start
    s = jnp.where(valid, s, -jnp.inf)
    m = jnp.max(s, axis=-1, keepdims=True)
    p = jnp.exp(s - m)
    l = jnp.sum(p, axis=-1)
    o = jnp.einsum('bhrnqk,bhrnkd->bhrnqd', p, vc) / l[..., None]
    lse = m[..., 0] + jnp.log(l)
    o = o.reshape(B, H, dilation, Q, Dh).transpose(0, 1, 3, 2, 4).reshape(B, H, Sp, Dh)
    lse = lse.reshape(B, H, dilation, Q).transpose(0, 1, 3, 2).reshape(B, H, Sp)
    return o, lse


def dilated_attention(q, k, v):
    S = q.shape[2]
    s_pad = -(-S // SEQ_ALIGN) * SEQ_ALIGN
    pad = ((0, 0), (0, 0), (0, s_pad - S), (0, 0))
    qp, kp, vp = jnp.pad(q, pad), jnp.pad(k, pad), jnp.pad(v, pad)
    outs, lses = [], []
    for window, dilation in DILATED_PATTERNS:
        o, lse = strided_band_attention(qp, kp, vp, window, dilation)
        outs.append(o)
        lses.append(lse)
    wts = jax.nn.softmax(jnp.stack(lses, axis=0), axis=0)
    o = jnp.einsum('pbhs,pbhsd->bhsd', wts, jnp.stack(outs, axis=0))
    return o[:, :, :S]


def token_shift(y):
    return jnp.pad(y, ((0, 0), (1, 0), (0, 0)))[:, :-1]


def rwkv7_recurrence(r, w, k, v, a_vec, b_vec):
    Bsz, _, H, N = r.shape

    def step(state, inp):
        r_t, w_t, k_t, v_t, a_t, b_t = inp
        sa = jnp.einsum('bhvk,bhk->bhv', state, a_t)
        state = (state * w_t[:, :, None, :] + sa[..., None] * b_t[:, :, None, :]
                 + v_t[..., None] * k_t[:, :, None, :])
        return state, jnp.einsum('bhvk,bhk->bhv', state, r_t)

    xs = tuple(jnp.moveaxis(t.astype(jnp.float32), 1, 0) for t in (r, w, k, v, a_vec, b_vec))
    init = jnp.zeros((Bsz, H, N, N), jnp.float32)
    _, o = lax.scan(step, init, xs)
    return jnp.moveaxis(o, 0, 1)


def setup_inputs(seed: int = 0) -> dict:
    key = jax.random.key(seed)
    ks = jax.random.split(key, 24)

    def nrm(k, shape, scale):
        return jax.random.normal(k, shape, jnp.float32) * scale

    x = nrm(ks[0], (BATCH, SEQ, D_MODEL), 1.0)
    offsets = jax.random.randint(ks[1], (BATCH, 1), 0, 4096, dtype=jnp.int32)
    positions = (offsets + jnp.arange(SEQ, dtype=jnp.int32)[None, :]).astype(jnp.int32)
    col_scale = jnp.ones((IN_COLS,), jnp.float32)
    col_scale = col_scale.at[2 * ATT_WIDTH:3 * ATT_WIDTH].set(BETA)
    col_scale = col_scale.at[ATT_COLS + 2 * RWKV_WIDTH:ATT_COLS + 3 * RWKV_WIDTH].set(BETA)
    w_in = nrm(ks[2], (DEPTH, D_MODEL, IN_COLS), D_MODEL ** -0.5) * col_scale
    w_out = nrm(ks[3], (DEPTH, MIX_WIDTH, D_MODEL), MIX_WIDTH ** -0.5 * BETA)
    mu_shift = jax.random.uniform(ks[4], (DEPTH, SHIFT_COLS), jnp.float32, 0.2, 0.8)
    w0 = jax.random.uniform(ks[5], (DEPTH, RWKV_WIDTH), jnp.float32, -3.0, 3.0)
    w_decay_up = nrm(ks[6], (DEPTH, DECAY_LORA, RWKV_WIDTH), 0.5 * DECAY_LORA ** -0.5)
    a0 = nrm(ks[7], (DEPTH, RWKV_WIDTH), 0.1)
    w_aaa_up = nrm(ks[8], (DEPTH, AAA_LORA, RWKV_WIDTH), 0.5 * AAA_LORA ** -0.5)
    w_gate_up = nrm(ks[9], (DEPTH, GATE_LORA, RWKV_WIDTH), GATE_LORA ** -0.5)
    k_k = 0.85 + nrm(ks[10], (DEPTH, RWKV_WIDTH), 0.05)
    k_a = 1.0 + nrm(ks[11], (DEPTH, RWKV_WIDTH), 0.05)
    r_k = nrm(ks[12], (DEPTH, N_RWKV_HEADS, HEAD_DIM), 0.1)
    gn_g = 1.0 + nrm(ks[13], (DEPTH, RWKV_WIDTH), 0.02)
    gn_b = nrm(ks[14], (DEPTH, RWKV_WIDTH), 0.02)
    ln_mix_g = 1.0 + nrm(ks[15], (DEPTH, D_MODEL), 0.02)
    ln_mix_b = nrm(ks[16], (DEPTH, D_MODEL), 0.02)
    w_ffn_gate = nrm(ks[17], (DEPTH, D_MODEL, FFN_HIDDEN), D_MODEL ** -0.5)
    w_ffn_up = nrm(ks[18], (DEPTH, D_MODEL, FFN_HIDDEN), D_MODEL ** -0.5)
    w_ffn_down = nrm(ks[19], (DEPTH, FFN_HIDDEN, D_MODEL), FFN_HIDDEN ** -0.5 * BETA)
    ln_ffn_g = 1.0 + nrm(ks[20], (DEPTH, D_MODEL), 0.02)
    ln_ffn_b = nrm(ks[21], (DEPTH, D_MODEL), 0.02)
    return {'x': x, 'positions': positions, 'w_in': w_in, 'w_out': w_out,
            'mu_shift': mu_shift, 'w0': w0, 'w_decay_up': w_decay_up, 'a0': a0,
            'w_aaa_up': w_aaa_up, 'w_gate_up': w_gate_up, 'k_k': k_k, 'k_a': k_a,
            'r_k': r_k, 'gn_g': gn_g, 'gn_b': gn_b, 'ln_mix_g': ln_mix_g,
            'ln_mix_b': ln_mix_b, 'w_ffn_gate': w_ffn_gate, 'w_ffn_up': w_ffn_up,
            'w_ffn_down': w_ffn_down, 'ln_ffn_g': ln_ffn_g, 'ln_ffn_b': ln_ffn_b}


def reference(x, positions, w_in, w_out, mu_shift, w0, w_decay_up, a0, w_aaa_up,
              w_gate_up, k_k, k_a, r_k, gn_g, gn_b, ln_mix_g, ln_mix_b,
              w_ffn_gate, w_ffn_up, w_ffn_down, ln_ffn_g, ln_ffn_b):
    B, S, _ = x.shape
    rw_split = [RWKV_WIDTH, 2 * RWKV_WIDTH, 3 * RWKV_WIDTH,
                3 * RWKV_WIDTH + DECAY_LORA, 3 * RWKV_WIDTH + DECAY_LORA + AAA_LORA]
    h = x
    for l in range(DEPTH):
        y = jnp.einsum('bsd,dc->bsc', h, w_in[l])
        y_att, y_rw = y[..., :ATT_COLS], y[..., ATT_COLS:]

        q, k, v = jnp.split(y_att, 3, axis=-1)

        def att_heads(t):
            return t.reshape(B, S, N_ATT_HEADS, HEAD_DIM).transpose(0, 2, 1, 3)

        q = partial_rope(att_heads(q), positions)
        k = partial_rope(att_heads(k), positions)
        o_att = dilated_attention(q, k, att_heads(v))
        o_att = o_att.transpose(0, 2, 1, 3).reshape(B, S, ATT_WIDTH).astype(h.dtype)

        y_rw = y_rw + mu_shift[l] * (token_shift(y_rw) - y_rw)
        r, kr, vr, dw, da, dg = jnp.split(y_rw, rw_split, axis=-1)
        w_logit = (w0[l] + jnp.tanh(dw) @ w_decay_up[l]).astype(jnp.float32)
        decay = jnp.exp(-DECAY_SCALE * jax.nn.sigmoid(w_logit))
        a = jax.nn.sigmoid((a0[l] + da @ w_aaa_up[l]).astype(jnp.float32))
        g = jax.nn.sigmoid(dg) @ w_gate_up[l]

        def rw_heads(t):
            return t.reshape(B, S, N_RWKV_HEADS, HEAD_DIM)

        kk = rw_heads((kr * k_k[l]).astype(jnp.float32))
        kk = kk / jnp.maximum(jnp.linalg.norm(kk, axis=-1, keepdims=True), L2_EPS)
        a_h = rw_heads(a)
        k_a_h = k_a[l].reshape(N_RWKV_HEADS, HEAD_DIM).astype(jnp.float32)
        k_mod = rw_heads(kr).astype(jnp.float32) * (1.0 + (a_h - 1.0) * k_a_h)
        r_h = rw_heads(r).astype(jnp.float32)
        v_h = rw_heads(vr).astype(jnp.float32)
        o = rwkv7_recurrence(r_h, rw_heads(decay), k_mod, v_h, -kk, kk * a_h)
        mu = jnp.mean(o, axis=-1, keepdims=True)
        var = jnp.mean(jnp.square(o - mu), axis=-1, keepdims=True)
        o = ((o - mu) * lax.rsqrt(var + GN_EPS)).reshape(B, S, RWKV_WIDTH) * gn_g[l] + gn_b[l]
        bonus = jnp.sum(r_h * k_mod * r_k[l], axis=-1, keepdims=True) * v_h
        o_rw = ((o + bonus.reshape(B, S, RWKV_WIDTH)) * g).astype(h.dtype)

        mix = jnp.concatenate([o_att, o_rw], axis=-1) @ w_out[l]
        h = layer_norm(ALPHA * h + mix, ln_mix_g[l], ln_mix_b[l])

        ffn = (jax.nn.silu(h @ w_ffn_gate[l]) * (h @ w_ffn_up[l])) @ w_ffn_down[l]
        h = layer_norm(ALPHA * h + ffn, ln_ffn_g[l], ln_ffn_b[l])
    return h
```

```python
import math
from contextlib import ExitStack

import numpy as np
import concourse.bass as bass
import concourse.mybir as mybir
from concourse.bass_utils import run_bass_kernel_spmd

F32 = mybir.dt.float32
BF16 = mybir.dt.bfloat16
I32 = mybir.dt.int32
AF = mybir.ActivationFunctionType
ALU = mybir.AluOpType
AX = mybir.AxisListType

D_MODEL = 1024
SEQ = 8192
BATCH = 4
HD = 64
FFN = 2816
HC = FFN // 128
ALPHA = 2.0 ** 0.25
LN_EPS = 1e-5
GN_EPS = 64e-5
L2_EPS = 1e-6
DECAY_SCALE = math.exp(-0.5)
NEG = -30000.0
ROPE_THETA = 500000.0
TOKH = SEQ // 2
XCH = 1024
NXCH = SEQ // XCH
B_TILES = SEQ // 128


class Buf:
    __slots__ = ("name", "w", "r", "psum")

    def __init__(self, name="", psum=False):
        self.name = name
        self.w = []
        self.r = {}
        self.psum = psum


class Sched:
    ENG = ("pe", "act", "dve", "pool", "sp")
    NDS = 8

    def __init__(self, nc, stack):
        self.nc = nc
        self.q = {e: [] for e in self.ENG}
        self.sem = {e: stack.enter_context(nc.semaphore("s_" + e)) for e in self.ENG}
        self.cnt = {e: 0 for e in self.ENG}
        self.seen = {e: {} for e in self.ENG}
        self.dq = ("sp", "act", "pool")
        self.dsem = {e: [stack.enter_context(nc.semaphore("d_%s%d" % (e, i)))
                         for i in range(self.NDS)] for e in self.dq}
        self.dcnt = {e: 0 for e in self.dq}
        self.dlast = {e: [None] * self.NDS for e in self.dq}
        self.extra = []

    def _wait(self, eng, tok):
        if tok is None:
            return
        sem, val, key, prod = tok
        if self.seen[eng].get(key, 0) >= val:
            return
        self.seen[eng][key] = val
        self.q[eng].append(lambda e, s=sem, v=val: e.wait_ge(s, v))

    def _deps(self, eng, reads, writes):
        for b in reads:
            for t in b.w:
                self._wait(eng, t)
            if b.psum:
                for t in b.r.values():
                    if t[3] != eng:
                        self._wait(eng, t)
        for b in writes:
            for t in b.w:
                if t[3] != eng:
                    self._wait(eng, t)
            for t in b.r.values():
                if t[3] != eng:
                    self._wait(eng, t)

    def _mark(self, tok, reads, writes, acc=False):
        for b in reads:
            b.r[tok[2]] = tok
        for b in writes:
            if acc:
                b.w.append(tok)
            else:
                b.w = [tok]
            b.r = {}

    def op(self, eng, fn, reads=(), writes=(), signal=True):
        self._deps(eng, reads, writes)
        sem = self.sem[eng]
        if signal:
            self.cnt[eng] += 1
            tok = (sem, self.cnt[eng], eng, eng)
            self.q[eng].append(lambda e, f=fn, s=sem: f(e).then_inc(s, 1))
        else:
            tok = (sem, self.cnt[eng] + 1, eng, eng)
            self.q[eng].append(lambda e, f=fn: f(e))
        self._mark(tok, reads, writes)
        return tok

    def mm(self, out, lhsT, rhs, reads, writes, start=True, stop=True, signal=True, **kw):
        return self.op("pe", lambda e: e.matmul(out, lhsT, rhs, start=start, stop=stop, **kw),
                       reads, writes, signal)

    def dma(self, queue, out, in_, reads=(), writes=(), acc=False, **kw):
        i = self.dcnt[queue]
        self.dcnt[queue] += 1
        slot = i % self.NDS
        self._wait(queue, self.dlast[queue][slot])
        self._deps(queue, reads, writes)
        sem = self.dsem[queue][slot]
        val = 16 * (i // self.NDS + 1)
        tok = (sem, val, "d_%s%d" % (queue, slot), "dma")
        self.dlast[queue][slot] = tok
        self.q[queue].append(
            lambda e, o=out, a=in_, s=sem, k=kw: e.dma_start(out=o, in_=a, **k).then_inc(s, 16))
        self._mark(tok, reads, writes, acc=acc)
        return tok

    def barrier(self):
        toks = [(self.sem[o], self.cnt[o], o, o) for o in self.ENG if self.cnt[o] > 0]
        for qn in self.dq:
            toks += [t for t in self.dlast[qn] if t is not None]
        toks += self.extra
        for e in self.ENG:
            for t in toks:
                if t[3] != e:
                    self._wait(e, t)

    def run(self):
        nc = self.nc
        q = self.q
        with nc.Block() as block:
            @block.tensor
            def _(e):
                for f in q["pe"]:
                    f(e)

            @block.scalar
            def _(e):
                for f in q["act"]:
                    f(e)

            @block.vector
            def _(e):
                for f in q["dve"]:
                    f(e)

            @block.gpsimd
            def _(e):
                for f in q["pool"]:
                    f(e)

            @block.sync
            def _(e):
                for f in q["sp"]:
                    f(e)
        self.q = {e: [] for e in self.ENG}


def bcast_rows(ap, n=128):
    return bass.AP(ap.tensor, ap.offset, [[0, n], [1, ap.shape[-1]]])


def phase_c(nc, S, D):
    GT = 512
    NG = TOKH // GT
    TPG = GT // 128
    with ExitStack() as ph:
        def sb(n, shp, dt):
            return ph.enter_context(nc.sbuf_tensor(n, shp, dt))

        def ps(n, shp, dt):
            return ph.enter_context(nc.psum_tensor(n, shp, dt))

        Gv = [g_.rearrange("(c p) t -> p c t", p=128) for g_ in D["G"]]
        ident = sb("c_ident", [128, 128], BF16)
        b_ident = Buf()
        S.dma("pool", ident[:], D["ident"], writes=[b_ident])
        sel = sb("c_sel", [128, 2], F32)
        b_sel = Buf()
        S.dma("sp", sel[:], D["sel"], writes=[b_sel])
        wout = sb("c_wout", [128, 8, 1024], BF16)
        b_wout = Buf()
        S.dma("pool", wout[:], D["w_out"].rearrange("(c p) n -> p c n", p=128), writes=[b_wout])
        wd = sb("c_wd", [128, HC, 1024], BF16)
        b_wd = [Buf() for _ in range(HC)]
        for h in range(HC):
            S.dma("pool", wd[:, h, :], D["wd"][h * 128:(h + 1) * 128, :], writes=[b_wd[h]])
        lnp = sb("c_lnp", [128, 4, 1024], F32)
        b_lnp = [Buf() for _ in range(4)]
        for i, nm in enumerate(("ln1g", "ln1b", "ln2g", "ln2b")):
            S.dma("sp", lnp[:, i, :], bcast_rows(D[nm]), writes=[b_lnp[i]])

        NW = 4
        wgu = [sb("c_wgu%d" % i, [128, 2, 8, 128], BF16) for i in range(NW)]
        b_wgu = [Buf() for _ in range(NW)]
        wgv = D["wg"].rearrange("(c p) n -> p c n", p=128)
        wuv = D["wu"].rearrange("(c p) n -> p c n", p=128)

        h1g = sb("c_h1g", [128, TPG, 1024], F32)
        b_h1g = [Buf() for _ in range(TPG)]
        h1T = sb("c_h1T", [128, 8, GT], BF16)
        b_h1T = [Buf() for _ in range(TPG)]
        actT = sb("c_actT", [128, HC, GT], BF16)
        b_actT = [Buf() for _ in range(HC)]
        NB = 2
        ocA = [sb("c_ocA%d" % i, [128, 8, 128], BF16) for i in range(NB)]
        ocB = [sb("c_ocB%d" % i, [128, 8, 128], BF16) for i in range(NB)]
        oc = [sb("c_oc%d" % i, [128, 8, 128], BF16) for i in range(NB)]
        b_ocA = [Buf() for _ in range(NB)]
        b_ocB = [Buf() for _ in range(NB)]
        b_oc = [Buf() for _ in range(NB)]
        xt = [sb("c_xt%d" % i, [128, 1024], F32) for i in range(NB)]
        b_xt = [Buf() for _ in range(NB)]
        hpre = sb("c_hpre", [128, 1024], F32)
        b_hpre = Buf()
        hn = sb("c_hn", [128, 1024], F32)
        b_hn = Buf()
        h1b = sb("c_h1b", [128, 1024], BF16)
        b_h1b = Buf()
        stats = sb("c_stats", [128, 2, 6], F32)
        b_stats = Buf()
        mv = sb("c_mv", [128, 4], F32)
        b_mv = Buf()
        sg = [sb("c_sg%d" % i, [128, 512], F32) for i in range(2)]
        b_sg = [Buf() for _ in range(2)]
        outt = [sb("c_outt%d" % i, [128, 1024], F32) for i in range(NB)]
        b_outt = [Buf() for _ in range(NB)]

        epsc = sb("c_eps", [128, 1], F32)
        b_epsc = Buf()
        S.op("dve", lambda e: e.memset(epsc[:], LN_EPS), writes=[b_epsc])
        pmix = ps("c_pmix", [128, 1024], F32)
        b_pmix = Buf(psum=True)
        pT = ps("c_pT", [128, 1024], BF16)
        b_pT = Buf(psum=True)
        pG = [ps("c_pG%d" % i, [128, 512], F32) for i in range(2)]
        pU = [ps("c_pU%d" % i, [128, 512], F32) for i in range(2)]
        b_pG = [Buf(psum=True) for _ in range(2)]
        b_pU = [Buf(psum=True) for _ in range(2)]

        def layer_norm(src_b, gi_, bi_, dst, b_dst):
            for hf in range(2):
                S.op("dve", lambda e, hf=hf: e.bn_stats(stats[:, hf, :], hpre[:, hf * 512:(hf + 1) * 512]),
                     reads=[src_b], writes=[b_stats])
            S.op("dve", lambda e: e.bn_aggr(mv[:, 0:2], stats[:].rearrange("p a b -> p (a b)")),
                 reads=[b_stats], writes=[b_mv])
            S.op("act", lambda e: e.activation(mv[:, 3:4], mv[:, 1:2], AF.Sqrt, bias=epsc[:, 0:1], scale=1.0),
                 reads=[b_mv, b_epsc], writes=[b_mv])
            S.op("dve", lambda e: e.reciprocal(mv[:, 2:3], mv[:, 3:4]), reads=[b_mv], writes=[b_mv])
            S.op("dve", lambda e: e.tensor_scalar(hn[:], hpre[:], mv[:, 0:1], mv[:, 2:3],
                                                  ALU.subtract, ALU.mult),
                 reads=[src_b, b_mv], writes=[b_hn])
            S.op("pool", lambda e: e.tensor_tensor(hn[:], hn[:], lnp[:, gi_, :], ALU.mult),
                 reads=[b_hn, b_lnp[gi_]], writes=[b_hn])
            S.op("pool", lambda e: e.tensor_tensor(dst, hn[:], lnp[:, bi_, :], ALU.add),
                 reads=[b_hn, b_lnp[bi_]], writes=[b_dst])

        nwl = 0
        for gi in range(NG):
            for ti in range(TPG):
                it = gi * TPG + ti
                sl = it % NB
                tok0 = it * 128
                ta, tb = tok0, TOKH + tok0
                S.dma("sp", ocA[sl][:], Gv[ta // XCH][:, :, ta % XCH:ta % XCH + 128], writes=[b_ocA[sl]])
                S.dma("sp", ocB[sl][:], Gv[tb // XCH][:, :, tb % XCH:tb % XCH + 128], writes=[b_ocB[sl]])
                S.dma("sp", xt[sl][:], D["xres"][tok0:tok0 + 128, :], writes=[b_xt[sl]])
                S.op("pool", lambda e, sl=sl: e.tensor_scalar(oc[sl][:], ocA[sl][:], sel[:, 0:1], None, ALU.mult),
                     reads=[b_ocA[sl], b_sel], writes=[b_oc[sl]])
                S.op("pool", lambda e, sl=sl: e.tensor_scalar(ocB[sl][:], ocB[sl][:], sel[:, 1:2], None, ALU.mult),
                     reads=[b_ocB[sl], b_sel], writes=[b_ocB[sl]])
                S.op("pool", lambda e, sl=sl: e.tensor_tensor(oc[sl][:], oc[sl][:], ocB[sl][:], ALU.add),
                     reads=[b_ocB[sl], b_oc[sl]], writes=[b_oc[sl]])
                for hf in range(2):
                    for c in range(8):
                        S.mm(pmix[:, hf * 512:(hf + 1) * 512], oc[sl][:, c, :], wout[:, c, hf * 512:(hf + 1) * 512],
                             reads=[b_oc[sl], b_wout], writes=[b_pmix], start=(c == 0), stop=(c == 7),
                             signal=(c == 7))
                for hf in range(2):
                    S.op("dve", lambda e, hf=hf, sl=sl: e.scalar_tensor_tensor(
                        hpre[:, hf * 512:(hf + 1) * 512], xt[sl][:, hf * 512:(hf + 1) * 512], ALPHA,
                        pmix[:, hf * 512:(hf + 1) * 512], ALU.mult, ALU.add),
                        reads=[b_xt[sl], b_pmix], writes=[b_hpre])
                layer_norm(b_hpre, 0, 1, h1g[:, ti, :], b_h1g[ti])
                S.op("act", lambda e, ti=ti: e.copy(h1b[:], h1g[:, ti, :]), reads=[b_h1g[ti]], writes=[b_h1b])
                for c in range(8):
                    S.op("pe", lambda e, c=c: e.transpose(pT[:, c * 128:(c + 1) * 128],
                                                          h1b[:, c * 128:(c + 1) * 128], ident[:]),
                         reads=[b_h1b, b_ident], writes=[b_pT], signal=(c == 7))
                S.op("act", lambda e, ti=ti: e.copy(h1T[:, :, ti * 128:(ti + 1) * 128],
                                                    pT[:].rearrange("p (c t) -> p c t", c=8)),
                     reads=[b_pT], writes=[b_h1T[ti]])
            for h in range(HC):
                ws = nwl % NW
                nwl += 1
                S.dma("pool", wgu[ws][:, 0, :, :], wgv[:, :, h * 128:(h + 1) * 128], writes=[b_wgu[ws]])
                S.dma("pool", wgu[ws][:, 1, :, :], wuv[:, :, h * 128:(h + 1) * 128], writes=[b_wgu[ws]], acc=True)
                pb = h % 2
                for c in range(8):
                    S.mm(pG[pb][:], wgu[ws][:, 0, c, :], h1T[:, c, :], reads=[b_wgu[ws]] + b_h1T,
                         writes=[b_pG[pb]], start=(c == 0), stop=(c == 7), signal=(c == 7))
                for c in range(8):
                    S.mm(pU[pb][:], wgu[ws][:, 1, c, :], h1T[:, c, :], reads=[b_wgu[ws]] + b_h1T,
                         writes=[b_pU[pb]], start=(c == 0), stop=(c == 7), signal=(c == 7))
                S.op("act", lambda e, pb=pb: e.activation(sg[pb][:], pG[pb][:], AF.Silu),
                     reads=[b_pG[pb]], writes=[b_sg[pb]])
                S.op("dve", lambda e, pb=pb, h=h: e.tensor_tensor(actT[:, h, :], pU[pb][:], sg[pb][:], ALU.mult),
                     reads=[b_pU[pb], b_sg[pb]], writes=[b_actT[h]])
            for ti in range(TPG):
                it = gi * TPG + ti
                sl = it % NB
                tok0 = it * 128
                for hf in range(2):
                    for h in range(HC):
                        S.mm(pmix[:, hf * 512:(hf + 1) * 512], actT[:, h, ti * 128:(ti + 1) * 128],
                             wd[:, h, hf * 512:(hf + 1) * 512], reads=[b_actT[h], b_wd[h]], writes=[b_pmix],
                             start=(h == 0), stop=(h == HC - 1), signal=(h == HC - 1))
                for hf in range(2):
                    S.op("dve", lambda e, hf=hf, ti=ti: e.scalar_tensor_tensor(
                        hpre[:, hf * 512:(hf + 1) * 512], h1g[:, ti, hf * 512:(hf + 1) * 512], ALPHA,
                        pmix[:, hf * 512:(hf + 1) * 512], ALU.mult, ALU.add),
                        reads=[b_h1g[ti], b_pmix], writes=[b_hpre])
                layer_norm(b_hpre, 2, 3, outt[sl][:], b_outt[sl])
                S.dma("sp", D["out"][tok0:tok0 + 128, :], outt[sl][:], reads=[b_outt[sl]])
        S.barrier()
        S.run()


def phase_a(nc, S, D):
    ST = 2048
    NST = SEQ // ST
    inv_freq = [float(np.float32(ROPE_THETA) ** np.float32(-i / 8.0)) for i in range(8)]
    TWO_PI = 2.0 * math.pi
    C1 = 6.28125
    C2 = TWO_PI - C1
    with ExitStack() as ph:
        def sb(n, shp, dt):
            return ph.enter_context(nc.sbuf_tensor(n, shp, dt))

        def ps(n, shp, dt):
            return ph.enter_context(nc.psum_tensor(n, shp, dt))

        identf = sb("a_identf", [128, 128], F32)
        b_identf = Buf()
        S.dma("sp", identf[:], D["ident"], writes=[b_identf])
        identb = sb("a_identb", [128, 128], BF16)
        b_identb = Buf()
        S.dma("pool", identb[:], D["ident"], writes=[b_identb])
        maskb = sb("a_maskb", [128, 256], BF16)
        b_maskb = Buf()
        S.dma("pool", maskb[:], D["maskT"], writes=[b_maskb])
        ones = sb("a_ones", [128, 128], F32)
        b_ones = Buf()
        S.op("dve", lambda e: e.memset(ones[:], 1.0), writes=[b_ones])
        watt = sb("a_watt", [128, 8, 768], BF16)
        b_watt = Buf()
        S.dma("pool", watt[:], D["w_att"].rearrange("(c p) n -> p c n", p=128), writes=[b_watt])

        posi = sb("a_posi", [128, 64], I32)
        b_posi = Buf()
        S.dma("sp", posi[:], D["pos"], writes=[b_posi])
        posf = sb("a_posf", [128, 64], F32)
        b_posf = Buf()
        S.op("dve", lambda e: e.tensor_copy(posf[:], posi[:]), reads=[b_posi], writes=[b_posf])
        ang = sb("a_ang", [128, 64, 8], F32)
        b_ang = Buf()
        for i in range(8):
            S.op("dve", lambda e, i=i: e.tensor_scalar(ang[:, :, i], posf[:], inv_freq[i], None, ALU.mult),
                 reads=[b_posf], writes=[b_ang])
        sinT = sb("a_sinT", [128, 64, 8], F32)
        cosT = sb("a_cosT", [128, 64, 8], F32)
        b_sinT = Buf()
        b_cosT = Buf()
        kq = sb("a_kq", [128, 512], I32)
        kf = sb("a_kf", [128, 512], F32)
        red = sb("a_red", [128, 512], F32)
        msk = sb("a_msk", [128, 512], F32)
        b_tmp = Buf()
        angf = ang[:].rearrange("p a b -> p (a b)")

        def wrap(dst):
            S.op("dve", lambda e: e.tensor_scalar(msk[:], dst, math.pi, -TWO_PI, ALU.is_gt, ALU.mult),
                 reads=[b_tmp], writes=[b_tmp])
            S.op("dve", lambda e: e.tensor_tensor(dst, dst, msk[:], ALU.add), reads=[b_tmp], writes=[b_tmp])

        S.op("dve", lambda e: e.tensor_scalar(kq[:], angf, 1.0 / TWO_PI, None, ALU.mult),
             reads=[b_ang], writes=[b_tmp])
        S.op("dve", lambda e: e.tensor_copy(kf[:], kq[:]), reads=[b_tmp], writes=[b_tmp])
        S.op("dve", lambda e: e.scalar_tensor_tensor(red[:], kf[:], -C1, angf, ALU.mult, ALU.add),
             reads=[b_tmp, b_ang], writes=[b_tmp])
        S.op("dve", lambda e: e.scalar_tensor_tensor(red[:], kf[:], -C2, red[:], ALU.mult, ALU.add),
             reads=[b_tmp], writes=[b_tmp])
        wrap(red[:])
        S.op("act", lambda e: e.activation(sinT[:].rearrange("p a b -> p (a b)"), red[:], AF.Sin),
             reads=[b_tmp], writes=[b_sinT])
        S.op("dve", lambda e: e.tensor_scalar(red[:], red[:], math.pi / 2.0, None, ALU.add),
             reads=[b_tmp, b_sinT], writes=[b_tmp])
        wrap(red[:])
        S.op("act", lambda e: e.activation(cosT[:].rearrange("p a b -> p (a b)"), red[:], AF.Sin),
             reads=[b_tmp], writes=[b_cosT])

        xTs = [sb("a_xT%d" % i, [128, 8, ST], BF16) for i in range(2)]
        b_xTs = [Buf() for _ in range(2)]
        xTv = D["xT"].rearrange("(c p) t -> p c t", p=128)
        qT = sb("a_qT", [128, 2, ST], BF16)
        b_qT = Buf()
        kT = [sb("a_kT%d" % i, [128, 2, ST], BF16) for i in range(2)]
        b_kT = [Buf() for _ in range(2)]
        V = [[sb("a_V%d_%d" % (l, i), [128, 16, 4, 65], BF16) for i in range(2)] for l in range(3)]
        b_V = [[Buf() for _ in range(2)] for l in range(3)]
        for l in range(3):
            for i in range(2):
                S.op("pool", lambda e, l=l, i=i: e.memset(V[l][i][:], 1.0), writes=[b_V[l][i]])
        ysb = sb("a_ysb", [128, 512], F32)
        b_ysb = Buf()
        ysb3 = ysb[:].rearrange("p (h d) -> p h d", d=64)
        rt = sb("a_rt", [128, 4, 8, 8], F32)
        b_rt = Buf()
        sq = sb("a_sq", [128, 512], F32)
        b_sq = Buf()
        ssq = sb("a_ssq", [128, 8], F32)
        b_ssq = Buf()
        rm = sb("a_rm", [128, 8], F32)
        b_rm = Buf()
        st4 = sb("a_st4", [4, 8], F32)
        b_st4 = Buf()
        dg4 = sb("a_dg4", [4, 4], F32)
        b_dg4 = Buf()
        negM = sb("a_negM", [128, 4], F32)
        b_negM = Buf()
        NPT = 3
        PT = [sb("a_PT%d" % i, [128, 256], BF16) for i in range(NPT)]
        b_PT = [Buf() for _ in range(NPT)]
        oacc = sb("a_oacc", [128, ST], F32)
        b_oacc = Buf()
        oT = [sb("a_oT%d" % i, [64, ST], BF16) for i in range(2)]
        b_oT = [Buf() for _ in range(2)]

        pqk = ps("a_pqk", [128, 512], F32)
        b_pqk = Buf(psum=True)
        pX = ps("a_pX", [128, 512], F32)
        b_pX = Buf(psum=True)
        pS = [ps("a_pS%d" % i, [128, 512], F32) for i in range(2)]
        b_pS = [Buf(psum=True) for _ in range(2)]
        pO = ps("a_pO", [128, ST], F32)
        b_pO = Buf(psum=True)

        S.op("dve", lambda e: e.memset(st4[:], 0.0), writes=[b_st4])

        nblk = 0
        for st in range(NST):
            xs = st % 2
            ks = st % 2
            kp = 1 - ks
            xT = xTs[xs]
            S.dma("pool", xT[:], xTv[:, :, st * ST:(st + 1) * ST], writes=[b_xTs[xs]])
            S.op("dve", lambda e: e.memset(rm[:], 0.0), reads=[], writes=[b_rm])
            for j in range(16):
                n = st * 16 + j
                for c in range(8):
                    S.mm(pqk[:], xT[:, c, j * 128:(j + 1) * 128], watt[:, c, 0:512],
                         reads=[b_xTs[xs], b_watt], writes=[b_pqk], start=(c == 0), stop=(c == 7), signal=(c == 7))
                S.op("act", lambda e: e.mul(ysb[:, 0:256], pqk[:, 0:256], 0.125), reads=[b_pqk], writes=[b_ysb])
                S.op("act", lambda e: e.copy(ysb[:, 256:512], pqk[:, 256:512]), reads=[b_pqk], writes=[b_ysb])
                cb = cosT[:, n:n + 1, :].to_broadcast([128, 8, 8])
                sbb = sinT[:, n:n + 1, :].to_broadcast([128, 8, 8])
                x1 = ysb3[:, :, 0:8]
                x2 = ysb3[:, :, 8:16]
                rtv = rt[:].rearrange("p a h d -> p a (h d)")
                for a, (xx, tt) in enumerate(((x1, cb), (x2, sbb), (x2, cb), (x1, sbb))):
                    S.op("pool", lambda e, a=a, xx=xx, tt=tt: e.tensor_tensor(rt[:, a, :, :], xx, tt, ALU.mult),
                         reads=[b_ysb, b_cosT, b_sinT], writes=[b_rt])
                S.op("pool", lambda e: e.tensor_tensor(x1, rt[:, 0, :, :], rt[:, 1, :, :], ALU.subtract),
                     reads=[b_rt, b_ysb], writes=[b_ysb])
                S.op("pool", lambda e: e.tensor_tensor(x2, rt[:, 2, :, :], rt[:, 3, :, :], ALU.add),
                     reads=[b_rt, b_ysb], writes=[b_ysb])
                S.op("dve", lambda e: e.tensor_tensor(sq[:], ysb[:], ysb[:], ALU.mult), reads=[b_ysb], writes=[b_sq])
                S.op("dve", lambda e: e.tensor_reduce(ssq[:], sq[:].rearrange("p (h d) -> p h d", d=64), AX.X, ALU.add),
                     reads=[b_sq], writes=[b_ssq])
                S.op("dve", lambda e: e.tensor_tensor(rm[:], rm[:], ssq[:], ALU.max), reads=[b_ssq, b_rm], writes=[b_rm])
                for blk in range(4):
                    S.op("pe", lambda e, blk=blk: e.transpose(pX[:, blk * 128:(blk + 1) * 128],
                                                               ysb[:, blk * 128:(blk + 1) * 128], identf[:]),
                         reads=[b_ysb, b_identf], writes=[b_pX], signal=(blk == 3))
                S.op("act", lambda e, j=j: e.copy(qT[:, :, j * 128:(j + 1) * 128],
                                                  pX[:, 0:256].rearrange("p (a t) -> p a t", a=2)),
                     reads=[b_pX], writes=[b_qT])
                S.op("act", lambda e, j=j, ks=ks: e.copy(kT[ks][:, :, j * 128:(j + 1) * 128],
                                                         pX[:, 256:512].rearrange("p (a t) -> p a t", a=2)),
                     reads=[b_pX], writes=[b_kT[ks]])
            S.op("pe", lambda e: e.transpose(pX[0:4, 0:128], rm[:, 0:4], identf[:]), reads=[b_rm, b_identf],
                 writes=[b_pX], signal=False)
            S.op("pe", lambda e: e.transpose(pX[0:4, 128:256], rm[:, 4:8], identf[:]), reads=[b_rm, b_identf],
                 writes=[b_pX])
            S.op("dve", lambda e: e.tensor_copy(st4[:, 2:3], st4[:, 1:2]), reads=[b_st4], writes=[b_st4])
            S.op("dve", lambda e: e.tensor_reduce(st4[:, 0:2], pX[0:4, 0:256].rearrange("p (a t) -> p a t", a=2),
                                                  AX.X, ALU.max), reads=[b_pX, b_st4], writes=[b_st4])
            S.op("dve", lambda e: e.tensor_tensor(st4[:, 3:4], st4[:, 1:2], st4[:, 2:3], ALU.max),
                 reads=[b_st4], writes=[b_st4])
            S.op("dve", lambda e: e.tensor_tensor(st4[:, 3:4], st4[:, 3:4], st4[:, 0:1], ALU.mult),
                 reads=[b_st4], writes=[b_st4])
            S.op("dve", lambda e: e.tensor_scalar(dg4[:], identf[0:4, 0:4], st4[:, 3:4], None, ALU.mult),
                 reads=[b_st4, b_identf], writes=[b_dg4])
            S.mm(pX[:, 256:260], ones[0:4, :], dg4[:], reads=[b_ones, b_dg4], writes=[b_pX])
            S.op("act", lambda e: e.activation(negM[:], pX[:, 256:260], AF.Sqrt), reads=[b_pX], writes=[b_negM])
            S.op("dve", lambda e: e.tensor_scalar(negM[:], negM[:], -1.0, None, ALU.mult), reads=[b_negM],
                 writes=[b_negM])
            for l, dil in enumerate((1, 4, 16)):
                for t16 in range(16):
                    if dil == 1:
                        a0, a1 = t16 * 128, (t16 + 1) * 128
                    elif dil == 4:
                        n4, r = t16 // 4, t16 % 4
                        a0, a1 = n4 * 512 + r, (n4 + 1) * 512
                    else:
                        a0, a1 = t16, ST
                    for c in range(8):
                        S.mm(pX[:, 0:256], xT[:, c, a0:a1:dil], watt[:, c, 512:768],
                             reads=[b_xTs[xs], b_watt], writes=[b_pX], start=(c == 0), stop=(c == 7), signal=(c == 7))
                    S.op("act", lambda e, l=l, t16=t16, ks=ks: e.copy(
                        V[l][ks][:, t16, :, 0:64], pX[:, 0:256].rearrange("p (h d) -> p h d", d=64)),
                        reads=[b_pX], writes=[b_V[l][ks]])
            for h in range(4):
                hp, p0 = h // 2, 64 * (h % 2)
                first = [True] * 4

                def block(qsl, cur, prev, outs):
                    nonlocal nblk
                    pb = nblk % 2
                    pt = nblk % NPT
                    nblk += 1
                    lo = 0 if prev is not None else 128
                    qap = qT[p0:p0 + 64, hp, qsl]
                    if prev is not None:
                        pslot, psl, pv_ap = prev
                        S.mm(pS[pb][:, 0:128], kT[pslot][p0:p0 + 64, hp, psl], qap,
                             reads=[b_kT[pslot], b_qT], writes=[b_pS[pb]], start=True, stop=False, signal=False)
                        S.mm(pS[pb][:, 0:128], identb[:], maskb[:, 0:128], reads=[b_identb, b_maskb],
                             writes=[b_pS[pb]], start=False, stop=True, signal=False)
                    S.mm(pS[pb][:, 128:256], kT[ks][p0:p0 + 64, hp, qsl], qap,
                         reads=[b_kT[ks], b_qT], writes=[b_pS[pb]], start=True, stop=False, signal=False)
                    S.mm(pS[pb][:, 128:256], identb[:], maskb[:, 128:256], reads=[b_identb, b_maskb],
                         writes=[b_pS[pb]], start=False, stop=True, signal=True)
                    S.op("act", lambda e, hh=h: e.activation(PT[pt][:, lo:256], pS[pb][:, lo:256], AF.Exp,
                                                             bias=negM[:, hh:hh + 1], scale=1.0),
                         reads=[b_pS[pb], b_negM], writes=[b_PT[pt]])
                    kts = ([(0, pv_ap, b_V_prev)] if prev is not None else []) + [(1, cur, b_V_cur)]
                    nmm = len(kts) * len(outs)
                    i = 0
                    for (kt, vap, vb) in kts:
                        for (ocol, pcol) in outs:
                            bank = ocol.start // 512
                            i += 1
                            S.mm(pO[0:65, ocol], vap, PT[pt][:, kt * 128 + pcol.start:kt * 128 + pcol.stop],
                                 reads=[b_PT[pt], vb], writes=[b_pO], start=first[bank], stop=False,
                                 signal=(i == nmm), skip_group_check=True)
                            first[bank] = False

                full = [(None, slice(0, 128))]
                for jb in range(16):
                    qsl = slice(jb * 128, (jb + 1) * 128)
                    b_V_cur = b_V[0][ks]
                    cur = V[0][ks][:, jb, h, :]
                    prev = None
                    if jb > 0:
                        prev = (ks, slice((jb - 1) * 128, jb * 128), V[0][ks][:, jb - 1, h, :])
                        b_V_prev = b_V[0][ks]
                    elif st > 0:
                        prev = (kp, slice(15 * 128, 16 * 128), V[0][kp][:, 15, h, :])
                        b_V_prev = b_V[0][kp]
                    block(qsl, cur, prev, [(slice(jb * 128, (jb + 1) * 128), slice(0, 128))])
                for n4 in range(4):
                    for r in range(4):
                        qsl = slice(n4 * 512 + r, (n4 + 1) * 512, 4)
                        b_V_cur = b_V[1][ks]
                        cur = V[1][ks][:, n4 * 4 + r, h, :]
                        prev = None
                        if n4 > 0:
                            prev = (ks, slice((n4 - 1) * 512 + r, n4 * 512, 4), V[1][ks][:, (n4 - 1) * 4 + r, h, :])
                            b_V_prev = b_V[1][ks]
                        elif st > 0:
                            prev = (kp, slice(3 * 512 + r, ST, 4), V[1][kp][:, 12 + r, h, :])
                            b_V_prev = b_V[1][kp]
                        block(qsl, cur, prev, [(qsl, slice(0, 128))])
                for r in range(16):
                    qsl = slice(r, ST, 16)
                    b_V_cur = b_V[2][ks]
                    cur = V[2][ks][:, r, h, :]
                    prev = None
                    if st > 0:
                        prev = (kp, qsl, V[2][kp][:, r, h, :])
                        b_V_prev = b_V[2][kp]
                    block(qsl, cur, prev, [(slice(b4 * 512 + r, (b4 + 1) * 512, 16), slice(32 * b4, 32 * b4 + 32))
                                           for b4 in range(4)])
                for b4 in range(4):
                    cs = slice(b4 * 512, (b4 + 1) * 512)
                    eng = "act" if b4 % 2 == 0 else "dve"
                    if eng == "act":
                        S.op("act", lambda e, cs=cs: e.copy(oacc[0:65, cs], pO[0:65, cs]), reads=[b_pO], writes=[b_oacc])
                    else:
                        S.op("dve", lambda e, cs=cs: e.tensor_copy(oacc[0:65, cs], pO[0:65, cs]), reads=[b_pO],
                             writes=[b_oacc])
                S.op("dve", lambda e: e.reciprocal(oacc[64:65, :], oacc[64:65, :]), reads=[b_oacc], writes=[b_oacc])
                for b4 in range(4):
                    cs = slice(b4 * 512, (b4 + 1) * 512)
                    S.mm(pO[0:64, cs], ones[64:65, 0:64], oacc[64:65, cs], reads=[b_ones, b_oacc], writes=[b_pO],
                         signal=(b4 == 3))
                os_ = (st * 4 + h) % 2
                for b4 in range(4):
                    cs = slice(b4 * 512, (b4 + 1) * 512)
                    S.op("dve", lambda e, cs=cs, os_=os_: e.tensor_tensor(oT[os_][:, cs], pO[0:64, cs], oacc[0:64, cs],
                                                                       ALU.mult),
                         reads=[b_pO, b_oacc], writes=[b_oT[os_]])
                for xk in range(ST // XCH):
                    S.dma("sp", D["omix"][st * (ST // XCH) + xk][h * 64:(h + 1) * 64, :],
                          oT[os_][:, xk * XCH:(xk + 1) * XCH], reads=[b_oT[os_]])
        S.barrier()
        S.run()


def _TT(o, a, b, op):
    return lambda e: e.tensor_tensor(o, a, b, op)


def _TS(o, a, s1, s2, op0, op1=None):
    if op1 is None:
        return lambda e: e.tensor_scalar(o, a, s1, s2, op0)
    return lambda e: e.tensor_scalar(o, a, s1, s2, op0, op1)


def _STT(o, a, s, b, op0, op1):
    return lambda e: e.scalar_tensor_tensor(o, a, s, b, op0, op1)


def _ACT(o, i, f, **kw):
    return lambda e: e.activation(o, i, f, **kw)


def _CP(o, i):
    return lambda e: e.copy(o, i)


def _TC(o, i):
    return lambda e: e.tensor_copy(o, i)


def _TR(o, i, ident):
    return lambda e: e.transpose(o, i, ident)


def phase_b(nc, S, D):
    STB = 1024
    NSTB = SEQ // STB
    TPS = STB // 128
    DS = DECAY_SCALE
    with ExitStack() as ph:
        def sb(n, shp, dt=F32):
            return ph.enter_context(nc.sbuf_tensor(n, shp, dt))

        def ps(n, shp, dt=F32):
            return ph.enter_context(nc.psum_tensor(n, shp, dt))

        identf = sb("b_identf", [128, 128])
        b_identf = Buf()
        S.dma("sp", identf[:], D["ident"], writes=[b_identf])
        cst = sb("b_cst", [128, 642])
        b_cst = Buf()
        S.dma("sp", cst[:], D["cstB"], writes=[b_cst])
        triI, triS, triA = cst[:, 0:128], cst[:, 128:256], cst[:, 256:384]
        chunkind = cst[:, 384:386]
        mask1, mask3, eye2 = cst[:, 386:514], cst[:, 514:578], cst[:, 578:642]
        vec = sb("b_vec", [128, 8, 256])
        b_vec = Buf()
        for i, nm in enumerate(("w0", "a0", "k_k", "k_a", "k_a", "r_k", "gn_g", "gn_b")):
            S.dma("sp", vec[:, i, :], bcast_rows(D[nm]), writes=[b_vec], acc=(i > 0))
        S.op("dve", _TS(vec[:, 4, :], vec[:, 4, :], -1.0, 1.0, ALU.mult, ALU.add), reads=[b_vec], writes=[b_vec])
        bias_wa = vec[:, 0:2, :].rearrange("p a b -> p (a b)")
        wdec = sb("b_wdec", [128, 512])
        b_wdec = Buf()
        S.op("dve", lambda e: e.memset(wdec[:], 0.0), writes=[b_wdec])
        S.dma("sp", wdec[0:64, 0:256], D["w_dec"], writes=[b_wdec], acc=True)
        S.dma("sp", wdec[64:128, 256:512], D["w_aaa"], writes=[b_wdec], acc=True)
        wgate = sb("b_wgate", [128, 256])
        b_wgate = Buf()
        S.dma("sp", wgate[:], D["w_gate"], writes=[b_wgate])
        epsg = sb("b_epsg", [128, 1])
        b_epsg = Buf()
        S.op("dve", lambda e: e.memset(epsg[:], GN_EPS), writes=[b_epsg])
        mub = sb("b_mub", [128, 1024])
        b_mub = Buf()
        S.dma("sp", mub[:], bcast_rows(D["mu"]), writes=[b_mub])
        omub = sb("b_omub", [128, 1024])
        b_omub = Buf()
        S.op("dve", _TS(omub[:], mub[:], -1.0, 1.0, ALU.mult, ALU.add), reads=[b_mub], writes=[b_omub])
        W1 = sb("b_W1", [128, 8, 1024], BF16)
        W2 = sb("b_W2", [128, 8, 1024], BF16)
        b_W = Buf()
        wst = [sb("b_wst%d" % i, [128, 1024]) for i in range(2)]
        b_wst = [Buf() for _ in range(2)]
        for c in range(8):
            S.dma("sp", wst[c % 2][:], D["w_rw"][c * 128:(c + 1) * 128, :], writes=[b_wst[c % 2]])
            S.op("dve", _TT(W1[:, c, :], wst[c % 2][:], omub[:], ALU.mult), reads=[b_wst[c % 2], b_omub], writes=[b_W])
            S.op("pool", _TT(W2[:, c, :], wst[c % 2][:], mub[:], ALU.mult), reads=[b_wst[c % 2], b_mub], writes=[b_W])

        xTv = D["xT"].rearrange("(c p) t -> p c t", p=128)
        xc = sb("b_xc", [128, 8, STB], BF16)
        xp = sb("b_xp", [128, 8, STB], BF16)
        b_xc = Buf()
        b_xp = Buf()

        Hs = sb("b_Hs", [128, 4, 64])
        b_H = Buf()
        S.op("dve", lambda e: e.memset(Hs[:], 0.0), writes=[b_H])

        def t256(n):
            return sb(n, [128, 256]), Buf()
        tl, b_tl = sb("b_tl", [128, 256]), Buf()
        lT, b_lT = sb("b_lT", [128, 256]), Buf()
        lg, b_lg = sb("b_lg", [128, 512]), Buf()
        sg, b_sg = sb("b_sg", [128, 512]), Buf()
        gs, b_gs = t256("b_gs")
        yA = sb("b_yA", [128, 512])
        b_rs = Buf()
        rs = yA[:, 0:256]
        krs = yA[:, 256:512]
        vs, b_vs = t256("b_vs")
        kk, b_kk = t256("b_kk")
        t1, b_t1 = t256("b_t1")
        t2, b_t2 = t256("b_t2")
        km, b_km = t256("b_km")
        bb, b_bb = t256("b_bb")
        E, b_E = sb("b_E", [128, 4, 256]), [Buf() for _ in range(4)]
        X4, b_X4 = sb("b_X4", [128, 4, 256]), Buf()
        BK, b_BK = sb("b_BK", [128, 2, 256]), Buf()
        XT, b_XT = sb("b_XT", [128, 4, 4, 64]), Buf()
        dgP, b_dgP = sb("b_dgP", [128, 4, 64]), Buf()
        AM1, b_AM1 = sb("b_AM1", [128, 4, 128]), Buf()
        AM2, b_AM2 = sb("b_AM2", [128, 4, 128]), Buf()
        Lm = [sb("b_L%d" % i, [128, 4, 2, 64]) for i in range(2)]
        b_Lm = [Buf() for _ in range(2)]
        Pm, b_Pm = sb("b_Pm", [128, 4, 64]), Buf()
        sm, b_sm = sb("b_sm", [128, 32]), Buf()
        PC, b_PC = sb("b_PC", [128, 4, 2]), Buf()
        Ws, b_Ws = t256("b_Ws")
        Us, b_Us = t256("b_Us")
        on, b_on = t256("b_on")
        gst, b_gst = sb("b_gst", [128, 4, 6]), Buf()
        gmv, b_gmv = sb("b_gmv", [128, 12]), Buf()
        orT = sb("b_orT", [128, 2, STB], BF16)
        b_orT = Buf()

        K = [ps("b_K%d" % i, [128, 512]) for i in range(8)]
        b_K = [Buf(psum=True) for _ in range(8)]

        def v3(ap):
            return ap.rearrange("p (h d) -> p h d", d=64)

        def bc(ap4):
            return ap4.unsqueeze(2).to_broadcast([128, 4, 64])

        for n in range(B_TILES):
            st, j = n // TPS, n % TPS
            if j == 0:
                S.dma("pool", xc[:], xTv[:, :, st * STB:(st + 1) * STB], writes=[b_xc])
                if st == 0:
                    S.op("dve", lambda e: e.memset(xp[:, :, 0:1], 0.0), writes=[b_xp])
                    S.dma("pool", xp[:, :, 1:STB], xTv[:, :, 0:STB - 1], writes=[b_xp], acc=True)
                else:
                    S.dma("pool", xp[:], xTv[:, :, st * STB - 1:(st + 1) * STB - 1], writes=[b_xp])
            tsl = slice(j * 128, (j + 1) * 128)
            for bk in range(2):
                cs = slice(bk * 512, (bk + 1) * 512)
                for c in range(8):
                    S.mm(K[bk][:], xc[:, c, tsl], W1[:, c, cs], reads=[b_xc, b_W], writes=[b_K[bk]],
                         start=(c == 0), stop=False, signal=False)
                for c in range(8):
                    S.mm(K[bk][:], xp[:, c, tsl], W2[:, c, cs], reads=[b_xp, b_W], writes=[b_K[bk]],
                         start=False, stop=(c == 7), signal=(c == 7))
            pA, pB = K[0], K[1]
            S.op("act", _ACT(tl[:, 0:64], pB[:, 256:320], AF.Tanh), reads=[b_K[1]], writes=[b_tl])
            S.op("act", _CP(tl[:, 64:128], pB[:, 320:384]), reads=[b_K[1]], writes=[b_tl])
            S.op("act", _ACT(tl[:, 128:256], pB[:, 384:512], AF.Sigmoid), reads=[b_K[1]], writes=[b_tl])
            S.op("act", _CP(vs[:], pB[:, 0:256]), reads=[b_K[1]], writes=[b_vs])
            S.op("act", _CP(yA[:], pA[:]), reads=[b_K[0]], writes=[b_rs])
            S.op("pe", _TR(K[3][:, 0:128], tl[:, 0:128], identf[:]), reads=[b_tl, b_identf], writes=[b_K[3]], signal=False)
            S.op("pe", _TR(K[3][:, 128:256], tl[:, 128:256], identf[:]), reads=[b_tl, b_identf], writes=[b_K[3]])
            S.op("dve", _TC(lT[:], K[3][:, 0:256]), reads=[b_K[3]], writes=[b_lT])
            S.mm(K[2][:], lT[:, 0:128], wdec[:], reads=[b_lT, b_wdec], writes=[b_K[2]], signal=False)
            S.mm(K[3][:, 256:512], lT[:, 128:256], wgate[:], reads=[b_lT, b_wgate], writes=[b_K[3]])
            S.op("dve", _TT(lg[:], K[2][:], bias_wa, ALU.add), reads=[b_K[2], b_vec], writes=[b_lg])
            S.op("act", _ACT(sg[:], lg[:], AF.Sigmoid), reads=[b_lg], writes=[b_sg])
            S.op("act", _CP(gs[:], K[3][:, 256:512]), reads=[b_K[3]], writes=[b_gs])
            sw, aa = sg[:, 0:256], sg[:, 256:512]
            S.mm(K[4][:, 0:256], triS, sw, reads=[b_cst, b_sg], writes=[b_K[4]], signal=False)
            S.mm(K[4][:, 256:512], triI, sw, reads=[b_cst, b_sg], writes=[b_K[4]], signal=False)
            S.mm(K[5][:, 0:256], triA, sw, reads=[b_cst, b_sg], writes=[b_K[5]], signal=False)
            for c in range(2):
                rows = slice(64 * c, 64 * c + 64)
                for h in range(4):
                    S.mm(K[5][rows, 256 + 2 * h:258 + 2 * h], sg[rows, h * 64:(h + 1) * 64], cst[rows, 384:386],
                         reads=[b_cst, b_sg], writes=[b_K[5]], signal=(c == 1 and h == 3))
            S.op("act", _ACT(E[:, 0, :], K[4][:, 0:256], AF.Exp, scale=-DS), reads=[b_K[4]], writes=[b_E[0]])
            S.op("act", _ACT(E[:, 1, :], K[4][:, 256:512], AF.Exp, scale=-DS), reads=[b_K[4]], writes=[b_E[1]])
            S.op("act", _ACT(E[:, 2, :], K[4][:, 256:512], AF.Exp, scale=DS), reads=[b_K[4]], writes=[b_E[2]])
            S.op("act", _ACT(E[:, 3, :], K[5][:, 0:256], AF.Exp, scale=-DS), reads=[b_K[5]], writes=[b_E[3]])
            S.op("act", _ACT(PC[:], K[5][:, 256:264], AF.Exp, scale=-DS), reads=[b_K[5]], writes=[b_PC])
            S.op("dve", _TT(kk[:], krs, vec[:, 2, :], ALU.mult), reads=[b_rs, b_vec], writes=[b_kk])
            S.op("pool", _TT(t1[:], kk[:], kk[:], ALU.mult), reads=[b_kk], writes=[b_t1])
            S.op("dve", lambda e: e.tensor_reduce(sm[:, 0:4], v3(t1[:]), AX.X, ALU.add), reads=[b_t1], writes=[b_sm])
            S.op("act", _ACT(sm[:, 4:8], sm[:, 0:4], AF.Sqrt), reads=[b_sm], writes=[b_sm])
            S.op("dve", _TS(sm[:, 4:8], sm[:, 4:8], L2_EPS, None, ALU.max), reads=[b_sm], writes=[b_sm])
            S.op("dve", lambda e: e.reciprocal(sm[:, 8:12], sm[:, 4:8]), reads=[b_sm], writes=[b_sm])
            S.op("dve", _TT(v3(kk[:]), v3(kk[:]), bc(sm[:, 8:12]), ALU.mult), reads=[b_kk, b_sm], writes=[b_kk])
            S.op("pool", _TT(t2[:], aa, vec[:, 3, :], ALU.mult), reads=[b_sg, b_vec], writes=[b_t2])
            S.op("pool", _TT(t2[:], t2[:], vec[:, 4, :], ALU.add), reads=[b_t2, b_vec], writes=[b_t2])
            S.op("dve", _TT(km[:], krs, t2[:], ALU.mult), reads=[b_rs, b_t2], writes=[b_km])
            S.op("pool", _TT(bb[:], kk[:], aa, ALU.mult), reads=[b_kk, b_sg], writes=[b_bb])
            S.op("pool", _TT(t1[:], rs, km[:], ALU.mult), reads=[b_rs, b_km, b_sm], writes=[b_t1])
            S.op("pool", _TT(t1[:], t1[:], vec[:, 5, :], ALU.mult), reads=[b_t1, b_vec], writes=[b_t1])
            S.op("dve", lambda e: e.tensor_reduce(sm[:, 12:16], v3(t1[:]), AX.X, ALU.add), reads=[b_t1], writes=[b_sm])
            S.op("dve", _STT(X4[:, 0, :], kk[:], -1.0, E[:, 0, :], ALU.mult, ALU.mult), reads=[b_kk, b_E[0]], writes=[b_X4])
            S.op("pool", _TT(X4[:, 1, :], rs, E[:, 1, :], ALU.mult), reads=[b_rs, b_E[1]], writes=[b_X4])
            S.op("dve", _TT(X4[:, 2, :], bb[:], E[:, 2, :], ALU.mult), reads=[b_bb, b_E[2]], writes=[b_X4])
            S.op("pool", _TT(X4[:, 3, :], km[:], E[:, 2, :], ALU.mult), reads=[b_km, b_E[2]], writes=[b_X4])
            S.op("dve", _TT(BK[:, 0, :], bb[:], E[:, 3, :], ALU.mult), reads=[b_bb, b_E[3]], writes=[b_BK])
            S.op("pool", _TT(BK[:, 1, :], km[:], E[:, 3, :], ALU.mult), reads=[b_km, b_E[3]], writes=[b_BK])
            for c in range(2):
                rows = slice(64 * c, 64 * c + 64)
                for q in range(4):
                    for h in range(4):
                        blk = q * 4 + h
                        S.mm(K[4 + blk // 8][rows, (blk % 8) * 64:(blk % 8 + 1) * 64],
                             X4[rows, q, h * 64:(h + 1) * 64], identf[rows, rows],
                             reads=[b_X4, b_identf], writes=[b_K[4 + blk // 8]], signal=(c == 1 and blk % 8 == 7))
            XTf = XT[:].rearrange("p q h t -> p (q h t)")
            S.op("act", _CP(XTf[:, 0:512], K[4][:]), reads=[b_K[4]], writes=[b_XT])
            S.op("dve", _TC(XTf[:, 512:1024], K[5][:]), reads=[b_K[5]], writes=[b_XT])
            for c in range(2):
                rows = slice(64 * c, 64 * c + 64)
                S.op("pool", _TT(dgP[rows, :, :], cst[rows, 578:642].unsqueeze(1).to_broadcast([64, 4, 64]),
                                 PC[rows, :, c:c + 1].to_broadcast([64, 4, 64]), ALU.mult),
                     reads=[b_PC, b_cst], writes=[b_dgP])
            for c in range(2):
                rows = slice(64 * c, 64 * c + 64)
                for h in range(4):
                    last = (c == 1 and h == 3)
                    S.mm(K[2][rows, h * 128:(h + 1) * 128], XT[rows, 2, h, :], XT[rows, 0:2, h, :],
                         reads=[b_XT], writes=[b_K[2]], signal=False)
                    S.mm(K[3][rows, h * 128:(h + 1) * 128], XT[rows, 3, h, :], XT[rows, 0:2, h, :],
                         reads=[b_XT], writes=[b_K[3]], signal=False)
                    S.mm(K[6][rows, h * 64:(h + 1) * 64], XT[rows, 0, h, :], XT[rows, 2, h, :],
                         reads=[b_XT], writes=[b_K[6]], signal=last)
            m1b = mask1.unsqueeze(1).to_broadcast([128, 4, 128])
            S.op("dve", _TT(AM1[:], K[2][:].rearrange("p (h t) -> p h t", h=4), m1b, ALU.mult),
                 reads=[b_K[2], b_cst], writes=[b_AM1])
            S.op("dve", _TT(AM2[:], K[3][:].rearrange("p (h t) -> p h t", h=4), m1b, ALU.mult),
                 reads=[b_K[3], b_cst], writes=[b_AM2])
            m3b = mask3.unsqueeze(1).to_broadcast([128, 4, 64])
            S.op("dve", _TT(Lm[0][:, :, 1, :], K[6][:, 0:256].rearrange("p (h t) -> p h t", h=4), m3b, ALU.mult),
                 reads=[b_K[6], b_cst], writes=[b_Lm[0]])
            S.op("pool", _TC(Lm[0][:, :, 0, :], AM1[:, :, 0:64]), reads=[b_AM1], writes=[b_Lm[0]])
            S.op("pool", _TT(Pm[:], AM1[:, :, 0:64], eye2.unsqueeze(1).to_broadcast([128, 4, 64]), ALU.add),
                 reads=[b_AM1, b_cst], writes=[b_Pm])
            cur = 0
            for lvl in range(5):
                nxt = 1 - cur
                lastlvl = (lvl == 4)
                for c in range(2):
                    rows = slice(64 * c, 64 * c + 64)
                    for h in range(4):
                        last = (c == 1 and h == 3)
                        Lc, LTc = Lm[cur][rows, h, 0, :], Lm[cur][rows, h, 1, :]
                        if not lastlvl:
                            S.mm(K[6][rows, (h * 2) * 64:(h * 2 + 1) * 64], LTc, Lc, reads=[b_Lm[cur]],
                                 writes=[b_K[6]], signal=False)
                        S.mm(K[6][rows, (h * 2 + 1) * 64:(h * 2 + 2) * 64], Lc, LTc, reads=[b_Lm[cur]],
                             writes=[b_K[6]], signal=last)
                eng = "act" if lvl % 2 == 0 else "dve"
                src = K[6][:].rearrange("p (h a t) -> p h a t", h=4, a=2)
                if lastlvl:
                    S.op("dve", _TC(Lm[nxt][:, :, 1, :], src[:, :, 1, :]), reads=[b_K[6]], writes=[b_Lm[nxt]])
                elif eng == "act":
                    S.op("act", _CP(Lm[nxt][:], src), reads=[b_K[6]], writes=[b_Lm[nxt]])
                else:
                    S.op("dve", _TC(Lm[nxt][:], src), reads=[b_K[6]], writes=[b_Lm[nxt]])
                for c in range(2):
                    rows = slice(64 * c, 64 * c + 64)
                    for h in range(4):
                        last = (c == 1 and h == 3)
                        S.mm(K[7][rows, h * 64:(h + 1) * 64], Lm[nxt][rows, h, 1, :], Pm[rows, h, :],
                             reads=[b_Lm[nxt], b_Pm], writes=[b_K[7]], signal=last)
                S.op("dve", _TT(Pm[:], K[7][:, 0:256].rearrange("p (h t) -> p h t", h=4), Pm[:], ALU.add),
                     reads=[b_K[7], b_Pm], writes=[b_Pm])
                cur = nxt
            pO_ = K[7]
            for c in range(2):
                rows = slice(64 * c, 64 * c + 64)
                orow = slice(64 * (1 - c), 64 * (1 - c) + 64)
                for h in range(4):
                    hc = slice(256 + h * 64, 256 + (h + 1) * 64)
                    vh = vs[rows, h * 64:(h + 1) * 64]
                    S.mm(K[6][rows, hc], AM2[rows, h, 0:64], vh, reads=[b_AM2, b_vs], writes=[b_K[6]],
                         start=True, stop=False, signal=False)
                    S.mm(K[6][rows, hc], XT[rows, 0, h, :], Hs[rows, h, :], reads=[b_XT, b_H], writes=[b_K[6]],
                         start=False, stop=True, signal=(h == 3))
                S.op("dve", _TC(Ws[rows, :], K[6][rows, 256:512]), reads=[b_K[6]], writes=[b_Ws])
                for h in range(4):
                    hc = slice(h * 64, (h + 1) * 64)
                    S.mm(K[6][rows, hc], Pm[rows, h, :], Ws[rows, hc], reads=[b_Pm, b_Ws], writes=[b_K[6]],
                         signal=(h == 3))
                S.op("act", _CP(Us[rows, :], K[6][rows, 0:256]), reads=[b_K[6]], writes=[b_Us])
                for h in range(4):
                    hc = slice(256 + h * 64, 256 + (h + 1) * 64)
                    vh = vs[rows, h * 64:(h + 1) * 64]
                    uh = Us[rows, h * 64:(h + 1) * 64]
                    S.mm(pO_[rows, hc], AM2[rows, h, 64:128], vh, reads=[b_AM2, b_vs], writes=[b_K[7]],
                         start=True, stop=False, signal=False)
                    S.mm(pO_[rows, hc], XT[rows, 1, h, :], Hs[rows, h, :], reads=[b_XT, b_H], writes=[b_K[7]],
                         start=False, stop=False, signal=False)
                    S.mm(pO_[rows, hc], AM1[rows, h, 64:128], uh, reads=[b_AM1, b_Us], writes=[b_K[7]],
                         start=False, stop=True, signal=False)
                for h in range(4):
                    oc_ = slice(256 + h * 64, 256 + (h + 1) * 64)
                    vh = vs[rows, h * 64:(h + 1) * 64]
                    uh = Us[rows, h * 64:(h + 1) * 64]
                    S.mm(K[5][orow, oc_], BK[rows, 1, h * 64:(h + 1) * 64], vh, reads=[b_BK, b_vs], writes=[b_K[5]],
                         start=True, stop=False, signal=False)
                    S.mm(K[5][orow, oc_], dgP[rows, h, :], Hs[rows, h, :], reads=[b_dgP, b_H], writes=[b_K[5]],
                         start=False, stop=False, signal=False)
                    S.mm(K[5][orow, oc_], BK[rows, 0, h * 64:(h + 1) * 64], uh, reads=[b_BK, b_Us], writes=[b_K[5]],
                         start=False, stop=True, signal=(h == 3))
                S.op("act", _CP(Hs[orow, :, :], K[5][orow, 256:512].rearrange("p (h v) -> p h v", h=4)),
                     reads=[b_K[5], b_K[7]], writes=[b_H])
            o3 = pO_[:, 256:512].rearrange("p (h d) -> p h d", d=64)
            S.op("act", _ACT(t2[:], pO_[:, 256:512], AF.Square), reads=[b_K[7], b_t2], writes=[b_t2])
            S.op("dve", lambda e, o3=o3: e.tensor_reduce(sm[:, 24:28], o3, AX.X, ALU.add), reads=[b_K[7]], writes=[b_sm])
            S.op("dve", lambda e: e.tensor_reduce(sm[:, 28:32], v3(t2[:]), AX.X, ALU.add), reads=[b_t2], writes=[b_sm])
            S.op("dve", _TS(gmv[:, 0:4], sm[:, 24:28], 1.0 / 64.0, None, ALU.mult), reads=[b_sm], writes=[b_gmv])
            S.op("dve", _TT(gmv[:, 4:8], gmv[:, 0:4], gmv[:, 0:4], ALU.mult), reads=[b_gmv], writes=[b_gmv])
            S.op("dve", _STT(gmv[:, 8:12], sm[:, 28:32], 1.0 / 64.0, gmv[:, 4:8], ALU.mult, ALU.subtract),
                 reads=[b_sm, b_gmv], writes=[b_gmv])
            S.op("act", _ACT(sm[:, 16:20], gmv[:, 8:12], AF.Sqrt, bias=epsg[:, 0:1], scale=1.0),
                 reads=[b_gmv, b_epsg], writes=[b_sm])
            S.op("dve", lambda e: e.reciprocal(sm[:, 20:24], sm[:, 16:20]), reads=[b_sm], writes=[b_sm])
            S.op("dve", _TT(v3(on[:]), o3, bc(gmv[:, 0:4]), ALU.subtract),
                 reads=[b_K[7], b_gmv], writes=[b_on])
            S.op("pool", _TT(v3(on[:]), v3(on[:]), bc(sm[:, 20:24]), ALU.mult), reads=[b_on, b_sm], writes=[b_on])
            S.op("pool", _TT(on[:], on[:], vec[:, 6, :], ALU.mult), reads=[b_on, b_vec], writes=[b_on])
            S.op("pool", _TT(on[:], on[:], vec[:, 7, :], ALU.add), reads=[b_on, b_vec], writes=[b_on])
            S.op("dve", _TT(v3(t1[:]), v3(vs[:]), bc(sm[:, 12:16]), ALU.mult), reads=[b_vs, b_sm, b_t1], writes=[b_t1])
            S.op("pool", _TT(on[:], on[:], t1[:], ALU.add), reads=[b_on, b_t1], writes=[b_on])
            S.op("pool", _TT(on[:], on[:], gs[:], ALU.mult), reads=[b_on, b_gs], writes=[b_on])
            for hp in range(2):
                S.op("pe", _TR(K[5][:, hp * 128:(hp + 1) * 128], on[:, hp * 128:(hp + 1) * 128], identf[:]),
                     reads=[b_on, b_identf], writes=[b_K[5]], signal=(hp == 1))
            S.op("act", _CP(orT[:, :, tsl], K[5][:, 0:256].rearrange("p (a t) -> p a t", a=2)),
                 reads=[b_K[5]], writes=[b_orT])
            if j == TPS - 1 or n == B_TILES - 1:
                for hp in range(2):
                    S.dma("sp", D["omix"][st][256 + hp * 128:256 + (hp + 1) * 128, :],
                          orT[:, hp, :], reads=[b_orT])
        S.barrier()
        S.run()


def exchange(nc, S, D, stack):
    ccs = stack.enter_context(nc.semaphore("s_cc"))
    groups = [[0, 1], [2, 3], [4, 5], [6, 7]]
    for k in range(NXCH):
        S.q["pool"].append(lambda e, k=k: e.collective_compute(
            "AllGather", ALU.bypass, replica_groups=groups, ins=[D["omix"][k]], outs=[D["G"][k]]).then_inc(ccs, 1))
    S.extra.append((ccs, NXCH, "s_cc", "cc"))
    S.barrier()
    S.run()


def build_program(phases="ABC", exch=True):
    nc = bass.Bass("TRN2", target_bir_lowering=False)
    D = {}

    def din(name, shape, dt=F32):
        D[name] = nc.dram_tensor(name, list(shape), dt, kind="ExternalInput").ap()

    din("ident", [128, 128])
    if "A" in phases or "B" in phases:
        din("xT", [D_MODEL, SEQ])
    if "A" in phases:
        din("pos", [128, 64], I32)
        din("w_att", [D_MODEL, 768])
        din("maskT", [128, 256])
    if "B" in phases:
        din("w_rw", [D_MODEL, 1024])
        din("mu", [1, 1024])
        for nm in ("w0", "a0", "k_k", "k_a", "r_k", "gn_g", "gn_b"):
            din(nm, [1, 256])
        din("w_dec", [64, 256])
        din("w_aaa", [64, 256])
        din("w_gate", [128, 256])
        din("cstB", [128, 642])
    if "C" in phases:
        din("xres", [TOKH, D_MODEL])
        din("w_out", [1024, 1024])
        din("wg", [1024, FFN])
        din("wu", [1024, FFN])
        din("wd", [FFN, 1024])
        for nm in ("ln1g", "ln1b", "ln2g", "ln2b"):
            din(nm, [1, 1024])
        din("sel", [128, 2])
        D["out"] = nc.dram_tensor("out", [TOKH, D_MODEL], F32, kind="ExternalOutput").ap()
    full = exch
    if full:
        D["omix"] = [nc.dram_tensor("omix%d" % k, [512, XCH], BF16, kind="Internal").ap() for k in range(NXCH)]
        D["G"] = [nc.dram_tensor("G%d" % k, [1024, XCH], BF16, kind="Internal").ap() for k in range(NXCH)]
    else:
        if "C" in phases:
            din("G", [NXCH, 1024, XCH], BF16)
            D["G"] = [D["G"][k] for k in range(NXCH)]
        if "A" in phases or "B" in phases:
            om = nc.dram_tensor("omix", [NXCH, 512, XCH], BF16, kind="ExternalOutput").ap()
            D["omix"] = [om[k] for k in range(NXCH)]
    with ExitStack() as st:
        S = Sched(nc, st)
        if "A" in phases:
            phase_a(nc, S, D)
        if "B" in phases:
            phase_b(nc, S, D)
        if full:
            exchange(nc, S, D, st)
        if "C" in phases:
            phase_c(nc, S, D)
    return nc


def att_mask():
    i_k = np.arange(128)[:, None]
    i_q = np.arange(128)[None, :]
    m = np.zeros((128, 256), np.float32)
    m[:, 0:128] = np.where(i_k >= i_q, 0.0, NEG)
    m[:, 128:256] = np.where(i_k <= i_q, 0.0, NEG)
    return m


def rwkv_consts():
    j = np.arange(128)[:, None]
    t = np.arange(128)[None, :]
    same = (j // 64) == (t // 64)
    c = np.zeros((128, 642), np.float32)
    c[:, 0:128] = same & (j <= t)
    c[:, 128:256] = same & (j < t)
    c[:, 256:384] = same & (j > t)
    c[:, 384] = (np.arange(128) < 64)
    c[:, 385] = (np.arange(128) >= 64)
    jj = (np.arange(128) % 64)[:, None]
    tt = np.arange(64)[None, :]
    c[:, 386:450] = jj < tt
    c[:, 450:514] = jj <= tt
    c[:, 514:578] = tt < jj
    c[:, 578:642] = jj == tt
    return c


def core_inputs(inp, c, phases="ABC"):
    b, g = c // 2, c % 2
    m = {"ident": np.eye(128, dtype=np.float32)}
    w_in = inp["w_in"][0]
    if "A" in phases or "B" in phases:
        m["xT"] = np.ascontiguousarray(inp["x"][b].T)
    if "A" in phases:
        m["pos"] = np.ascontiguousarray(inp["positions"][b].reshape(64, 128).T).astype(np.int32)
        cols = np.concatenate([np.arange(256 * g, 256 * g + 256) + off for off in (0, 512, 1024)])
        m["w_att"] = np.ascontiguousarray(w_in[:, cols])
        m["maskT"] = att_mask()
    if "B" in phases:
        hs = slice(256 * g, 256 * g + 256)
        rcols = np.concatenate([1536 + off + np.arange(256 * g, 256 * g + 256) for off in (0, 512, 1024)]
                               + [1536 + 1536 + np.arange(256)])
        m["w_rw"] = np.ascontiguousarray(w_in[:, rcols])
        m["mu"] = np.ascontiguousarray(inp["mu_shift"][0][rcols - 1536][None, :])
        for nm in ("w0", "a0", "k_k", "k_a", "gn_g", "gn_b"):
            m[nm] = np.ascontiguousarray(inp[nm][0][hs][None, :])
        m["r_k"] = np.ascontiguousarray(inp["r_k"][0][4 * g:4 * g + 4].reshape(1, 256))
        m["w_dec"] = np.ascontiguousarray(inp["w_decay_up"][0][:, hs])
        m["w_aaa"] = np.ascontiguousarray(inp["w_aaa_up"][0][:, hs])
        m["w_gate"] = np.ascontiguousarray(inp["w_gate_up"][0][:, hs])
        m["cstB"] = rwkv_consts()
    if "C" in phases:
        fi = lambda r: np.concatenate([np.arange(256 * r, 256 * r + 256), 512 + np.arange(256 * r, 256 * r + 256)])
        perm = np.concatenate([fi(0), fi(1)])
        sel = np.zeros((128, 2), np.float32)
        sel[:, g] = 1.0
        m.update(xres=np.ascontiguousarray(inp["x"][b, g * TOKH:(g + 1) * TOKH]),
                 w_out=np.ascontiguousarray(inp["w_out"][0][perm]),
                 wg=inp["w_ffn_gate"][0], wu=inp["w_ffn_up"][0], wd=inp["w_ffn_down"][0],
                 ln1g=inp["ln_mix_g"], ln1b=inp["ln_mix_b"], ln2g=inp["ln_ffn_g"], ln2b=inp["ln_ffn_b"],
                 sel=sel)
    return m


_NC_CACHE = {}


def kernel(**inputs):
    inp = {k: np.asarray(v) for k, v in inputs.items()}
    if "nc" not in _NC_CACHE:
        _NC_CACHE["nc"] = build_program("ABC")
    nc = _NC_CACHE["nc"]
    in_maps = [core_inputs(inp, c, "ABC") for c in range(8)]
    res = run_bass_kernel_spmd(nc, in_maps, core_ids=list(range(8)))
    out = np.empty((BATCH, SEQ, D_MODEL), np.float32)
    for c in range(8):
        b, g = c // 2, c % 2
        out[b, g * TOKH:(g + 1) * TOKH] = np.asarray(res.results[c]["out"], dtype=np.float32)
    return out
```

```python
import math
from contextlib import ExitStack

import numpy as np
import concourse.bass as bass
import concourse.mybir as mybir
from concourse.bass_utils import run_bass_kernel_spmd

F32 = mybir.dt.float32
BF16 = mybir.dt.bfloat16
I32 = mybir.dt.int32
AF = mybir.ActivationFunctionType
ALU = mybir.AluOpType
AX = mybir.AxisListType

D_MODEL = 1024
SEQ = 8192
BATCH = 4
HD = 64
FFN = 2816
HC = FFN // 128
ALPHA = 2.0 ** 0.25
LN_EPS = 1e-5
GN_EPS = 64e-5
L2_EPS = 1e-6
DECAY_SCALE = math.exp(-0.5)
NEG = -30000.0
ROPE_THETA = 500000.0
TOKH = SEQ // 2
XCH = 1024
NXCH = SEQ // XCH
B_TILES = SEQ // 128


class Buf:
    __slots__ = ("name", "w", "r", "psum")

    def __init__(self, name="", psum=False):
        self.name = name
        self.w = []
        self.r = {}
        self.psum = psum


class Sched:
    ENG = ("pe", "act", "dve", "pool", "sp")
    NDS = 8

    def __init__(self, nc, stack):
        self.nc = nc
        self.q = {e: [] for e in self.ENG}
        self.sem = {e: stack.enter_context(nc.semaphore("s_" + e)) for e in self.ENG}
        self.cnt = {e: 0 for e in self.ENG}
        self.seen = {e: {} for e in self.ENG}
        self.dq = ("sp", "act", "pool")
        self.dsem = {e: [stack.enter_context(nc.semaphore("d_%s%d" % (e, i)))
                         for i in range(self.NDS)] for e in self.dq}
        self.dcnt = {e: 0 for e in self.dq}
        self.dlast = {e: [None] * self.NDS for e in self.dq}
        self.extra = []

    def _wait(self, eng, tok):
        if tok is None:
            return
        sem, val, key, prod = tok
        if self.seen[eng].get(key, 0) >= val:
            return
        self.seen[eng][key] = val
        self.q[eng].append(lambda e, s=sem, v=val: e.wait_ge(s, v))

    def _deps(self, eng, reads, writes):
        for b in reads:
            for t in b.w:
                self._wait(eng, t)
            if b.psum:
                for t in b.r.values():
                    if t[3] != eng:
                        self._wait(eng, t)
        for b in writes:
            for t in b.w:
                if t[3] != eng:
                    self._wait(eng, t)
            for t in b.r.values():
                if t[3] != eng:
                    self._wait(eng, t)

    def _mark(self, tok, reads, writes, acc=False):
        for b in reads:
            b.r[tok[2]] = tok
        for b in writes:
            if acc:
                b.w.append(tok)
            else:
                b.w = [tok]
            b.r = {}

    def op(self, eng, fn, reads=(), writes=(), signal=True):
        self._deps(eng, reads, writes)
        sem = self.sem[eng]
        if signal:
            self.cnt[eng] += 1
            tok = (sem, self.cnt[eng], eng, eng)
            self.q[eng].append(lambda e, f=fn, s=sem: f(e).then_inc(s, 1))
        else:
            tok = (sem, self.cnt[eng] + 1, eng, eng)
            self.q[eng].append(lambda e, f=fn: f(e))
        self._mark(tok, reads, writes)
        return tok

    def mm(self, out, lhsT, rhs, reads, writes, start=True, stop=True, signal=True, **kw):
        return self.op("pe", lambda e: e.matmul(out, lhsT, rhs, start=start, stop=stop, **kw),
                       reads, writes, signal)

    def dma(self, queue, out, in_, reads=(), writes=(), acc=False, **kw):
        i = self.dcnt[queue]
        self.dcnt[queue] += 1
        slot = i % self.NDS
        self._wait(queue, self.dlast[queue][slot])
        self._deps(queue, reads, writes)
        sem = self.dsem[queue][slot]
        val = 16 * (i // self.NDS + 1)
        tok = (sem, val, "d_%s%d" % (queue, slot), "dma")
        self.dlast[queue][slot] = tok
        self.q[queue].append(
            lambda e, o=out, a=in_, s=sem, k=kw: e.dma_start(out=o, in_=a, **k).then_inc(s, 16))
        self._mark(tok, reads, writes, acc=acc)
        return tok

    def barrier(self):
        toks = [(self.sem[o], self.cnt[o], o, o) for o in self.ENG if self.cnt[o] > 0]
        for qn in self.dq:
            toks += [t for t in self.dlast[qn] if t is not None]
        toks += self.extra
        for e in self.ENG:
            for t in toks:
                if t[3] != e:
                    self._wait(e, t)

    def run(self):
        nc = self.nc
        q = self.q
        with nc.Block() as block:
            @block.tensor
            def _(e):
                for f in q["pe"]:
                    f(e)

            @block.scalar
            def _(e):
                for f in q["act"]:
                    f(e)

            @block.vector
            def _(e):
                for f in q["dve"]:
                    f(e)

            @block.gpsimd
            def _(e):
                for f in q["pool"]:
                    f(e)

            @block.sync
            def _(e):
                for f in q["sp"]:
                    f(e)
        self.q = {e: [] for e in self.ENG}


def bcast_rows(ap, n=128):
    return bass.AP(ap.tensor, ap.offset, [[0, n], [1, ap.shape[-1]]])


def prep_ffn_weights(nc, S, D):
    wgv = D["wg"].rearrange("(c p) n -> p c n", p=128)
    wuv = D["wu"].rearrange("(c p) n -> p c n", p=128)
    D["b_wgs"] = [Buf() for _ in range(HC)]
    for h in range(HC):
        S.dma("pool", D["wgs"][h, :, 0, :, :], wgv[:, :, h * 128:(h + 1) * 128], writes=[D["b_wgs"][h]])
        S.dma("pool", D["wgs"][h, :, 1, :, :], wuv[:, :, h * 128:(h + 1) * 128], writes=[D["b_wgs"][h]], acc=True)


def phase_c(nc, S, D):
    GT = 512
    NG = TOKH // GT
    TPG = GT // 128
    OCW = 256
    with ExitStack() as ph:
        def sb(n, shp, dt):
            return ph.enter_context(nc.sbuf_tensor(n, shp, dt))

        def ps(n, shp, dt):
            return ph.enter_context(nc.psum_tensor(n, shp, dt))

        Gv = [g_.rearrange("(c p) t -> p c t", p=128) for g_ in D["G"]]
        ident = sb("c_ident", [128, 128], BF16)
        b_ident = Buf()
        S.dma("pool", ident[:], D["ident"], writes=[b_ident])
        sel = sb("c_sel", [128, 2], F32)
        b_sel = Buf()
        S.dma("sp", sel[:], D["sel"], writes=[b_sel])
        woutA = sb("c_woutA", [128, 8, 1024], BF16)
        woutB = sb("c_woutB", [128, 8, 1024], BF16)
        b_wout = Buf()
        b_woutB = Buf()
        S.dma("pool", woutA[:], D["w_out"].rearrange("(c p) n -> p c n", p=128), writes=[b_wout])
        S.op("dve", lambda e: e.tensor_scalar(woutB[:], woutA[:], sel[:, 1:2], None, ALU.mult),
             reads=[b_wout, b_sel], writes=[b_woutB])
        S.op("dve", lambda e: e.tensor_scalar(woutA[:], woutA[:], sel[:, 0:1], None, ALU.mult),
             reads=[b_wout, b_sel, b_woutB], writes=[b_wout])
        wd = sb("c_wd", [128, HC, 1024], BF16)
        b_wd = [Buf() for _ in range(HC)]
        for h in range(HC):
            S.dma("pool", wd[:, h, :], D["wd"][h * 128:(h + 1) * 128, :], writes=[b_wd[h]])
        lnp = sb("c_lnp", [128, 4, 1024], F32)
        b_lnp = [Buf() for _ in range(4)]
        for i, nm in enumerate(("ln1g", "ln1b", "ln2g", "ln2b")):
            S.dma("sp", lnp[:, i, :], bcast_rows(D[nm]), writes=[b_lnp[i]])

        NW = 4
        wgu = [sb("c_wgu%d" % i, [128, 2, 8, 128], BF16) for i in range(NW)]
        b_wgu = [Buf() for _ in range(NW)]

        h1g = sb("c_h1g", [128, TPG, 1024], F32)
        b_h1g = [Buf() for _ in range(TPG)]
        h1T = sb("c_h1T", [128, 8, GT], BF16)
        b_h1T = [Buf() for _ in range(TPG)]
        actT = sb("c_actT", [128, HC, GT], BF16)
        b_actT = [Buf() for _ in range(HC)]
        NB = 2
        ocA = [sb("c_ocA%d" % i, [128, 8, OCW], BF16) for i in range(NB)]
        ocB = [sb("c_ocB%d" % i, [128, 8, OCW], BF16) for i in range(NB)]
        b_ocA = [Buf() for _ in range(NB)]
        b_ocB = [Buf() for _ in range(NB)]
        xt = [sb("c_xt%d" % i, [128, 1024], F32) for i in range(NB)]
        b_xt = [Buf() for _ in range(NB)]
        hpre = sb("c_hpre", [128, 1024], F32)
        b_hpre = Buf()
        hn = sb("c_hn", [128, 1024], F32)
        b_hn = Buf()
        h1b = sb("c_h1b", [128, 1024], BF16)
        b_h1b = Buf()
        stats = sb("c_stats", [128, 2, 6], F32)
        b_stats = Buf()
        mv = sb("c_mv", [128, 4], F32)
        b_mv = Buf()
        sg = [sb("c_sg%d" % i, [128, 512], BF16) for i in range(2)]
        b_sg = [Buf() for _ in range(2)]
        outt = [sb("c_outt%d" % i, [128, 1024], F32) for i in range(NB)]
        b_outt = [Buf() for _ in range(NB)]

        epsc = sb("c_eps", [128, 1], F32)
        b_epsc = Buf()
        S.op("dve", lambda e: e.memset(epsc[:], LN_EPS), writes=[b_epsc])
        pmix = ps("c_pmix", [128, 1024], F32)
        b_pmix = Buf(psum=True)
        pT = ps("c_pT", [128, 1024], BF16)
        b_pT = Buf(psum=True)
        pG = [ps("c_pG%d" % i, [128, 512], F32) for i in range(2)]
        pU = [ps("c_pU%d" % i, [128, 512], F32) for i in range(2)]
        b_pG = [Buf(psum=True) for _ in range(2)]
        b_pU = [Buf(psum=True) for _ in range(2)]

        def layer_norm(src_b, gi_, bi_, dst, b_dst):
            for hf in range(2):
                S.op("dve", lambda e, hf=hf: e.bn_stats(stats[:, hf, :], hpre[:, hf * 512:(hf + 1) * 512]),
                     reads=[src_b], writes=[b_stats])
            S.op("dve", lambda e: e.bn_aggr(mv[:, 0:2], stats[:].rearrange("p a b -> p (a b)")),
                 reads=[b_stats], writes=[b_mv])
            S.op("act", lambda e: e.activation(mv[:, 3:4], mv[:, 1:2], AF.Sqrt, bias=epsc[:, 0:1], scale=1.0),
                 reads=[b_mv, b_epsc], writes=[b_mv])
            S.op("dve", lambda e: e.reciprocal(mv[:, 2:3], mv[:, 3:4]), reads=[b_mv], writes=[b_mv])
            S.op("dve", lambda e: e.tensor_scalar(hn[:], hpre[:], mv[:, 0:1], mv[:, 2:3],
                                                  ALU.subtract, ALU.mult),
                 reads=[src_b, b_mv], writes=[b_hn])
            S.op("pool", lambda e: e.tensor_tensor(hn[:], hn[:], lnp[:, gi_, :], ALU.mult),
                 reads=[b_hn, b_lnp[gi_]], writes=[b_hn])
            S.op("dve", lambda e: e.tensor_tensor(dst, hn[:], lnp[:, bi_, :], ALU.add),
                 reads=[b_hn, b_lnp[bi_]], writes=[b_dst])

        nwl = 0
        for gi in range(NG):
            for ti in range(TPG):
                it = gi * TPG + ti
                sl = it % NB
                tok0 = it * 128
                oi = (tok0 // OCW) % NB
                oo = tok0 % OCW
                if oo == 0:
                    ta, tb = tok0, TOKH + tok0
                    S.dma("sp", ocA[oi][:], Gv[ta // XCH][:, :, ta % XCH:ta % XCH + OCW], writes=[b_ocA[oi]])
                    S.dma("sp", ocB[oi][:], Gv[tb // XCH][:, :, tb % XCH:tb % XCH + OCW], writes=[b_ocB[oi]])
                S.dma("sp", xt[sl][:], D["xres"][tok0:tok0 + 128, :], writes=[b_xt[sl]])
                for hf in range(2):
                    cs = slice(hf * 512, (hf + 1) * 512)
                    for c in range(8):
                        S.mm(pmix[:, cs], ocA[oi][:, c, oo:oo + 128], woutA[:, c, cs],
                             reads=[b_ocA[oi], b_wout], writes=[b_pmix], start=(c == 0), stop=False, signal=False)
                    for c in range(8):
                        S.mm(pmix[:, cs], ocB[oi][:, c, oo:oo + 128], woutB[:, c, cs],
                             reads=[b_ocB[oi], b_woutB], writes=[b_pmix], start=False, stop=(c == 7),
                             signal=(c == 7))
                for hf in range(2):
                    S.op("dve", lambda e, hf=hf, sl=sl: e.scalar_tensor_tensor(
                        hpre[:, hf * 512:(hf + 1) * 512], xt[sl][:, hf * 512:(hf + 1) * 512], ALPHA,
                        pmix[:, hf * 512:(hf + 1) * 512], ALU.mult, ALU.add),
                        reads=[b_xt[sl], b_pmix], writes=[b_hpre])
                layer_norm(b_hpre, 0, 1, h1g[:, ti, :], b_h1g[ti])
                S.op("act", lambda e, ti=ti: e.copy(h1b[:], h1g[:, ti, :]), reads=[b_h1g[ti]], writes=[b_h1b])
                for c in range(8):
                    S.op("pe", lambda e, c=c: e.transpose(pT[:, c * 128:(c + 1) * 128],
                                                          h1b[:, c * 128:(c + 1) * 128], ident[:]),
                         reads=[b_h1b, b_ident], writes=[b_pT], signal=(c == 7))
                S.op("act", lambda e, ti=ti: e.copy(h1T[:, :, ti * 128:(ti + 1) * 128],
                                                    pT[:].rearrange("p (c t) -> p c t", c=8)),
                     reads=[b_pT], writes=[b_h1T[ti]])
            for h in range(HC):
                ws = nwl % NW
                nwl += 1
                S.dma("sp", wgu[ws][:].rearrange("p a c n -> p (a c n)"),
                      D["wgs"][h].rearrange("p a c n -> p (a c n)"), reads=[D["b_wgs"][h]], writes=[b_wgu[ws]])
                pb = h % 2
                for c in range(8):
                    S.mm(pG[pb][:], wgu[ws][:, 0, c, :], h1T[:, c, :], reads=[b_wgu[ws]] + b_h1T,
                         writes=[b_pG[pb]], start=(c == 0), stop=(c == 7), signal=(c == 7))
                for c in range(8):
                    S.mm(pU[pb][:], wgu[ws][:, 1, c, :], h1T[:, c, :], reads=[b_wgu[ws]] + b_h1T,
                         writes=[b_pU[pb]], start=(c == 0), stop=(c == 7), signal=(c == 7))
                S.op("act", lambda e, pb=pb: e.activation(sg[pb][:], pG[pb][:], AF.Silu),
                     reads=[b_pG[pb]], writes=[b_sg[pb]])
                S.op("dve", lambda e, pb=pb, h=h: e.tensor_tensor(actT[:, h, :], pU[pb][:], sg[pb][:], ALU.mult),
                     reads=[b_pU[pb], b_sg[pb]], writes=[b_actT[h]])
            for ti in range(TPG):
                it = gi * TPG + ti
                sl = it % NB
                tok0 = it * 128
                for hf in range(2):
                    for h in range(HC):
                        S.mm(pmix[:, hf * 512:(hf + 1) * 512], actT[:, h, ti * 128:(ti + 1) * 128],
                             wd[:, h, hf * 512:(hf + 1) * 512], reads=[b_actT[h], b_wd[h]], writes=[b_pmix],
                             start=(h == 0), stop=(h == HC - 1), signal=(h == HC - 1))
                for hf in range(2):
                    S.op("dve", lambda e, hf=hf, ti=ti: e.scalar_tensor_tensor(
                        hpre[:, hf * 512:(hf + 1) * 512], h1g[:, ti, hf * 512:(hf + 1) * 512], ALPHA,
                        pmix[:, hf * 512:(hf + 1) * 512], ALU.mult, ALU.add),
                        reads=[b_h1g[ti], b_pmix], writes=[b_hpre])
                layer_norm(b_hpre, 2, 3, outt[sl][:], b_outt[sl])
                S.dma("sp", D["out"][tok0:tok0 + 128, :], outt[sl][:], reads=[b_outt[sl]])
        S.barrier()
        S.run()


def phase_a(nc, S, D):
    ST = 2048
    NST = SEQ // ST
    inv_freq = [float(np.float32(ROPE_THETA) ** np.float32(-i / 8.0)) for i in range(8)]
    TWO_PI = 2.0 * math.pi
    C1 = 6.28125
    C2 = TWO_PI - C1
    with ExitStack() as ph:
        def sb(n, shp, dt):
            return ph.enter_context(nc.sbuf_tensor(n, shp, dt))

        def ps(n, shp, dt):
            return ph.enter_context(nc.psum_tensor(n, shp, dt))

        identf = sb("a_identf", [128, 128], F32)
        b_identf = Buf()
        S.dma("sp", identf[:], D["ident"], writes=[b_identf])
        identb = sb("a_identb", [128, 128], BF16)
        b_identb = Buf()
        S.dma("pool", identb[:], D["ident"], writes=[b_identb])
        maskb = sb("a_maskb", [128, 256], BF16)
        b_maskb = Buf()
        S.dma("pool", maskb[:], D["maskT"], writes=[b_maskb])
        ones = sb("a_ones", [128, 128], F32)
        b_ones = Buf()
        S.op("dve", lambda e: e.memset(ones[:], 1.0), writes=[b_ones])
        watt = sb("a_watt", [128, 8, 768], BF16)
        b_watt = Buf()
        S.dma("pool", watt[:], D["w_att"].rearrange("(c p) n -> p c n", p=128), writes=[b_watt])

        posi = sb("a_posi", [128, 64], I32)
        b_posi = Buf()
        S.dma("sp", posi[:], D["pos"], writes=[b_posi])
        posf = sb("a_posf", [128, 64], F32)
        b_posf = Buf()
        S.op("dve", lambda e: e.tensor_copy(posf[:], posi[:]), reads=[b_posi], writes=[b_posf])
        ang = sb("a_ang", [128, 64, 8], F32)
        b_ang = Buf()
        for i in range(8):
            S.op("dve", lambda e, i=i: e.tensor_scalar(ang[:, :, i], posf[:], inv_freq[i], None, ALU.mult),
                 reads=[b_posf], writes=[b_ang])
        sinT = sb("a_sinT", [128, 64, 8], F32)
        cosT = sb("a_cosT", [128, 64, 8], F32)
        b_sinT = Buf()
        b_cosT = Buf()
        kq = sb("a_kq", [128, 512], I32)
        kf = sb("a_kf", [128, 512], F32)
        red = sb("a_red", [128, 512], F32)
        msk = sb("a_msk", [128, 512], F32)
        b_tmp = Buf()
        angf = ang[:].rearrange("p a b -> p (a b)")

        def wrap(dst):
            S.op("dve", lambda e: e.tensor_scalar(msk[:], dst, math.pi, -TWO_PI, ALU.is_gt, ALU.mult),
                 reads=[b_tmp], writes=[b_tmp])
            S.op("dve", lambda e: e.tensor_tensor(dst, dst, msk[:], ALU.add), reads=[b_tmp], writes=[b_tmp])

        S.op("dve", lambda e: e.tensor_scalar(kq[:], angf, 1.0 / TWO_PI, None, ALU.mult),
             reads=[b_ang], writes=[b_tmp])
        S.op("dve", lambda e: e.tensor_copy(kf[:], kq[:]), reads=[b_tmp], writes=[b_tmp])
        S.op("dve", lambda e: e.scalar_tensor_tensor(red[:], kf[:], -C1, angf, ALU.mult, ALU.add),
             reads=[b_tmp, b_ang], writes=[b_tmp])
        S.op("dve", lambda e: e.scalar_tensor_tensor(red[:], kf[:], -C2, red[:], ALU.mult, ALU.add),
             reads=[b_tmp], writes=[b_tmp])
        wrap(red[:])
        S.op("act", lambda e: e.activation(sinT[:].rearrange("p a b -> p (a b)"), red[:], AF.Sin),
             reads=[b_tmp], writes=[b_sinT])
        S.op("dve", lambda e: e.tensor_scalar(red[:], red[:], math.pi / 2.0, None, ALU.add),
             reads=[b_tmp, b_sinT], writes=[b_tmp])
        wrap(red[:])
        S.op("act", lambda e: e.activation(cosT[:].rearrange("p a b -> p (a b)"), red[:], AF.Sin),
             reads=[b_tmp], writes=[b_cosT])

        xTs = [sb("a_xT%d" % i, [128, 8, ST], BF16) for i in range(2)]
        b_xTs = [Buf() for _ in range(2)]
        xTv = D["xT"].rearrange("(c p) t -> p c t", p=128)
        qT = sb("a_qT", [128, 2, ST], BF16)
        b_qT = Buf()
        kT = [sb("a_kT%d" % i, [128, 2, ST], BF16) for i in range(2)]
        b_kT = [Buf() for _ in range(2)]
        V = [[sb("a_V%d_%d" % (l, i), [128, 16, 4, 65], BF16) for i in range(2)] for l in range(3)]
        b_V = [[Buf() for _ in range(2)] for l in range(3)]
        for l in range(3):
            for i in range(2):
                S.op("pool", lambda e, l=l, i=i: e.memset(V[l][i][:], 1.0), writes=[b_V[l][i]])
        ysb = sb("a_ysb", [128, 512], F32)
        b_ysb = Buf()
        ysb3 = ysb[:].rearrange("p (h d) -> p h d", d=64)
        rt = sb("a_rt", [128, 4, 8, 8], F32)
        b_rt = Buf()
        sq = sb("a_sq", [128, 512], F32)
        b_sq = Buf()
        ssq = sb("a_ssq", [128, 8], F32)
        b_ssq = Buf()
        rm = sb("a_rm", [128, 8], F32)
        b_rm = Buf()
        st4 = sb("a_st4", [4, 8], F32)
        b_st4 = Buf()
        dg4 = sb("a_dg4", [4, 4], F32)
        b_dg4 = Buf()
        negM = sb("a_negM", [128, 4], F32)
        b_negM = Buf()
        NPT = 3
        PT = [sb("a_PT%d" % i, [128, 256], BF16) for i in range(NPT)]
        b_PT = [Buf() for _ in range(NPT)]
        oacc = sb("a_oacc", [128, ST], F32)
        b_oacc = Buf()
        oT = [sb("a_oT%d" % i, [64, ST], BF16) for i in range(2)]
        b_oT = [Buf() for _ in range(2)]

        pqk = ps("a_pqk", [128, 512], F32)
        b_pqk = Buf(psum=True)
        pX = ps("a_pX", [128, 512], F32)
        b_pX = Buf(psum=True)
        pS = [ps("a_pS%d" % i, [128, 512], F32) for i in range(2)]
        b_pS = [Buf(psum=True) for _ in range(2)]
        pO = ps("a_pO", [128, ST], F32)
        b_pO = Buf(psum=True)

        S.op("dve", lambda e: e.memset(st4[:], 0.0), writes=[b_st4])

        nblk = 0
        for st in range(NST):
            xs = st % 2
            ks = st % 2
            kp = 1 - ks
            xT = xTs[xs]
            S.dma("pool", xT[:], xTv[:, :, st * ST:(st + 1) * ST], writes=[b_xTs[xs]])
            S.op("dve", lambda e: e.memset(rm[:], 0.0), reads=[], writes=[b_rm])
            for j in range(16):
                n = st * 16 + j
                for c in range(8):
                    S.mm(pqk[:], xT[:, c, j * 128:(j + 1) * 128], watt[:, c, 0:512],
                         reads=[b_xTs[xs], b_watt], writes=[b_pqk], start=(c == 0), stop=(c == 7), signal=(c == 7))
                S.op("act", lambda e: e.mul(ysb[:, 0:256], pqk[:, 0:256], 0.125), reads=[b_pqk], writes=[b_ysb])
                S.op("act", lambda e: e.copy(ysb[:, 256:512], pqk[:, 256:512]), reads=[b_pqk], writes=[b_ysb])
                cb = cosT[:, n:n + 1, :].to_broadcast([128, 8, 8])
                sbb = sinT[:, n:n + 1, :].to_broadcast([128, 8, 8])
                x1 = ysb3[:, :, 0:8]
                x2 = ysb3[:, :, 8:16]
                rtv = rt[:].rearrange("p a h d -> p a (h d)")
                for a, (xx, tt) in enumerate(((x1, cb), (x2, sbb), (x2, cb), (x1, sbb))):
                    S.op("pool", lambda e, a=a, xx=xx, tt=tt: e.tensor_tensor(rt[:, a, :, :], xx, tt, ALU.mult),
                         reads=[b_ysb, b_cosT, b_sinT], writes=[b_rt])
                S.op("pool", lambda e: e.tensor_tensor(x1, rt[:, 0, :, :], rt[:, 1, :, :], ALU.subtract),
                     reads=[b_rt, b_ysb], writes=[b_ysb])
                S.op("pool", lambda e: e.tensor_tensor(x2, rt[:, 2, :, :], rt[:, 3, :, :], ALU.add),
                     reads=[b_rt, b_ysb], writes=[b_ysb])
                S.op("dve", lambda e: e.tensor_tensor(sq[:], ysb[:], ysb[:], ALU.mult), reads=[b_ysb], writes=[b_sq])
                S.op("dve", lambda e: e.tensor_reduce(ssq[:], sq[:].rearrange("p (h d) -> p h d", d=64), AX.X, ALU.add),
                     reads=[b_sq], writes=[b_ssq])
                S.op("dve", lambda e: e.tensor_tensor(rm[:], rm[:], ssq[:], ALU.max), reads=[b_ssq, b_rm], writes=[b_rm])
                for blk in range(4):
                    S.op("pe", lambda e, blk=blk: e.transpose(pX[:, blk * 128:(blk + 1) * 128],
                                                               ysb[:, blk * 128:(blk + 1) * 128], identf[:]),
                         reads=[b_ysb, b_identf], writes=[b_pX], signal=(blk == 3))
                S.op("act", lambda e, j=j: e.copy(qT[:, :, j * 128:(j + 1) * 128],
                                                  pX[:, 0:256].rearrange("p (a t) -> p a t", a=2)),
                     reads=[b_pX], writes=[b_qT])
                S.op("act", lambda e, j=j, ks=ks: e.copy(kT[ks][:, :, j * 128:(j + 1) * 128],
                                                         pX[:, 256:512].rearrange("p (a t) -> p a t", a=2)),
                     reads=[b_pX], writes=[b_kT[ks]])
            S.op("pe", lambda e: e.transpose(pX[0:4, 0:128], rm[:, 0:4], identf[:]), reads=[b_rm, b_identf],
                 writes=[b_pX], signal=False)
            S.op("pe", lambda e: e.transpose(pX[0:4, 128:256], rm[:, 4:8], identf[:]), reads=[b_rm, b_identf],
                 writes=[b_pX])
            S.op("dve", lambda e: e.tensor_copy(st4[:, 2:3], st4[:, 1:2]), reads=[b_st4], writes=[b_st4])
            S.op("dve", lambda e: e.tensor_reduce(st4[:, 0:2], pX[0:4, 0:256].rearrange("p (a t) -> p a t", a=2),
                                                  AX.X, ALU.max), reads=[b_pX, b_st4], writes=[b_st4])
            S.op("dve", lambda e: e.tensor_tensor(st4[:, 3:4], st4[:, 1:2], st4[:, 2:3], ALU.max),
                 reads=[b_st4], writes=[b_st4])
            S.op("dve", lambda e: e.tensor_tensor(st4[:, 3:4], st4[:, 3:4], st4[:, 0:1], ALU.mult),
                 reads=[b_st4], writes=[b_st4])
            S.op("dve", lambda e: e.tensor_scalar(dg4[:], identf[0:4, 0:4], st4[:, 3:4], None, ALU.mult),
                 reads=[b_st4, b_identf], writes=[b_dg4])
            S.mm(pX[:, 256:260], ones[0:4, :], dg4[:], reads=[b_ones, b_dg4], writes=[b_pX])
            S.op("act", lambda e: e.activation(negM[:], pX[:, 256:260], AF.Sqrt), reads=[b_pX], writes=[b_negM])
            S.op("dve", lambda e: e.tensor_scalar(negM[:], negM[:], -1.0, None, ALU.mult), reads=[b_negM],
                 writes=[b_negM])
            for l, dil in enumerate((1, 4, 16)):
                for t16 in range(16):
                    if dil == 1:
                        a0, a1 = t16 * 128, (t16 + 1) * 128
                    elif dil == 4:
                        n4, r = t16 // 4, t16 % 4
                        a0, a1 = n4 * 512 + r, (n4 + 1) * 512
                    else:
                        a0, a1 = t16, ST
                    for c in range(8):
                        S.mm(pX[:, 0:256], xT[:, c, a0:a1:dil], watt[:, c, 512:768],
                             reads=[b_xTs[xs], b_watt], writes=[b_pX], start=(c == 0), stop=(c == 7), signal=(c == 7))
                    S.op("act", lambda e, l=l, t16=t16, ks=ks: e.copy(
                        V[l][ks][:, t16, :, 0:64], pX[:, 0:256].rearrange("p (h d) -> p h d", d=64)),
                        reads=[b_pX], writes=[b_V[l][ks]])
            for h in range(4):
                hp, p0 = h // 2, 64 * (h % 2)
                first = [True] * 4

                def block(qsl, cur, prev, outs):
                    nonlocal nblk
                    pb = nblk % 2
                    pt = nblk % NPT
                    nblk += 1
                    lo = 0 if prev is not None else 128
                    qap = qT[p0:p0 + 64, hp, qsl]
                    if prev is not None:
                        pslot, psl, pv_ap = prev
                        S.mm(pS[pb][:, 0:128], kT[pslot][p0:p0 + 64, hp, psl], qap,
                             reads=[b_kT[pslot], b_qT], writes=[b_pS[pb]], start=True, stop=False, signal=False)
                        S.mm(pS[pb][:, 0:128], identb[:], maskb[:, 0:128], reads=[b_identb, b_maskb],
                             writes=[b_pS[pb]], start=False, stop=True, signal=False)
                    S.mm(pS[pb][:, 128:256], kT[ks][p0:p0 + 64, hp, qsl], qap,
                         reads=[b_kT[ks], b_qT], writes=[b_pS[pb]], start=True, stop=False, signal=False)
                    S.mm(pS[pb][:, 128:256], identb[:], maskb[:, 128:256], reads=[b_identb, b_maskb],
                         writes=[b_pS[pb]], start=False, stop=True, signal=True)
                    S.op("act", lambda e, hh=h: e.activation(PT[pt][:, lo:256], pS[pb][:, lo:256], AF.Exp,
                                                             bias=negM[:, hh:hh + 1], scale=1.0),
                         reads=[b_pS[pb], b_negM], writes=[b_PT[pt]])
                    kts = ([(0, pv_ap, b_V_prev)] if prev is not None else []) + [(1, cur, b_V_cur)]
                    nmm = len(kts) * len(outs)
                    i = 0
                    for (kt, vap, vb) in kts:
                        for (ocol, pcol) in outs:
                            bank = ocol.start // 512
                            i += 1
                            S.mm(pO[0:65, ocol], vap, PT[pt][:, kt * 128 + pcol.start:kt * 128 + pcol.stop],
                                 reads=[b_PT[pt], vb], writes=[b_pO], start=first[bank], stop=False,
                                 signal=(i == nmm), skip_group_check=True)
                            first[bank] = False

                full = [(None, slice(0, 128))]
                for jb in range(16):
                    qsl = slice(jb * 128, (jb + 1) * 128)
                    b_V_cur = b_V[0][ks]
                    cur = V[0][ks][:, jb, h, :]
                    prev = None
                    if jb > 0:
                        prev = (ks, slice((jb - 1) * 128, jb * 128), V[0][ks][:, jb - 1, h, :])
                        b_V_prev = b_V[0][ks]
                    elif st > 0:
                        prev = (kp, slice(15 * 128, 16 * 128), V[0][kp][:, 15, h, :])
                        b_V_prev = b_V[0][kp]
                    block(qsl, cur, prev, [(slice(jb * 128, (jb + 1) * 128), slice(0, 128))])
                for n4 in range(4):
                    for r in range(4):
                        qsl = slice(n4 * 512 + r, (n4 + 1) * 512, 4)
                        b_V_cur = b_V[1][ks]
                        cur = V[1][ks][:, n4 * 4 + r, h, :]
                        prev = None
                        if n4 > 0:
                            prev = (ks, slice((n4 - 1) * 512 + r, n4 * 512, 4), V[1][ks][:, (n4 - 1) * 4 + r, h, :])
                            b_V_prev = b_V[1][ks]
                        elif st > 0:
                            prev = (kp, slice(3 * 512 + r, ST, 4), V[1][kp][:, 12 + r, h, :])
                            b_V_prev = b_V[1][kp]
                        block(qsl, cur, prev, [(qsl, slice(0, 128))])
                for r in range(16):
                    qsl = slice(r, ST, 16)
                    b_V_cur = b_V[2][ks]
                    cur = V[2][ks][:, r, h, :]
                    prev = None
                    if st > 0:
                        prev = (kp, qsl, V[2][kp][:, r, h, :])
                        b_V_prev = b_V[2][kp]
                    block(qsl, cur, prev, [(slice(b4 * 512 + r, (b4 + 1) * 512, 16), slice(32 * b4, 32 * b4 + 32))
                                           for b4 in range(4)])
                for b4 in range(4):
                    cs = slice(b4 * 512, (b4 + 1) * 512)
                    eng = "act" if b4 % 2 == 0 else "dve"
                    if eng == "act":
                        S.op("act", lambda e, cs=cs: e.copy(oacc[0:65, cs], pO[0:65, cs]), reads=[b_pO], writes=[b_oacc])
                    else:
                        S.op("dve", lambda e, cs=cs: e.tensor_copy(oacc[0:65, cs], pO[0:65, cs]), reads=[b_pO],
                             writes=[b_oacc])
                S.op("dve", lambda e: e.reciprocal(oacc[64:65, :], oacc[64:65, :]), reads=[b_oacc], writes=[b_oacc])
                for b4 in range(4):
                    cs = slice(b4 * 512, (b4 + 1) * 512)
                    S.mm(pO[0:64, cs], ones[64:65, 0:64], oacc[64:65, cs], reads=[b_ones, b_oacc], writes=[b_pO],
                         signal=(b4 == 3))
                os_ = (st * 4 + h) % 2
                for b4 in range(4):
                    cs = slice(b4 * 512, (b4 + 1) * 512)
                    S.op("dve", lambda e, cs=cs, os_=os_: e.tensor_tensor(oT[os_][:, cs], pO[0:64, cs], oacc[0:64, cs],
                                                                       ALU.mult),
                         reads=[b_pO, b_oacc], writes=[b_oT[os_]])
                for xk in range(ST // XCH):
                    S.dma("sp", D["omix"][st * (ST // XCH) + xk][h * 64:(h + 1) * 64, :],
                          oT[os_][:, xk * XCH:(xk + 1) * XCH], reads=[b_oT[os_]])
        S.barrier()
        S.run()


def _TT(o, a, b, op):
    return lambda e: e.tensor_tensor(o, a, b, op)


def _TS(o, a, s1, s2, op0, op1=None):
    if op1 is None:
        return lambda e: e.tensor_scalar(o, a, s1, s2, op0)
    return lambda e: e.tensor_scalar(o, a, s1, s2, op0, op1)


def _STT(o, a, s, b, op0, op1):
    return lambda e: e.scalar_tensor_tensor(o, a, s, b, op0, op1)


def _ACT(o, i, f, **kw):
    return lambda e: e.activation(o, i, f, **kw)


def _CP(o, i):
    return lambda e: e.copy(o, i)


def _TC(o, i):
    return lambda e: e.tensor_copy(o, i)


def _TR(o, i, ident):
    return lambda e: e.transpose(o, i, ident)


def phase_b(nc, S, D):
    STB = 1024
    NSTB = SEQ // STB
    TPS = STB // 128
    DS = DECAY_SCALE
    with ExitStack() as ph:
        def sb(n, shp, dt=F32):
            return ph.enter_context(nc.sbuf_tensor(n, shp, dt))

        def ps(n, shp, dt=F32):
            return ph.enter_context(nc.psum_tensor(n, shp, dt))

        identf = sb("b_identf", [128, 128])
        b_identf = Buf()
        S.dma("sp", identf[:], D["ident"], writes=[b_identf])
        identb = sb("b_identb", [128, 128], BF16)
        b_identb = Buf()
        S.dma("pool", identb[:], D["ident"], writes=[b_identb])
        cst = sb("b_cst", [128, 642])
        b_cst = Buf()
        S.dma("sp", cst[:], D["cstB"], writes=[b_cst])
        triI, triS, triA = cst[:, 0:128], cst[:, 128:256], cst[:, 256:384]
        chunkind = cst[:, 384:386]
        mask1, mask3, eye2 = cst[:, 386:514], cst[:, 514:578], cst[:, 578:642]
        vec = sb("b_vec", [128, 8, 256])
        b_vec = Buf()
        for i, nm in enumerate(("w0", "a0", "k_k", "k_a", "k_a", "r_k", "gn_g", "gn_b")):
            S.dma("sp", vec[:, i, :], bcast_rows(D[nm]), writes=[b_vec], acc=(i > 0))
        S.op("dve", _TS(vec[:, 4, :], vec[:, 4, :], -1.0, 1.0, ALU.mult, ALU.add), reads=[b_vec], writes=[b_vec])
        bias_wa = vec[:, 0:2, :].rearrange("p a b -> p (a b)")
        wdec = sb("b_wdec", [128, 512])
        b_wdec = Buf()
        S.op("dve", lambda e: e.memset(wdec[:], 0.0), writes=[b_wdec])
        S.dma("sp", wdec[0:64, 0:256], D["w_dec"], writes=[b_wdec], acc=True)
        S.dma("sp", wdec[64:128, 256:512], D["w_aaa"], writes=[b_wdec], acc=True)
        wgate = sb("b_wgate", [128, 256])
        b_wgate = Buf()
        S.dma("sp", wgate[:], D["w_gate"], writes=[b_wgate])
        epsg = sb("b_epsg", [128, 1])
        b_epsg = Buf()
        S.op("dve", lambda e: e.memset(epsg[:], GN_EPS), writes=[b_epsg])
        mub = sb("b_mub", [128, 1024])
        b_mub = Buf()
        S.dma("sp", mub[:], bcast_rows(D["mu"]), writes=[b_mub])
        omub = sb("b_omub", [128, 1024])
        b_omub = Buf()
        S.op("dve", _TS(omub[:], mub[:], -1.0, 1.0, ALU.mult, ALU.add), reads=[b_mub], writes=[b_omub])
        W1 = sb("b_W1", [128, 8, 1024], BF16)
        W2 = sb("b_W2", [128, 8, 1024], BF16)
        b_W = Buf()
        wst = [sb("b_wst%d" % i, [128, 1024]) for i in range(2)]
        b_wst = [Buf() for _ in range(2)]
        for c in range(8):
            S.dma("sp", wst[c % 2][:], D["w_rw"][c * 128:(c + 1) * 128, :], writes=[b_wst[c % 2]])
            S.op("dve", _TT(W1[:, c, :], wst[c % 2][:], omub[:], ALU.mult), reads=[b_wst[c % 2], b_omub], writes=[b_W])
            S.op("pool", _TT(W2[:, c, :], wst[c % 2][:], mub[:], ALU.mult), reads=[b_wst[c % 2], b_mub], writes=[b_W])

        xTv = D["xT"].rearrange("(c p) t -> p c t", p=128)
        xc = sb("b_xc", [128, 8, STB], BF16)
        xp = sb("b_xp", [128, 8, STB], BF16)
        b_xc = Buf()
        b_xp = Buf()

        Hs = sb("b_Hs", [128, 4, 64], BF16)
        b_H = Buf()
        S.op("dve", lambda e: e.memset(Hs[:], 0.0), writes=[b_H])

        def t256(n):
            return sb(n, [128, 256]), Buf()
        tl, b_tl = sb("b_tl", [128, 256]), Buf()
        lT, b_lT = sb("b_lT", [128, 256]), Buf()
        lg, b_lg = sb("b_lg", [128, 512]), Buf()
        sg, b_sg = sb("b_sg", [128, 512]), Buf()
        gs, b_gs = t256("b_gs")
        yA = sb("b_yA", [128, 512])
        b_rs = Buf()
        rs = yA[:, 0:256]
        krs = yA[:, 256:512]
        vs, b_vs = sb("b_vs", [128, 256], BF16), Buf()
        kk, b_kk = t256("b_kk")
        t1, b_t1 = t256("b_t1")
        t2, b_t2 = t256("b_t2")
        km, b_km = t256("b_km")
        bb, b_bb = t256("b_bb")
        E, b_E = sb("b_E", [128, 4, 256]), [Buf() for _ in range(4)]
        X4, b_X4 = sb("b_X4", [128, 4, 256], BF16), Buf()
        BK, b_BK = sb("b_BK", [128, 2, 256], BF16), Buf()
        XT, b_XT = sb("b_XT", [128, 4, 4, 64], BF16), Buf()
        dgP, b_dgP = sb("b_dgP", [128, 4, 64], BF16), Buf()
        AM1, b_AM1 = sb("b_AM1", [128, 4, 128], BF16), Buf()
        AM2, b_AM2 = sb("b_AM2", [128, 4, 128], BF16), Buf()
        Lm = [sb("b_L%d" % i, [128, 4, 2, 64], BF16) for i in range(2)]
        b_Lm = [Buf() for _ in range(2)]
        Pm, b_Pm = sb("b_Pm", [128, 4, 64], BF16), Buf()
        sm, b_sm = sb("b_sm", [128, 32]), Buf()
        PC, b_PC = sb("b_PC", [128, 4, 2]), Buf()
        Ws, b_Ws = sb("b_Ws", [128, 256], BF16), Buf()
        Us, b_Us = sb("b_Us", [128, 256], BF16), Buf()
        on, b_on = t256("b_on")
        gst, b_gst = sb("b_gst", [128, 4, 6]), Buf()
        gmv, b_gmv = sb("b_gmv", [128, 12]), Buf()
        orT = sb("b_orT", [128, 2, STB], BF16)
        b_orT = Buf()

        K = [ps("b_K%d" % i, [128, 512]) for i in range(8)]
        b_K = [Buf(psum=True) for _ in range(8)]

        def v3(ap):
            return ap.rearrange("p (h d) -> p h d", d=64)

        def bc(ap4):
            return ap4.unsqueeze(2).to_broadcast([128, 4, 64])

        for n in range(B_TILES):
            st, j = n // TPS, n % TPS
            if j == 0:
                S.dma("pool", xc[:], xTv[:, :, st * STB:(st + 1) * STB], writes=[b_xc])
                if st == 0:
                    S.op("dve", lambda e: e.memset(xp[:, :, 0:1], 0.0), writes=[b_xp])
                    S.dma("pool", xp[:, :, 1:STB], xTv[:, :, 0:STB - 1], writes=[b_xp], acc=True)
                else:
                    S.dma("pool", xp[:], xTv[:, :, st * STB - 1:(st + 1) * STB - 1], writes=[b_xp])
            tsl = slice(j * 128, (j + 1) * 128)
            for bk in range(2):
                cs = slice(bk * 512, (bk + 1) * 512)
                for c in range(8):
                    S.mm(K[bk][:], xc[:, c, tsl], W1[:, c, cs], reads=[b_xc, b_W], writes=[b_K[bk]],
                         start=(c == 0), stop=False, signal=False)
                for c in range(8):
                    S.mm(K[bk][:], xp[:, c, tsl], W2[:, c, cs], reads=[b_xp, b_W], writes=[b_K[bk]],
                         start=False, stop=(c == 7), signal=(c == 7))
            pA, pB = K[0], K[1]
            S.op("act", _ACT(tl[:, 0:64], pB[:, 256:320], AF.Tanh), reads=[b_K[1]], writes=[b_tl])
            S.op("act", _CP(tl[:, 64:128], pB[:, 320:384]), reads=[b_K[1]], writes=[b_tl])
            S.op("act", _ACT(tl[:, 128:256], pB[:, 384:512], AF.Sigmoid), reads=[b_K[1]], writes=[b_tl])
            S.op("act", _CP(vs[:], pB[:, 0:256]), reads=[b_K[1]], writes=[b_vs])
            S.op("act", _CP(yA[:], pA[:]), reads=[b_K[0]], writes=[b_rs])
            S.op("pe", _TR(K[3][:, 0:128], tl[:, 0:128], identf[:]), reads=[b_tl, b_identf], writes=[b_K[3]], signal=False)
            S.op("pe", _TR(K[3][:, 128:256], tl[:, 128:256], identf[:]), reads=[b_tl, b_identf], writes=[b_K[3]])
            S.op("dve", _TC(lT[:], K[3][:, 0:256]), reads=[b_K[3]], writes=[b_lT])
            S.mm(K[2][:], lT[:, 0:128], wdec[:], reads=[b_lT, b_wdec], writes=[b_K[2]], signal=False)
            S.mm(K[3][:, 256:512], lT[:, 128:256], wgate[:], reads=[b_lT, b_wgate], writes=[b_K[3]])
            S.op("dve", _TT(lg[:], K[2][:], bias_wa, ALU.add), reads=[b_K[2], b_vec], writes=[b_lg])
            S.op("act", _ACT(sg[:], lg[:], AF.Sigmoid), reads=[b_lg], writes=[b_sg])
            S.op("act", _CP(gs[:], K[3][:, 256:512]), reads=[b_K[3]], writes=[b_gs])
            sw, aa = sg[:, 0:256], sg[:, 256:512]
            S.mm(K[4][:, 0:256], triS, sw, reads=[b_cst, b_sg], writes=[b_K[4]], signal=False)
            S.mm(K[4][:, 256:512], triI, sw, reads=[b_cst, b_sg], writes=[b_K[4]], signal=False)
            S.mm(K[5][:, 0:256], triA, sw, reads=[b_cst, b_sg], writes=[b_K[5]], signal=False)
            for c in range(2):
                rows = slice(64 * c, 64 * c + 64)
                for h in range(4):
                    S.mm(K[5][rows, 256 + 2 * h:258 + 2 * h], sg[rows, h * 64:(h + 1) * 64], cst[rows, 384:386],
                         reads=[b_cst, b_sg], writes=[b_K[5]], signal=(c == 1 and h == 3))
            S.op("act", _ACT(E[:, 0, :], K[4][:, 0:256], AF.Exp, scale=-DS), reads=[b_K[4]], writes=[b_E[0]])
            S.op("act", _ACT(E[:, 1, :], K[4][:, 256:512], AF.Exp, scale=-DS), reads=[b_K[4]], writes=[b_E[1]])
            S.op("act", _ACT(E[:, 2, :], K[4][:, 256:512], AF.Exp, scale=DS), reads=[b_K[4]], writes=[b_E[2]])
            S.op("act", _ACT(E[:, 3, :], K[5][:, 0:256], AF.Exp, scale=-DS), reads=[b_K[5]], writes=[b_E[3]])
            S.op("act", _ACT(PC[:], K[5][:, 256:264], AF.Exp, scale=-DS), reads=[b_K[5]], writes=[b_PC])
            S.op("dve", _TT(kk[:], krs, vec[:, 2, :], ALU.mult), reads=[b_rs, b_vec], writes=[b_kk])
            S.op("pool", _TT(t1[:], kk[:], kk[:], ALU.mult), reads=[b_kk], writes=[b_t1])
            S.op("dve", lambda e: e.tensor_reduce(sm[:, 0:4], v3(t1[:]), AX.X, ALU.add), reads=[b_t1], writes=[b_sm])
            S.op("act", _ACT(sm[:, 4:8], sm[:, 0:4], AF.Sqrt), reads=[b_sm], writes=[b_sm])
            S.op("dve", _TS(sm[:, 4:8], sm[:, 4:8], L2_EPS, None, ALU.max), reads=[b_sm], writes=[b_sm])
            S.op("dve", lambda e: e.reciprocal(sm[:, 8:12], sm[:, 4:8]), reads=[b_sm], writes=[b_sm])
            S.op("dve", _TT(v3(kk[:]), v3(kk[:]), bc(sm[:, 8:12]), ALU.mult), reads=[b_kk, b_sm], writes=[b_kk])
            S.op("pool", _TT(t2[:], aa, vec[:, 3, :], ALU.mult), reads=[b_sg, b_vec], writes=[b_t2])
            S.op("pool", _TT(t2[:], t2[:], vec[:, 4, :], ALU.add), reads=[b_t2, b_vec], writes=[b_t2])
            S.op("dve", _TT(km[:], krs, t2[:], ALU.mult), reads=[b_rs, b_t2], writes=[b_km])
            S.op("pool", _TT(bb[:], kk[:], aa, ALU.mult), reads=[b_kk, b_sg], writes=[b_bb])
            S.op("pool", _TT(t1[:], rs, km[:], ALU.mult), reads=[b_rs, b_km, b_sm], writes=[b_t1])
            S.op("pool", _TT(t1[:], t1[:], vec[:, 5, :], ALU.mult), reads=[b_t1, b_vec], writes=[b_t1])
            S.op("dve", lambda e: e.tensor_reduce(sm[:, 12:16], v3(t1[:]), AX.X, ALU.add), reads=[b_t1], writes=[b_sm])
            S.op("dve", _STT(X4[:, 0, :], kk[:], -1.0, E[:, 0, :], ALU.mult, ALU.mult), reads=[b_kk, b_E[0]], writes=[b_X4])
            S.op("pool", _TT(X4[:, 1, :], rs, E[:, 1, :], ALU.mult), reads=[b_rs, b_E[1]], writes=[b_X4])
            S.op("dve", _TT(X4[:, 2, :], bb[:], E[:, 2, :], ALU.mult), reads=[b_bb, b_E[2]], writes=[b_X4])
            S.op("pool", _TT(X4[:, 3, :], km[:], E[:, 2, :], ALU.mult), reads=[b_km, b_E[2]], writes=[b_X4])
            S.op("dve", _TT(BK[:, 0, :], bb[:], E[:, 3, :], ALU.mult), reads=[b_bb, b_E[3]], writes=[b_BK])
            S.op("pool", _TT(BK[:, 1, :], km[:], E[:, 3, :], ALU.mult), reads=[b_km, b_E[3]], writes=[b_BK])
            for c in range(2):
                rows = slice(64 * c, 64 * c + 64)
                for q in range(4):
                    for h in range(4):
                        blk = q * 4 + h
                        S.mm(K[4 + blk // 8][rows, (blk % 8) * 64:(blk % 8 + 1) * 64],
                             X4[rows, q, h * 64:(h + 1) * 64], identb[rows, rows],
                             reads=[b_X4, b_identb], writes=[b_K[4 + blk // 8]], signal=(c == 1 and blk % 8 == 7))
            XTf = XT[:].rearrange("p q h t -> p (q h t)")
            S.op("act", _CP(XTf[:, 0:512], K[4][:]), reads=[b_K[4]], writes=[b_XT])
            S.op("dve", _TC(XTf[:, 512:1024], K[5][:]), reads=[b_K[5]], writes=[b_XT])
            for c in range(2):
                rows = slice(64 * c, 64 * c + 64)
                S.op("pool", _TT(dgP[rows, :, :], cst[rows, 578:642].unsqueeze(1).to_broadcast([64, 4, 64]),
                                 PC[rows, :, c:c + 1].to_broadcast([64, 4, 64]), ALU.mult),
                     reads=[b_PC, b_cst], writes=[b_dgP])
            for c in range(2):
                rows = slice(64 * c, 64 * c + 64)
                for h in range(4):
                    last = (c == 1 and h == 3)
                    S.mm(K[2][rows, h * 128:(h + 1) * 128], XT[rows, 2, h, :], XT[rows, 0:2, h, :],
                         reads=[b_XT], writes=[b_K[2]], signal=False)
                    S.mm(K[3][rows, h * 128:(h + 1) * 128], XT[rows, 3, h, :], XT[rows, 0:2, h, :],
                         reads=[b_XT], writes=[b_K[3]], signal=False)
                    S.mm(K[6][rows, h * 64:(h + 1) * 64], XT[rows, 0, h, :], XT[rows, 2, h, :],
                         reads=[b_XT], writes=[b_K[6]], signal=last)
            m1b = mask1.unsqueeze(1).to_broadcast([128, 4, 128])
            S.op("dve", _TT(AM1[:], K[2][:].rearrange("p (h t) -> p h t", h=4), m1b, ALU.mult),
                 reads=[b_K[2], b_cst], writes=[b_AM1])
            S.op("dve", _TT(AM2[:], K[3][:].rearrange("p (h t) -> p h t", h=4), m1b, ALU.mult),
                 reads=[b_K[3], b_cst], writes=[b_AM2])
            m3b = mask3.unsqueeze(1).to_broadcast([128, 4, 64])
            S.op("dve", _TT(Lm[0][:, :, 1, :], K[6][:, 0:256].rearrange("p (h t) -> p h t", h=4), m3b, ALU.mult),
                 reads=[b_K[6], b_cst], writes=[b_Lm[0]])
            S.op("pool", _TC(Lm[0][:, :, 0, :], AM1[:, :, 0:64]), reads=[b_AM1], writes=[b_Lm[0]])
            S.op("pool", _TT(Pm[:], AM1[:, :, 0:64], eye2.unsqueeze(1).to_broadcast([128, 4, 64]), ALU.add),
                 reads=[b_AM1, b_cst], writes=[b_Pm])
            cur = 0
            for lvl in range(5):
                nxt = 1 - cur
                lastlvl = (lvl == 4)
                for c in range(2):
                    rows = slice(64 * c, 64 * c + 64)
                    for h in range(4):
                        last = (c == 1 and h == 3)
                        Lc, LTc = Lm[cur][rows, h, 0, :], Lm[cur][rows, h, 1, :]
                        if not lastlvl:
                            S.mm(K[6][rows, (h * 2) * 64:(h * 2 + 1) * 64], LTc, Lc, reads=[b_Lm[cur]],
                                 writes=[b_K[6]], signal=False)
                        S.mm(K[6][rows, (h * 2 + 1) * 64:(h * 2 + 2) * 64], Lc, LTc, reads=[b_Lm[cur]],
                             writes=[b_K[6]], signal=last)
                eng = "act" if lvl % 2 == 0 else "dve"
                src = K[6][:].rearrange("p (h a t) -> p h a t", h=4, a=2)
                if lastlvl:
                    S.op("dve", _TC(Lm[nxt][:, :, 1, :], src[:, :, 1, :]), reads=[b_K[6]], writes=[b_Lm[nxt]])
                elif eng == "act":
                    S.op("act", _CP(Lm[nxt][:], src), reads=[b_K[6]], writes=[b_Lm[nxt]])
                else:
                    S.op("dve", _TC(Lm[nxt][:], src), reads=[b_K[6]], writes=[b_Lm[nxt]])
                for c in range(2):
                    rows = slice(64 * c, 64 * c + 64)
                    for h in range(4):
                        last = (c == 1 and h == 3)
                        S.mm(K[7][rows, h * 64:(h + 1) * 64], Lm[nxt][rows, h, 1, :], Pm[rows, h, :],
                             reads=[b_Lm[nxt], b_Pm], writes=[b_K[7]], signal=last)
                S.op("dve", _TT(Pm[:], K[7][:, 0:256].rearrange("p (h t) -> p h t", h=4), Pm[:], ALU.add),
                     reads=[b_K[7], b_Pm], writes=[b_Pm])
                cur = nxt
            pO_ = K[7]
            for c in range(2):
                rows = slice(64 * c, 64 * c + 64)
                orow = slice(64 * (1 - c), 64 * (1 - c) + 64)
                for h in range(4):
                    hc = slice(256 + h * 64, 256 + (h + 1) * 64)
                    vh = vs[rows, h * 64:(h + 1) * 64]
                    S.mm(K[6][rows, hc], AM2[rows, h, 0:64], vh, reads=[b_AM2, b_vs], writes=[b_K[6]],
                         start=True, stop=False, signal=False)
                    S.mm(K[6][rows, hc], XT[rows, 0, h, :], Hs[rows, h, :], reads=[b_XT, b_H], writes=[b_K[6]],
                         start=False, stop=True, signal=(h == 3))
                S.op("dve", _TC(Ws[rows, :], K[6][rows, 256:512]), reads=[b_K[6]], writes=[b_Ws])
                for h in range(4):
                    hc = slice(h * 64, (h + 1) * 64)
                    S.mm(K[6][rows, hc], Pm[rows, h, :], Ws[rows, hc], reads=[b_Pm, b_Ws], writes=[b_K[6]],
                         signal=(h == 3))
                S.op("act", _CP(Us[rows, :], K[6][rows, 0:256]), reads=[b_K[6]], writes=[b_Us])
                for h in range(4):
                    hc = slice(256 + h * 64, 256 + (h + 1) * 64)
                    vh = vs[rows, h * 64:(h + 1) * 64]
                    uh = Us[rows, h * 64:(h + 1) * 64]
                    S.mm(pO_[rows, hc], AM2[rows, h, 64:128], vh, reads=[b_AM2, b_vs], writes=[b_K[7]],
                         start=True, stop=False, signal=False)
                    S.mm(pO_[rows, hc], XT[rows, 1, h, :], Hs[rows, h, :], reads=[b_XT, b_H], writes=[b_K[7]],
                         start=False, stop=False, signal=False)
                    S.mm(pO_[rows, hc], AM1[rows, h, 64:128], uh, reads=[b_AM1, b_Us], writes=[b_K[7]],
                         start=False, stop=True, signal=False)
                for h in range(4):
                    oc_ = slice(256 + h * 64, 256 + (h + 1) * 64)
                    vh = vs[rows, h * 64:(h + 1) * 64]
                    uh = Us[rows, h * 64:(h + 1) * 64]
                    S.mm(K[5][orow, oc_], BK[rows, 1, h * 64:(h + 1) * 64], vh, reads=[b_BK, b_vs], writes=[b_K[5]],
                         start=True, stop=False, signal=False)
                    S.mm(K[5][orow, oc_], dgP[rows, h, :], Hs[rows, h, :], reads=[b_dgP, b_H], writes=[b_K[5]],
                         start=False, stop=False, signal=False)
                    S.mm(K[5][orow, oc_], BK[rows, 0, h * 64:(h + 1) * 64], uh, reads=[b_BK, b_Us], writes=[b_K[5]],
                         start=False, stop=True, signal=(h == 3))
                S.op("act", _CP(Hs[orow, :, :], K[5][orow, 256:512].rearrange("p (h v) -> p h v", h=4)),
                     reads=[b_K[5], b_K[7]], writes=[b_H])
            o3 = pO_[:, 256:512].rearrange("p (h d) -> p h d", d=64)
            S.op("act", _ACT(t2[:], pO_[:, 256:512], AF.Square), reads=[b_K[7], b_t2], writes=[b_t2])
            S.op("dve", lambda e, o3=o3: e.tensor_reduce(sm[:, 24:28], o3, AX.X, ALU.add), reads=[b_K[7]], writes=[b_sm])
            S.op("dve", lambda e: e.tensor_reduce(sm[:, 28:32], v3(t2[:]), AX.X, ALU.add), reads=[b_t2], writes=[b_sm])
            S.op("dve", _TS(gmv[:, 0:4], sm[:, 24:28], 1.0 / 64.0, None, ALU.mult), reads=[b_sm], writes=[b_gmv])
            S.op("dve", _TT(gmv[:, 4:8], gmv[:, 0:4], gmv[:, 0:4], ALU.mult), reads=[b_gmv], writes=[b_gmv])
            S.op("dve", _STT(gmv[:, 8:12], sm[:, 28:32], 1.0 / 64.0, gmv[:, 4:8], ALU.mult, ALU.subtract),
                 reads=[b_sm, b_gmv], writes=[b_gmv])
            S.op("act", _ACT(sm[:, 16:20], gmv[:, 8:12], AF.Sqrt, bias=epsg[:, 0:1], scale=1.0),
                 reads=[b_gmv, b_epsg], writes=[b_sm])
            S.op("dve", lambda e: e.reciprocal(sm[:, 20:24], sm[:, 16:20]), reads=[b_sm], writes=[b_sm])
            S.op("dve", _TT(v3(on[:]), o3, bc(gmv[:, 0:4]), ALU.subtract),
                 reads=[b_K[7], b_gmv], writes=[b_on])
            S.op("pool", _TT(v3(on[:]), v3(on[:]), bc(sm[:, 20:24]), ALU.mult), reads=[b_on, b_sm], writes=[b_on])
            S.op("pool", _TT(on[:], on[:], vec[:, 6, :], ALU.mult), reads=[b_on, b_vec], writes=[b_on])
            S.op("pool", _TT(on[:], on[:], vec[:, 7, :], ALU.add), reads=[b_on, b_vec], writes=[b_on])
            S.op("dve", _TT(v3(t1[:]), v3(vs[:]), bc(sm[:, 12:16]), ALU.mult), reads=[b_vs, b_sm, b_t1], writes=[b_t1])
            S.op("pool", _TT(on[:], on[:], t1[:], ALU.add), reads=[b_on, b_t1], writes=[b_on])
            S.op("pool", _TT(on[:], on[:], gs[:], ALU.mult), reads=[b_on, b_gs], writes=[b_on])
            for hp in range(2):
                S.op("pe", _TR(K[5][:, hp * 128:(hp + 1) * 128], on[:, hp * 128:(hp + 1) * 128], identf[:]),
                     reads=[b_on, b_identf], writes=[b_K[5]], signal=(hp == 1))
            S.op("act", _CP(orT[:, :, tsl], K[5][:, 0:256].rearrange("p (a t) -> p a t", a=2)),
                 reads=[b_K[5]], writes=[b_orT])
            if j == TPS - 1 or n == B_TILES - 1:
                for hp in range(2):
                    S.dma("sp", D["omix"][st][256 + hp * 128:256 + (hp + 1) * 128, :],
                          orT[:, hp, :], reads=[b_orT])
        S.barrier()
        S.run()


def exchange(nc, S, D, stack):
    ccs = stack.enter_context(nc.semaphore("s_cc"))
    groups = [[0, 1], [2, 3], [4, 5], [6, 7]]
    for k in range(NXCH):
        S.q["pool"].append(lambda e, k=k: e.collective_compute(
            "AllGather", ALU.bypass, replica_groups=groups, ins=[D["omix"][k]], outs=[D["G"][k]]).then_inc(ccs, 1))
    S.extra.append((ccs, NXCH, "s_cc", "cc"))
    S.barrier()
    S.run()


def build_program(phases="ABC", exch=True):
    nc = bass.Bass("TRN2", target_bir_lowering=False)
    D = {}

    def din(name, shape, dt=F32):
        D[name] = nc.dram_tensor(name, list(shape), dt, kind="ExternalInput").ap()

    din("ident", [128, 128])
    if "A" in phases or "B" in phases:
        din("xT", [D_MODEL, SEQ])
    if "A" in phases:
        din("pos", [128, 64], I32)
        din("w_att", [D_MODEL, 768])
        din("maskT", [128, 256])
    if "B" in phases:
        din("w_rw", [D_MODEL, 1024])
        din("mu", [1, 1024])
        for nm in ("w0", "a0", "k_k", "k_a", "r_k", "gn_g", "gn_b"):
            din(nm, [1, 256])
        din("w_dec", [64, 256])
        din("w_aaa", [64, 256])
        din("w_gate", [128, 256])
        din("cstB", [128, 642])
    if "C" in phases:
        din("xres", [TOKH, D_MODEL])
        din("w_out", [1024, 1024])
        din("wg", [1024, FFN])
        din("wu", [1024, FFN])
        din("wd", [FFN, 1024])
        for nm in ("ln1g", "ln1b", "ln2g", "ln2b"):
            din(nm, [1, 1024])
        din("sel", [128, 2])
        D["out"] = nc.dram_tensor("out", [TOKH, D_MODEL], F32, kind="ExternalOutput").ap()
    full = exch
    if full:
        D["omix"] = [nc.dram_tensor("omix%d" % k, [512, XCH], BF16, kind="Internal").ap() for k in range(NXCH)]
        D["G"] = [nc.dram_tensor("G%d" % k, [1024, XCH], BF16, kind="Internal").ap() for k in range(NXCH)]
    else:
        if "C" in phases:
            din("G", [NXCH, 1024, XCH], BF16)
            D["G"] = [D["G"][k] for k in range(NXCH)]
        if "A" in phases or "B" in phases:
            om = nc.dram_tensor("omix", [NXCH, 512, XCH], BF16, kind="ExternalOutput").ap()
            D["omix"] = [om[k] for k in range(NXCH)]
    with ExitStack() as st:
        S = Sched(nc, st)
        if "C" in phases:
            D["wgs"] = nc.dram_tensor("wgs", [HC, 128, 2, 8, 128], BF16, kind="Internal").ap()
            prep_ffn_weights(nc, S, D)
        if "A" in phases:
            phase_a(nc, S, D)
        if "B" in phases:
            phase_b(nc, S, D)
        if full:
            exchange(nc, S, D, st)
        if "C" in phases:
            phase_c(nc, S, D)
    return nc


def att_mask():
    i_k = np.arange(128)[:, None]
    i_q = np.arange(128)[None, :]
    m = np.zeros((128, 256), np.float32)
    m[:, 0:128] = np.where(i_k >= i_q, 0.0, NEG)
    m[:, 128:256] = np.where(i_k <= i_q, 0.0, NEG)
    return m


def rwkv_consts():
    j = np.arange(128)[:, None]
    t = np.arange(128)[None, :]
    same = (j // 64) == (t // 64)
    c = np.zeros((128, 642), np.float32)
    c[:, 0:128] = same & (j <= t)
    c[:, 128:256] = same & (j < t)
    c[:, 256:384] = same & (j > t)
    c[:, 384] = (np.arange(128) < 64)
    c[:, 385] = (np.arange(128) >= 64)
    jj = (np.arange(128) % 64)[:, None]
    tt = np.arange(64)[None, :]
    c[:, 386:450] = jj < tt
    c[:, 450:514] = jj <= tt
    c[:, 514:578] = tt < jj
    c[:, 578:642] = jj == tt
    return c


def core_inputs(inp, c, phases="ABC"):
    b, g = c // 2, c % 2
    m = {"ident": np.eye(128, dtype=np.float32)}
    w_in = inp["w_in"][0]
    if "A" in phases or "B" in phases:
        m["xT"] = np.ascontiguousarray(inp["x"][b].T)
    if "A" in phases:
        m["pos"] = np.ascontiguousarray(inp["positions"][b].reshape(64, 128).T).astype(np.int32)
        cols = np.concatenate([np.arange(256 * g, 256 * g + 256) + off for off in (0, 512, 1024)])
        m["w_att"] = np.ascontiguousarray(w_in[:, cols])
        m["maskT"] = att_mask()
    if "B" in phases:
        hs = slice(256 * g, 256 * g + 256)
        rcols = np.concatenate([1536 + off + np.arange(256 * g, 256 * g + 256) for off in (0, 512, 1024)]
                               + [1536 + 1536 + np.arange(256)])
        m["w_rw"] = np.ascontiguousarray(w_in[:, rcols])
        m["mu"] = np.ascontiguousarray(inp["mu_shift"][0][rcols - 1536][None, :])
        for nm in ("w0", "a0", "k_k", "k_a", "gn_g", "gn_b"):
            m[nm] = np.ascontiguousarray(inp[nm][0][hs][None, :])
        m["r_k"] = np.ascontiguousarray(inp["r_k"][0][4 * g:4 * g + 4].reshape(1, 256))
        m["w_dec"] = np.ascontiguousarray(inp["w_decay_up"][0][:, hs])
        m["w_aaa"] = np.ascontiguousarray(inp["w_aaa_up"][0][:, hs])
        m["w_gate"] = np.ascontiguousarray(inp["w_gate_up"][0][:, hs])
        m["cstB"] = rwkv_consts()
    if "C" in phases:
        fi = lambda r: np.concatenate([np.arange(256 * r, 256 * r + 256), 512 + np.arange(256 * r, 256 * r + 256)])
        perm = np.concatenate([fi(0), fi(1)])
        sel = np.zeros((128, 2), np.float32)
        sel[:, g] = 1.0
        m.update(xres=np.ascontiguousarray(inp["x"][b, g * TOKH:(g + 1) * TOKH]),
                 w_out=np.ascontiguousarray(inp["w_out"][0][perm]),
                 wg=inp["w_ffn_gate"][0], wu=inp["w_ffn_up"][0], wd=inp["w_ffn_down"][0],
                 ln1g=inp["ln_mix_g"], ln1b=inp["ln_mix_b"], ln2g=inp["ln_ffn_g"], ln2b=inp["ln_ffn_b"],
                 sel=sel)
    return m


_NC_CACHE = {}


def kernel(**inputs):
    inp = {k: np.asarray(v) for k, v in inputs.items()}
    if "nc" not in _NC_CACHE:
        _NC_CACHE["nc"] = build_program("ABC")
    nc = _NC_CACHE["nc"]
    in_maps = [core_inputs(inp, c, "ABC") for c in range(8)]
    res = run_bass_kernel_spmd(nc, in_maps, core_ids=list(range(8)))
    out = np.empty((BATCH, SEQ, D_MODEL), np.float32)
    for c in range(8):
        b, g = c // 2, c % 2
        out[b, g * TOKH:(g + 1) * TOKH] = np.asarray(res.results[c]["out"], dtype=np.float32)
    return out
```

```python
import math
from contextlib import ExitStack

import numpy as np
import concourse.bass as bass
import concourse.mybir as mybir
from concourse.bass_utils import run_bass_kernel_spmd

F32 = mybir.dt.float32
BF16 = mybir.dt.bfloat16
I32 = mybir.dt.int32
AF = mybir.ActivationFunctionType
ALU = mybir.AluOpType
AX = mybir.AxisListType

D_MODEL = 1024
SEQ = 8192
BATCH = 4
HD = 64
FFN = 2816
HC = FFN // 128
ALPHA = 2.0 ** 0.25
LN_EPS = 1e-5
GN_EPS = 64e-5
L2_EPS = 1e-6
DECAY_SCALE = math.exp(-0.5)
NEG = -30000.0
ROPE_THETA = 500000.0
TOKH = SEQ // 2
XCH = 1024
NXCH = SEQ // XCH
B_TILES = SEQ // 128


class Buf:
    __slots__ = ("name", "w", "r", "psum")

    def __init__(self, name="", psum=False):
        self.name = name
        self.w = []
        self.r = {}
        self.psum = psum


class Sched:
    ENG = ("pe", "act", "dve", "pool", "sp")
    NDS = 8

    def __init__(self, nc, stack):
        self.nc = nc
        self.q = {e: [] for e in self.ENG}
        self.sem = {e: stack.enter_context(nc.semaphore("s_" + e)) for e in self.ENG}
        self.cnt = {e: 0 for e in self.ENG}
        self.seen = {e: {} for e in self.ENG}
        self.dq = ("sp", "act", "pool")
        self.dsem = {e: [stack.enter_context(nc.semaphore("d_%s%d" % (e, i)))
                         for i in range(self.NDS)] for e in self.dq}
        self.dcnt = {e: 0 for e in self.dq}
        self.dlast = {e: [None] * self.NDS for e in self.dq}
        self.extra = []

    def _wait(self, eng, tok):
        if tok is None:
            return
        sem, val, key, prod = tok
        if self.seen[eng].get(key, 0) >= val:
            return
        self.seen[eng][key] = val
        self.q[eng].append(lambda e, s=sem, v=val: e.wait_ge(s, v))

    def _deps(self, eng, reads, writes):
        for b in reads:
            for t in b.w:
                self._wait(eng, t)
            if b.psum:
                for t in b.r.values():
                    if t[3] != eng:
                        self._wait(eng, t)
        for b in writes:
            for t in b.w:
                if t[3] != eng:
                    self._wait(eng, t)
            for t in b.r.values():
                if t[3] != eng:
                    self._wait(eng, t)

    def _mark(self, tok, reads, writes, acc=False):
        for b in reads:
            b.r[tok[2]] = tok
        for b in writes:
            if acc:
                b.w.append(tok)
            else:
                b.w = [tok]
            b.r = {}

    def op(self, eng, fn, reads=(), writes=(), signal=True):
        self._deps(eng, reads, writes)
        sem = self.sem[eng]
        if signal:
            self.cnt[eng] += 1
            tok = (sem, self.cnt[eng], eng, eng)
            self.q[eng].append(lambda e, f=fn, s=sem: f(e).then_inc(s, 1))
        else:
            tok = (sem, self.cnt[eng] + 1, eng, eng)
            self.q[eng].append(lambda e, f=fn: f(e))
        self._mark(tok, reads, writes)
        return tok

    def mm(self, out, lhsT, rhs, reads, writes, start=True, stop=True, signal=True, **kw):
        return self.op("pe", lambda e: e.matmul(out, lhsT, rhs, start=start, stop=stop, **kw),
                       reads, writes, signal)

    def dma(self, queue, out, in_, reads=(), writes=(), acc=False, **kw):
        i = self.dcnt[queue]
        self.dcnt[queue] += 1
        slot = i % self.NDS
        self._wait(queue, self.dlast[queue][slot])
        self._deps(queue, reads, writes)
        sem = self.dsem[queue][slot]
        val = 16 * (i // self.NDS + 1)
        tok = (sem, val, "d_%s%d" % (queue, slot), "dma")
        self.dlast[queue][slot] = tok
        self.q[queue].append(
            lambda e, o=out, a=in_, s=sem, k=kw: e.dma_start(out=o, in_=a, **k).then_inc(s, 16))
        self._mark(tok, reads, writes, acc=acc)
        return tok

    def barrier(self):
        toks = [(self.sem[o], self.cnt[o], o, o) for o in self.ENG if self.cnt[o] > 0]
        for qn in self.dq:
            toks += [t for t in self.dlast[qn] if t is not None]
        toks += self.extra
        for e in self.ENG:
            for t in toks:
                if t[3] != e:
                    self._wait(e, t)

    def run(self):
        nc = self.nc
        q = self.q
        with nc.Block() as block:
            @block.tensor
            def _(e):
                for f in q["pe"]:
                    f(e)

            @block.scalar
            def _(e):
                for f in q["act"]:
                    f(e)

            @block.vector
            def _(e):
                for f in q["dve"]:
                    f(e)

            @block.gpsimd
            def _(e):
                for f in q["pool"]:
                    f(e)

            @block.sync
            def _(e):
                for f in q["sp"]:
                    f(e)
        self.q = {e: [] for e in self.ENG}


def bcast_rows(ap, n=128):
    return bass.AP(ap.tensor, ap.offset, [[0, n], [1, ap.shape[-1]]])


def prep_ffn_weights(nc, S, D):
    wgv = D["wg"].rearrange("(c p) n -> p c n", p=128)
    wuv = D["wu"].rearrange("(c p) n -> p c n", p=128)
    D["b_wgs"] = [Buf() for _ in range(HC)]
    for h in range(HC):
        S.dma("pool", D["wgs"][h, :, 0, :, :], wgv[:, :, h * 128:(h + 1) * 128], writes=[D["b_wgs"][h]])
        S.dma("pool", D["wgs"][h, :, 1, :, :], wuv[:, :, h * 128:(h + 1) * 128], writes=[D["b_wgs"][h]], acc=True)


def phase_c(nc, S, D):
    GT = 512
    NG = TOKH // GT
    TPG = GT // 128
    OCW = 256
    with ExitStack() as ph:
        def sb(n, shp, dt):
            return ph.enter_context(nc.sbuf_tensor(n, shp, dt))

        def ps(n, shp, dt):
            return ph.enter_context(nc.psum_tensor(n, shp, dt))

        Gv = [g_.rearrange("(c p) t -> p c t", p=128) for g_ in D["G"]]
        ident = sb("c_ident", [128, 128], BF16)
        b_ident = Buf()
        S.dma("pool", ident[:], D["ident"], writes=[b_ident])
        sel = sb("c_sel", [128, 2], F32)
        b_sel = Buf()
        S.dma("sp", sel[:], D["sel"], writes=[b_sel])
        woutA = sb("c_woutA", [128, 8, 1024], BF16)
        woutB = sb("c_woutB", [128, 8, 1024], BF16)
        b_wout = Buf()
        b_woutB = Buf()
        S.dma("pool", woutA[:], D["w_out"].rearrange("(c p) n -> p c n", p=128), writes=[b_wout])
        S.op("dve", lambda e: e.tensor_scalar(woutB[:], woutA[:], sel[:, 1:2], None, ALU.mult),
             reads=[b_wout, b_sel], writes=[b_woutB])
        S.op("dve", lambda e: e.tensor_scalar(woutA[:], woutA[:], sel[:, 0:1], None, ALU.mult),
             reads=[b_wout, b_sel, b_woutB], writes=[b_wout])
        wd = sb("c_wd", [128, HC, 1024], BF16)
        b_wd = [Buf() for _ in range(HC)]
        for h in range(HC):
            S.dma("pool", wd[:, h, :], D["wd"][h * 128:(h + 1) * 128, :], writes=[b_wd[h]])
        lnp = sb("c_lnp", [128, 4, 1024], F32)
        b_lnp = [Buf() for _ in range(4)]
        for i, nm in enumerate(("ln1g", "ln1b", "ln2g", "ln2b")):
            S.dma("sp", lnp[:, i, :], bcast_rows(D[nm]), writes=[b_lnp[i]])

        NW = 4
        wgu = [sb("c_wgu%d" % i, [128, 2, 8, 128], BF16) for i in range(NW)]
        b_wgu = [Buf() for _ in range(NW)]

        h1g = sb("c_h1g", [128, TPG, 1024], F32)
        b_h1g = [Buf() for _ in range(TPG)]
        h1T = sb("c_h1T", [128, 8, GT], BF16)
        b_h1T = [Buf() for _ in range(TPG)]
        actT = sb("c_actT", [128, HC, GT], BF16)
        b_actT = [Buf() for _ in range(HC)]
        NB = 2
        ocA = [sb("c_ocA%d" % i, [128, 8, OCW], BF16) for i in range(NB)]
        ocB = [sb("c_ocB%d" % i, [128, 8, OCW], BF16) for i in range(NB)]
        b_ocA = [Buf() for _ in range(NB)]
        b_ocB = [Buf() for _ in range(NB)]
        xt = [sb("c_xt%d" % i, [128, 1024], F32) for i in range(NB)]
        b_xt = [Buf() for _ in range(NB)]
        hpre = sb("c_hpre", [128, 1024], F32)
        b_hpre = Buf()
        hn = sb("c_hn", [128, 1024], F32)
        b_hn = Buf()
        h1b = sb("c_h1b", [128, 1024], BF16)
        b_h1b = Buf()
        stats = sb("c_stats", [128, 2, 6], F32)
        b_stats = Buf()
        mv = sb("c_mv", [128, 4], F32)
        b_mv = Buf()
        sg = [sb("c_sg%d" % i, [128, 512], BF16) for i in range(2)]
        b_sg = [Buf() for _ in range(2)]
        outt = [sb("c_outt%d" % i, [128, 1024], F32) for i in range(NB)]
        b_outt = [Buf() for _ in range(NB)]

        epsc = sb("c_eps", [128, 1], F32)
        b_epsc = Buf()
        S.op("dve", lambda e: e.memset(epsc[:], LN_EPS), writes=[b_epsc])
        pmix = ps("c_pmix", [128, 1024], F32)
        b_pmix = Buf(psum=True)
        pT = ps("c_pT", [128, 1024], BF16)
        b_pT = Buf(psum=True)
        pG = [ps("c_pG%d" % i, [128, 512], F32) for i in range(2)]
        pU = [ps("c_pU%d" % i, [128, 512], F32) for i in range(2)]
        b_pG = [Buf(psum=True) for _ in range(2)]
        b_pU = [Buf(psum=True) for _ in range(2)]

        def layer_norm(src_b, gi_, bi_, dst, b_dst):
            for hf in range(2):
                S.op("dve", lambda e, hf=hf: e.bn_stats(stats[:, hf, :], hpre[:, hf * 512:(hf + 1) * 512]),
                     reads=[src_b], writes=[b_stats])
            S.op("dve", lambda e: e.bn_aggr(mv[:, 0:2], stats[:].rearrange("p a b -> p (a b)")),
                 reads=[b_stats], writes=[b_mv])
            S.op("act", lambda e: e.activation(mv[:, 3:4], mv[:, 1:2], AF.Sqrt, bias=epsc[:, 0:1], scale=1.0),
                 reads=[b_mv, b_epsc], writes=[b_mv])
            S.op("dve", lambda e: e.reciprocal(mv[:, 2:3], mv[:, 3:4]), reads=[b_mv], writes=[b_mv])
            S.op("dve", lambda e: e.tensor_scalar(hn[:], hpre[:], mv[:, 0:1], mv[:, 2:3],
                                                  ALU.subtract, ALU.mult),
                 reads=[src_b, b_mv], writes=[b_hn])
            S.op("pool", lambda e: e.tensor_tensor(hn[:], hn[:], lnp[:, gi_, :], ALU.mult),
                 reads=[b_hn, b_lnp[gi_]], writes=[b_hn])
            S.op("dve", lambda e: e.tensor_tensor(dst, hn[:], lnp[:, bi_, :], ALU.add),
                 reads=[b_hn, b_lnp[bi_]], writes=[b_dst])

        nwl = 0
        for gi in range(NG):
            for ti in range(TPG):
                it = gi * TPG + ti
                sl = it % NB
                tok0 = it * 128
                oi = (tok0 // OCW) % NB
                oo = tok0 % OCW
                if oo == 0:
                    ta, tb = tok0, TOKH + tok0
                    S.dma("sp", ocA[oi][:], Gv[ta // XCH][:, :, ta % XCH:ta % XCH + OCW], writes=[b_ocA[oi]])
                    S.dma("sp", ocB[oi][:], Gv[tb // XCH][:, :, tb % XCH:tb % XCH + OCW], writes=[b_ocB[oi]])
                S.dma("sp", xt[sl][:], D["xres"][tok0:tok0 + 128, :], writes=[b_xt[sl]])
                for hf in range(2):
                    cs = slice(hf * 512, (hf + 1) * 512)
                    for c in range(8):
                        S.mm(pmix[:, cs], ocA[oi][:, c, oo:oo + 128], woutA[:, c, cs],
                             reads=[b_ocA[oi], b_wout], writes=[b_pmix], start=(c == 0), stop=False, signal=False)
                    for c in range(8):
                        S.mm(pmix[:, cs], ocB[oi][:, c, oo:oo + 128], woutB[:, c, cs],
                             reads=[b_ocB[oi], b_woutB], writes=[b_pmix], start=False, stop=(c == 7),
                             signal=(c == 7))
                for hf in range(2):
                    S.op("dve", lambda e, hf=hf, sl=sl: e.scalar_tensor_tensor(
                        hpre[:, hf * 512:(hf + 1) * 512], xt[sl][:, hf * 512:(hf + 1) * 512], ALPHA,
                        pmix[:, hf * 512:(hf + 1) * 512], ALU.mult, ALU.add),
                        reads=[b_xt[sl], b_pmix], writes=[b_hpre])
                layer_norm(b_hpre, 0, 1, h1g[:, ti, :], b_h1g[ti])
                S.op("act", lambda e, ti=ti: e.copy(h1b[:], h1g[:, ti, :]), reads=[b_h1g[ti]], writes=[b_h1b])
                for c in range(8):
                    S.op("pe", lambda e, c=c: e.transpose(pT[:, c * 128:(c + 1) * 128],
                                                          h1b[:, c * 128:(c + 1) * 128], ident[:]),
                         reads=[b_h1b, b_ident], writes=[b_pT], signal=(c == 7))
                S.op("act", lambda e, ti=ti: e.copy(h1T[:, :, ti * 128:(ti + 1) * 128],
                                                    pT[:].rearrange("p (c t) -> p c t", c=8)),
                     reads=[b_pT], writes=[b_h1T[ti]])
            for h in range(HC):
                ws = nwl % NW
                nwl += 1
                S.dma("sp", wgu[ws][:].rearrange("p a c n -> p (a c n)"),
                      D["wgs"][h].rearrange("p a c n -> p (a c n)"), reads=[D["b_wgs"][h]], writes=[b_wgu[ws]])
                pb = h % 2
                for c in range(8):
                    S.mm(pG[pb][:], wgu[ws][:, 0, c, :], h1T[:, c, :], reads=[b_wgu[ws]] + b_h1T,
                         writes=[b_pG[pb]], start=(c == 0), stop=(c == 7), signal=(c == 7))
                for c in range(8):
                    S.mm(pU[pb][:], wgu[ws][:, 1, c, :], h1T[:, c, :], reads=[b_wgu[ws]] + b_h1T,
                         writes=[b_pU[pb]], start=(c == 0), stop=(c == 7), signal=(c == 7))
                S.op("act", lambda e, pb=pb: e.activation(sg[pb][:], pG[pb][:], AF.Silu),
                     reads=[b_pG[pb]], writes=[b_sg[pb]])
                S.op("dve", lambda e, pb=pb, h=h: e.tensor_tensor(actT[:, h, :], pU[pb][:], sg[pb][:], ALU.mult),
                     reads=[b_pU[pb], b_sg[pb]], writes=[b_actT[h]])
            for ti in range(TPG):
                it = gi * TPG + ti
                sl = it % NB
                tok0 = it * 128
                for hf in range(2):
                    for h in range(HC):
                        S.mm(pmix[:, hf * 512:(hf + 1) * 512], actT[:, h, ti * 128:(ti + 1) * 128],
                             wd[:, h, hf * 512:(hf + 1) * 512], reads=[b_actT[h], b_wd[h]], writes=[b_pmix],
                             start=(h == 0), stop=(h == HC - 1), signal=(h == HC - 1))
                for hf in range(2):
                    S.op("dve", lambda e, hf=hf, ti=ti: e.scalar_tensor_tensor(
                        hpre[:, hf * 512:(hf + 1) * 512], h1g[:, ti, hf * 512:(hf + 1) * 512], ALPHA,
                        pmix[:, hf * 512:(hf + 1) * 512], ALU.mult, ALU.add),
                        reads=[b_h1g[ti], b_pmix], writes=[b_hpre])
                layer_norm(b_hpre, 2, 3, outt[sl][:], b_outt[sl])
                S.dma("sp", D["out"][tok0:tok0 + 128, :], outt[sl][:], reads=[b_outt[sl]])
        S.barrier()
        S.run()


def phase_a(nc, S, D):
    ST = 2048
    NST = SEQ // ST
    inv_freq = [float(np.float32(ROPE_THETA) ** np.float32(-i / 8.0)) for i in range(8)]
    TWO_PI = 2.0 * math.pi
    C1 = 6.28125
    C2 = TWO_PI - C1
    with ExitStack() as ph:
        def sb(n, shp, dt):
            return ph.enter_context(nc.sbuf_tensor(n, shp, dt))

        def ps(n, shp, dt):
            return ph.enter_context(nc.psum_tensor(n, shp, dt))

        identf = sb("a_identf", [128, 128], F32)
        b_identf = Buf()
        S.dma("sp", identf[:], D["ident"], writes=[b_identf])
        identb = sb("a_identb", [128, 128], BF16)
        b_identb = Buf()
        S.dma("pool", identb[:], D["ident"], writes=[b_identb])
        maskb = sb("a_maskb", [128, 256], BF16)
        b_maskb = Buf()
        S.dma("pool", maskb[:], D["maskT"], writes=[b_maskb])
        ones = sb("a_ones", [128, 128], F32)
        b_ones = Buf()
        S.op("dve", lambda e: e.memset(ones[:], 1.0), writes=[b_ones])
        watt = sb("a_watt", [128, 8, 768], BF16)
        b_watt = Buf()
        S.dma("pool", watt[:], D["w_att"].rearrange("(c p) n -> p c n", p=128), writes=[b_watt])

        posi = sb("a_posi", [128, 64], I32)
        b_posi = Buf()
        S.dma("sp", posi[:], D["pos"], writes=[b_posi])
        posf = sb("a_posf", [128, 64], F32)
        b_posf = Buf()
        S.op("dve", lambda e: e.tensor_copy(posf[:], posi[:]), reads=[b_posi], writes=[b_posf])
        ang = sb("a_ang", [128, 64, 8], F32)
        b_ang = Buf()
        for i in range(8):
            S.op("dve", lambda e, i=i: e.tensor_scalar(ang[:, :, i], posf[:], inv_freq[i], None, ALU.mult),
                 reads=[b_posf], writes=[b_ang])
        sinT = sb("a_sinT", [128, 64, 8], F32)
        cosT = sb("a_cosT", [128, 64, 8], F32)
        b_sinT = Buf()
        b_cosT = Buf()
        kq = sb("a_kq", [128, 512], I32)
        kf = sb("a_kf", [128, 512], F32)
        red = sb("a_red", [128, 512], F32)
        msk = sb("a_msk", [128, 512], F32)
        b_tmp = Buf()
        angf = ang[:].rearrange("p a b -> p (a b)")

        def wrap(dst):
            S.op("dve", lambda e: e.tensor_scalar(msk[:], dst, math.pi, -TWO_PI, ALU.is_gt, ALU.mult),
                 reads=[b_tmp], writes=[b_tmp])
            S.op("dve", lambda e: e.tensor_tensor(dst, dst, msk[:], ALU.add), reads=[b_tmp], writes=[b_tmp])

        S.op("dve", lambda e: e.tensor_scalar(kq[:], angf, 1.0 / TWO_PI, None, ALU.mult),
             reads=[b_ang], writes=[b_tmp])
        S.op("dve", lambda e: e.tensor_copy(kf[:], kq[:]), reads=[b_tmp], writes=[b_tmp])
        S.op("dve", lambda e: e.scalar_tensor_tensor(red[:], kf[:], -C1, angf, ALU.mult, ALU.add),
             reads=[b_tmp, b_ang], writes=[b_tmp])
        S.op("dve", lambda e: e.scalar_tensor_tensor(red[:], kf[:], -C2, red[:], ALU.mult, ALU.add),
             reads=[b_tmp], writes=[b_tmp])
        wrap(red[:])
        S.op("act", lambda e: e.activation(sinT[:].rearrange("p a b -> p (a b)"), red[:], AF.Sin),
             reads=[b_tmp], writes=[b_sinT])
        S.op("dve", lambda e: e.tensor_scalar(red[:], red[:], math.pi / 2.0, None, ALU.add),
             reads=[b_tmp, b_sinT], writes=[b_tmp])
        wrap(red[:])
        S.op("act", lambda e: e.activation(cosT[:].rearrange("p a b -> p (a b)"), red[:], AF.Sin),
             reads=[b_tmp], writes=[b_cosT])

        xTs = [sb("a_xT%d" % i, [128, 8, ST], BF16) for i in range(2)]
        b_xTs = [Buf() for _ in range(2)]
        xTv = D["xT"].rearrange("(c p) t -> p c t", p=128)
        qT = sb("a_qT", [128, 2, ST], BF16)
        b_qT = Buf()
        kT = [sb("a_kT%d" % i, [128, 2, ST], BF16) for i in range(2)]
        b_kT = [Buf() for _ in range(2)]
        V = [[sb("a_V%d_%d" % (l, i), [128, 16, 4, 65], BF16) for i in range(2)] for l in range(3)]
        b_V = [[Buf() for _ in range(2)] for l in range(3)]
        for l in range(3):
            for i in range(2):
                S.op("pool", lambda e, l=l, i=i: e.memset(V[l][i][:], 1.0), writes=[b_V[l][i]])
        ysb = sb("a_ysb", [128, 512], F32)
        b_ysb = Buf()
        ysb3 = ysb[:].rearrange("p (h d) -> p h d", d=64)
        rt = sb("a_rt", [128, 4, 8, 8], F32)
        b_rt = Buf()
        sq = sb("a_sq", [128, 512], F32)
        b_sq = Buf()
        ssq = sb("a_ssq", [128, 8], F32)
        b_ssq = Buf()
        rm = sb("a_rm", [128, 8], F32)
        b_rm = Buf()
        st4 = sb("a_st4", [4, 8], F32)
        b_st4 = Buf()
        dg4 = sb("a_dg4", [4, 4], F32)
        b_dg4 = Buf()
        negM = sb("a_negM", [128, 4], F32)
        b_negM = Buf()
        NPT = 3
        PT = [sb("a_PT%d" % i, [128, 256], BF16) for i in range(NPT)]
        b_PT = [Buf() for _ in range(NPT)]
        oacc = sb("a_oacc", [128, ST], F32)
        b_oacc = Buf()
        oT = [sb("a_oT%d" % i, [64, ST], BF16) for i in range(2)]
        b_oT = [Buf() for _ in range(2)]

        pqk = ps("a_pqk", [128, 512], F32)
        b_pqk = Buf(psum=True)
        pX = ps("a_pX", [128, 512], F32)
        b_pX = Buf(psum=True)
        pS = [ps("a_pS%d" % i, [128, 512], F32) for i in range(2)]
        b_pS = [Buf(psum=True) for _ in range(2)]
        pO = ps("a_pO", [128, ST], F32)
        b_pO = Buf(psum=True)

        S.op("dve", lambda e: e.memset(st4[:], 0.0), writes=[b_st4])

        nblk = 0
        for st in range(NST):
            xs = st % 2
            ks = st % 2
            kp = 1 - ks
            xT = xTs[xs]
            S.dma("pool", xT[:], xTv[:, :, st * ST:(st + 1) * ST], writes=[b_xTs[xs]])
            S.op("dve", lambda e: e.memset(rm[:], 0.0), reads=[], writes=[b_rm])
            for j in range(16):
                n = st * 16 + j
                for c in range(8):
                    S.mm(pqk[:], xT[:, c, j * 128:(j + 1) * 128], watt[:, c, 0:512],
                         reads=[b_xTs[xs], b_watt], writes=[b_pqk], start=(c == 0), stop=(c == 7), signal=(c == 7))
                S.op("act", lambda e: e.mul(ysb[:, 0:256], pqk[:, 0:256], 0.125), reads=[b_pqk], writes=[b_ysb])
                S.op("act", lambda e: e.copy(ysb[:, 256:512], pqk[:, 256:512]), reads=[b_pqk], writes=[b_ysb])
                cb = cosT[:, n:n + 1, :].to_broadcast([128, 8, 8])
                sbb = sinT[:, n:n + 1, :].to_broadcast([128, 8, 8])
                x1 = ysb3[:, :, 0:8]
                x2 = ysb3[:, :, 8:16]
                rtv = rt[:].rearrange("p a h d -> p a (h d)")
                for a, (xx, tt) in enumerate(((x1, cb), (x2, sbb), (x2, cb), (x1, sbb))):
                    S.op("pool", lambda e, a=a, xx=xx, tt=tt: e.tensor_tensor(rt[:, a, :, :], xx, tt, ALU.mult),
                         reads=[b_ysb, b_cosT, b_sinT], writes=[b_rt])
                S.op("pool", lambda e: e.tensor_tensor(x1, rt[:, 0, :, :], rt[:, 1, :, :], ALU.subtract),
                     reads=[b_rt, b_ysb], writes=[b_ysb])
                S.op("pool", lambda e: e.tensor_tensor(x2, rt[:, 2, :, :], rt[:, 3, :, :], ALU.add),
                     reads=[b_rt, b_ysb], writes=[b_ysb])
                S.op("dve", lambda e: e.tensor_tensor(sq[:], ysb[:], ysb[:], ALU.mult), reads=[b_ysb], writes=[b_sq])
                S.op("dve", lambda e: e.tensor_reduce(ssq[:], sq[:].rearrange("p (h d) -> p h d", d=64), AX.X, ALU.add),
                     reads=[b_sq], writes=[b_ssq])
                S.op("dve", lambda e: e.tensor_tensor(rm[:], rm[:], ssq[:], ALU.max), reads=[b_ssq, b_rm], writes=[b_rm])
                for blk in range(4):
                    S.op("pe", lambda e, blk=blk: e.transpose(pX[:, blk * 128:(blk + 1) * 128],
                                                               ysb[:, blk * 128:(blk + 1) * 128], identf[:]),
                         reads=[b_ysb, b_identf], writes=[b_pX], signal=(blk == 3))
                S.op("act", lambda e, j=j: e.copy(qT[:, :, j * 128:(j + 1) * 128],
                                                  pX[:, 0:256].rearrange("p (a t) -> p a t", a=2)),
                     reads=[b_pX], writes=[b_qT])
                S.op("act", lambda e, j=j, ks=ks: e.copy(kT[ks][:, :, j * 128:(j + 1) * 128],
                                                         pX[:, 256:512].rearrange("p (a t) -> p a t", a=2)),
                     reads=[b_pX], writes=[b_kT[ks]])
            S.op("pe", lambda e: e.transpose(pX[0:4, 0:128], rm[:, 0:4], identf[:]), reads=[b_rm, b_identf],
                 writes=[b_pX], signal=False)
            S.op("pe", lambda e: e.transpose(pX[0:4, 128:256], rm[:, 4:8], identf[:]), reads=[b_rm, b_identf],
                 writes=[b_pX])
            S.op("dve", lambda e: e.tensor_copy(st4[:, 2:3], st4[:, 1:2]), reads=[b_st4], writes=[b_st4])
            S.op("dve", lambda e: e.tensor_reduce(st4[:, 0:2], pX[0:4, 0:256].rearrange("p (a t) -> p a t", a=2),
                                                  AX.X, ALU.max), reads=[b_pX, b_st4], writes=[b_st4])
            S.op("dve", lambda e: e.tensor_tensor(st4[:, 3:4], st4[:, 1:2], st4[:, 2:3], ALU.max),
                 reads=[b_st4], writes=[b_st4])
            S.op("dve", lambda e: e.tensor_tensor(st4[:, 3:4], st4[:, 3:4], st4[:, 0:1], ALU.mult),
                 reads=[b_st4], writes=[b_st4])
            S.op("dve", lambda e: e.tensor_scalar(dg4[:], identf[0:4, 0:4], st4[:, 3:4], None, ALU.mult),
                 reads=[b_st4, b_identf], writes=[b_dg4])
            S.mm(pX[:, 256:260], ones[0:4, :], dg4[:], reads=[b_ones, b_dg4], writes=[b_pX])
            S.op("act", lambda e: e.activation(negM[:], pX[:, 256:260], AF.Sqrt), reads=[b_pX], writes=[b_negM])
            S.op("dve", lambda e: e.tensor_scalar(negM[:], negM[:], -1.0, None, ALU.mult), reads=[b_negM],
                 writes=[b_negM])
            for l, dil in enumerate((1, 4, 16)):
                for t16 in range(16):
                    if dil == 1:
                        a0, a1 = t16 * 128, (t16 + 1) * 128
                    elif dil == 4:
                        n4, r = t16 // 4, t16 % 4
                        a0, a1 = n4 * 512 + r, (n4 + 1) * 512
                    else:
                        a0, a1 = t16, ST
                    for c in range(8):
                        S.mm(pX[:, 0:256], xT[:, c, a0:a1:dil], watt[:, c, 512:768],
                             reads=[b_xTs[xs], b_watt], writes=[b_pX], start=(c == 0), stop=(c == 7), signal=(c == 7))
                    S.op("act", lambda e, l=l, t16=t16, ks=ks: e.copy(
                        V[l][ks][:, t16, :, 0:64], pX[:, 0:256].rearrange("p (h d) -> p h d", d=64)),
                        reads=[b_pX], writes=[b_V[l][ks]])
            for h in range(4):
                hp, p0 = h // 2, 64 * (h % 2)
                first = [True] * 4

                def block(qsl, cur, prev, outs):
                    nonlocal nblk
                    pb = nblk % 2
                    pt = nblk % NPT
                    nblk += 1
                    lo = 0 if prev is not None else 128
                    qap = qT[p0:p0 + 64, hp, qsl]
                    if prev is not None:
                        pslot, psl, pv_ap = prev
                        S.mm(pS[pb][:, 0:128], kT[pslot][p0:p0 + 64, hp, psl], qap,
                             reads=[b_kT[pslot], b_qT], writes=[b_pS[pb]], start=True, stop=False, signal=False)
                        S.mm(pS[pb][:, 0:128], identb[:], maskb[:, 0:128], reads=[b_identb, b_maskb],
                             writes=[b_pS[pb]], start=False, stop=True, signal=False)
                    S.mm(pS[pb][:, 128:256], kT[ks][p0:p0 + 64, hp, qsl], qap,
                         reads=[b_kT[ks], b_qT], writes=[b_pS[pb]], start=True, stop=False, signal=False)
                    S.mm(pS[pb][:, 128:256], identb[:], maskb[:, 128:256], reads=[b_identb, b_maskb],
                         writes=[b_pS[pb]], start=False, stop=True, signal=True)
                    S.op("act", lambda e, hh=h: e.activation(PT[pt][:, lo:256], pS[pb][:, lo:256], AF.Exp,
                                                             bias=negM[:, hh:hh + 1], scale=1.0),
                         reads=[b_pS[pb], b_negM], writes=[b_PT[pt]])
                    kts = ([(0, pv_ap, b_V_prev)] if prev is not None else []) + [(1, cur, b_V_cur)]
                    nmm = len(kts) * len(outs)
                    i = 0
                    for (kt, vap, vb) in kts:
                        for (ocol, pcol) in outs:
                            bank = ocol.start // 512
                            i += 1
                            S.mm(pO[0:65, ocol], vap, PT[pt][:, kt * 128 + pcol.start:kt * 128 + pcol.stop],
                                 reads=[b_PT[pt], vb], writes=[b_pO], start=first[bank], stop=False,
                                 signal=(i == nmm), skip_group_check=True)
                            first[bank] = False

                full = [(None, slice(0, 128))]
                for jb in range(16):
                    qsl = slice(jb * 128, (jb + 1) * 128)
                    b_V_cur = b_V[0][ks]
                    cur = V[0][ks][:, jb, h, :]
                    prev = None
                    if jb > 0:
                        prev = (ks, slice((jb - 1) * 128, jb * 128), V[0][ks][:, jb - 1, h, :])
                        b_V_prev = b_V[0][ks]
                    elif st > 0:
                        prev = (kp, slice(15 * 128, 16 * 128), V[0][kp][:, 15, h, :])
                        b_V_prev = b_V[0][kp]
                    block(qsl, cur, prev, [(slice(jb * 128, (jb + 1) * 128), slice(0, 128))])
                for n4 in range(4):
                    for r in range(4):
                        qsl = slice(n4 * 512 + r, (n4 + 1) * 512, 4)
                        b_V_cur = b_V[1][ks]
                        cur = V[1][ks][:, n4 * 4 + r, h, :]
                        prev = None
                        if n4 > 0:
                            prev = (ks, slice((n4 - 1) * 512 + r, n4 * 512, 4), V[1][ks][:, (n4 - 1) * 4 + r, h, :])
                            b_V_prev = b_V[1][ks]
                        elif st > 0:
                            prev = (kp, slice(3 * 512 + r, ST, 4), V[1][kp][:, 12 + r, h, :])
                            b_V_prev = b_V[1][kp]
                        block(qsl, cur, prev, [(qsl, slice(0, 128))])
                for r in range(16):
                    qsl = slice(r, ST, 16)
                    b_V_cur = b_V[2][ks]
                    cur = V[2][ks][:, r, h, :]
                    prev = None
                    if st > 0:
                        prev = (kp, qsl, V[2][kp][:, r, h, :])
                        b_V_prev = b_V[2][kp]
                    block(qsl, cur, prev, [(slice(b4 * 512 + r, (b4 + 1) * 512, 16), slice(32 * b4, 32 * b4 + 32))
                                           for b4 in range(4)])
                for b4 in range(4):
                    cs = slice(b4 * 512, (b4 + 1) * 512)
                    eng = "act" if b4 % 2 == 0 else "dve"
                    if eng == "act":
                        S.op("act", lambda e, cs=cs: e.copy(oacc[0:65, cs], pO[0:65, cs]), reads=[b_pO], writes=[b_oacc])
                    else:
                        S.op("dve", lambda e, cs=cs: e.tensor_copy(oacc[0:65, cs], pO[0:65, cs]), reads=[b_pO],
                             writes=[b_oacc])
                S.op("dve", lambda e: e.reciprocal(oacc[64:65, :], oacc[64:65, :]), reads=[b_oacc], writes=[b_oacc])
                for b4 in range(4):
                    cs = slice(b4 * 512, (b4 + 1) * 512)
                    S.mm(pO[0:64, cs], ones[64:65, 0:64], oacc[64:65, cs], reads=[b_ones, b_oacc], writes=[b_pO],
                         signal=(b4 == 3))
                os_ = (st * 4 + h) % 2
                for b4 in range(4):
                    cs = slice(b4 * 512, (b4 + 1) * 512)
                    S.op("dve", lambda e, cs=cs, os_=os_: e.tensor_tensor(oT[os_][:, cs], pO[0:64, cs], oacc[0:64, cs],
                                                                       ALU.mult),
                         reads=[b_pO, b_oacc], writes=[b_oT[os_]])
                for xk in range(ST // XCH):
                    S.dma("sp", D["omix"][st * (ST // XCH) + xk][h * 64:(h + 1) * 64, :],
                          oT[os_][:, xk * XCH:(xk + 1) * XCH], reads=[b_oT[os_]])
        S.barrier()
        S.run()


def _TT(o, a, b, op):
    return lambda e: e.tensor_tensor(o, a, b, op)


def _TS(o, a, s1, s2, op0, op1=None):
    if op1 is None:
        return lambda e: e.tensor_scalar(o, a, s1, s2, op0)
    return lambda e: e.tensor_scalar(o, a, s1, s2, op0, op1)


def _STT(o, a, s, b, op0, op1):
    return lambda e: e.scalar_tensor_tensor(o, a, s, b, op0, op1)


def _ACT(o, i, f, **kw):
    return lambda e: e.activation(o, i, f, **kw)


def _CP(o, i):
    return lambda e: e.copy(o, i)


def _TC(o, i):
    return lambda e: e.tensor_copy(o, i)


def _TR(o, i, ident):
    return lambda e: e.transpose(o, i, ident)


def phase_b(nc, S, D):
    STB = 1024
    NSTB = SEQ // STB
    TPS = STB // 128
    DS = DECAY_SCALE
    with ExitStack() as ph:
        def sb(n, shp, dt=F32):
            return ph.enter_context(nc.sbuf_tensor(n, shp, dt))

        def ps(n, shp, dt=F32):
            return ph.enter_context(nc.psum_tensor(n, shp, dt))

        identf = sb("b_identf", [128, 128])
        b_identf = Buf()
        S.dma("sp", identf[:], D["ident"], writes=[b_identf])
        identb = sb("b_identb", [128, 128], BF16)
        b_identb = Buf()
        S.dma("pool", identb[:], D["ident"], writes=[b_identb])
        cst = sb("b_cst", [128, 642])
        b_cst = Buf()
        S.dma("sp", cst[:], D["cstB"], writes=[b_cst])
        triI, triS, triA = cst[:, 0:128], cst[:, 128:256], cst[:, 256:384]
        chunkind = cst[:, 384:386]
        mask1, mask3, eye2 = cst[:, 386:514], cst[:, 514:578], cst[:, 578:642]
        vec = sb("b_vec", [128, 8, 256])
        b_vec = Buf()
        for i, nm in enumerate(("w0", "a0", "k_k", "k_a", "k_a", "r_k", "gn_g", "gn_b")):
            S.dma("sp", vec[:, i, :], bcast_rows(D[nm]), writes=[b_vec], acc=(i > 0))
        S.op("dve", _TS(vec[:, 4, :], vec[:, 4, :], -1.0, 1.0, ALU.mult, ALU.add), reads=[b_vec], writes=[b_vec])
        bias_wa = vec[:, 0:2, :].rearrange("p a b -> p (a b)")
        wdec = sb("b_wdec", [128, 512])
        b_wdec = Buf()
        S.op("dve", lambda e: e.memset(wdec[:], 0.0), writes=[b_wdec])
        S.dma("sp", wdec[0:64, 0:256], D["w_dec"], writes=[b_wdec], acc=True)
        S.dma("sp", wdec[64:128, 256:512], D["w_aaa"], writes=[b_wdec], acc=True)
        wgate = sb("b_wgate", [128, 256])
        b_wgate = Buf()
        S.dma("sp", wgate[:], D["w_gate"], writes=[b_wgate])
        epsg = sb("b_epsg", [128, 1])
        b_epsg = Buf()
        S.op("dve", lambda e: e.memset(epsg[:], GN_EPS), writes=[b_epsg])
        mub = sb("b_mub", [128, 1024])
        b_mub = Buf()
        S.dma("sp", mub[:], bcast_rows(D["mu"]), writes=[b_mub])
        omub = sb("b_omub", [128, 1024])
        b_omub = Buf()
        S.op("dve", _TS(omub[:], mub[:], -1.0, 1.0, ALU.mult, ALU.add), reads=[b_mub], writes=[b_omub])
        W1 = sb("b_W1", [128, 8, 1024], BF16)
        W2 = sb("b_W2", [128, 8, 1024], BF16)
        b_W = Buf()
        wst = [sb("b_wst%d" % i, [128, 1024]) for i in range(2)]
        b_wst = [Buf() for _ in range(2)]
        for c in range(8):
            S.dma("sp", wst[c % 2][:], D["w_rw"][c * 128:(c + 1) * 128, :], writes=[b_wst[c % 2]])
            S.op("dve", _TT(W1[:, c, :], wst[c % 2][:], omub[:], ALU.mult), reads=[b_wst[c % 2], b_omub], writes=[b_W])
            S.op("pool", _TT(W2[:, c, :], wst[c % 2][:], mub[:], ALU.mult), reads=[b_wst[c % 2], b_mub], writes=[b_W])

        xTv = D["xT"].rearrange("(c p) t -> p c t", p=128)
        xc = sb("b_xc", [128, 8, STB], BF16)
        xp = sb("b_xp", [128, 8, STB], BF16)
        b_xc = Buf()
        b_xp = Buf()

        Hs = sb("b_Hs", [128, 4, 64], BF16)
        b_H = Buf()
        S.op("dve", lambda e: e.memset(Hs[:], 0.0), writes=[b_H])

        def t256(n):
            return sb(n, [128, 256]), Buf()
        tl, b_tl = sb("b_tl", [128, 256]), Buf()
        lT, b_lT = sb("b_lT", [128, 256]), Buf()
        lg, b_lg = sb("b_lg", [128, 512]), Buf()
        sg, b_sg = sb("b_sg", [128, 512]), Buf()
        gs, b_gs = t256("b_gs")
        yA = sb("b_yA", [128, 512])
        b_rs = Buf()
        rs = yA[:, 0:256]
        krs = yA[:, 256:512]
        vs, b_vs = sb("b_vs", [128, 256], BF16), Buf()
        kk, b_kk = t256("b_kk")
        t1, b_t1 = t256("b_t1")
        t2, b_t2 = t256("b_t2")
        km, b_km = t256("b_km")
        bb, b_bb = t256("b_bb")
        E, b_E = sb("b_E", [128, 4, 256]), [Buf() for _ in range(4)]
        X4, b_X4 = sb("b_X4", [128, 4, 256], BF16), Buf()
        BK, b_BK = sb("b_BK", [128, 2, 256], BF16), Buf()
        XT, b_XT = sb("b_XT", [128, 4, 4, 64], BF16), Buf()
        dgP, b_dgP = sb("b_dgP", [128, 4, 64], BF16), Buf()
        AM1, b_AM1 = sb("b_AM1", [128, 4, 128], BF16), Buf()
        AM2, b_AM2 = sb("b_AM2", [128, 4, 128], BF16), Buf()
        Lm = [sb("b_L%d" % i, [128, 4, 2, 64], BF16) for i in range(2)]
        b_Lm = [Buf() for _ in range(2)]
        Pm, b_Pm = sb("b_Pm", [128, 4, 64], BF16), Buf()
        sm, b_sm = sb("b_sm", [128, 32]), Buf()
        PC, b_PC = sb("b_PC", [128, 4, 2]), Buf()
        Ws, b_Ws = sb("b_Ws", [128, 256], BF16), Buf()
        Us, b_Us = sb("b_Us", [128, 256], BF16), Buf()
        on, b_on = t256("b_on")
        gst, b_gst = sb("b_gst", [128, 4, 6]), Buf()
        gmv, b_gmv = sb("b_gmv", [128, 12]), Buf()
        orT = sb("b_orT", [128, 2, STB], BF16)
        b_orT = Buf()

        K = [ps("b_K%d" % i, [128, 512]) for i in range(8)]
        b_K = [Buf(psum=True) for _ in range(8)]

        def v3(ap):
            return ap.rearrange("p (h d) -> p h d", d=64)

        def bc(ap4):
            return ap4.unsqueeze(2).to_broadcast([128, 4, 64])

        def dbl(name, shp, dt=F32):
            return [sb("%s_%d" % (name, i), shp, dt) for i in range(2)], [Buf() for _ in range(2)]
        AM1d, b_AM1d = dbl("b_AM1d", [128, 4, 128], BF16)
        AM2d, b_AM2d = dbl("b_AM2d", [128, 4, 128], BF16)
        Pmd, b_Pmd = dbl("b_Pmd", [128, 4, 64], BF16)
        XTd, b_XTd = dbl("b_XTd", [128, 4, 4, 64], BF16)
        BKd, b_BKd = dbl("b_BKd", [128, 2, 256], BF16)
        vsd, b_vsd = dbl("b_vsd", [128, 256], BF16)
        dgPd, b_dgPd = dbl("b_dgPd", [128, 4, 64], BF16)
        gsd, b_gsd = dbl("b_gsd", [128, 256])
        bsd, b_bsd = dbl("b_bsd", [128, 4])
        smG, b_smG = sb("b_smG", [128, 16]), Buf()
        tG, b_tG = sb("b_tG", [128, 256]), Buf()

        def front(n):
            st, j = n // TPS, n % TPS
            d = n % 2
            AM1, b_AM1, AM2, b_AM2 = AM1d[d], b_AM1d[d], AM2d[d], b_AM2d[d]
            Pm, b_Pm, XT, b_XT, BK, b_BK = Pmd[d], b_Pmd[d], XTd[d], b_XTd[d], BKd[d], b_BKd[d]
            vs, b_vs, dgP, b_dgP, gs, b_gs, bs, b_bs = vsd[d], b_vsd[d], dgPd[d], b_dgPd[d], gsd[d], b_gsd[d], bsd[d], b_bsd[d]
            if j == 0:
                S.dma("pool", xc[:], xTv[:, :, st * STB:(st + 1) * STB], writes=[b_xc])
                if st == 0:
                    S.op("dve", lambda e: e.memset(xp[:, :, 0:1], 0.0), writes=[b_xp])
                    S.dma("pool", xp[:, :, 1:STB], xTv[:, :, 0:STB - 1], writes=[b_xp], acc=True)
                else:
                    S.dma("pool", xp[:], xTv[:, :, st * STB - 1:(st + 1) * STB - 1], writes=[b_xp])
            tsl = slice(j * 128, (j + 1) * 128)
            for bk in range(2):
                cs = slice(bk * 512, (bk + 1) * 512)
                for c in range(8):
                    S.mm(K[bk][:], xc[:, c, tsl], W1[:, c, cs], reads=[b_xc, b_W], writes=[b_K[bk]],
                         start=(c == 0), stop=False, signal=False)
                for c in range(8):
                    S.mm(K[bk][:], xp[:, c, tsl], W2[:, c, cs], reads=[b_xp, b_W], writes=[b_K[bk]],
                         start=False, stop=(c == 7), signal=(c == 7))
            yield
            pA, pB = K[0], K[1]
            S.op("act", _ACT(tl[:, 0:64], pB[:, 256:320], AF.Tanh), reads=[b_K[1]], writes=[b_tl])
            S.op("act", _CP(tl[:, 64:128], pB[:, 320:384]), reads=[b_K[1]], writes=[b_tl])
            S.op("act", _ACT(tl[:, 128:256], pB[:, 384:512], AF.Sigmoid), reads=[b_K[1]], writes=[b_tl])
            S.op("act", _CP(vs[:], pB[:, 0:256]), reads=[b_K[1]], writes=[b_vs])
            S.op("act", _CP(yA[:], pA[:]), reads=[b_K[0]], writes=[b_rs])
            yield
            S.op("pe", _TR(K[3][:, 0:128], tl[:, 0:128], identf[:]), reads=[b_tl, b_identf], writes=[b_K[3]], signal=False)
            S.op("pe", _TR(K[3][:, 128:256], tl[:, 128:256], identf[:]), reads=[b_tl, b_identf], writes=[b_K[3]])
            yield
            S.op("dve", _TC(lT[:], K[3][:, 0:256]), reads=[b_K[3]], writes=[b_lT])
            yield
            S.mm(K[2][:], lT[:, 0:128], wdec[:], reads=[b_lT, b_wdec], writes=[b_K[2]], signal=False)
            S.mm(K[3][:, 256:512], lT[:, 128:256], wgate[:], reads=[b_lT, b_wgate], writes=[b_K[3]])
            yield
            S.op("dve", _TT(lg[:], K[2][:], bias_wa, ALU.add), reads=[b_K[2], b_vec], writes=[b_lg])
            S.op("act", _ACT(sg[:], lg[:], AF.Sigmoid), reads=[b_lg], writes=[b_sg])
            S.op("act", _CP(gs[:], K[3][:, 256:512]), reads=[b_K[3]], writes=[b_gs])
            sw, aa = sg[:, 0:256], sg[:, 256:512]
            yield
            S.mm(K[4][:, 0:256], triS, sw, reads=[b_cst, b_sg], writes=[b_K[4]], signal=False)
            S.mm(K[4][:, 256:512], triI, sw, reads=[b_cst, b_sg], writes=[b_K[4]], signal=False)
            S.mm(K[5][:, 0:256], triA, sw, reads=[b_cst, b_sg], writes=[b_K[5]], signal=False)
            for c in range(2):
                rows = slice(64 * c, 64 * c + 64)
                for h in range(4):
                    S.mm(K[5][rows, 256 + 2 * h:258 + 2 * h], sg[rows, h * 64:(h + 1) * 64], cst[rows, 384:386],
                         reads=[b_cst, b_sg], writes=[b_K[5]], signal=(c == 1 and h == 3))
            yield
            S.op("act", _ACT(E[:, 0, :], K[4][:, 0:256], AF.Exp, scale=-DS), reads=[b_K[4]], writes=[b_E[0]])
            S.op("act", _ACT(E[:, 1, :], K[4][:, 256:512], AF.Exp, scale=-DS), reads=[b_K[4]], writes=[b_E[1]])
            S.op("act", _ACT(E[:, 2, :], K[4][:, 256:512], AF.Exp, scale=DS), reads=[b_K[4]], writes=[b_E[2]])
            S.op("act", _ACT(E[:, 3, :], K[5][:, 0:256], AF.Exp, scale=-DS), reads=[b_K[5]], writes=[b_E[3]])
            S.op("act", _ACT(PC[:], K[5][:, 256:264], AF.Exp, scale=-DS), reads=[b_K[5]], writes=[b_PC])
            S.op("dve", _TT(kk[:], krs, vec[:, 2, :], ALU.mult), reads=[b_rs, b_vec], writes=[b_kk])
            S.op("pool", _TT(t1[:], kk[:], kk[:], ALU.mult), reads=[b_kk], writes=[b_t1])
            S.op("pool", _TT(t2[:], aa, vec[:, 3, :], ALU.mult), reads=[b_sg, b_vec], writes=[b_t2])
            S.op("pool", _TT(t2[:], t2[:], vec[:, 4, :], ALU.add), reads=[b_t2, b_vec], writes=[b_t2])
            yield
            S.op("dve", lambda e: e.tensor_reduce(sm[:, 0:4], v3(t1[:]), AX.X, ALU.add), reads=[b_t1], writes=[b_sm])
            S.op("act", _ACT(sm[:, 4:8], sm[:, 0:4], AF.Sqrt), reads=[b_sm], writes=[b_sm])
            S.op("dve", _TT(km[:], krs, t2[:], ALU.mult), reads=[b_rs, b_t2], writes=[b_km])
            yield
            S.op("dve", _TS(sm[:, 4:8], sm[:, 4:8], L2_EPS, None, ALU.max), reads=[b_sm], writes=[b_sm])
            S.op("dve", lambda e: e.reciprocal(sm[:, 8:12], sm[:, 4:8]), reads=[b_sm], writes=[b_sm])
            S.op("dve", _TT(v3(kk[:]), v3(kk[:]), bc(sm[:, 8:12]), ALU.mult), reads=[b_kk, b_sm], writes=[b_kk])
            S.op("pool", _TT(t1[:], rs, km[:], ALU.mult), reads=[b_rs, b_km, b_sm], writes=[b_t1])
            S.op("pool", _TT(t1[:], t1[:], vec[:, 5, :], ALU.mult), reads=[b_t1, b_vec], writes=[b_t1])
            yield
            S.op("pool", _TT(bb[:], kk[:], aa, ALU.mult), reads=[b_kk, b_sg], writes=[b_bb])
            S.op("dve", lambda e: e.tensor_reduce(bs[:], v3(t1[:]), AX.X, ALU.add), reads=[b_t1], writes=[b_bs])
            S.op("dve", _STT(X4[:, 0, :], kk[:], -1.0, E[:, 0, :], ALU.mult, ALU.mult), reads=[b_kk, b_E[0]], writes=[b_X4])
            S.op("pool", _TT(X4[:, 1, :], rs, E[:, 1, :], ALU.mult), reads=[b_rs, b_E[1]], writes=[b_X4])
            yield
            S.op("dve", _TT(X4[:, 2, :], bb[:], E[:, 2, :], ALU.mult), reads=[b_bb, b_E[2]], writes=[b_X4])
            S.op("pool", _TT(X4[:, 3, :], km[:], E[:, 2, :], ALU.mult), reads=[b_km, b_E[2]], writes=[b_X4])
            S.op("dve", _TT(BK[:, 0, :], bb[:], E[:, 3, :], ALU.mult), reads=[b_bb, b_E[3]], writes=[b_BK])
            S.op("pool", _TT(BK[:, 1, :], km[:], E[:, 3, :], ALU.mult), reads=[b_km, b_E[3]], writes=[b_BK])
            yield
            for c in range(2):
                rows = slice(64 * c, 64 * c + 64)
                for q in range(4):
                    for h in range(4):
                        blk = q * 4 + h
                        S.mm(K[4 + blk // 8][rows, (blk % 8) * 64:(blk % 8 + 1) * 64],
                             X4[rows, q, h * 64:(h + 1) * 64], identb[rows, rows],
                             reads=[b_X4, b_identb], writes=[b_K[4 + blk // 8]], signal=(c == 1 and blk % 8 == 7))
            yield
            XTf = XT[:].rearrange("p q h t -> p (q h t)")
            S.op("act", _CP(XTf[:, 0:512], K[4][:]), reads=[b_K[4]], writes=[b_XT])
            S.op("dve", _TC(XTf[:, 512:1024], K[5][:]), reads=[b_K[5]], writes=[b_XT])
            for c in range(2):
                rows = slice(64 * c, 64 * c + 64)
                S.op("pool", _TT(dgP[rows, :, :], cst[rows, 578:642].unsqueeze(1).to_broadcast([64, 4, 64]),
                                 PC[rows, :, c:c + 1].to_broadcast([64, 4, 64]), ALU.mult),
                     reads=[b_PC, b_cst], writes=[b_dgP])
            yield
            for c in range(2):
                rows = slice(64 * c, 64 * c + 64)
                for h in range(4):
                    last = (c == 1 and h == 3)
                    S.mm(K[2][rows, h * 128:(h + 1) * 128], XT[rows, 2, h, :], XT[rows, 0:2, h, :],
                         reads=[b_XT], writes=[b_K[2]], signal=False)
                    S.mm(K[3][rows, h * 128:(h + 1) * 128], XT[rows, 3, h, :], XT[rows, 0:2, h, :],
                         reads=[b_XT], writes=[b_K[3]], signal=False)
                    S.mm(K[4][rows, h * 64:(h + 1) * 64], XT[rows, 0, h, :], XT[rows, 2, h, :],
                         reads=[b_XT], writes=[b_K[4]], signal=last)
            yield
            m1b = mask1.unsqueeze(1).to_broadcast([128, 4, 128])
            S.op("dve", _TT(AM1[:], K[2][:].rearrange("p (h t) -> p h t", h=4), m1b, ALU.mult),
                 reads=[b_K[2], b_cst], writes=[b_AM1])
            m3b = mask3.unsqueeze(1).to_broadcast([128, 4, 64])
            S.op("dve", _TT(Lm[0][:, :, 1, :], K[4][:, 0:256].rearrange("p (h t) -> p h t", h=4), m3b, ALU.mult),
                 reads=[b_K[4], b_cst], writes=[b_Lm[0]])
            S.op("pool", _TC(Lm[0][:, :, 0, :], AM1[:, :, 0:64]), reads=[b_AM1], writes=[b_Lm[0]])
            S.op("pool", _TT(Pm[:], AM1[:, :, 0:64], eye2.unsqueeze(1).to_broadcast([128, 4, 64]), ALU.add),
                 reads=[b_AM1, b_cst], writes=[b_Pm])
            S.op("dve", _TT(AM2[:], K[3][:].rearrange("p (h t) -> p h t", h=4), m1b, ALU.mult),
                 reads=[b_K[3], b_cst], writes=[b_AM2])
            yield
            cur = 0
            for rnd in range(6):
                nxt = 1 - cur
                do_sq = rnd < 5
                do_p = rnd >= 1
                for c in range(2):
                    rows = slice(64 * c, 64 * c + 64)
                    for h in range(4):
                        last = (c == 1 and h == 3)
                        Lc, LTc = Lm[cur][rows, h, 0, :], Lm[cur][rows, h, 1, :]
                        if do_sq and rnd < 4:
                            S.mm(K[5][rows, (h * 2) * 64:(h * 2 + 1) * 64], LTc, Lc, reads=[b_Lm[cur]],
                                 writes=[b_K[5]], signal=False)
                        if do_sq:
                            S.mm(K[5][rows, (h * 2 + 1) * 64:(h * 2 + 2) * 64], Lc, LTc, reads=[b_Lm[cur]],
                                 writes=[b_K[5]], signal=(last and not do_p))
                        if do_p:
                            S.mm(K[4][rows, 256 + h * 64:256 + (h + 1) * 64], LTc, Pm[rows, h, :],
                                 reads=[b_Lm[cur], b_Pm], writes=[b_K[4]], signal=last)
                yield
                src = K[5][:].rearrange("p (h a t) -> p h a t", h=4, a=2)
                if do_sq:
                    if rnd < 4:
                        S.op("act", _CP(Lm[nxt][:], src), reads=[b_K[5]], writes=[b_Lm[nxt]])
                    else:
                        S.op("act", _CP(Lm[nxt][:, :, 1, :], src[:, :, 1, :]), reads=[b_K[5]], writes=[b_Lm[nxt]])
                if do_p:
                    S.op("dve", _TT(Pm[:], K[4][:, 256:512].rearrange("p (h t) -> p h t", h=4), Pm[:], ALU.add),
                         reads=[b_K[4], b_Pm], writes=[b_Pm])
                cur = nxt
                yield

        def back(n):
            st, j = n // TPS, n % TPS
            d = n % 2
            tsl = slice(j * 128, (j + 1) * 128)
            AM1, b_AM1, AM2, b_AM2 = AM1d[d], b_AM1d[d], AM2d[d], b_AM2d[d]
            Pm, b_Pm, XT, b_XT, BK, b_BK = Pmd[d], b_Pmd[d], XTd[d], b_XTd[d], BKd[d], b_BKd[d]
            vs, b_vs, dgP, b_dgP, gs, b_gs, bs, b_bs = vsd[d], b_vsd[d], dgPd[d], b_dgPd[d], gsd[d], b_gsd[d], bsd[d], b_bsd[d]
            pO_ = K[7]
            for c in range(2):
                rows = slice(64 * c, 64 * c + 64)
                orow = slice(64 * (1 - c), 64 * (1 - c) + 64)
                for h in range(4):
                    hc = slice(256 + h * 64, 256 + (h + 1) * 64)
                    vh = vs[rows, h * 64:(h + 1) * 64]
                    S.mm(K[6][rows, hc], AM2[rows, h, 0:64], vh, reads=[b_AM2, b_vs], writes=[b_K[6]],
                         start=True, stop=False, signal=False)
                    S.mm(K[6][rows, hc], XT[rows, 0, h, :], Hs[rows, h, :], reads=[b_XT, b_H], writes=[b_K[6]],
                         start=False, stop=True, signal=(h == 3))
                yield
                S.op("dve", _TC(Ws[rows, :], K[6][rows, 256:512]), reads=[b_K[6]], writes=[b_Ws])
                yield
                for h in range(4):
                    hc = slice(h * 64, (h + 1) * 64)
                    S.mm(K[6][rows, hc], Pm[rows, h, :], Ws[rows, hc], reads=[b_Pm, b_Ws], writes=[b_K[6]],
                         signal=(h == 3))
                yield
                S.op("act", _CP(Us[rows, :], K[6][rows, 0:256]), reads=[b_K[6]], writes=[b_Us])
                yield
                for h in range(4):
                    hc = slice(256 + h * 64, 256 + (h + 1) * 64)
                    vh = vs[rows, h * 64:(h + 1) * 64]
                    uh = Us[rows, h * 64:(h + 1) * 64]
                    S.mm(pO_[rows, hc], AM2[rows, h, 64:128], vh, reads=[b_AM2, b_vs], writes=[b_K[7]],
                         start=True, stop=False, signal=False)
                    S.mm(pO_[rows, hc], XT[rows, 1, h, :], Hs[rows, h, :], reads=[b_XT, b_H], writes=[b_K[7]],
                         start=False, stop=False, signal=False)
                    S.mm(pO_[rows, hc], AM1[rows, h, 64:128], uh, reads=[b_AM1, b_Us], writes=[b_K[7]],
                         start=False, stop=True, signal=False)
                for h in range(4):
                    oc_ = slice(256 + h * 64, 256 + (h + 1) * 64)
                    vh = vs[rows, h * 64:(h + 1) * 64]
                    uh = Us[rows, h * 64:(h + 1) * 64]
                    S.mm(K[6][orow, oc_], BK[rows, 1, h * 64:(h + 1) * 64], vh, reads=[b_BK, b_vs], writes=[b_K[6]],
                         start=True, stop=False, signal=False)
                    S.mm(K[6][orow, oc_], dgP[rows, h, :], Hs[rows, h, :], reads=[b_dgP, b_H], writes=[b_K[6]],
                         start=False, stop=False, signal=False)
                    S.mm(K[6][orow, oc_], BK[rows, 0, h * 64:(h + 1) * 64], uh, reads=[b_BK, b_Us], writes=[b_K[6]],
                         start=False, stop=True, signal=(h == 3))
                yield
                S.op("act", _CP(Hs[orow, :, :], K[6][orow, 256:512].rearrange("p (h v) -> p h v", h=4)),
                     reads=[b_K[6], b_K[7]], writes=[b_H])
                yield
            o3 = pO_[:, 256:512].rearrange("p (h d) -> p h d", d=64)
            S.op("act", _CP(on[:], pO_[:, 256:512]), reads=[b_K[7]], writes=[b_on])
            yield
            S.op("act", _ACT(tG[:], on[:], AF.Square), reads=[b_on], writes=[b_tG])
            S.op("dve", lambda e: e.tensor_reduce(smG[:, 0:4], v3(on[:]), AX.X, ALU.add), reads=[b_on], writes=[b_smG])
            yield
            S.op("dve", lambda e: e.tensor_reduce(smG[:, 4:8], v3(tG[:]), AX.X, ALU.add), reads=[b_tG], writes=[b_smG])
            S.op("dve", _TS(gmv[:, 0:4], smG[:, 0:4], 1.0 / 64.0, None, ALU.mult), reads=[b_smG], writes=[b_gmv])
            S.op("dve", _TT(gmv[:, 4:8], gmv[:, 0:4], gmv[:, 0:4], ALU.mult), reads=[b_gmv], writes=[b_gmv])
            S.op("dve", _STT(gmv[:, 8:12], smG[:, 4:8], 1.0 / 64.0, gmv[:, 4:8], ALU.mult, ALU.subtract),
                 reads=[b_smG, b_gmv], writes=[b_gmv])
            yield
            S.op("act", _ACT(smG[:, 8:12], gmv[:, 8:12], AF.Sqrt, bias=epsg[:, 0:1], scale=1.0),
                 reads=[b_gmv, b_epsg], writes=[b_smG])
            S.op("dve", lambda e: e.reciprocal(smG[:, 12:16], smG[:, 8:12]), reads=[b_smG], writes=[b_smG])
            S.op("dve", _TT(v3(on[:]), v3(on[:]), bc(gmv[:, 0:4]), ALU.subtract), reads=[b_on, b_gmv], writes=[b_on])
            yield
            S.op("pool", _TT(v3(on[:]), v3(on[:]), bc(smG[:, 12:16]), ALU.mult), reads=[b_on, b_smG], writes=[b_on])
            S.op("pool", _TT(on[:], on[:], vec[:, 6, :], ALU.mult), reads=[b_on, b_vec], writes=[b_on])
            S.op("pool", _TT(on[:], on[:], vec[:, 7, :], ALU.add), reads=[b_on, b_vec], writes=[b_on])
            S.op("dve", _TT(v3(tG[:]), v3(vs[:]), bc(bs[:]), ALU.mult), reads=[b_vs, b_bs, b_tG], writes=[b_tG])
            yield
            S.op("pool", _TT(on[:], on[:], tG[:], ALU.add), reads=[b_on, b_tG], writes=[b_on])
            S.op("pool", _TT(on[:], on[:], gs[:], ALU.mult), reads=[b_on, b_gs], writes=[b_on])
            yield
            for hp in range(2):
                S.op("pe", _TR(K[7][:, hp * 128:(hp + 1) * 128], on[:, hp * 128:(hp + 1) * 128], identf[:]),
                     reads=[b_on, b_identf], writes=[b_K[7]], signal=(hp == 1))
            yield
            S.op("act", _CP(orT[:, :, tsl], K[7][:, 0:256].rearrange("p (a t) -> p a t", a=2)),
                 reads=[b_K[7]], writes=[b_orT])
            if j == TPS - 1 or n == B_TILES - 1:
                for hp in range(2):
                    S.dma("sp", D["omix"][st][256 + hp * 128:256 + (hp + 1) * 128, :],
                          orT[:, hp, :], reads=[b_orT])

        for n in range(B_TILES + 1):
            gf = front(n) if n < B_TILES else None
            gb = back(n - 1) if n >= 1 else None
            while gf is not None or gb is not None:
                if gf is not None:
                    try:
                        next(gf)
                    except StopIteration:
                        gf = None
                if gb is not None:
                    try:
                        next(gb)
                    except StopIteration:
                        gb = None
        S.barrier()
        S.run()


def exchange(nc, S, D, stack):
    ccs = stack.enter_context(nc.semaphore("s_cc"))
    groups = [[0, 1], [2, 3], [4, 5], [6, 7]]
    for k in range(NXCH):
        S.q["pool"].append(lambda e, k=k: e.collective_compute(
            "AllGather", ALU.bypass, replica_groups=groups, ins=[D["omix"][k]], outs=[D["G"][k]]).then_inc(ccs, 1))
    S.extra.append((ccs, NXCH, "s_cc", "cc"))
    S.barrier()
    S.run()


def build_program(phases="ABC", exch=True):
    nc = bass.Bass("TRN2", target_bir_lowering=False)
    D = {}

    def din(name, shape, dt=F32):
        D[name] = nc.dram_tensor(name, list(shape), dt, kind="ExternalInput").ap()

    din("ident", [128, 128])
    if "A" in phases or "B" in phases:
        din("xT", [D_MODEL, SEQ])
    if "A" in phases:
        din("pos", [128, 64], I32)
        din("w_att", [D_MODEL, 768])
        din("maskT", [128, 256])
    if "B" in phases:
        din("w_rw", [D_MODEL, 1024])
        din("mu", [1, 1024])
        for nm in ("w0", "a0", "k_k", "k_a", "r_k", "gn_g", "gn_b"):
            din(nm, [1, 256])
        din("w_dec", [64, 256])
        din("w_aaa", [64, 256])
        din("w_gate", [128, 256])
        din("cstB", [128, 642])
    if "C" in phases:
        din("xres", [TOKH, D_MODEL])
        din("w_out", [1024, 1024])
        din("wg", [1024, FFN])
        din("wu", [1024, FFN])
        din("wd", [FFN, 1024])
        for nm in ("ln1g", "ln1b", "ln2g", "ln2b"):
            din(nm, [1, 1024])
        din("sel", [128, 2])
        D["out"] = nc.dram_tensor("out", [TOKH, D_MODEL], F32, kind="ExternalOutput").ap()
    full = exch
    if full:
        D["omix"] = [nc.dram_tensor("omix%d" % k, [512, XCH], BF16, kind="Internal").ap() for k in range(NXCH)]
        D["G"] = [nc.dram_tensor("G%d" % k, [1024, XCH], BF16, kind="Internal").ap() for k in range(NXCH)]
    else:
        if "C" in phases:
            din("G", [NXCH, 1024, XCH], BF16)
            D["G"] = [D["G"][k] for k in range(NXCH)]
        if "A" in phases or "B" in phases:
            om = nc.dram_tensor("omix", [NXCH, 512, XCH], BF16, kind="ExternalOutput").ap()
            D["omix"] = [om[k] for k in range(NXCH)]
    with ExitStack() as st:
        S = Sched(nc, st)
        if "C" in phases:
            D["wgs"] = nc.dram_tensor("wgs", [HC, 128, 2, 8, 128], BF16, kind="Internal").ap()
            prep_ffn_weights(nc, S, D)
        if "A" in phases:
            phase_a(nc, S, D)
        if "B" in phases:
            phase_b(nc, S, D)
        if full:
            exchange(nc, S, D, st)
        if "C" in phases:
            phase_c(nc, S, D)
    return nc


def att_mask():
    i_k = np.arange(128)[:, None]
    i_q = np.arange(128)[None, :]
    m = np.zeros((128, 256), np.float32)
    m[:, 0:128] = np.where(i_k >= i_q, 0.0, NEG)
    m[:, 128:256] = np.where(i_k <= i_q, 0.0, NEG)
    return m


def rwkv_consts():
    j = np.arange(128)[:, None]
    t = np.arange(128)[None, :]
    same = (j // 64) == (t // 64)
    c = np.zeros((128, 642), np.float32)
    c[:, 0:128] = same & (j <= t)
    c[:, 128:256] = same & (j < t)
    c[:, 256:384] = same & (j > t)
    c[:, 384] = (np.arange(128) < 64)
    c[:, 385] = (np.arange(128) >= 64)
    jj = (np.arange(128) % 64)[:, None]
    tt = np.arange(64)[None, :]
    c[:, 386:450] = jj < tt
    c[:, 450:514] = jj <= tt
    c[:, 514:578] = tt < jj
    c[:, 578:642] = jj == tt
    return c


def core_inputs(inp, c, phases="ABC"):
    b, g = c // 2, c % 2
    m = {"ident": np.eye(128, dtype=np.float32)}
    w_in = inp["w_in"][0]
    if "A" in phases or "B" in phases:
        m["xT"] = np.ascontiguousarray(inp["x"][b].T)
    if "A" in phases:
        m["pos"] = np.ascontiguousarray(inp["positions"][b].reshape(64, 128).T).astype(np.int32)
        cols = np.concatenate([np.arange(256 * g, 256 * g + 256) + off for off in (0, 512, 1024)])
        m["w_att"] = np.ascontiguousarray(w_in[:, cols])
        m["maskT"] = att_mask()
    if "B" in phases:
        hs = slice(256 * g, 256 * g + 256)
        rcols = np.concatenate([1536 + off + np.arange(256 * g, 256 * g + 256) for off in (0, 512, 1024)]
                               + [1536 + 1536 + np.arange(256)])
        m["w_rw"] = np.ascontiguousarray(w_in[:, rcols])
        m["mu"] = np.ascontiguousarray(inp["mu_shift"][0][rcols - 1536][None, :])
        for nm in ("w0", "a0", "k_k", "k_a", "gn_g", "gn_b"):
            m[nm] = np.ascontiguousarray(inp[nm][0][hs][None, :])
        m["r_k"] = np.ascontiguousarray(inp["r_k"][0][4 * g:4 * g + 4].reshape(1, 256))
        m["w_dec"] = np.ascontiguousarray(inp["w_decay_up"][0][:, hs])
        m["w_aaa"] = np.ascontiguousarray(inp["w_aaa_up"][0][:, hs])
        m["w_gate"] = np.ascontiguousarray(inp["w_gate_up"][0][:, hs])
        m["cstB"] = rwkv_consts()
    if "C" in phases:
        fi = lambda r: np.concatenate([np.arange(256 * r, 256 * r + 256), 512 + np.arange(256 * r, 256 * r + 256)])
        perm = np.concatenate([fi(0), fi(1)])
        sel = np.zeros((128, 2), np.float32)
        sel[:, g] = 1.0
        m.update(xres=np.ascontiguousarray(inp["x"][b, g * TOKH:(g + 1) * TOKH]),
                 w_out=np.ascontiguousarray(inp["w_out"][0][perm]),
                 wg=inp["w_ffn_gate"][0], wu=inp["w_ffn_up"][0], wd=inp["w_ffn_down"][0],
                 ln1g=inp["ln_mix_g"], ln1b=inp["ln_mix_b"], ln2g=inp["ln_ffn_g"], ln2b=inp["ln_ffn_b"],
                 sel=sel)
    return m


_NC_CACHE = {}


def kernel(**inputs):
    inp = {k: np.asarray(v) for k, v in inputs.items()}
    if "nc" not in _NC_CACHE:
        _NC_CACHE["nc"] = build_program("ABC")
    nc = _NC_CACHE["nc"]
    in_maps = [core_inputs(inp, c, "ABC") for c in range(8)]
    res = run_bass_kernel_spmd(nc, in_maps, core_ids=list(range(8)))
    out = np.empty((BATCH, SEQ, D_MODEL), np.float32)
    for c in range(8):
        b, g = c // 2, c % 2
        out[b, g * TOKH:(g + 1) * TOKH] = np.asarray(res.results[c]["out"], dtype=np.float32)
    return out
```

```python
import math
from contextlib import ExitStack

import numpy as np
import concourse.bass as bass
import concourse.mybir as mybir
from concourse.bass_utils import run_bass_kernel_spmd

F32 = mybir.dt.float32
BF16 = mybir.dt.bfloat16
I32 = mybir.dt.int32
AF = mybir.ActivationFunctionType
ALU = mybir.AluOpType
AX = mybir.AxisListType

D_MODEL = 1024
SEQ = 8192
BATCH = 4
HD = 64
FFN = 2816
HC = FFN // 128
ALPHA = 2.0 ** 0.25
LN_EPS = 1e-5
GN_EPS = 64e-5
L2_EPS = 1e-6
DECAY_SCALE = math.exp(-0.5)
NEG = -30000.0
ROPE_THETA = 500000.0
TOKH = SEQ // 2
XCH = 1024
NXCH = SEQ // XCH
B_TILES = SEQ // 128


class Buf:
    __slots__ = ("name", "w", "r", "psum")

    def __init__(self, name="", psum=False):
        self.name = name
        self.w = []
        self.r = {}
        self.psum = psum


class Sched:
    ENG = ("pe", "act", "dve", "pool", "sp")
    NDS = 8

    def __init__(self, nc, stack):
        self.nc = nc
        self.q = {e: [] for e in self.ENG}
        self.sem = {e: stack.enter_context(nc.semaphore("s_" + e)) for e in self.ENG}
        self.cnt = {e: 0 for e in self.ENG}
        self.seen = {e: {} for e in self.ENG}
        self.lastop = {e: None for e in self.ENG}
        self.dq = ("sp", "act", "pool")
        self.dsem = {e: [stack.enter_context(nc.semaphore("d_%s%d" % (e, i)))
                         for i in range(self.NDS)] for e in self.dq}
        self.dcnt = {e: 0 for e in self.dq}
        self.dlast = {e: [None] * self.NDS for e in self.dq}
        self.extra = []

    def _force(self, prod):
        rec = self.lastop[prod]
        assert rec is not None and not rec[2]
        rec[2] = True
        self.cnt[prod] += 1

    def _wait(self, eng, tok):
        if tok is None:
            return
        sem, val, key, prod = tok
        if prod in self.cnt and val > self.cnt[prod]:
            self._force(prod)
            assert val <= self.cnt[prod]
        if self.seen[eng].get(key, 0) >= val:
            return
        self.seen[eng][key] = val
        self.q[eng].append(["w", sem, val])

    def _deps(self, eng, reads, writes):
        for b in reads:
            for t in b.w:
                self._wait(eng, t)
            if b.psum:
                for t in b.r.values():
                    if t[3] != eng:
                        self._wait(eng, t)
        for b in writes:
            for t in b.w:
                if t[3] != eng:
                    self._wait(eng, t)
            for t in b.r.values():
                if t[3] != eng:
                    self._wait(eng, t)

    def _mark(self, tok, reads, writes, acc=False):
        for b in reads:
            b.r[tok[2]] = tok
        for b in writes:
            if acc:
                b.w.append(tok)
            else:
                b.w = [tok]
            b.r = {}

    def op(self, eng, fn, reads=(), writes=(), signal=True):
        self._deps(eng, reads, writes)
        sem = self.sem[eng]
        rec = ["o", fn, bool(signal), sem]
        self.q[eng].append(rec)
        self.lastop[eng] = rec
        if signal:
            self.cnt[eng] += 1
            tok = (sem, self.cnt[eng], eng, eng)
        else:
            tok = (sem, self.cnt[eng] + 1, eng, eng)
        self._mark(tok, reads, writes)
        return tok

    def mm(self, out, lhsT, rhs, reads, writes, start=True, stop=True, signal=True, **kw):
        return self.op("pe", lambda e: e.matmul(out, lhsT, rhs, start=start, stop=stop, **kw),
                       reads, writes, signal)

    def dma(self, queue, out, in_, reads=(), writes=(), acc=False, **kw):
        i = self.dcnt[queue]
        self.dcnt[queue] += 1
        slot = i % self.NDS
        self._wait(queue, self.dlast[queue][slot])
        self._deps(queue, reads, writes)
        sem = self.dsem[queue][slot]
        val = 16 * (i // self.NDS + 1)
        tok = (sem, val, "d_%s%d" % (queue, slot), "dma")
        self.dlast[queue][slot] = tok
        self.q[queue].append(["d", out, in_, sem, kw])
        self._mark(tok, reads, writes, acc=acc)
        return tok

    def barrier(self):
        for o in self.ENG:
            rec = self.lastop[o]
            if rec is not None and not rec[2]:
                self._force(o)
        toks = [(self.sem[o], self.cnt[o], o, o) for o in self.ENG if self.cnt[o] > 0]
        for qn in self.dq:
            toks += [t for t in self.dlast[qn] if t is not None]
        toks += self.extra
        for e in self.ENG:
            for t in toks:
                if t[3] != e:
                    self._wait(e, t)

    @staticmethod
    def _replay(e, recs):
        for r in recs:
            k = r[0]
            if k == "w":
                e.wait_ge(r[1], r[2])
            elif k == "o":
                ins = r[1](e)
                if r[2]:
                    ins.then_inc(r[3], 1)
            elif k == "d":
                e.dma_start(out=r[1], in_=r[2], **r[4]).then_inc(r[3], 16)
            else:
                r[1](e)

    def run(self):
        nc = self.nc
        q = self.q
        rp = self._replay
        with nc.Block() as block:
            @block.tensor
            def _(e):
                rp(e, q["pe"])

            @block.scalar
            def _(e):
                rp(e, q["act"])

            @block.vector
            def _(e):
                rp(e, q["dve"])

            @block.gpsimd
            def _(e):
                rp(e, q["pool"])

            @block.sync
            def _(e):
                rp(e, q["sp"])
        self.q = {e: [] for e in self.ENG}
        self.lastop = {e: None for e in self.ENG}


def bcast_rows(ap, n=128):
    return bass.AP(ap.tensor, ap.offset, [[0, n], [1, ap.shape[-1]]])


def prep_ffn_weights(nc, S, D):
    wgv = D["wg"].rearrange("(c p) n -> p c n", p=128)
    wuv = D["wu"].rearrange("(c p) n -> p c n", p=128)
    D["b_wgs"] = [Buf() for _ in range(HC)]
    for h in range(HC):
        S.dma("pool", D["wgs"][h, :, 0, :, :], wgv[:, :, h * 128:(h + 1) * 128], writes=[D["b_wgs"][h]])
        S.dma("pool", D["wgs"][h, :, 1, :, :], wuv[:, :, h * 128:(h + 1) * 128], writes=[D["b_wgs"][h]], acc=True)


def phase_c(nc, S, D):
    GT = 512
    NG = TOKH // GT
    TPG = GT // 128
    OCW = 256
    with ExitStack() as ph:
        def sb(n, shp, dt):
            return ph.enter_context(nc.sbuf_tensor(n, shp, dt))

        def ps(n, shp, dt):
            return ph.enter_context(nc.psum_tensor(n, shp, dt))

        Gv = [g_.rearrange("(c p) t -> p c t", p=128) for g_ in D["G"]]
        ident = sb("c_ident", [128, 128], BF16)
        b_ident = Buf()
        S.dma("pool", ident[:], D["ident"], writes=[b_ident])
        sel = sb("c_sel", [128, 2], F32)
        b_sel = Buf()
        S.dma("sp", sel[:], D["sel"], writes=[b_sel])
        woutA = sb("c_woutA", [128, 8, 1024], BF16)
        woutB = sb("c_woutB", [128, 8, 1024], BF16)
        b_wout = Buf()
        b_woutB = Buf()
        S.dma("pool", woutA[:], D["w_out"].rearrange("(c p) n -> p c n", p=128), writes=[b_wout])
        S.op("dve", lambda e: e.tensor_scalar(woutB[:], woutA[:], sel[:, 1:2], None, ALU.mult),
             reads=[b_wout, b_sel], writes=[b_woutB])
        S.op("dve", lambda e: e.tensor_scalar(woutA[:], woutA[:], sel[:, 0:1], None, ALU.mult),
             reads=[b_wout, b_sel, b_woutB], writes=[b_wout])
        wd = sb("c_wd", [128, HC, 1024], BF16)
        b_wd = [Buf() for _ in range(HC)]
        for h in range(HC):
            S.dma("pool", wd[:, h, :], D["wd"][h * 128:(h + 1) * 128, :], writes=[b_wd[h]])
        lnp = sb("c_lnp", [128, 4, 1024], F32)
        b_lnp = [Buf() for _ in range(4)]
        for i, nm in enumerate(("ln1g", "ln1b", "ln2g", "ln2b")):
            S.dma("sp", lnp[:, i, :], bcast_rows(D[nm]), writes=[b_lnp[i]])

        NW = 4
        wgu = [sb("c_wgu%d" % i, [128, 2, 8, 128], BF16) for i in range(NW)]
        b_wgu = [Buf() for _ in range(NW)]

        h1g = sb("c_h1g", [128, TPG, 1024], F32)
        b_h1g = [Buf() for _ in range(TPG)]
        h1T = sb("c_h1T", [128, 8, GT], BF16)
        b_h1T = [Buf() for _ in range(TPG)]
        actT = sb("c_actT", [128, HC, GT], BF16)
        b_actT = [Buf() for _ in range(HC)]
        NB = 2
        ocA = [sb("c_ocA%d" % i, [128, 8, OCW], BF16) for i in range(NB)]
        ocB = [sb("c_ocB%d" % i, [128, 8, OCW], BF16) for i in range(NB)]
        b_ocA = [Buf() for _ in range(NB)]
        b_ocB = [Buf() for _ in range(NB)]
        xt = [sb("c_xt%d" % i, [128, 1024], F32) for i in range(NB)]
        b_xt = [Buf() for _ in range(NB)]
        hpre = sb("c_hpre", [128, 1024], F32)
        b_hpre = Buf()
        hn = sb("c_hn", [128, 1024], F32)
        b_hn = Buf()
        h1b = sb("c_h1b", [128, 1024], BF16)
        b_h1b = Buf()
        stats = sb("c_stats", [128, 2, 6], F32)
        b_stats = Buf()
        mv = sb("c_mv", [128, 4], F32)
        b_mv = Buf()
        sg = [sb("c_sg%d" % i, [128, 512], BF16) for i in range(2)]
        b_sg = [Buf() for _ in range(2)]
        outt = [sb("c_outt%d" % i, [128, 1024], F32) for i in range(NB)]
        b_outt = [Buf() for _ in range(NB)]

        epsc = sb("c_eps", [128, 1], F32)
        b_epsc = Buf()
        S.op("dve", lambda e: e.memset(epsc[:], LN_EPS), writes=[b_epsc])
        pmix = ps("c_pmix", [128, 1024], F32)
        b_pmix = Buf(psum=True)
        pT = ps("c_pT", [128, 1024], BF16)
        b_pT = Buf(psum=True)
        pG = [ps("c_pG%d" % i, [128, 512], F32) for i in range(2)]
        pU = [ps("c_pU%d" % i, [128, 512], F32) for i in range(2)]
        b_pG = [Buf(psum=True) for _ in range(2)]
        b_pU = [Buf(psum=True) for _ in range(2)]

        def layer_norm(src_b, gi_, bi_, dst, b_dst):
            for hf in range(2):
                S.op("dve", lambda e, hf=hf: e.bn_stats(stats[:, hf, :], hpre[:, hf * 512:(hf + 1) * 512]),
                     reads=[src_b], writes=[b_stats])
            S.op("dve", lambda e: e.bn_aggr(mv[:, 0:2], stats[:].rearrange("p a b -> p (a b)")),
                 reads=[b_stats], writes=[b_mv])
            S.op("act", lambda e: e.activation(mv[:, 3:4], mv[:, 1:2], AF.Sqrt, bias=epsc[:, 0:1], scale=1.0),
                 reads=[b_mv, b_epsc], writes=[b_mv])
            S.op("dve", lambda e: e.reciprocal(mv[:, 2:3], mv[:, 3:4]), reads=[b_mv], writes=[b_mv])
            S.op("dve", lambda e: e.tensor_scalar(hn[:], hpre[:], mv[:, 0:1], mv[:, 2:3],
                                                  ALU.subtract, ALU.mult),
                 reads=[src_b, b_mv], writes=[b_hn])
            S.op("pool", lambda e: e.tensor_tensor(hn[:], hn[:], lnp[:, gi_, :], ALU.mult),
                 reads=[b_hn, b_lnp[gi_]], writes=[b_hn])
            S.op("dve", lambda e: e.tensor_tensor(dst, hn[:], lnp[:, bi_, :], ALU.add),
                 reads=[b_hn, b_lnp[bi_]], writes=[b_dst])

        nwl = 0
        for gi in range(NG):
            for ti in range(TPG):
                it = gi * TPG + ti
                sl = it % NB
                tok0 = it * 128
                oi = (tok0 // OCW) % NB
                oo = tok0 % OCW
                if oo == 0:
                    ta, tb = tok0, TOKH + tok0
                    S.dma("sp", ocA[oi][:], Gv[ta // XCH][:, :, ta % XCH:ta % XCH + OCW], writes=[b_ocA[oi]])
                    S.dma("sp", ocB[oi][:], Gv[tb // XCH][:, :, tb % XCH:tb % XCH + OCW], writes=[b_ocB[oi]])
                S.dma("sp", xt[sl][:], D["xres"][tok0:tok0 + 128, :], writes=[b_xt[sl]])
                for hf in range(2):
                    cs = slice(hf * 512, (hf + 1) * 512)
                    for c in range(8):
                        S.mm(pmix[:, cs], ocA[oi][:, c, oo:oo + 128], woutA[:, c, cs],
                             reads=[b_ocA[oi], b_wout], writes=[b_pmix], start=(c == 0), stop=False, signal=False)
                    for c in range(8):
                        S.mm(pmix[:, cs], ocB[oi][:, c, oo:oo + 128], woutB[:, c, cs],
                             reads=[b_ocB[oi], b_woutB], writes=[b_pmix], start=False, stop=(c == 7),
                             signal=(c == 7))
                for hf in range(2):
                    S.op("dve", lambda e, hf=hf, sl=sl: e.scalar_tensor_tensor(
                        hpre[:, hf * 512:(hf + 1) * 512], xt[sl][:, hf * 512:(hf + 1) * 512], ALPHA,
                        pmix[:, hf * 512:(hf + 1) * 512], ALU.mult, ALU.add),
                        reads=[b_xt[sl], b_pmix], writes=[b_hpre])
                layer_norm(b_hpre, 0, 1, h1g[:, ti, :], b_h1g[ti])
                S.op("act", lambda e, ti=ti: e.copy(h1b[:], h1g[:, ti, :]), reads=[b_h1g[ti]], writes=[b_h1b])
                for c in range(8):
                    S.op("pe", lambda e, c=c: e.transpose(pT[:, c * 128:(c + 1) * 128],
                                                          h1b[:, c * 128:(c + 1) * 128], ident[:]),
                         reads=[b_h1b, b_ident], writes=[b_pT], signal=(c == 7))
                S.op("act", lambda e, ti=ti: e.copy(h1T[:, :, ti * 128:(ti + 1) * 128],
                                                    pT[:].rearrange("p (c t) -> p c t", c=8)),
                     reads=[b_pT], writes=[b_h1T[ti]])
            for h in range(HC):
                ws = nwl % NW
                nwl += 1
                S.dma("sp", wgu[ws][:].rearrange("p a c n -> p (a c n)"),
                      D["wgs"][h].rearrange("p a c n -> p (a c n)"), reads=[D["b_wgs"][h]], writes=[b_wgu[ws]])
                pb = h % 2
                for c in range(8):
                    S.mm(pG[pb][:], wgu[ws][:, 0, c, :], h1T[:, c, :], reads=[b_wgu[ws]] + b_h1T,
                         writes=[b_pG[pb]], start=(c == 0), stop=(c == 7), signal=(c == 7))
                for c in range(8):
                    S.mm(pU[pb][:], wgu[ws][:, 1, c, :], h1T[:, c, :], reads=[b_wgu[ws]] + b_h1T,
                         writes=[b_pU[pb]], start=(c == 0), stop=(c == 7), signal=(c == 7))
                S.op("act", lambda e, pb=pb: e.activation(sg[pb][:], pG[pb][:], AF.Silu),
                     reads=[b_pG[pb]], writes=[b_sg[pb]])
                S.op("dve", lambda e, pb=pb, h=h: e.tensor_tensor(actT[:, h, :], pU[pb][:], sg[pb][:], ALU.mult),
                     reads=[b_pU[pb], b_sg[pb]], writes=[b_actT[h]])
            for ti in range(TPG):
                it = gi * TPG + ti
                sl = it % NB
                tok0 = it * 128
                for hf in range(2):
                    for h in range(HC):
                        S.mm(pmix[:, hf * 512:(hf + 1) * 512], actT[:, h, ti * 128:(ti + 1) * 128],
                             wd[:, h, hf * 512:(hf + 1) * 512], reads=[b_actT[h], b_wd[h]], writes=[b_pmix],
                             start=(h == 0), stop=(h == HC - 1), signal=(h == HC - 1))
                for hf in range(2):
                    S.op("dve", lambda e, hf=hf, ti=ti: e.scalar_tensor_tensor(
                        hpre[:, hf * 512:(hf + 1) * 512], h1g[:, ti, hf * 512:(hf + 1) * 512], ALPHA,
                        pmix[:, hf * 512:(hf + 1) * 512], ALU.mult, ALU.add),
                        reads=[b_h1g[ti], b_pmix], writes=[b_hpre])
                layer_norm(b_hpre, 2, 3, outt[sl][:], b_outt[sl])
                S.dma("sp", D["out"][tok0:tok0 + 128, :], outt[sl][:], reads=[b_outt[sl]])
        S.barrier()
        S.run()


def phase_a(nc, S, D):
    ST = 2048
    NST = SEQ // ST
    inv_freq = [float(np.float32(ROPE_THETA) ** np.float32(-i / 8.0)) for i in range(8)]
    TWO_PI = 2.0 * math.pi
    C1 = 6.28125
    C2 = TWO_PI - C1
    with ExitStack() as ph:
        def sb(n, shp, dt):
            return ph.enter_context(nc.sbuf_tensor(n, shp, dt))

        def ps(n, shp, dt):
            return ph.enter_context(nc.psum_tensor(n, shp, dt))

        identf = sb("a_identf", [128, 128], F32)
        b_identf = Buf()
        S.dma("sp", identf[:], D["ident"], writes=[b_identf])
        identb = sb("a_identb", [128, 128], BF16)
        b_identb = Buf()
        S.dma("pool", identb[:], D["ident"], writes=[b_identb])
        maskb = sb("a_maskb", [128, 256], BF16)
        b_maskb = Buf()
        S.dma("pool", maskb[:], D["maskT"], writes=[b_maskb])
        ones = sb("a_ones", [128, 128], F32)
        b_ones = Buf()
        S.op("dve", lambda e: e.memset(ones[:], 1.0), writes=[b_ones])
        watt = sb("a_watt", [128, 8, 768], BF16)
        b_watt = Buf()
        S.dma("pool", watt[:], D["w_att"].rearrange("(c p) n -> p c n", p=128), writes=[b_watt])

        posi = sb("a_posi", [128, 64], I32)
        b_posi = Buf()
        S.dma("sp", posi[:], D["pos"], writes=[b_posi])
        posf = sb("a_posf", [128, 64], F32)
        b_posf = Buf()
        S.op("dve", lambda e: e.tensor_copy(posf[:], posi[:]), reads=[b_posi], writes=[b_posf])
        ang = sb("a_ang", [128, 64, 8], F32)
        b_ang = Buf()
        for i in range(8):
            S.op("dve", lambda e, i=i: e.tensor_scalar(ang[:, :, i], posf[:], inv_freq[i], None, ALU.mult),
                 reads=[b_posf], writes=[b_ang])
        sinT = sb("a_sinT", [128, 64, 8], F32)
        cosT = sb("a_cosT", [128, 64, 8], F32)
        b_sinT = Buf()
        b_cosT = Buf()
        kq = sb("a_kq", [128, 512], I32)
        kf = sb("a_kf", [128, 512], F32)
        red = sb("a_red", [128, 512], F32)
        msk = sb("a_msk", [128, 512], F32)
        b_tmp = Buf()
        angf = ang[:].rearrange("p a b -> p (a b)")

        def wrap(dst):
            S.op("dve", lambda e: e.tensor_scalar(msk[:], dst, math.pi, -TWO_PI, ALU.is_gt, ALU.mult),
                 reads=[b_tmp], writes=[b_tmp])
            S.op("dve", lambda e: e.tensor_tensor(dst, dst, msk[:], ALU.add), reads=[b_tmp], writes=[b_tmp])

        S.op("dve", lambda e: e.tensor_scalar(kq[:], angf, 1.0 / TWO_PI, None, ALU.mult),
             reads=[b_ang], writes=[b_tmp])
        S.op("dve", lambda e: e.tensor_copy(kf[:], kq[:]), reads=[b_tmp], writes=[b_tmp])
        S.op("dve", lambda e: e.scalar_tensor_tensor(red[:], kf[:], -C1, angf, ALU.mult, ALU.add),
             reads=[b_tmp, b_ang], writes=[b_tmp])
        S.op("dve", lambda e: e.scalar_tensor_tensor(red[:], kf[:], -C2, red[:], ALU.mult, ALU.add),
             reads=[b_tmp], writes=[b_tmp])
        wrap(red[:])
        S.op("act", lambda e: e.activation(sinT[:].rearrange("p a b -> p (a b)"), red[:], AF.Sin),
             reads=[b_tmp], writes=[b_sinT])
        S.op("dve", lambda e: e.tensor_scalar(red[:], red[:], math.pi / 2.0, None, ALU.add),
             reads=[b_tmp, b_sinT], writes=[b_tmp])
        wrap(red[:])
        S.op("act", lambda e: e.activation(cosT[:].rearrange("p a b -> p (a b)"), red[:], AF.Sin),
             reads=[b_tmp], writes=[b_cosT])

        xTs = [sb("a_xT%d" % i, [128, 8, ST], BF16) for i in range(2)]
        b_xTs = [Buf() for _ in range(2)]
        xTv = D["xT"].rearrange("(c p) t -> p c t", p=128)
        qT = sb("a_qT", [128, 2, ST], BF16)
        b_qT = Buf()
        kT = [sb("a_kT%d" % i, [128, 2, ST], BF16) for i in range(2)]
        b_kT = [Buf() for _ in range(2)]
        V = [[sb("a_V%d_%d" % (l, i), [128, 16, 4, 65], BF16) for i in range(2)] for l in range(3)]
        b_V = [[Buf() for _ in range(2)] for l in range(3)]
        for l in range(3):
            for i in range(2):
                S.op("pool", lambda e, l=l, i=i: e.memset(V[l][i][:], 1.0), writes=[b_V[l][i]])
        ysb = sb("a_ysb", [128, 512], F32)
        b_ysb = Buf()
        ysb3 = ysb[:].rearrange("p (h d) -> p h d", d=64)
        rt = sb("a_rt", [128, 4, 8, 8], F32)
        b_rt = Buf()
        sq = sb("a_sq", [128, 512], F32)
        b_sq = Buf()
        ssq = sb("a_ssq", [128, 8], F32)
        b_ssq = Buf()
        rm = sb("a_rm", [128, 8], F32)
        b_rm = Buf()
        st4 = sb("a_st4", [4, 8], F32)
        b_st4 = Buf()
        dg4 = sb("a_dg4", [4, 4], F32)
        b_dg4 = Buf()
        negM = sb("a_negM", [128, 4], F32)
        b_negM = Buf()
        NPT = 3
        PT = [sb("a_PT%d" % i, [128, 256], BF16) for i in range(NPT)]
        b_PT = [Buf() for _ in range(NPT)]
        oacc = sb("a_oacc", [128, ST], F32)
        b_oacc = Buf()
        oT = [sb("a_oT%d" % i, [64, ST], BF16) for i in range(2)]
        b_oT = [Buf() for _ in range(2)]

        pqk = ps("a_pqk", [128, 512], F32)
        b_pqk = Buf(psum=True)
        pX = ps("a_pX", [128, 512], F32)
        b_pX = Buf(psum=True)
        pS = [ps("a_pS%d" % i, [128, 512], F32) for i in range(2)]
        b_pS = [Buf(psum=True) for _ in range(2)]
        pO = ps("a_pO", [128, ST], F32)
        b_pO = Buf(psum=True)

        S.op("dve", lambda e: e.memset(st4[:], 0.0), writes=[b_st4])

        nblk = 0
        for st in range(NST):
            xs = st % 2
            ks = st % 2
            kp = 1 - ks
            xT = xTs[xs]
            S.dma("pool", xT[:], xTv[:, :, st * ST:(st + 1) * ST], writes=[b_xTs[xs]])
            S.op("dve", lambda e: e.memset(rm[:], 0.0), reads=[], writes=[b_rm])
            for j in range(16):
                n = st * 16 + j
                for c in range(8):
                    S.mm(pqk[:], xT[:, c, j * 128:(j + 1) * 128], watt[:, c, 0:512],
                         reads=[b_xTs[xs], b_watt], writes=[b_pqk], start=(c == 0), stop=(c == 7), signal=(c == 7))
                S.op("act", lambda e: e.mul(ysb[:, 0:256], pqk[:, 0:256], 0.125), reads=[b_pqk], writes=[b_ysb])
                S.op("act", lambda e: e.copy(ysb[:, 256:512], pqk[:, 256:512]), reads=[b_pqk], writes=[b_ysb])
                cb = cosT[:, n:n + 1, :].to_broadcast([128, 8, 8])
                sbb = sinT[:, n:n + 1, :].to_broadcast([128, 8, 8])
                x1 = ysb3[:, :, 0:8]
                x2 = ysb3[:, :, 8:16]
                rtv = rt[:].rearrange("p a h d -> p a (h d)")
                for a, (xx, tt) in enumerate(((x1, cb), (x2, sbb), (x2, cb), (x1, sbb))):
                    S.op("pool", lambda e, a=a, xx=xx, tt=tt: e.tensor_tensor(rt[:, a, :, :], xx, tt, ALU.mult),
                         reads=[b_ysb, b_cosT, b_sinT], writes=[b_rt])
                S.op("pool", lambda e: e.tensor_tensor(x1, rt[:, 0, :, :], rt[:, 1, :, :], ALU.subtract),
                     reads=[b_rt, b_ysb], writes=[b_ysb])
                S.op("pool", lambda e: e.tensor_tensor(x2, rt[:, 2, :, :], rt[:, 3, :, :], ALU.add),
                     reads=[b_rt, b_ysb], writes=[b_ysb])
                S.op("dve", lambda e: e.tensor_tensor(sq[:], ysb[:], ysb[:], ALU.mult), reads=[b_ysb], writes=[b_sq])
                S.op("dve", lambda e: e.tensor_reduce(ssq[:], sq[:].rearrange("p (h d) -> p h d", d=64), AX.X, ALU.add),
                     reads=[b_sq], writes=[b_ssq])
                S.op("dve", lambda e: e.tensor_tensor(rm[:], rm[:], ssq[:], ALU.max), reads=[b_ssq, b_rm], writes=[b_rm])
                for blk in range(4):
                    S.op("pe", lambda e, blk=blk: e.transpose(pX[:, blk * 128:(blk + 1) * 128],
                                                               ysb[:, blk * 128:(blk + 1) * 128], identf[:]),
                         reads=[b_ysb, b_identf], writes=[b_pX], signal=(blk == 3))
                S.op("act", lambda e, j=j: e.copy(qT[:, :, j * 128:(j + 1) * 128],
                                                  pX[:, 0:256].rearrange("p (a t) -> p a t", a=2)),
                     reads=[b_pX], writes=[b_qT])
                S.op("act", lambda e, j=j, ks=ks: e.copy(kT[ks][:, :, j * 128:(j + 1) * 128],
                                                         pX[:, 256:512].rearrange("p (a t) -> p a t", a=2)),
                     reads=[b_pX], writes=[b_kT[ks]])
            S.op("pe", lambda e: e.transpose(pX[0:4, 0:128], rm[:, 0:4], identf[:]), reads=[b_rm, b_identf],
                 writes=[b_pX], signal=False)
            S.op("pe", lambda e: e.transpose(pX[0:4, 128:256], rm[:, 4:8], identf[:]), reads=[b_rm, b_identf],
                 writes=[b_pX])
            S.op("dve", lambda e: e.tensor_copy(st4[:, 2:3], st4[:, 1:2]), reads=[b_st4], writes=[b_st4])
            S.op("dve", lambda e: e.tensor_reduce(st4[:, 0:2], pX[0:4, 0:256].rearrange("p (a t) -> p a t", a=2),
                                                  AX.X, ALU.max), reads=[b_pX, b_st4], writes=[b_st4])
            S.op("dve", lambda e: e.tensor_tensor(st4[:, 3:4], st4[:, 1:2], st4[:, 2:3], ALU.max),
                 reads=[b_st4], writes=[b_st4])
            S.op("dve", lambda e: e.tensor_tensor(st4[:, 3:4], st4[:, 3:4], st4[:, 0:1], ALU.mult),
                 reads=[b_st4], writes=[b_st4])
            S.op("dve", lambda e: e.tensor_scalar(dg4[:], identf[0:4, 0:4], st4[:, 3:4], None, ALU.mult),
                 reads=[b_st4, b_identf], writes=[b_dg4])
            S.mm(pX[:, 256:260], ones[0:4, :], dg4[:], reads=[b_ones, b_dg4], writes=[b_pX])
            S.op("act", lambda e: e.activation(negM[:], pX[:, 256:260], AF.Sqrt), reads=[b_pX], writes=[b_negM])
            S.op("dve", lambda e: e.tensor_scalar(negM[:], negM[:], -1.0, None, ALU.mult), reads=[b_negM],
                 writes=[b_negM])
            for l, dil in enumerate((1, 4, 16)):
                for t16 in range(16):
                    if dil == 1:
                        a0, a1 = t16 * 128, (t16 + 1) * 128
                    elif dil == 4:
                        n4, r = t16 // 4, t16 % 4
                        a0, a1 = n4 * 512 + r, (n4 + 1) * 512
                    else:
                        a0, a1 = t16, ST
                    for c in range(8):
                        S.mm(pX[:, 0:256], xT[:, c, a0:a1:dil], watt[:, c, 512:768],
                             reads=[b_xTs[xs], b_watt], writes=[b_pX], start=(c == 0), stop=(c == 7), signal=(c == 7))
                    S.op("act", lambda e, l=l, t16=t16, ks=ks: e.copy(
                        V[l][ks][:, t16, :, 0:64], pX[:, 0:256].rearrange("p (h d) -> p h d", d=64)),
                        reads=[b_pX], writes=[b_V[l][ks]])
            for h in range(4):
                hp, p0 = h // 2, 64 * (h % 2)
                first = [True] * 4

                def block(qsl, cur, prev, outs):
                    nonlocal nblk
                    pb = nblk % 2
                    pt = nblk % NPT
                    nblk += 1
                    lo = 0 if prev is not None else 128
                    qap = qT[p0:p0 + 64, hp, qsl]
                    if prev is not None:
                        pslot, psl, pv_ap = prev
                        S.mm(pS[pb][:, 0:128], kT[pslot][p0:p0 + 64, hp, psl], qap,
                             reads=[b_kT[pslot], b_qT], writes=[b_pS[pb]], start=True, stop=False, signal=False)
                        S.mm(pS[pb][:, 0:128], identb[:], maskb[:, 0:128], reads=[b_identb, b_maskb],
                             writes=[b_pS[pb]], start=False, stop=True, signal=False)
                    S.mm(pS[pb][:, 128:256], kT[ks][p0:p0 + 64, hp, qsl], qap,
                         reads=[b_kT[ks], b_qT], writes=[b_pS[pb]], start=True, stop=False, signal=False)
                    S.mm(pS[pb][:, 128:256], identb[:], maskb[:, 128:256], reads=[b_identb, b_maskb],
                         writes=[b_pS[pb]], start=False, stop=True, signal=True)
                    S.op("act", lambda e, hh=h: e.activation(PT[pt][:, lo:256], pS[pb][:, lo:256], AF.Exp,
                                                             bias=negM[:, hh:hh + 1], scale=1.0),
                         reads=[b_pS[pb], b_negM], writes=[b_PT[pt]])
                    kts = ([(0, pv_ap, b_V_prev)] if prev is not None else []) + [(1, cur, b_V_cur)]
                    nmm = len(kts) * len(outs)
                    i = 0
                    for (kt, vap, vb) in kts:
                        for (ocol, pcol) in outs:
                            bank = ocol.start // 512
                            i += 1
                            S.mm(pO[0:65, ocol], vap, PT[pt][:, kt * 128 + pcol.start:kt * 128 + pcol.stop],
                                 reads=[b_PT[pt], vb], writes=[b_pO], start=first[bank], stop=False,
                                 signal=(i == nmm), skip_group_check=True)
                            first[bank] = False

                full = [(None, slice(0, 128))]
                for jb in range(16):
                    qsl = slice(jb * 128, (jb + 1) * 128)
                    b_V_cur = b_V[0][ks]
                    cur = V[0][ks][:, jb, h, :]
                    prev = None
                    if jb > 0:
                        prev = (ks, slice((jb - 1) * 128, jb * 128), V[0][ks][:, jb - 1, h, :])
                        b_V_prev = b_V[0][ks]
                    elif st > 0:
                        prev = (kp, slice(15 * 128, 16 * 128), V[0][kp][:, 15, h, :])
                        b_V_prev = b_V[0][kp]
                    block(qsl, cur, prev, [(slice(jb * 128, (jb + 1) * 128), slice(0, 128))])
                for n4 in range(4):
                    for r in range(4):
                        qsl = slice(n4 * 512 + r, (n4 + 1) * 512, 4)
                        b_V_cur = b_V[1][ks]
                        cur = V[1][ks][:, n4 * 4 + r, h, :]
                        prev = None
                        if n4 > 0:
                            prev = (ks, slice((n4 - 1) * 512 + r, n4 * 512, 4), V[1][ks][:, (n4 - 1) * 4 + r, h, :])
                            b_V_prev = b_V[1][ks]
                        elif st > 0:
                            prev = (kp, slice(3 * 512 + r, ST, 4), V[1][kp][:, 12 + r, h, :])
                            b_V_prev = b_V[1][kp]
                        block(qsl, cur, prev, [(qsl, slice(0, 128))])
                for r in range(16):
                    qsl = slice(r, ST, 16)
                    b_V_cur = b_V[2][ks]
                    cur = V[2][ks][:, r, h, :]
                    prev = None
                    if st > 0:
                        prev = (kp, qsl, V[2][kp][:, r, h, :])
                        b_V_prev = b_V[2][kp]
                    block(qsl, cur, prev, [(slice(b4 * 512 + r, (b4 + 1) * 512, 16), slice(32 * b4, 32 * b4 + 32))
                                           for b4 in range(4)])
                for b4 in range(4):
                    cs = slice(b4 * 512, (b4 + 1) * 512)
                    eng = "act" if b4 % 2 == 0 else "dve"
                    if eng == "act":
                        S.op("act", lambda e, cs=cs: e.copy(oacc[0:65, cs], pO[0:65, cs]), reads=[b_pO], writes=[b_oacc])
                    else:
                        S.op("dve", lambda e, cs=cs: e.tensor_copy(oacc[0:65, cs], pO[0:65, cs]), reads=[b_pO],
                             writes=[b_oacc])
                S.op("dve", lambda e: e.reciprocal(oacc[64:65, :], oacc[64:65, :]), reads=[b_oacc], writes=[b_oacc])
                for b4 in range(4):
                    cs = slice(b4 * 512, (b4 + 1) * 512)
                    S.mm(pO[0:64, cs], ones[64:65, 0:64], oacc[64:65, cs], reads=[b_ones, b_oacc], writes=[b_pO],
                         signal=(b4 == 3))
                os_ = (st * 4 + h) % 2
                for b4 in range(4):
                    cs = slice(b4 * 512, (b4 + 1) * 512)
                    S.op("dve", lambda e, cs=cs, os_=os_: e.tensor_tensor(oT[os_][:, cs], pO[0:64, cs], oacc[0:64, cs],
                                                                       ALU.mult),
                         reads=[b_pO, b_oacc], writes=[b_oT[os_]])
                for xk in range(ST // XCH):
                    S.dma("sp", D["omix"][st * (ST // XCH) + xk][h * 64:(h + 1) * 64, :],
                          oT[os_][:, xk * XCH:(xk + 1) * XCH], reads=[b_oT[os_]])
        S.barrier()
        S.run()


def _TT(o, a, b, op):
    return lambda e: e.tensor_tensor(o, a, b, op)


def _TS(o, a, s1, s2, op0, op1=None):
    if op1 is None:
        return lambda e: e.tensor_scalar(o, a, s1, s2, op0)
    return lambda e: e.tensor_scalar(o, a, s1, s2, op0, op1)


def _STT(o, a, s, b, op0, op1):
    return lambda e: e.scalar_tensor_tensor(o, a, s, b, op0, op1)


def _ACT(o, i, f, **kw):
    return lambda e: e.activation(o, i, f, **kw)


def _CP(o, i):
    return lambda e: e.copy(o, i)


def _TC(o, i):
    return lambda e: e.tensor_copy(o, i)


def _TR(o, i, ident):
    return lambda e: e.transpose(o, i, ident)


def phase_b(nc, S, D):
    STB = 1024
    NSTB = SEQ // STB
    TPS = STB // 128
    DS = DECAY_SCALE
    with ExitStack() as ph:
        def sb(n, shp, dt=F32):
            return ph.enter_context(nc.sbuf_tensor(n, shp, dt))

        def ps(n, shp, dt=F32):
            return ph.enter_context(nc.psum_tensor(n, shp, dt))

        identf = sb("b_identf", [128, 128])
        b_identf = Buf()
        S.dma("sp", identf[:], D["ident"], writes=[b_identf])
        identb = sb("b_identb", [128, 128], BF16)
        b_identb = Buf()
        S.dma("pool", identb[:], D["ident"], writes=[b_identb])
        cst = sb("b_cst", [128, 642])
        b_cst = Buf()
        S.dma("sp", cst[:], D["cstB"], writes=[b_cst])
        triI, triS, triA = cst[:, 0:128], cst[:, 128:256], cst[:, 256:384]
        chunkind = cst[:, 384:386]
        mask1, mask3, eye2 = cst[:, 386:514], cst[:, 514:578], cst[:, 578:642]
        vec = sb("b_vec", [128, 8, 256])
        b_vec = Buf()
        for i, nm in enumerate(("w0", "a0", "k_k", "k_a", "k_a", "r_k", "gn_g", "gn_b")):
            S.dma("sp", vec[:, i, :], bcast_rows(D[nm]), writes=[b_vec], acc=(i > 0))
        S.op("dve", _TS(vec[:, 4, :], vec[:, 4, :], -1.0, 1.0, ALU.mult, ALU.add), reads=[b_vec], writes=[b_vec])
        bias_wa = vec[:, 0:2, :].rearrange("p a b -> p (a b)")
        wdec = sb("b_wdec", [128, 512])
        b_wdec = Buf()
        S.op("dve", lambda e: e.memset(wdec[:], 0.0), writes=[b_wdec])
        S.dma("sp", wdec[0:64, 0:256], D["w_dec"], writes=[b_wdec], acc=True)
        S.dma("sp", wdec[64:128, 256:512], D["w_aaa"], writes=[b_wdec], acc=True)
        wgate = sb("b_wgate", [128, 256])
        b_wgate = Buf()
        S.dma("sp", wgate[:], D["w_gate"], writes=[b_wgate])
        epsg = sb("b_epsg", [128, 1])
        b_epsg = Buf()
        S.op("dve", lambda e: e.memset(epsg[:], GN_EPS), writes=[b_epsg])
        mub = sb("b_mub", [128, 1024])
        b_mub = Buf()
        S.dma("sp", mub[:], bcast_rows(D["mu"]), writes=[b_mub])
        omub = sb("b_omub", [128, 1024])
        b_omub = Buf()
        S.op("dve", _TS(omub[:], mub[:], -1.0, 1.0, ALU.mult, ALU.add), reads=[b_mub], writes=[b_omub])
        W1 = sb("b_W1", [128, 8, 1024], BF16)
        W2 = sb("b_W2", [128, 8, 1024], BF16)
        b_W = Buf()
        wst = [sb("b_wst%d" % i, [128, 1024]) for i in range(2)]
        b_wst = [Buf() for _ in range(2)]
        for c in range(8):
            S.dma("sp", wst[c % 2][:], D["w_rw"][c * 128:(c + 1) * 128, :], writes=[b_wst[c % 2]])
            S.op("dve", _TT(W1[:, c, :], wst[c % 2][:], omub[:], ALU.mult), reads=[b_wst[c % 2], b_omub], writes=[b_W])
            S.op("pool", _TT(W2[:, c, :], wst[c % 2][:], mub[:], ALU.mult), reads=[b_wst[c % 2], b_mub], writes=[b_W])

        xTv = D["xT"].rearrange("(c p) t -> p c t", p=128)
        xc = sb("b_xc", [128, 8, STB], BF16)
        xp = sb("b_xp", [128, 8, STB], BF16)
        b_xc = Buf()
        b_xp = Buf()

        Hs = sb("b_Hs", [128, 4, 64], BF16)
        b_H = Buf()
        S.op("dve", lambda e: e.memset(Hs[:], 0.0), writes=[b_H])

        def t256(n):
            return sb(n, [128, 256]), Buf()
        tl, b_tl = sb("b_tl", [128, 256]), Buf()
        lT, b_lT = sb("b_lT", [128, 256]), Buf()
        lg, b_lg = sb("b_lg", [128, 512]), Buf()
        sg, b_sg = sb("b_sg", [128, 512]), Buf()
        gs, b_gs = t256("b_gs")
        yA = sb("b_yA", [128, 512])
        b_rs = Buf()
        rs = yA[:, 0:256]
        krs = yA[:, 256:512]
        vs, b_vs = sb("b_vs", [128, 256], BF16), Buf()
        kk, b_kk = t256("b_kk")
        t1, b_t1 = t256("b_t1")
        t2, b_t2 = t256("b_t2")
        km, b_km = t256("b_km")
        bb, b_bb = t256("b_bb")
        E, b_E = sb("b_E", [128, 4, 256]), [Buf() for _ in range(4)]
        X4, b_X4 = sb("b_X4", [128, 4, 256], BF16), Buf()
        BK, b_BK = sb("b_BK", [128, 2, 256], BF16), Buf()
        XT, b_XT = sb("b_XT", [128, 4, 4, 64], BF16), Buf()
        dgP, b_dgP = sb("b_dgP", [128, 4, 64], BF16), Buf()
        AM1, b_AM1 = sb("b_AM1", [128, 4, 128], BF16), Buf()
        AM2, b_AM2 = sb("b_AM2", [128, 4, 128], BF16), Buf()
        Lm = [sb("b_L%d" % i, [128, 4, 2, 64], BF16) for i in range(2)]
        b_Lm = [Buf() for _ in range(2)]
        Pm, b_Pm = sb("b_Pm", [128, 4, 64], BF16), Buf()
        sm, b_sm = sb("b_sm", [128, 32]), Buf()
        PC, b_PC = sb("b_PC", [128, 4, 2]), Buf()
        Ws, b_Ws = sb("b_Ws", [128, 256], BF16), Buf()
        Us, b_Us = sb("b_Us", [128, 256], BF16), Buf()
        on, b_on = t256("b_on")
        gst, b_gst = sb("b_gst", [128, 4, 6]), Buf()
        gmv, b_gmv = sb("b_gmv", [128, 12]), Buf()
        orT = sb("b_orT", [128, 2, STB], BF16)
        b_orT = Buf()

        K = [ps("b_K%d" % i, [128, 512]) for i in range(8)]
        b_K = [Buf(psum=True) for _ in range(8)]

        def v3(ap):
            return ap.rearrange("p (h d) -> p h d", d=64)

        def bc(ap4):
            return ap4.unsqueeze(2).to_broadcast([128, 4, 64])

        def dbl(name, shp, dt=F32):
            return [sb("%s_%d" % (name, i), shp, dt) for i in range(2)], [Buf() for _ in range(2)]
        AM1d, b_AM1d = dbl("b_AM1d", [128, 4, 128], BF16)
        AM2d, b_AM2d = dbl("b_AM2d", [128, 4, 128], BF16)
        Pmd, b_Pmd = dbl("b_Pmd", [128, 4, 64], BF16)
        XTd, b_XTd = dbl("b_XTd", [128, 4, 4, 64], BF16)
        def trp(name, shp, dt=F32):
            return [sb("%s_%d" % (name, i), shp, dt) for i in range(3)], [Buf() for _ in range(3)]
        BKd, b_BKd = trp("b_BKd", [128, 2, 256], BF16)
        vsd, b_vsd = trp("b_vsd", [128, 256], BF16)
        dgPd, b_dgPd = dbl("b_dgPd", [128, 4, 64], BF16)
        gsd, b_gsd = trp("b_gsd", [128, 256])
        bsd, b_bsd = trp("b_bsd", [128, 4])
        X4d, b_X4d = dbl("b_X4d", [128, 4, 256], BF16)
        PCd, b_PCd = dbl("b_PCd", [128, 4, 2])
        smG, b_smG = sb("b_smG", [128, 16]), Buf()
        tG, b_tG = sb("b_tG", [128, 256]), Buf()

        def front1(n):
            st, j = n // TPS, n % TPS
            d, t3 = n % 2, n % 3
            BK, b_BK, vs, b_vs, gs, b_gs, bs, b_bs = BKd[t3], b_BKd[t3], vsd[t3], b_vsd[t3], gsd[t3], b_gsd[t3], bsd[t3], b_bsd[t3]
            X4, b_X4, PC, b_PC = X4d[d], b_X4d[d], PCd[d], b_PCd[d]
            if j == 0:
                S.dma("pool", xc[:], xTv[:, :, st * STB:(st + 1) * STB], writes=[b_xc])
                if st == 0:
                    S.op("dve", lambda e: e.memset(xp[:, :, 0:1], 0.0), writes=[b_xp])
                    S.dma("pool", xp[:, :, 1:STB], xTv[:, :, 0:STB - 1], writes=[b_xp], acc=True)
                else:
                    S.dma("pool", xp[:], xTv[:, :, st * STB - 1:(st + 1) * STB - 1], writes=[b_xp])
            tsl = slice(j * 128, (j + 1) * 128)
            for bk in range(2):
                cs = slice(bk * 512, (bk + 1) * 512)
                for c in range(8):
                    S.mm(K[bk][:], xc[:, c, tsl], W1[:, c, cs], reads=[b_xc, b_W], writes=[b_K[bk]],
                         start=(c == 0), stop=False, signal=False)
                for c in range(8):
                    S.mm(K[bk][:], xp[:, c, tsl], W2[:, c, cs], reads=[b_xp, b_W], writes=[b_K[bk]],
                         start=False, stop=(c == 7), signal=(c == 7))
            yield
            pA, pB = K[0], K[1]
            S.op("act", _ACT(tl[:, 0:64], pB[:, 256:320], AF.Tanh), reads=[b_K[1]], writes=[b_tl])
            S.op("act", _CP(tl[:, 64:128], pB[:, 320:384]), reads=[b_K[1]], writes=[b_tl])
            S.op("act", _ACT(tl[:, 128:256], pB[:, 384:512], AF.Sigmoid), reads=[b_K[1]], writes=[b_tl])
            S.op("act", _CP(vs[:], pB[:, 0:256]), reads=[b_K[1]], writes=[b_vs])
            S.op("act", _CP(yA[:], pA[:]), reads=[b_K[0]], writes=[b_rs])
            yield
            S.op("pe", _TR(K[3][:, 0:128], tl[:, 0:128], identf[:]), reads=[b_tl, b_identf], writes=[b_K[3]], signal=False)
            S.op("pe", _TR(K[3][:, 128:256], tl[:, 128:256], identf[:]), reads=[b_tl, b_identf], writes=[b_K[3]])
            yield
            S.op("dve", _TC(lT[:], K[3][:, 0:256]), reads=[b_K[3]], writes=[b_lT])
            yield
            S.mm(K[2][:], lT[:, 0:128], wdec[:], reads=[b_lT, b_wdec], writes=[b_K[2]], signal=False)
            S.mm(K[3][:, 256:512], lT[:, 128:256], wgate[:], reads=[b_lT, b_wgate], writes=[b_K[3]])
            yield
            S.op("dve", _TT(lg[:], K[2][:], bias_wa, ALU.add), reads=[b_K[2], b_vec], writes=[b_lg])
            S.op("act", _ACT(sg[:], lg[:], AF.Sigmoid), reads=[b_lg], writes=[b_sg])
            S.op("act", _CP(gs[:], K[3][:, 256:512]), reads=[b_K[3]], writes=[b_gs])
            sw, aa = sg[:, 0:256], sg[:, 256:512]
            yield
            S.mm(K[2][:, 0:256], triS, sw, reads=[b_cst, b_sg], writes=[b_K[2]], signal=False)
            S.mm(K[2][:, 256:512], triI, sw, reads=[b_cst, b_sg], writes=[b_K[2]], signal=False)
            S.mm(K[3][:, 0:256], triA, sw, reads=[b_cst, b_sg], writes=[b_K[3]], signal=False)
            for c in range(2):
                rows = slice(64 * c, 64 * c + 64)
                for h in range(4):
                    S.mm(K[3][rows, 256 + 2 * h:258 + 2 * h], sg[rows, h * 64:(h + 1) * 64], cst[rows, 384:386],
                         reads=[b_cst, b_sg], writes=[b_K[3]], signal=(c == 1 and h == 3))
            yield
            S.op("act", _ACT(E[:, 0, :], K[2][:, 0:256], AF.Exp, scale=-DS), reads=[b_K[2]], writes=[b_E[0]])
            S.op("act", _ACT(E[:, 1, :], K[2][:, 256:512], AF.Exp, scale=-DS), reads=[b_K[2]], writes=[b_E[1]])
            S.op("act", _ACT(E[:, 2, :], K[2][:, 256:512], AF.Exp, scale=DS), reads=[b_K[2]], writes=[b_E[2]])
            S.op("act", _ACT(E[:, 3, :], K[3][:, 0:256], AF.Exp, scale=-DS), reads=[b_K[3]], writes=[b_E[3]])
            S.op("act", _ACT(PC[:], K[3][:, 256:264], AF.Exp, scale=-DS), reads=[b_K[3]], writes=[b_PC])
            S.op("dve", _TT(kk[:], krs, vec[:, 2, :], ALU.mult), reads=[b_rs, b_vec], writes=[b_kk])
            S.op("pool", _TT(t1[:], kk[:], kk[:], ALU.mult), reads=[b_kk], writes=[b_t1])
            S.op("pool", _TT(t2[:], aa, vec[:, 3, :], ALU.mult), reads=[b_sg, b_vec], writes=[b_t2])
            S.op("pool", _TT(t2[:], t2[:], vec[:, 4, :], ALU.add), reads=[b_t2, b_vec], writes=[b_t2])
            yield
            S.op("dve", lambda e: e.tensor_reduce(sm[:, 0:4], v3(t1[:]), AX.X, ALU.add), reads=[b_t1], writes=[b_sm])
            S.op("act", _ACT(sm[:, 4:8], sm[:, 0:4], AF.Sqrt), reads=[b_sm], writes=[b_sm])
            S.op("dve", _TT(km[:], krs, t2[:], ALU.mult), reads=[b_rs, b_t2], writes=[b_km])
            yield
            S.op("dve", _TS(sm[:, 4:8], sm[:, 4:8], L2_EPS, None, ALU.max), reads=[b_sm], writes=[b_sm])
            S.op("dve", lambda e: e.reciprocal(sm[:, 8:12], sm[:, 4:8]), reads=[b_sm], writes=[b_sm])
            S.op("dve", _TT(v3(kk[:]), v3(kk[:]), bc(sm[:, 8:12]), ALU.mult), reads=[b_kk, b_sm], writes=[b_kk])
            S.op("pool", _TT(t1[:], rs, km[:], ALU.mult), reads=[b_rs, b_km, b_sm], writes=[b_t1])
            S.op("pool", _TT(t1[:], t1[:], vec[:, 5, :], ALU.mult), reads=[b_t1, b_vec], writes=[b_t1])
            yield
            S.op("pool", _TT(bb[:], kk[:], aa, ALU.mult), reads=[b_kk, b_sg], writes=[b_bb])
            S.op("dve", lambda e: e.tensor_reduce(bs[:], v3(t1[:]), AX.X, ALU.add), reads=[b_t1], writes=[b_bs])
            S.op("dve", _STT(X4[:, 0, :], kk[:], -1.0, E[:, 0, :], ALU.mult, ALU.mult), reads=[b_kk, b_E[0]], writes=[b_X4])
            S.op("pool", _TT(X4[:, 1, :], rs, E[:, 1, :], ALU.mult), reads=[b_rs, b_E[1]], writes=[b_X4])
            yield
            S.op("dve", _TT(X4[:, 2, :], bb[:], E[:, 2, :], ALU.mult), reads=[b_bb, b_E[2]], writes=[b_X4])
            S.op("pool", _TT(X4[:, 3, :], km[:], E[:, 2, :], ALU.mult), reads=[b_km, b_E[2]], writes=[b_X4])
            S.op("dve", _TT(BK[:, 0, :], bb[:], E[:, 3, :], ALU.mult), reads=[b_bb, b_E[3]], writes=[b_BK])
            S.op("pool", _TT(BK[:, 1, :], km[:], E[:, 3, :], ALU.mult), reads=[b_km, b_E[3]], writes=[b_BK])

        def front2(n):
            d = n % 2
            AM1, b_AM1, AM2, b_AM2 = AM1d[d], b_AM1d[d], AM2d[d], b_AM2d[d]
            Pm, b_Pm, XT, b_XT, dgP, b_dgP = Pmd[d], b_Pmd[d], XTd[d], b_XTd[d], dgPd[d], b_dgPd[d]
            X4, b_X4, PC, b_PC = X4d[d], b_X4d[d], PCd[d], b_PCd[d]
            for c in range(2):
                rows = slice(64 * c, 64 * c + 64)
                for q in range(4):
                    for h in range(4):
                        blk = q * 4 + h
                        S.mm(K[4 + blk // 8][rows, (blk % 8) * 64:(blk % 8 + 1) * 64],
                             X4[rows, q, h * 64:(h + 1) * 64], identb[rows, rows],
                             reads=[b_X4, b_identb], writes=[b_K[4 + blk // 8]], signal=(c == 1 and blk % 8 == 7))
            yield
            XTf = XT[:].rearrange("p q h t -> p (q h t)")
            S.op("act", _CP(XTf[:, 0:512], K[4][:]), reads=[b_K[4]], writes=[b_XT])
            S.op("dve", _TC(XTf[:, 512:1024], K[5][:]), reads=[b_K[5]], writes=[b_XT])
            for c in range(2):
                rows = slice(64 * c, 64 * c + 64)
                S.op("pool", _TT(dgP[rows, :, :], cst[rows, 578:642].unsqueeze(1).to_broadcast([64, 4, 64]),
                                 PC[rows, :, c:c + 1].to_broadcast([64, 4, 64]), ALU.mult),
                     reads=[b_PC, b_cst], writes=[b_dgP])
            yield
            for c in range(2):
                rows = slice(64 * c, 64 * c + 64)
                for h in range(4):
                    last = (c == 1 and h == 3)
                    S.mm(K[4][rows, h * 128:(h + 1) * 128], XT[rows, 2, h, :], XT[rows, 0:2, h, :],
                         reads=[b_XT], writes=[b_K[4]], signal=False)
                    S.mm(K[5][rows, h * 128:(h + 1) * 128], XT[rows, 3, h, :], XT[rows, 0:2, h, :],
                         reads=[b_XT], writes=[b_K[5]], signal=last)
            yield
            m1b = mask1.unsqueeze(1).to_broadcast([128, 4, 128])
            S.op("dve", _TT(AM1[:], K[4][:].rearrange("p (h t) -> p h t", h=4), m1b, ALU.mult),
                 reads=[b_K[4], b_cst], writes=[b_AM1])
            S.op("pool", _TC(Lm[0][:, :, 0, :], AM1[:, :, 0:64]), reads=[b_AM1], writes=[b_Lm[0]])
            S.op("pool", _TT(Pm[:], AM1[:, :, 0:64], eye2.unsqueeze(1).to_broadcast([128, 4, 64]), ALU.add),
                 reads=[b_AM1, b_cst], writes=[b_Pm])
            yield
            for c in range(2):
                rows = slice(64 * c, 64 * c + 64)
                for h in range(4):
                    S.mm(K[4][rows, h * 64:(h + 1) * 64], XT[rows, 0, h, :], XT[rows, 2, h, :],
                         reads=[b_XT], writes=[b_K[4]], signal=(c == 1 and h == 3))
            S.op("dve", _TT(AM2[:], K[5][:].rearrange("p (h t) -> p h t", h=4), m1b, ALU.mult),
                 reads=[b_K[5], b_cst], writes=[b_AM2])
            yield
            m3b = mask3.unsqueeze(1).to_broadcast([128, 4, 64])
            S.op("dve", _TT(Lm[0][:, :, 1, :], K[4][:, 0:256].rearrange("p (h t) -> p h t", h=4), m3b, ALU.mult),
                 reads=[b_K[4], b_cst], writes=[b_Lm[0]])
            yield
            cur = 0
            for rnd in range(6):
                nxt = 1 - cur
                do_sq = rnd < 5
                do_p = rnd >= 1
                for c in range(2):
                    rows = slice(64 * c, 64 * c + 64)
                    for h in range(4):
                        last = (c == 1 and h == 3)
                        Lc, LTc = Lm[cur][rows, h, 0, :], Lm[cur][rows, h, 1, :]
                        if do_sq and rnd < 4:
                            S.mm(K[5][rows, (h * 2) * 64:(h * 2 + 1) * 64], LTc, Lc, reads=[b_Lm[cur]],
                                 writes=[b_K[5]], signal=False)
                        if do_sq:
                            S.mm(K[5][rows, (h * 2 + 1) * 64:(h * 2 + 2) * 64], Lc, LTc, reads=[b_Lm[cur]],
                                 writes=[b_K[5]], signal=(last and not do_p))
                        if do_p:
                            S.mm(K[4][rows, 256 + h * 64:256 + (h + 1) * 64], LTc, Pm[rows, h, :],
                                 reads=[b_Lm[cur], b_Pm], writes=[b_K[4]], signal=last)
                yield
                src = K[5][:].rearrange("p (h a t) -> p h a t", h=4, a=2)
                if do_sq:
                    if rnd < 4:
                        S.op("act", _CP(Lm[nxt][:], src), reads=[b_K[5]], writes=[b_Lm[nxt]])
                    else:
                        S.op("act", _CP(Lm[nxt][:, :, 1, :], src[:, :, 1, :]), reads=[b_K[5]], writes=[b_Lm[nxt]])
                if do_p:
                    S.op("dve", _TT(Pm[:], K[4][:, 256:512].rearrange("p (h t) -> p h t", h=4), Pm[:], ALU.add),
                         reads=[b_K[4], b_Pm], writes=[b_Pm])
                cur = nxt
                yield

        def back(n):
            st, j = n // TPS, n % TPS
            d = n % 2
            tsl = slice(j * 128, (j + 1) * 128)
            t3 = n % 3
            AM1, b_AM1, AM2, b_AM2 = AM1d[d], b_AM1d[d], AM2d[d], b_AM2d[d]
            Pm, b_Pm, XT, b_XT, BK, b_BK = Pmd[d], b_Pmd[d], XTd[d], b_XTd[d], BKd[t3], b_BKd[t3]
            vs, b_vs, dgP, b_dgP, gs, b_gs, bs, b_bs = vsd[t3], b_vsd[t3], dgPd[d], b_dgPd[d], gsd[t3], b_gsd[t3], bsd[t3], b_bsd[t3]
            pO_ = K[7]
            for c in range(2):
                rows = slice(64 * c, 64 * c + 64)
                orow = slice(64 * (1 - c), 64 * (1 - c) + 64)
                for h in range(4):
                    hc = slice(256 + h * 64, 256 + (h + 1) * 64)
                    vh = vs[rows, h * 64:(h + 1) * 64]
                    S.mm(K[6][rows, hc], AM2[rows, h, 0:64], vh, reads=[b_AM2, b_vs], writes=[b_K[6]],
                         start=True, stop=False, signal=False)
                    S.mm(K[6][rows, hc], XT[rows, 0, h, :], Hs[rows, h, :], reads=[b_XT, b_H], writes=[b_K[6]],
                         start=False, stop=True, signal=(h == 3))
                yield
                S.op("dve", _TC(Ws[rows, :], K[6][rows, 256:512]), reads=[b_K[6]], writes=[b_Ws])
                yield
                for h in range(4):
                    hc = slice(h * 64, (h + 1) * 64)
                    S.mm(K[6][rows, hc], Pm[rows, h, :], Ws[rows, hc], reads=[b_Pm, b_Ws], writes=[b_K[6]],
                         signal=(h == 3))
                yield
                S.op("act", _CP(Us[rows, :], K[6][rows, 0:256]), reads=[b_K[6]], writes=[b_Us])
                yield
                for h in range(4):
                    hc = slice(256 + h * 64, 256 + (h + 1) * 64)
                    vh = vs[rows, h * 64:(h + 1) * 64]
                    uh = Us[rows, h * 64:(h + 1) * 64]
                    S.mm(pO_[rows, hc], AM2[rows, h, 64:128], vh, reads=[b_AM2, b_vs], writes=[b_K[7]],
                         start=True, stop=False, signal=False)
                    S.mm(pO_[rows, hc], XT[rows, 1, h, :], Hs[rows, h, :], reads=[b_XT, b_H], writes=[b_K[7]],
                         start=False, stop=False, signal=False)
                    S.mm(pO_[rows, hc], AM1[rows, h, 64:128], uh, reads=[b_AM1, b_Us], writes=[b_K[7]],
                         start=False, stop=True, signal=False)
                for h in range(4):
                    oc_ = slice(256 + h * 64, 256 + (h + 1) * 64)
                    vh = vs[rows, h * 64:(h + 1) * 64]
                    uh = Us[rows, h * 64:(h + 1) * 64]
                    S.mm(K[6][orow, oc_], BK[rows, 1, h * 64:(h + 1) * 64], vh, reads=[b_BK, b_vs], writes=[b_K[6]],
                         start=True, stop=False, signal=False)
                    S.mm(K[6][orow, oc_], dgP[rows, h, :], Hs[rows, h, :], reads=[b_dgP, b_H], writes=[b_K[6]],
                         start=False, stop=False, signal=False)
                    S.mm(K[6][orow, oc_], BK[rows, 0, h * 64:(h + 1) * 64], uh, reads=[b_BK, b_Us], writes=[b_K[6]],
                         start=False, stop=True, signal=(h == 3))
                yield
                S.op("act", _CP(Hs[orow, :, :], K[6][orow, 256:512].rearrange("p (h v) -> p h v", h=4)),
                     reads=[b_K[6], b_K[7]], writes=[b_H])
                yield
            o3 = pO_[:, 256:512].rearrange("p (h d) -> p h d", d=64)
            S.op("act", _CP(on[:], pO_[:, 256:512]), reads=[b_K[7]], writes=[b_on])
            yield
            S.op("act", _ACT(tG[:], on[:], AF.Square), reads=[b_on], writes=[b_tG])
            S.op("dve", lambda e: e.tensor_reduce(smG[:, 0:4], v3(on[:]), AX.X, ALU.add), reads=[b_on], writes=[b_smG])
            yield
            S.op("dve", lambda e: e.tensor_reduce(smG[:, 4:8], v3(tG[:]), AX.X, ALU.add), reads=[b_tG], writes=[b_smG])
            S.op("dve", _TS(gmv[:, 0:4], smG[:, 0:4], 1.0 / 64.0, None, ALU.mult), reads=[b_smG], writes=[b_gmv])
            S.op("dve", _TT(gmv[:, 4:8], gmv[:, 0:4], gmv[:, 0:4], ALU.mult), reads=[b_gmv], writes=[b_gmv])
            S.op("dve", _STT(gmv[:, 8:12], smG[:, 4:8], 1.0 / 64.0, gmv[:, 4:8], ALU.mult, ALU.subtract),
                 reads=[b_smG, b_gmv], writes=[b_gmv])
            yield
            S.op("act", _ACT(smG[:, 8:12], gmv[:, 8:12], AF.Sqrt, bias=epsg[:, 0:1], scale=1.0),
                 reads=[b_gmv, b_epsg], writes=[b_smG])
            S.op("dve", lambda e: e.reciprocal(smG[:, 12:16], smG[:, 8:12]), reads=[b_smG], writes=[b_smG])
            S.op("dve", _TT(v3(on[:]), v3(on[:]), bc(gmv[:, 0:4]), ALU.subtract), reads=[b_on, b_gmv], writes=[b_on])
            yield
            S.op("pool", _TT(v3(on[:]), v3(on[:]), bc(smG[:, 12:16]), ALU.mult), reads=[b_on, b_smG], writes=[b_on])
            S.op("pool", _TT(on[:], on[:], vec[:, 6, :], ALU.mult), reads=[b_on, b_vec], writes=[b_on])
            S.op("pool", _TT(on[:], on[:], vec[:, 7, :], ALU.add), reads=[b_on, b_vec], writes=[b_on])
            S.op("dve", _TT(v3(tG[:]), v3(vs[:]), bc(bs[:]), ALU.mult), reads=[b_vs, b_bs, b_tG], writes=[b_tG])
            yield
            S.op("pool", _TT(on[:], on[:], tG[:], ALU.add), reads=[b_on, b_tG], writes=[b_on])
            S.op("pool", _TT(on[:], on[:], gs[:], ALU.mult), reads=[b_on, b_gs], writes=[b_on])
            yield
            for hp in range(2):
                S.op("pe", _TR(K[7][:, hp * 128:(hp + 1) * 128], on[:, hp * 128:(hp + 1) * 128], identf[:]),
                     reads=[b_on, b_identf], writes=[b_K[7]], signal=(hp == 1))
            yield
            S.op("act", _CP(orT[:, :, tsl], K[7][:, 0:256].rearrange("p (a t) -> p a t", a=2)),
                 reads=[b_K[7]], writes=[b_orT])
            if j == TPS - 1 or n == B_TILES - 1:
                for hp in range(2):
                    S.dma("sp", D["omix"][st][256 + hp * 128:256 + (hp + 1) * 128, :],
                          orT[:, hp, :], reads=[b_orT])

        for n in range(B_TILES + 2):
            gens = []
            if n < B_TILES:
                gens.append(front1(n))
            if 0 <= n - 1 < B_TILES:
                gens.append(front2(n - 1))
            if 0 <= n - 2 < B_TILES:
                gens.append(back(n - 2))
            while gens:
                for g_ in list(gens):
                    try:
                        next(g_)
                    except StopIteration:
                        gens.remove(g_)
        S.barrier()
        S.run()


def exchange(nc, S, D, stack):
    ccs = stack.enter_context(nc.semaphore("s_cc"))
    groups = [[0, 1], [2, 3], [4, 5], [6, 7]]
    for k in range(NXCH):
        S.q["pool"].append(["x", lambda e, k=k: e.collective_compute(
            "AllGather", ALU.bypass, replica_groups=groups, ins=[D["omix"][k]], outs=[D["G"][k]]).then_inc(ccs, 1)])
    S.extra.append((ccs, NXCH, "s_cc", "cc"))
    S.barrier()
    S.run()


def build_program(phases="ABC", exch=True):
    nc = bass.Bass("TRN2", target_bir_lowering=False)
    D = {}

    def din(name, shape, dt=F32):
        D[name] = nc.dram_tensor(name, list(shape), dt, kind="ExternalInput").ap()

    din("ident", [128, 128])
    if "A" in phases or "B" in phases:
        din("xT", [D_MODEL, SEQ])
    if "A" in phases:
        din("pos", [128, 64], I32)
        din("w_att", [D_MODEL, 768])
        din("maskT", [128, 256])
    if "B" in phases:
        din("w_rw", [D_MODEL, 1024])
        din("mu", [1, 1024])
        for nm in ("w0", "a0", "k_k", "k_a", "r_k", "gn_g", "gn_b"):
            din(nm, [1, 256])
        din("w_dec", [64, 256])
        din("w_aaa", [64, 256])
        din("w_gate", [128, 256])
        din("cstB", [128, 642])
    if "C" in phases:
        din("xres", [TOKH, D_MODEL])
        din("w_out", [1024, 1024])
        din("wg", [1024, FFN])
        din("wu", [1024, FFN])
        din("wd", [FFN, 1024])
        for nm in ("ln1g", "ln1b", "ln2g", "ln2b"):
            din(nm, [1, 1024])
        din("sel", [128, 2])
        D["out"] = nc.dram_tensor("out", [TOKH, D_MODEL], F32, kind="ExternalOutput").ap()
    full = exch
    if full:
        D["omix"] = [nc.dram_tensor("omix%d" % k, [512, XCH], BF16, kind="Internal").ap() for k in range(NXCH)]
        D["G"] = [nc.dram_tensor("G%d" % k, [1024, XCH], BF16, kind="Internal").ap() for k in range(NXCH)]
    else:
        if "C" in phases:
            din("G", [NXCH, 1024, XCH], BF16)
            D["G"] = [D["G"][k] for k in range(NXCH)]
        if "A" in phases or "B" in phases:
            om = nc.dram_tensor("omix", [NXCH, 512, XCH], BF16, kind="ExternalOutput").ap()
            D["omix"] = [om[k] for k in range(NXCH)]
    with ExitStack() as st:
        S = Sched(nc, st)
        if "C" in phases:
            D["wgs"] = nc.dram_tensor("wgs", [HC, 128, 2, 8, 128], BF16, kind="Internal").ap()
            prep_ffn_weights(nc, S, D)
        if "A" in phases:
            phase_a(nc, S, D)
        if "B" in phases:
            phase_b(nc, S, D)
        if full:
            exchange(nc, S, D, st)
        if "C" in phases:
            phase_c(nc, S, D)
    return nc


def att_mask():
    i_k = np.arange(128)[:, None]
    i_q = np.arange(128)[None, :]
    m = np.zeros((128, 256), np.float32)
    m[:, 0:128] = np.where(i_k >= i_q, 0.0, NEG)
    m[:, 128:256] = np.where(i_k <= i_q, 0.0, NEG)
    return m


def rwkv_consts():
    j = np.arange(128)[:, None]
    t = np.arange(128)[None, :]
    same = (j // 64) == (t // 64)
    c = np.zeros((128, 642), np.float32)
    c[:, 0:128] = same & (j <= t)
    c[:, 128:256] = same & (j < t)
    c[:, 256:384] = same & (j > t)
    c[:, 384] = (np.arange(128) < 64)
    c[:, 385] = (np.arange(128) >= 64)
    jj = (np.arange(128) % 64)[:, None]
    tt = np.arange(64)[None, :]
    c[:, 386:450] = jj < tt
    c[:, 450:514] = jj <= tt
    c[:, 514:578] = tt < jj
    c[:, 578:642] = jj == tt
    return c


def core_inputs(inp, c, phases="ABC"):
    b, g = c // 2, c % 2
    m = {"ident": np.eye(128, dtype=np.float32)}
    w_in = inp["w_in"][0]
    if "A" in phases or "B" in phases:
        m["xT"] = np.ascontiguousarray(inp["x"][b].T)
    if "A" in phases:
        m["pos"] = np.ascontiguousarray(inp["positions"][b].reshape(64, 128).T).astype(np.int32)
        cols = np.concatenate([np.arange(256 * g, 256 * g + 256) + off for off in (0, 512, 1024)])
        m["w_att"] = np.ascontiguousarray(w_in[:, cols])
        m["maskT"] = att_mask()
    if "B" in phases:
        hs = slice(256 * g, 256 * g + 256)
        rcols = np.concatenate([1536 + off + np.arange(256 * g, 256 * g + 256) for off in (0, 512, 1024)]
                               + [1536 + 1536 + np.arange(256)])
        m["w_rw"] = np.ascontiguousarray(w_in[:, rcols])
        m["mu"] = np.ascontiguousarray(inp["mu_shift"][0][rcols - 1536][None, :])
        for nm in ("w0", "a0", "k_k", "k_a", "gn_g", "gn_b"):
            m[nm] = np.ascontiguousarray(inp[nm][0][hs][None, :])
        m["r_k"] = np.ascontiguousarray(inp["r_k"][0][4 * g:4 * g + 4].reshape(1, 256))
        m["w_dec"] = np.ascontiguousarray(inp["w_decay_up"][0][:, hs])
        m["w_aaa"] = np.ascontiguousarray(inp["w_aaa_up"][0][:, hs])
        m["w_gate"] = np.ascontiguousarray(inp["w_gate_up"][0][:, hs])
        m["cstB"] = rwkv_consts()
    if "C" in phases:
        fi = lambda r: np.concatenate([np.arange(256 * r, 256 * r + 256), 512 + np.arange(256 * r, 256 * r + 256)])
        perm = np.concatenate([fi(0), fi(1)])
        sel = np.zeros((128, 2), np.float32)
        sel[:, g] = 1.0
        m.update(xres=np.ascontiguousarray(inp["x"][b, g * TOKH:(g + 1) * TOKH]),
                 w_out=np.ascontiguousarray(inp["w_out"][0][perm]),
                 wg=inp["w_ffn_gate"][0], wu=inp["w_ffn_up"][0], wd=inp["w_ffn_down"][0],
                 ln1g=inp["ln_mix_g"], ln1b=inp["ln_mix_b"], ln2g=inp["ln_ffn_g"], ln2b=inp["ln_ffn_b"],
                 sel=sel)
    return m


_NC_CACHE = {}


def kernel(**inputs):
    inp = {k: np.asarray(v) for k, v in inputs.items()}
    if "nc" not in _NC_CACHE:
        _NC_CACHE["nc"] = build_program("ABC")
    nc = _NC_CACHE["nc"]
    in_maps = [core_inputs(inp, c, "ABC") for c in range(8)]
    res = run_bass_kernel_spmd(nc, in_maps, core_ids=list(range(8)))
    out = np.empty((BATCH, SEQ, D_MODEL), np.float32)
    for c in range(8):
        b, g = c // 2, c % 2
        out[b, g * TOKH:(g + 1) * TOKH] = np.asarray(res.results[c]["out"], dtype=np.float32)
    return out
```

```python
import math
from contextlib import ExitStack

import numpy as np
import concourse.bass as bass
import concourse.mybir as mybir
from concourse.bass_utils import run_bass_kernel_spmd

F32 = mybir.dt.float32
BF16 = mybir.dt.bfloat16
I32 = mybir.dt.int32
AF = mybir.ActivationFunctionType
ALU = mybir.AluOpType
AX = mybir.AxisListType

D_MODEL = 1024
SEQ = 8192
BATCH = 4
HD = 64
FFN = 2816
HC = FFN // 128
ALPHA = 2.0 ** 0.25
LN_EPS = 1e-5
GN_EPS = 64e-5
L2_EPS = 1e-6
DECAY_SCALE = math.exp(-0.5)
NEG = -30000.0
ROPE_THETA = 500000.0
TOKH = SEQ // 2
XCH = 1024
NXCH = SEQ // XCH
B_TILES = SEQ // 128


class Buf:
    __slots__ = ("name", "w", "r", "psum")

    def __init__(self, name="", psum=False):
        self.name = name
        self.w = []
        self.r = {}
        self.psum = psum


class Sched:
    ENG = ("pe", "act", "dve", "pool", "sp")
    NDS = 8

    def __init__(self, nc, stack):
        self.nc = nc
        self.q = {e: [] for e in self.ENG}
        self.sem = {e: stack.enter_context(nc.semaphore("s_" + e)) for e in self.ENG}
        self.cnt = {e: 0 for e in self.ENG}
        self.seen = {e: {} for e in self.ENG}
        self.lastop = {e: None for e in self.ENG}
        self.dq = ("sp", "act", "pool")
        self.dsem = {e: [stack.enter_context(nc.semaphore("d_%s%d" % (e, i)))
                         for i in range(self.NDS)] for e in self.dq}
        self.dcnt = {e: 0 for e in self.dq}
        self.dlast = {e: [None] * self.NDS for e in self.dq}
        self.extra = []

    def _force(self, prod):
        rec = self.lastop[prod]
        assert rec is not None and not rec[2]
        rec[2] = True
        self.cnt[prod] += 1

    def _wait(self, eng, tok):
        if tok is None:
            return
        sem, val, key, prod = tok
        if prod in self.cnt and val > self.cnt[prod]:
            self._force(prod)
            assert val <= self.cnt[prod]
        if self.seen[eng].get(key, 0) >= val:
            return
        self.seen[eng][key] = val
        self.q[eng].append(["w", sem, val])

    def _deps(self, eng, reads, writes):
        for b in reads:
            for t in b.w:
                self._wait(eng, t)
            if b.psum:
                for t in b.r.values():
                    if t[3] != eng:
                        self._wait(eng, t)
        for b in writes:
            for t in b.w:
                if t[3] != eng:
                    self._wait(eng, t)
            for t in b.r.values():
                if t[3] != eng:
                    self._wait(eng, t)

    def _mark(self, tok, reads, writes, acc=False):
        for b in reads:
            b.r[tok[2]] = tok
        for b in writes:
            if acc:
                b.w.append(tok)
            else:
                b.w = [tok]
            b.r = {}

    def op(self, eng, fn, reads=(), writes=(), signal=True):
        self._deps(eng, reads, writes)
        sem = self.sem[eng]
        rec = ["o", fn, bool(signal), sem]
        self.q[eng].append(rec)
        self.lastop[eng] = rec
        if signal:
            self.cnt[eng] += 1
            tok = (sem, self.cnt[eng], eng, eng)
        else:
            tok = (sem, self.cnt[eng] + 1, eng, eng)
        self._mark(tok, reads, writes)
        return tok

    def mm(self, out, lhsT, rhs, reads, writes, start=True, stop=True, signal=True, **kw):
        return self.op("pe", lambda e: e.matmul(out, lhsT, rhs, start=start, stop=stop, **kw),
                       reads, writes, signal)

    def dma(self, queue, out, in_, reads=(), writes=(), acc=False, **kw):
        i = self.dcnt[queue]
        self.dcnt[queue] += 1
        slot = i % self.NDS
        self._wait(queue, self.dlast[queue][slot])
        self._deps(queue, reads, writes)
        sem = self.dsem[queue][slot]
        val = 16 * (i // self.NDS + 1)
        tok = (sem, val, "d_%s%d" % (queue, slot), "dma")
        self.dlast[queue][slot] = tok
        self.q[queue].append(["d", out, in_, sem, kw])
        self._mark(tok, reads, writes, acc=acc)
        return tok

    def barrier(self):
        for o in self.ENG:
            rec = self.lastop[o]
            if rec is not None and not rec[2]:
                self._force(o)
        toks = [(self.sem[o], self.cnt[o], o, o) for o in self.ENG if self.cnt[o] > 0]
        for qn in self.dq:
            toks += [t for t in self.dlast[qn] if t is not None]
        toks += self.extra
        for e in self.ENG:
            for t in toks:
                if t[3] != e:
                    self._wait(e, t)

    @staticmethod
    def _replay(e, recs):
        for r in recs:
            k = r[0]
            if k == "w":
                e.wait_ge(r[1], r[2])
            elif k == "o":
                ins = r[1](e)
                if r[2]:
                    ins.then_inc(r[3], 1)
            elif k == "d":
                e.dma_start(out=r[1], in_=r[2], **r[4]).then_inc(r[3], 16)
            else:
                r[1](e)

    def run(self):
        nc = self.nc
        q = self.q
        rp = self._replay
        with nc.Block() as block:
            @block.tensor
            def _(e):
                rp(e, q["pe"])

            @block.scalar
            def _(e):
                rp(e, q["act"])

            @block.vector
            def _(e):
                rp(e, q["dve"])

            @block.gpsimd
            def _(e):
                rp(e, q["pool"])

            @block.sync
            def _(e):
                rp(e, q["sp"])
        self.q = {e: [] for e in self.ENG}
        self.lastop = {e: None for e in self.ENG}


def bcast_rows(ap, n=128):
    return bass.AP(ap.tensor, ap.offset, [[0, n], [1, ap.shape[-1]]])


def prep_ffn_weights(nc, S, D):
    wgv = D["wg"].rearrange("(c p) n -> p c n", p=128)
    wuv = D["wu"].rearrange("(c p) n -> p c n", p=128)
    D["b_wgs"] = [Buf() for _ in range(HC)]
    for h in range(HC):
        S.dma("pool", D["wgs"][h, :, 0, :, :], wgv[:, :, h * 128:(h + 1) * 128], writes=[D["b_wgs"][h]])
        S.dma("pool", D["wgs"][h, :, 1, :, :], wuv[:, :, h * 128:(h + 1) * 128], writes=[D["b_wgs"][h]], acc=True)


def phase_c(nc, S, D):
    GT = 512
    NG = TOKH // GT
    TPG = GT // 128
    OCW = 256
    with ExitStack() as ph:
        def sb(n, shp, dt):
            return ph.enter_context(nc.sbuf_tensor(n, shp, dt))

        def ps(n, shp, dt):
            return ph.enter_context(nc.psum_tensor(n, shp, dt))

        Gv = [g_.rearrange("(c p) t -> p c t", p=128) for g_ in D["G"]]
        ident = sb("c_ident", [128, 128], BF16)
        b_ident = Buf()
        S.dma("pool", ident[:], D["ident"], writes=[b_ident])
        sel = sb("c_sel", [128, 2], F32)
        b_sel = Buf()
        S.dma("sp", sel[:], D["sel"], writes=[b_sel])
        woutA = sb("c_woutA", [128, 8, 1024], BF16)
        woutB = sb("c_woutB", [128, 8, 1024], BF16)
        b_wout = Buf()
        b_woutB = Buf()
        S.dma("pool", woutA[:], D["w_out"].rearrange("(c p) n -> p c n", p=128), writes=[b_wout])
        S.op("dve", lambda e: e.tensor_scalar(woutB[:], woutA[:], sel[:, 1:2], None, ALU.mult),
             reads=[b_wout, b_sel], writes=[b_woutB])
        S.op("dve", lambda e: e.tensor_scalar(woutA[:], woutA[:], sel[:, 0:1], None, ALU.mult),
             reads=[b_wout, b_sel, b_woutB], writes=[b_wout])
        wd = sb("c_wd", [128, HC, 1024], BF16)
        b_wd = [Buf() for _ in range(HC)]
        for h in range(HC):
            S.dma("pool", wd[:, h, :], D["wd"][h * 128:(h + 1) * 128, :], writes=[b_wd[h]])
        lnp = sb("c_lnp", [128, 4, 1024], F32)
        b_lnp = [Buf() for _ in range(4)]
        for i, nm in enumerate(("ln1g", "ln1b", "ln2g", "ln2b")):
            S.dma("sp", lnp[:, i, :], bcast_rows(D[nm]), writes=[b_lnp[i]])

        NW = 4
        wgu = [sb("c_wgu%d" % i, [128, 2, 8, 128], BF16) for i in range(NW)]
        b_wgu = [Buf() for _ in range(NW)]

        h1g = sb("c_h1g", [128, TPG, 1024], F32)
        b_h1g = [Buf() for _ in range(TPG)]
        h1T = sb("c_h1T", [128, 8, GT], BF16)
        b_h1T = [Buf() for _ in range(TPG)]
        actT = sb("c_actT", [128, HC, GT], BF16)
        b_actT = [Buf() for _ in range(HC)]
        NB = 2
        ocA = [sb("c_ocA%d" % i, [128, 8, OCW], BF16) for i in range(NB)]
        ocB = [sb("c_ocB%d" % i, [128, 8, OCW], BF16) for i in range(NB)]
        b_ocA = [Buf() for _ in range(NB)]
        b_ocB = [Buf() for _ in range(NB)]
        xt = [sb("c_xt%d" % i, [128, 1024], F32) for i in range(NB)]
        b_xt = [Buf() for _ in range(NB)]
        hpre = sb("c_hpre", [128, 1024], F32)
        b_hpre = Buf()
        hn = sb("c_hn", [128, 1024], F32)
        b_hn = Buf()
        h1b = sb("c_h1b", [128, 1024], BF16)
        b_h1b = Buf()
        stats = sb("c_stats", [128, 2, 6], F32)
        b_stats = Buf()
        mv = sb("c_mv", [128, 4], F32)
        b_mv = Buf()
        sg = [sb("c_sg%d" % i, [128, 512], BF16) for i in range(2)]
        b_sg = [Buf() for _ in range(2)]
        outt = [sb("c_outt%d" % i, [128, 1024], F32) for i in range(NB)]
        b_outt = [Buf() for _ in range(NB)]

        epsc = sb("c_eps", [128, 1], F32)
        b_epsc = Buf()
        S.op("dve", lambda e: e.memset(epsc[:], LN_EPS), writes=[b_epsc])
        pmix = ps("c_pmix", [128, 1024], F32)
        b_pmix = Buf(psum=True)
        pT = ps("c_pT", [128, 1024], BF16)
        b_pT = Buf(psum=True)
        pG = [ps("c_pG%d" % i, [128, 512], F32) for i in range(2)]
        pU = [ps("c_pU%d" % i, [128, 512], F32) for i in range(2)]
        b_pG = [Buf(psum=True) for _ in range(2)]
        b_pU = [Buf(psum=True) for _ in range(2)]

        def layer_norm(src_b, gi_, bi_, dst, b_dst):
            for hf in range(2):
                S.op("dve", lambda e, hf=hf: e.bn_stats(stats[:, hf, :], hpre[:, hf * 512:(hf + 1) * 512]),
                     reads=[src_b], writes=[b_stats])
            S.op("dve", lambda e: e.bn_aggr(mv[:, 0:2], stats[:].rearrange("p a b -> p (a b)")),
                 reads=[b_stats], writes=[b_mv])
            S.op("act", lambda e: e.activation(mv[:, 3:4], mv[:, 1:2], AF.Sqrt, bias=epsc[:, 0:1], scale=1.0),
                 reads=[b_mv, b_epsc], writes=[b_mv])
            S.op("dve", lambda e: e.reciprocal(mv[:, 2:3], mv[:, 3:4]), reads=[b_mv], writes=[b_mv])
            S.op("dve", lambda e: e.tensor_scalar(hn[:], hpre[:], mv[:, 0:1], mv[:, 2:3],
                                                  ALU.subtract, ALU.mult),
                 reads=[src_b, b_mv], writes=[b_hn])
            S.op("pool", lambda e: e.tensor_tensor(hn[:], hn[:], lnp[:, gi_, :], ALU.mult),
                 reads=[b_hn, b_lnp[gi_]], writes=[b_hn])
            S.op("dve", lambda e: e.tensor_tensor(dst, hn[:], lnp[:, bi_, :], ALU.add),
                 reads=[b_hn, b_lnp[bi_]], writes=[b_dst])

        nwl = 0
        for gi in range(NG):
            for ti in range(TPG):
                it = gi * TPG + ti
                sl = it % NB
                tok0 = it * 128
                oi = (tok0 // OCW) % NB
                oo = tok0 % OCW
                if oo == 0:
                    ta, tb = tok0, TOKH + tok0
                    S.dma("sp", ocA[oi][:], Gv[ta // XCH][:, :, ta % XCH:ta % XCH + OCW], writes=[b_ocA[oi]])
                    S.dma("sp", ocB[oi][:], Gv[tb // XCH][:, :, tb % XCH:tb % XCH + OCW], writes=[b_ocB[oi]])
                S.dma("sp", xt[sl][:], D["xres"][tok0:tok0 + 128, :], writes=[b_xt[sl]])
                for hf in range(2):
                    cs = slice(hf * 512, (hf + 1) * 512)
                    for c in range(8):
                        S.mm(pmix[:, cs], ocA[oi][:, c, oo:oo + 128], woutA[:, c, cs],
                             reads=[b_ocA[oi], b_wout], writes=[b_pmix], start=(c == 0), stop=False, signal=False)
                    for c in range(8):
                        S.mm(pmix[:, cs], ocB[oi][:, c, oo:oo + 128], woutB[:, c, cs],
                             reads=[b_ocB[oi], b_woutB], writes=[b_pmix], start=False, stop=(c == 7),
                             signal=(c == 7))
                for hf in range(2):
                    S.op("dve", lambda e, hf=hf, sl=sl: e.scalar_tensor_tensor(
                        hpre[:, hf * 512:(hf + 1) * 512], xt[sl][:, hf * 512:(hf + 1) * 512], ALPHA,
                        pmix[:, hf * 512:(hf + 1) * 512], ALU.mult, ALU.add),
                        reads=[b_xt[sl], b_pmix], writes=[b_hpre])
                layer_norm(b_hpre, 0, 1, h1g[:, ti, :], b_h1g[ti])
                S.op("act", lambda e, ti=ti: e.copy(h1b[:], h1g[:, ti, :]), reads=[b_h1g[ti]], writes=[b_h1b])
                for c in range(8):
                    S.op("pe", lambda e, c=c: e.transpose(pT[:, c * 128:(c + 1) * 128],
                                                          h1b[:, c * 128:(c + 1) * 128], ident[:]),
                         reads=[b_h1b, b_ident], writes=[b_pT], signal=(c == 7))
                S.op("act", lambda e, ti=ti: e.copy(h1T[:, :, ti * 128:(ti + 1) * 128],
                                                    pT[:].rearrange("p (c t) -> p c t", c=8)),
                     reads=[b_pT], writes=[b_h1T[ti]])
            for h in range(HC):
                ws = nwl % NW
                nwl += 1
                S.dma("sp", wgu[ws][:].rearrange("p a c n -> p (a c n)"),
                      D["wgs"][h].rearrange("p a c n -> p (a c n)"), reads=[D["b_wgs"][h]], writes=[b_wgu[ws]])
                pb = h % 2
                for c in range(8):
                    S.mm(pG[pb][:], wgu[ws][:, 0, c, :], h1T[:, c, :], reads=[b_wgu[ws]] + b_h1T,
                         writes=[b_pG[pb]], start=(c == 0), stop=(c == 7), signal=(c == 7))
                for c in range(8):
                    S.mm(pU[pb][:], wgu[ws][:, 1, c, :], h1T[:, c, :], reads=[b_wgu[ws]] + b_h1T,
                         writes=[b_pU[pb]], start=(c == 0), stop=(c == 7), signal=(c == 7))
                S.op("act", lambda e, pb=pb: e.activation(sg[pb][:], pG[pb][:], AF.Silu),
                     reads=[b_pG[pb]], writes=[b_sg[pb]])
                S.op("dve", lambda e, pb=pb, h=h: e.tensor_tensor(actT[:, h, :], pU[pb][:], sg[pb][:], ALU.mult),
                     reads=[b_pU[pb], b_sg[pb]], writes=[b_actT[h]])
            for ti in range(TPG):
                it = gi * TPG + ti
                sl = it % NB
                tok0 = it * 128
                for hf in range(2):
                    for h in range(HC):
                        S.mm(pmix[:, hf * 512:(hf + 1) * 512], actT[:, h, ti * 128:(ti + 1) * 128],
                             wd[:, h, hf * 512:(hf + 1) * 512], reads=[b_actT[h], b_wd[h]], writes=[b_pmix],
                             start=(h == 0), stop=(h == HC - 1), signal=(h == HC - 1))
                for hf in range(2):
                    S.op("dve", lambda e, hf=hf, ti=ti: e.scalar_tensor_tensor(
                        hpre[:, hf * 512:(hf + 1) * 512], h1g[:, ti, hf * 512:(hf + 1) * 512], ALPHA,
                        pmix[:, hf * 512:(hf + 1) * 512], ALU.mult, ALU.add),
                        reads=[b_h1g[ti], b_pmix], writes=[b_hpre])
                layer_norm(b_hpre, 2, 3, outt[sl][:], b_outt[sl])
                S.dma("sp", D["out"][tok0:tok0 + 128, :], outt[sl][:], reads=[b_outt[sl]])
        S.barrier()
        S.run()


def phase_a(nc, S, D):
    ST = 2048
    NST = SEQ // ST
    inv_freq = [float(np.float32(ROPE_THETA) ** np.float32(-i / 8.0)) for i in range(8)]
    TWO_PI = 2.0 * math.pi
    C1 = 6.28125
    C2 = TWO_PI - C1
    with ExitStack() as ph:
        def sb(n, shp, dt):
            return ph.enter_context(nc.sbuf_tensor(n, shp, dt))

        def ps(n, shp, dt):
            return ph.enter_context(nc.psum_tensor(n, shp, dt))

        identf = sb("a_identf", [128, 128], F32)
        b_identf = Buf()
        S.dma("sp", identf[:], D["ident"], writes=[b_identf])
        identb = sb("a_identb", [128, 128], BF16)
        b_identb = Buf()
        S.dma("pool", identb[:], D["ident"], writes=[b_identb])
        maskb = sb("a_maskb", [128, 256], BF16)
        b_maskb = Buf()
        S.dma("pool", maskb[:], D["maskT"], writes=[b_maskb])
        ones = sb("a_ones", [128, 128], F32)
        b_ones = Buf()
        S.op("dve", lambda e: e.memset(ones[:], 1.0), writes=[b_ones])
        watt = sb("a_watt", [128, 8, 768], BF16)
        b_watt = Buf()
        S.dma("pool", watt[:], D["w_att"].rearrange("(c p) n -> p c n", p=128), writes=[b_watt])

        posi = sb("a_posi", [128, 64], I32)
        b_posi = Buf()
        S.dma("sp", posi[:], D["pos"], writes=[b_posi])
        posf = sb("a_posf", [128, 64], F32)
        b_posf = Buf()
        S.op("dve", lambda e: e.tensor_copy(posf[:], posi[:]), reads=[b_posi], writes=[b_posf])
        ang = sb("a_ang", [128, 64, 8], F32)
        b_ang = Buf()
        for i in range(8):
            S.op("dve", lambda e, i=i: e.tensor_scalar(ang[:, :, i], posf[:], inv_freq[i], None, ALU.mult),
                 reads=[b_posf], writes=[b_ang])
        sinT = sb("a_sinT", [128, 64, 8], F32)
        cosT = sb("a_cosT", [128, 64, 8], F32)
        b_sinT = Buf()
        b_cosT = Buf()
        kq = sb("a_kq", [128, 512], I32)
        kf = sb("a_kf", [128, 512], F32)
        red = sb("a_red", [128, 512], F32)
        msk = sb("a_msk", [128, 512], F32)
        b_tmp = Buf()
        angf = ang[:].rearrange("p a b -> p (a b)")

        def wrap(dst):
            S.op("dve", lambda e: e.tensor_scalar(msk[:], dst, math.pi, -TWO_PI, ALU.is_gt, ALU.mult),
                 reads=[b_tmp], writes=[b_tmp])
            S.op("dve", lambda e: e.tensor_tensor(dst, dst, msk[:], ALU.add), reads=[b_tmp], writes=[b_tmp])

        S.op("dve", lambda e: e.tensor_scalar(kq[:], angf, 1.0 / TWO_PI, None, ALU.mult),
             reads=[b_ang], writes=[b_tmp])
        S.op("dve", lambda e: e.tensor_copy(kf[:], kq[:]), reads=[b_tmp], writes=[b_tmp])
        S.op("dve", lambda e: e.scalar_tensor_tensor(red[:], kf[:], -C1, angf, ALU.mult, ALU.add),
             reads=[b_tmp, b_ang], writes=[b_tmp])
        S.op("dve", lambda e: e.scalar_tensor_tensor(red[:], kf[:], -C2, red[:], ALU.mult, ALU.add),
             reads=[b_tmp], writes=[b_tmp])
        wrap(red[:])
        S.op("act", lambda e: e.activation(sinT[:].rearrange("p a b -> p (a b)"), red[:], AF.Sin),
             reads=[b_tmp], writes=[b_sinT])
        S.op("dve", lambda e: e.tensor_scalar(red[:], red[:], math.pi / 2.0, None, ALU.add),
             reads=[b_tmp, b_sinT], writes=[b_tmp])
        wrap(red[:])
        S.op("act", lambda e: e.activation(cosT[:].rearrange("p a b -> p (a b)"), red[:], AF.Sin),
             reads=[b_tmp], writes=[b_cosT])

        xTs = [sb("a_xT%d" % i, [128, 8, ST], BF16) for i in range(2)]
        b_xTs = [Buf() for _ in range(2)]
        xTv = D["xT"].rearrange("(c p) t -> p c t", p=128)
        qT = sb("a_qT", [128, 2, ST], BF16)
        b_qT = Buf()
        kT = [sb("a_kT%d" % i, [128, 2, ST], BF16) for i in range(2)]
        b_kT = [Buf() for _ in range(2)]
        V = [[sb("a_V%d_%d" % (l, i), [128, 16, 4, 65], BF16) for i in range(2)] for l in range(3)]
        b_V = [[Buf() for _ in range(2)] for l in range(3)]
        for l in range(3):
            for i in range(2):
                S.op("pool", lambda e, l=l, i=i: e.memset(V[l][i][:], 1.0), writes=[b_V[l][i]])
        ysbs = [sb("a_ysb%d" % i, [128, 512], F32) for i in range(2)]
        b_ysbs = [Buf() for _ in range(2)]
        rts = [sb("a_rt%d" % i, [128, 4, 8, 8], F32) for i in range(2)]
        b_rts = [Buf() for _ in range(2)]
        sqs = [sb("a_sq%d" % i, [128, 512], F32) for i in range(2)]
        b_sqs = [Buf() for _ in range(2)]
        ssqs = [sb("a_ssq%d" % i, [128, 8], F32) for i in range(2)]
        b_ssqs = [Buf() for _ in range(2)]
        rm = sb("a_rm", [128, 8], F32)
        b_rm = Buf()
        st4 = sb("a_st4", [4, 8], F32)
        b_st4 = Buf()
        dg4 = sb("a_dg4", [4, 4], F32)
        b_dg4 = Buf()
        negM = sb("a_negM", [128, 4], F32)
        b_negM = Buf()
        NPT = 3
        PT = [sb("a_PT%d" % i, [128, 256], BF16) for i in range(NPT)]
        b_PT = [Buf() for _ in range(NPT)]
        oacc = sb("a_oacc", [128, ST], F32)
        b_oacc = Buf()
        oT = [sb("a_oT%d" % i, [64, ST], BF16) for i in range(2)]
        b_oT = [Buf() for _ in range(2)]

        pqk = ps("a_pqk", [128, 512], F32)
        b_pqk = Buf(psum=True)
        pX = ps("a_pX", [128, 512], F32)
        b_pX = Buf(psum=True)
        pS = [ps("a_pS%d" % i, [128, 512], F32) for i in range(2)]
        b_pS = [Buf(psum=True) for _ in range(2)]
        pO = ps("a_pO", [128, ST], F32)
        b_pO = Buf(psum=True)

        S.op("dve", lambda e: e.memset(st4[:], 0.0), writes=[b_st4])

        nblk = 0
        for st in range(NST):
            xs = st % 2
            ks = st % 2
            kp = 1 - ks
            xT = xTs[xs]
            S.dma("pool", xT[:], xTv[:, :, st * ST:(st + 1) * ST], writes=[b_xTs[xs]])
            S.op("dve", lambda e: e.memset(rm[:], 0.0), reads=[], writes=[b_rm])
            def qk_tile(j, par):
                n = st * 16 + j
                pq_, bq_ = (pqk, b_pqk) if par == 0 else (pS[0], b_pS[0])
                px_, bx_ = (pX, b_pX) if par == 0 else (pS[1], b_pS[1])
                ys, b_ys, rt_, b_rt_, sq_, b_sq_, ssq_, b_ssq_ = ysbs[par], b_ysbs[par], rts[par], b_rts[par], \
                    sqs[par], b_sqs[par], ssqs[par], b_ssqs[par]
                ys3 = ys[:].rearrange("p (h d) -> p h d", d=64)
                for c in range(8):
                    S.mm(pq_[:], xT[:, c, j * 128:(j + 1) * 128], watt[:, c, 0:512],
                         reads=[b_xTs[xs], b_watt], writes=[bq_], start=(c == 0), stop=(c == 7), signal=(c == 7))
                yield
                S.op("act", _mul(ys[:, 0:256], pq_[:, 0:256], 0.125), reads=[bq_], writes=[b_ys])
                S.op("act", _CP(ys[:, 256:512], pq_[:, 256:512]), reads=[bq_], writes=[b_ys])
                yield
                cb = cosT[:, n:n + 1, :].to_broadcast([128, 8, 8])
                sbb = sinT[:, n:n + 1, :].to_broadcast([128, 8, 8])
                x1 = ys3[:, :, 0:8]
                x2 = ys3[:, :, 8:16]
                for a, (xx, tt) in enumerate(((x1, cb), (x2, sbb), (x2, cb), (x1, sbb))):
                    S.op("pool", _TT(rt_[:, a, :, :], xx, tt, ALU.mult), reads=[b_ys, b_cosT, b_sinT], writes=[b_rt_])
                yield
                S.op("pool", _TT(x1, rt_[:, 0, :, :], rt_[:, 1, :, :], ALU.subtract), reads=[b_rt_, b_ys], writes=[b_ys])
                S.op("pool", _TT(x2, rt_[:, 2, :, :], rt_[:, 3, :, :], ALU.add), reads=[b_rt_, b_ys], writes=[b_ys])
                yield
                S.op("dve", _TT(sq_[:], ys[:], ys[:], ALU.mult), reads=[b_ys], writes=[b_sq_])
                for blk in range(4):
                    S.op("pe", _TR(px_[:, blk * 128:(blk + 1) * 128], ys[:, blk * 128:(blk + 1) * 128], identf[:]),
                         reads=[b_ys, b_identf], writes=[bx_], signal=(blk == 3))
                yield
                S.op("dve", _RED(ssq_[:], sq_[:].rearrange("p (h d) -> p h d", d=64)), reads=[b_sq_], writes=[b_ssq_])
                S.op("act", _CP(qT[:, :, j * 128:(j + 1) * 128], px_[:, 0:256].rearrange("p (a t) -> p a t", a=2)),
                     reads=[bx_], writes=[b_qT])
                S.op("act", _CP(kT[ks][:, :, j * 128:(j + 1) * 128], px_[:, 256:512].rearrange("p (a t) -> p a t", a=2)),
                     reads=[bx_], writes=[b_kT[ks]])
                yield
                S.op("dve", _TT(rm[:], rm[:], ssq_[:], ALU.max), reads=[b_ssq_, b_rm], writes=[b_rm])

            for j0 in range(0, 16, 2):
                gens = [qk_tile(j0, 0), qk_tile(j0 + 1, 1)]
                while gens:
                    for g_ in list(gens):
                        try:
                            next(g_)
                        except StopIteration:
                            gens.remove(g_)
            S.op("pe", lambda e: e.transpose(pX[0:4, 0:128], rm[:, 0:4], identf[:]), reads=[b_rm, b_identf],
                 writes=[b_pX], signal=False)
            S.op("pe", lambda e: e.transpose(pX[0:4, 128:256], rm[:, 4:8], identf[:]), reads=[b_rm, b_identf],
                 writes=[b_pX])
            S.op("dve", lambda e: e.tensor_copy(st4[:, 2:3], st4[:, 1:2]), reads=[b_st4], writes=[b_st4])
            S.op("dve", lambda e: e.tensor_reduce(st4[:, 0:2], pX[0:4, 0:256].rearrange("p (a t) -> p a t", a=2),
                                                  AX.X, ALU.max), reads=[b_pX, b_st4], writes=[b_st4])
            S.op("dve", lambda e: e.tensor_tensor(st4[:, 3:4], st4[:, 1:2], st4[:, 2:3], ALU.max),
                 reads=[b_st4], writes=[b_st4])
            S.op("dve", lambda e: e.tensor_tensor(st4[:, 3:4], st4[:, 3:4], st4[:, 0:1], ALU.mult),
                 reads=[b_st4], writes=[b_st4])
            S.op("dve", lambda e: e.tensor_scalar(dg4[:], identf[0:4, 0:4], st4[:, 3:4], None, ALU.mult),
                 reads=[b_st4, b_identf], writes=[b_dg4])
            S.mm(pX[:, 256:260], ones[0:4, :], dg4[:], reads=[b_ones, b_dg4], writes=[b_pX])
            S.op("act", lambda e: e.activation(negM[:], pX[:, 256:260], AF.Sqrt), reads=[b_pX], writes=[b_negM])
            S.op("dve", lambda e: e.tensor_scalar(negM[:], negM[:], -1.0, None, ALU.mult), reads=[b_negM],
                 writes=[b_negM])
            for l, dil in enumerate((1, 4, 16)):
                for t16 in range(16):
                    if dil == 1:
                        a0, a1 = t16 * 128, (t16 + 1) * 128
                    elif dil == 4:
                        n4, r = t16 // 4, t16 % 4
                        a0, a1 = n4 * 512 + r, (n4 + 1) * 512
                    else:
                        a0, a1 = t16, ST
                    pv_, bv_ = (pX, b_pX) if t16 % 2 == 0 else (pqk, b_pqk)
                    for c in range(8):
                        S.mm(pv_[:, 0:256], xT[:, c, a0:a1:dil], watt[:, c, 512:768],
                             reads=[b_xTs[xs], b_watt], writes=[bv_], start=(c == 0), stop=(c == 7), signal=(c == 7))
                    S.op("act", _CP(V[l][ks][:, t16, :, 0:64], pv_[:, 0:256].rearrange("p (h d) -> p h d", d=64)),
                         reads=[bv_], writes=[b_V[l][ks]])
            for h in range(4):
                hp, p0 = h // 2, 64 * (h % 2)
                first = [True] * 4
                pend = []

                def block(qsl, cur, prev, outs):
                    nonlocal nblk
                    pb = nblk % 2
                    pt = nblk % NPT
                    nblk += 1
                    lo = 0 if prev is not None else 128
                    qap = qT[p0:p0 + 64, hp, qsl]
                    if prev is not None:
                        pslot, psl, pv_ap = prev
                        S.mm(pS[pb][:, 0:128], kT[pslot][p0:p0 + 64, hp, psl], qap,
                             reads=[b_kT[pslot], b_qT], writes=[b_pS[pb]], start=True, stop=False, signal=False)
                        S.mm(pS[pb][:, 0:128], identb[:], maskb[:, 0:128], reads=[b_identb, b_maskb],
                             writes=[b_pS[pb]], start=False, stop=True, signal=False)
                    S.mm(pS[pb][:, 128:256], kT[ks][p0:p0 + 64, hp, qsl], qap,
                         reads=[b_kT[ks], b_qT], writes=[b_pS[pb]], start=True, stop=False, signal=False)
                    S.mm(pS[pb][:, 128:256], identb[:], maskb[:, 128:256], reads=[b_identb, b_maskb],
                         writes=[b_pS[pb]], start=False, stop=True, signal=True)
                    S.op("act", lambda e, hh=h: e.activation(PT[pt][:, lo:256], pS[pb][:, lo:256], AF.Exp,
                                                             bias=negM[:, hh:hh + 1], scale=1.0),
                         reads=[b_pS[pb], b_negM], writes=[b_PT[pt]])
                    kts = ([(0, pv_ap, b_V_prev)] if prev is not None else []) + [(1, cur, b_V_cur)]

                    def pv_part():
                        nmm = len(kts) * len(outs)
                        i = 0
                        for (kt, vap, vb) in kts:
                            for (ocol, pcol) in outs:
                                bank = ocol.start // 512
                                i += 1
                                S.mm(pO[0:65, ocol], vap, PT[pt][:, kt * 128 + pcol.start:kt * 128 + pcol.stop],
                                     reads=[b_PT[pt], vb], writes=[b_pO], start=first[bank], stop=False,
                                     signal=(i == nmm), skip_group_check=True)
                                first[bank] = False
                    if pend:
                        pend.pop()()
                    pend.append(pv_part)

                full = [(None, slice(0, 128))]
                for jb in range(16):
                    qsl = slice(jb * 128, (jb + 1) * 128)
                    b_V_cur = b_V[0][ks]
                    cur = V[0][ks][:, jb, h, :]
                    prev = None
                    if jb > 0:
                        prev = (ks, slice((jb - 1) * 128, jb * 128), V[0][ks][:, jb - 1, h, :])
                        b_V_prev = b_V[0][ks]
                    elif st > 0:
                        prev = (kp, slice(15 * 128, 16 * 128), V[0][kp][:, 15, h, :])
                        b_V_prev = b_V[0][kp]
                    block(qsl, cur, prev, [(slice(jb * 128, (jb + 1) * 128), slice(0, 128))])
                for n4 in range(4):
                    for r in range(4):
                        qsl = slice(n4 * 512 + r, (n4 + 1) * 512, 4)
                        b_V_cur = b_V[1][ks]
                        cur = V[1][ks][:, n4 * 4 + r, h, :]
                        prev = None
                        if n4 > 0:
                            prev = (ks, slice((n4 - 1) * 512 + r, n4 * 512, 4), V[1][ks][:, (n4 - 1) * 4 + r, h, :])
                            b_V_prev = b_V[1][ks]
                        elif st > 0:
                            prev = (kp, slice(3 * 512 + r, ST, 4), V[1][kp][:, 12 + r, h, :])
                            b_V_prev = b_V[1][kp]
                        block(qsl, cur, prev, [(qsl, slice(0, 128))])
                for r in range(16):
                    qsl = slice(r, ST, 16)
                    b_V_cur = b_V[2][ks]
                    cur = V[2][ks][:, r, h, :]
                    prev = None
                    if st > 0:
                        prev = (kp, qsl, V[2][kp][:, r, h, :])
                        b_V_prev = b_V[2][kp]
                    block(qsl, cur, prev, [(slice(b4 * 512 + r, (b4 + 1) * 512, 16), slice(32 * b4, 32 * b4 + 32))
                                           for b4 in range(4)])
                if pend:
                    pend.pop()()
                for b4 in range(4):
                    cs = slice(b4 * 512, (b4 + 1) * 512)
                    eng = "act" if b4 % 2 == 0 else "dve"
                    if eng == "act":
                        S.op("act", lambda e, cs=cs: e.copy(oacc[0:65, cs], pO[0:65, cs]), reads=[b_pO], writes=[b_oacc])
                    else:
                        S.op("dve", lambda e, cs=cs: e.tensor_copy(oacc[0:65, cs], pO[0:65, cs]), reads=[b_pO],
                             writes=[b_oacc])
                S.op("dve", lambda e: e.reciprocal(oacc[64:65, :], oacc[64:65, :]), reads=[b_oacc], writes=[b_oacc])
                for b4 in range(4):
                    cs = slice(b4 * 512, (b4 + 1) * 512)
                    S.mm(pO[0:64, cs], ones[64:65, 0:64], oacc[64:65, cs], reads=[b_ones, b_oacc], writes=[b_pO],
                         signal=(b4 == 3))
                os_ = (st * 4 + h) % 2
                for b4 in range(4):
                    cs = slice(b4 * 512, (b4 + 1) * 512)
                    S.op("dve", lambda e, cs=cs, os_=os_: e.tensor_tensor(oT[os_][:, cs], pO[0:64, cs], oacc[0:64, cs],
                                                                       ALU.mult),
                         reads=[b_pO, b_oacc], writes=[b_oT[os_]])
                for xk in range(ST // XCH):
                    S.dma("sp", D["omix"][st * (ST // XCH) + xk][h * 64:(h + 1) * 64, :],
                          oT[os_][:, xk * XCH:(xk + 1) * XCH], reads=[b_oT[os_]])
        S.barrier()
        S.run()


def _TT(o, a, b, op):
    return lambda e: e.tensor_tensor(o, a, b, op)


def _TS(o, a, s1, s2, op0, op1=None):
    if op1 is None:
        return lambda e: e.tensor_scalar(o, a, s1, s2, op0)
    return lambda e: e.tensor_scalar(o, a, s1, s2, op0, op1)


def _STT(o, a, s, b, op0, op1):
    return lambda e: e.scalar_tensor_tensor(o, a, s, b, op0, op1)


def _ACT(o, i, f, **kw):
    return lambda e: e.activation(o, i, f, **kw)


def _CP(o, i):
    return lambda e: e.copy(o, i)


def _TC(o, i):
    return lambda e: e.tensor_copy(o, i)


def _TR(o, i, ident):
    return lambda e: e.transpose(o, i, ident)


def _mul(o, i, m):
    return lambda e: e.mul(o, i, m)


def _RED(o, i):
    return lambda e: e.tensor_reduce(o, i, AX.X, ALU.add)


def phase_b(nc, S, D):
    STB = 1024
    NSTB = SEQ // STB
    TPS = STB // 128
    DS = DECAY_SCALE
    with ExitStack() as ph:
        def sb(n, shp, dt=F32):
            return ph.enter_context(nc.sbuf_tensor(n, shp, dt))

        def ps(n, shp, dt=F32):
            return ph.enter_context(nc.psum_tensor(n, shp, dt))

        identf = sb("b_identf", [128, 128])
        b_identf = Buf()
        S.dma("sp", identf[:], D["ident"], writes=[b_identf])
        identb = sb("b_identb", [128, 128], BF16)
        b_identb = Buf()
        S.dma("pool", identb[:], D["ident"], writes=[b_identb])
        cst = sb("b_cst", [128, 642])
        b_cst = Buf()
        S.dma("sp", cst[:], D["cstB"], writes=[b_cst])
        triI, triS, triA = cst[:, 0:128], cst[:, 128:256], cst[:, 256:384]
        chunkind = cst[:, 384:386]
        mask1, mask3, eye2 = cst[:, 386:514], cst[:, 514:578], cst[:, 578:642]
        vec = sb("b_vec", [128, 8, 256])
        b_vec = Buf()
        for i, nm in enumerate(("w0", "a0", "k_k", "k_a", "k_a", "r_k", "gn_g", "gn_b")):
            S.dma("sp", vec[:, i, :], bcast_rows(D[nm]), writes=[b_vec], acc=(i > 0))
        S.op("dve", _TS(vec[:, 4, :], vec[:, 4, :], -1.0, 1.0, ALU.mult, ALU.add), reads=[b_vec], writes=[b_vec])
        bias_wa = vec[:, 0:2, :].rearrange("p a b -> p (a b)")
        wdec = sb("b_wdec", [128, 512])
        b_wdec = Buf()
        S.op("dve", lambda e: e.memset(wdec[:], 0.0), writes=[b_wdec])
        S.dma("sp", wdec[0:64, 0:256], D["w_dec"], writes=[b_wdec], acc=True)
        S.dma("sp", wdec[64:128, 256:512], D["w_aaa"], writes=[b_wdec], acc=True)
        wgate = sb("b_wgate", [128, 256])
        b_wgate = Buf()
        S.dma("sp", wgate[:], D["w_gate"], writes=[b_wgate])
        epsg = sb("b_epsg", [128, 1])
        b_epsg = Buf()
        S.op("dve", lambda e: e.memset(epsg[:], GN_EPS), writes=[b_epsg])
        mub = sb("b_mub", [128, 1024])
        b_mub = Buf()
        S.dma("sp", mub[:], bcast_rows(D["mu"]), writes=[b_mub])
        omub = sb("b_omub", [128, 1024])
        b_omub = Buf()
        S.op("dve", _TS(omub[:], mub[:], -1.0, 1.0, ALU.mult, ALU.add), reads=[b_mub], writes=[b_omub])
        W1 = sb("b_W1", [128, 8, 1024], BF16)
        W2 = sb("b_W2", [128, 8, 1024], BF16)
        b_W = Buf()
        wst = [sb("b_wst%d" % i, [128, 1024]) for i in range(2)]
        b_wst = [Buf() for _ in range(2)]
        for c in range(8):
            S.dma("sp", wst[c % 2][:], D["w_rw"][c * 128:(c + 1) * 128, :], writes=[b_wst[c % 2]])
            S.op("dve", _TT(W1[:, c, :], wst[c % 2][:], omub[:], ALU.mult), reads=[b_wst[c % 2], b_omub], writes=[b_W])
            S.op("pool", _TT(W2[:, c, :], wst[c % 2][:], mub[:], ALU.mult), reads=[b_wst[c % 2], b_mub], writes=[b_W])

        xTv = D["xT"].rearrange("(c p) t -> p c t", p=128)
        xc = sb("b_xc", [128, 8, STB], BF16)
        xp = sb("b_xp", [128, 8, STB], BF16)
        b_xc = Buf()
        b_xp = Buf()

        Hs = sb("b_Hs", [128, 4, 64], BF16)
        b_H = Buf()
        S.op("dve", lambda e: e.memset(Hs[:], 0.0), writes=[b_H])

        def t256(n):
            return sb(n, [128, 256]), Buf()
        tl, b_tl = sb("b_tl", [128, 256]), Buf()
        lT, b_lT = sb("b_lT", [128, 256]), Buf()
        lg, b_lg = sb("b_lg", [128, 512]), Buf()
        sg, b_sg = sb("b_sg", [128, 512]), Buf()
        gs, b_gs = t256("b_gs")
        yA = sb("b_yA", [128, 512])
        b_rs = Buf()
        rs = yA[:, 0:256]
        krs = yA[:, 256:512]
        vs, b_vs = sb("b_vs", [128, 256], BF16), Buf()
        kk, b_kk = t256("b_kk")
        t1, b_t1 = t256("b_t1")
        t2, b_t2 = t256("b_t2")
        km, b_km = t256("b_km")
        bb, b_bb = t256("b_bb")
        E, b_E = sb("b_E", [128, 4, 256]), [Buf() for _ in range(4)]
        X4, b_X4 = sb("b_X4", [128, 4, 256], BF16), Buf()
        BK, b_BK = sb("b_BK", [128, 2, 256], BF16), Buf()
        XT, b_XT = sb("b_XT", [128, 4, 4, 64], BF16), Buf()
        dgP, b_dgP = sb("b_dgP", [128, 4, 64], BF16), Buf()
        AM1, b_AM1 = sb("b_AM1", [128, 4, 128], BF16), Buf()
        AM2, b_AM2 = sb("b_AM2", [128, 4, 128], BF16), Buf()
        Lm = [sb("b_L%d" % i, [128, 4, 2, 64], BF16) for i in range(2)]
        b_Lm = [Buf() for _ in range(2)]
        Pm, b_Pm = sb("b_Pm", [128, 4, 64], BF16), Buf()
        sm, b_sm = sb("b_sm", [128, 32]), Buf()
        PC, b_PC = sb("b_PC", [128, 4, 2]), Buf()
        Ws, b_Ws = sb("b_Ws", [128, 256], BF16), Buf()
        Us, b_Us = sb("b_Us", [128, 256], BF16), Buf()
        on, b_on = t256("b_on")
        gst, b_gst = sb("b_gst", [128, 4, 6]), Buf()
        gmv, b_gmv = sb("b_gmv", [128, 12]), Buf()
        orT = sb("b_orT", [128, 2, STB], BF16)
        b_orT = Buf()

        K = [ps("b_K%d" % i, [128, 512]) for i in range(8)]
        b_K = [Buf(psum=True) for _ in range(8)]

        def v3(ap):
            return ap.rearrange("p (h d) -> p h d", d=64)

        def bc(ap4):
            return ap4.unsqueeze(2).to_broadcast([128, 4, 64])

        def dbl(name, shp, dt=F32):
            return [sb("%s_%d" % (name, i), shp, dt) for i in range(2)], [Buf() for _ in range(2)]
        AM1d, b_AM1d = dbl("b_AM1d", [128, 4, 128], BF16)
        AM2d, b_AM2d = dbl("b_AM2d", [128, 4, 128], BF16)
        Pmd, b_Pmd = dbl("b_Pmd", [128, 4, 64], BF16)
        XTd, b_XTd = dbl("b_XTd", [128, 4, 4, 64], BF16)
        def trp(name, shp, dt=F32):
            return [sb("%s_%d" % (name, i), shp, dt) for i in range(3)], [Buf() for _ in range(3)]
        BKd, b_BKd = trp("b_BKd", [128, 2, 256], BF16)
        vsd, b_vsd = trp("b_vsd", [128, 256], BF16)
        dgPd, b_dgPd = dbl("b_dgPd", [128, 4, 64], BF16)
        gsd, b_gsd = trp("b_gsd", [128, 256])
        bsd, b_bsd = trp("b_bsd", [128, 4])
        X4d, b_X4d = dbl("b_X4d", [128, 4, 256], BF16)
        PCd, b_PCd = dbl("b_PCd", [128, 4, 2])
        smG, b_smG = sb("b_smG", [128, 16]), Buf()
        tG, b_tG = sb("b_tG", [128, 256]), Buf()

        def front1(n):
            st, j = n // TPS, n % TPS
            d, t3 = n % 2, n % 3
            BK, b_BK, vs, b_vs, gs, b_gs, bs, b_bs = BKd[t3], b_BKd[t3], vsd[t3], b_vsd[t3], gsd[t3], b_gsd[t3], bsd[t3], b_bsd[t3]
            X4, b_X4, PC, b_PC = X4d[d], b_X4d[d], PCd[d], b_PCd[d]
            if j == 0:
                S.dma("pool", xc[:], xTv[:, :, st * STB:(st + 1) * STB], writes=[b_xc])
                if st == 0:
                    S.op("dve", lambda e: e.memset(xp[:, :, 0:1], 0.0), writes=[b_xp])
                    S.dma("pool", xp[:, :, 1:STB], xTv[:, :, 0:STB - 1], writes=[b_xp], acc=True)
                else:
                    S.dma("pool", xp[:], xTv[:, :, st * STB - 1:(st + 1) * STB - 1], writes=[b_xp])
            tsl = slice(j * 128, (j + 1) * 128)
            for bk in range(2):
                cs = slice(bk * 512, (bk + 1) * 512)
                for c in range(8):
                    S.mm(K[bk][:], xc[:, c, tsl], W1[:, c, cs], reads=[b_xc, b_W], writes=[b_K[bk]],
                         start=(c == 0), stop=False, signal=False)
                for c in range(8):
                    S.mm(K[bk][:], xp[:, c, tsl], W2[:, c, cs], reads=[b_xp, b_W], writes=[b_K[bk]],
                         start=False, stop=(c == 7), signal=(c == 7))
            yield
            pA, pB = K[0], K[1]
            S.op("act", _ACT(tl[:, 0:64], pB[:, 256:320], AF.Tanh), reads=[b_K[1]], writes=[b_tl])
            S.op("act", _CP(tl[:, 64:128], pB[:, 320:384]), reads=[b_K[1]], writes=[b_tl])
            S.op("act", _ACT(tl[:, 128:256], pB[:, 384:512], AF.Sigmoid), reads=[b_K[1]], writes=[b_tl])
            S.op("act", _CP(vs[:], pB[:, 0:256]), reads=[b_K[1]], writes=[b_vs])
            S.op("act", _CP(yA[:], pA[:]), reads=[b_K[0]], writes=[b_rs])
            yield
            S.op("pe", _TR(K[3][:, 0:128], tl[:, 0:128], identf[:]), reads=[b_tl, b_identf], writes=[b_K[3]], signal=False)
            S.op("pe", _TR(K[3][:, 128:256], tl[:, 128:256], identf[:]), reads=[b_tl, b_identf], writes=[b_K[3]])
            yield
            S.op("dve", _TC(lT[:], K[3][:, 0:256]), reads=[b_K[3]], writes=[b_lT])
            yield
            S.mm(K[2][:], lT[:, 0:128], wdec[:], reads=[b_lT, b_wdec], writes=[b_K[2]], signal=False)
            S.mm(K[3][:, 256:512], lT[:, 128:256], wgate[:], reads=[b_lT, b_wgate], writes=[b_K[3]])
            yield
            S.op("dve", _TT(lg[:], K[2][:], bias_wa, ALU.add), reads=[b_K[2], b_vec], writes=[b_lg])
            S.op("act", _ACT(sg[:], lg[:], AF.Sigmoid), reads=[b_lg], writes=[b_sg])
            S.op("act", _CP(gs[:], K[3][:, 256:512]), reads=[b_K[3]], writes=[b_gs])
            sw, aa = sg[:, 0:256], sg[:, 256:512]
            yield
            S.mm(K[2][:, 0:256], triS, sw, reads=[b_cst, b_sg], writes=[b_K[2]], signal=False)
            S.mm(K[2][:, 256:512], triI, sw, reads=[b_cst, b_sg], writes=[b_K[2]], signal=False)
            S.mm(K[3][:, 0:256], triA, sw, reads=[b_cst, b_sg], writes=[b_K[3]], signal=False)
            for c in range(2):
                rows = slice(64 * c, 64 * c + 64)
                for h in range(4):
                    S.mm(K[3][rows, 256 + 2 * h:258 + 2 * h], sg[rows, h * 64:(h + 1) * 64], cst[rows, 384:386],
                         reads=[b_cst, b_sg], writes=[b_K[3]], signal=(c == 1 and h == 3))
            yield
            S.op("act", _ACT(E[:, 0, :], K[2][:, 0:256], AF.Exp, scale=-DS), reads=[b_K[2]], writes=[b_E[0]])
            S.op("act", _ACT(E[:, 1, :], K[2][:, 256:512], AF.Exp, scale=-DS), reads=[b_K[2]], writes=[b_E[1]])
            S.op("act", _ACT(E[:, 2, :], K[2][:, 256:512], AF.Exp, scale=DS), reads=[b_K[2]], writes=[b_E[2]])
            S.op("act", _ACT(E[:, 3, :], K[3][:, 0:256], AF.Exp, scale=-DS), reads=[b_K[3]], writes=[b_E[3]])
            S.op("act", _ACT(PC[:], K[3][:, 256:264], AF.Exp, scale=-DS), reads=[b_K[3]], writes=[b_PC])
            S.op("dve", _TT(kk[:], krs, vec[:, 2, :], ALU.mult), reads=[b_rs, b_vec], writes=[b_kk])
            S.op("pool", _TT(t1[:], kk[:], kk[:], ALU.mult), reads=[b_kk], writes=[b_t1])
            S.op("pool", _TT(t2[:], aa, vec[:, 3, :], ALU.mult), reads=[b_sg, b_vec], writes=[b_t2])
            S.op("pool", _TT(t2[:], t2[:], vec[:, 4, :], ALU.add), reads=[b_t2, b_vec], writes=[b_t2])
            yield
            S.op("dve", lambda e: e.tensor_reduce(sm[:, 0:4], v3(t1[:]), AX.X, ALU.add), reads=[b_t1], writes=[b_sm])
            S.op("act", _ACT(sm[:, 4:8], sm[:, 0:4], AF.Sqrt), reads=[b_sm], writes=[b_sm])
            S.op("dve", _TT(km[:], krs, t2[:], ALU.mult), reads=[b_rs, b_t2], writes=[b_km])
            yield
            S.op("dve", _TS(sm[:, 4:8], sm[:, 4:8], L2_EPS, None, ALU.max), reads=[b_sm], writes=[b_sm])
            S.op("dve", lambda e: e.reciprocal(sm[:, 8:12], sm[:, 4:8]), reads=[b_sm], writes=[b_sm])
            S.op("dve", _TT(v3(kk[:]), v3(kk[:]), bc(sm[:, 8:12]), ALU.mult), reads=[b_kk, b_sm], writes=[b_kk])
            S.op("pool", _TT(t1[:], rs, km[:], ALU.mult), reads=[b_rs, b_km, b_sm], writes=[b_t1])
            S.op("pool", _TT(t1[:], t1[:], vec[:, 5, :], ALU.mult), reads=[b_t1, b_vec], writes=[b_t1])
            yield
            S.op("pool", _TT(bb[:], kk[:], aa, ALU.mult), reads=[b_kk, b_sg], writes=[b_bb])
            S.op("dve", lambda e: e.tensor_reduce(bs[:], v3(t1[:]), AX.X, ALU.add), reads=[b_t1], writes=[b_bs])
            S.op("dve", _STT(X4[:, 0, :], kk[:], -1.0, E[:, 0, :], ALU.mult, ALU.mult), reads=[b_kk, b_E[0]], writes=[b_X4])
            S.op("pool", _TT(X4[:, 1, :], rs, E[:, 1, :], ALU.mult), reads=[b_rs, b_E[1]], writes=[b_X4])
            yield
            S.op("dve", _TT(X4[:, 2, :], bb[:], E[:, 2, :], ALU.mult), reads=[b_bb, b_E[2]], writes=[b_X4])
            S.op("pool", _TT(X4[:, 3, :], km[:], E[:, 2, :], ALU.mult), reads=[b_km, b_E[2]], writes=[b_X4])
            S.op("dve", _TT(BK[:, 0, :], bb[:], E[:, 3, :], ALU.mult), reads=[b_bb, b_E[3]], writes=[b_BK])
            S.op("pool", _TT(BK[:, 1, :], km[:], E[:, 3, :], ALU.mult), reads=[b_km, b_E[3]], writes=[b_BK])

        def front2(n):
            d = n % 2
            AM1, b_AM1, AM2, b_AM2 = AM1d[d], b_AM1d[d], AM2d[d], b_AM2d[d]
            Pm, b_Pm, XT, b_XT, dgP, b_dgP = Pmd[d], b_Pmd[d], XTd[d], b_XTd[d], dgPd[d], b_dgPd[d]
            X4, b_X4, PC, b_PC = X4d[d], b_X4d[d], PCd[d], b_PCd[d]
            for c in range(2):
                rows = slice(64 * c, 64 * c + 64)
                for q in range(4):
                    for h in range(4):
                        blk = q * 4 + h
                        S.mm(K[4 + blk // 8][rows, (blk % 8) * 64:(blk % 8 + 1) * 64],
                             X4[rows, q, h * 64:(h + 1) * 64], identb[rows, rows],
                             reads=[b_X4, b_identb], writes=[b_K[4 + blk // 8]], signal=(c == 1 and blk % 8 == 7))
            yield
            XTf = XT[:].rearrange("p q h t -> p (q h t)")
            S.op("act", _CP(XTf[:, 0:512], K[4][:]), reads=[b_K[4]], writes=[b_XT])
            S.op("dve", _TC(XTf[:, 512:1024], K[5][:]), reads=[b_K[5]], writes=[b_XT])
            for c in range(2):
                rows = slice(64 * c, 64 * c + 64)
                S.op("pool", _TT(dgP[rows, :, :], cst[rows, 578:642].unsqueeze(1).to_broadcast([64, 4, 64]),
                                 PC[rows, :, c:c + 1].to_broadcast([64, 4, 64]), ALU.mult),
                     reads=[b_PC, b_cst], writes=[b_dgP])
            yield
            for c in range(2):
                rows = slice(64 * c, 64 * c + 64)
                for h in range(4):
                    last = (c == 1 and h == 3)
                    S.mm(K[4][rows, h * 128:(h + 1) * 128], XT[rows, 2, h, :], XT[rows, 0:2, h, :],
                         reads=[b_XT], writes=[b_K[4]], signal=False)
                    S.mm(K[5][rows, h * 128:(h + 1) * 128], XT[rows, 3, h, :], XT[rows, 0:2, h, :],
                         reads=[b_XT], writes=[b_K[5]], signal=last)
            yield
            m1b = mask1.unsqueeze(1).to_broadcast([128, 4, 128])
            S.op("dve", _TT(AM1[:], K[4][:].rearrange("p (h t) -> p h t", h=4), m1b, ALU.mult),
                 reads=[b_K[4], b_cst], writes=[b_AM1])
            S.op("pool", _TC(Lm[0][:, :, 0, :], AM1[:, :, 0:64]), reads=[b_AM1], writes=[b_Lm[0]])
            S.op("pool", _TT(Pm[:], AM1[:, :, 0:64], eye2.unsqueeze(1).to_broadcast([128, 4, 64]), ALU.add),
                 reads=[b_AM1, b_cst], writes=[b_Pm])
            yield
            for c in range(2):
                rows = slice(64 * c, 64 * c + 64)
                for h in range(4):
                    S.mm(K[4][rows, h * 64:(h + 1) * 64], XT[rows, 0, h, :], XT[rows, 2, h, :],
                         reads=[b_XT], writes=[b_K[4]], signal=(c == 1 and h == 3))
            S.op("dve", _TT(AM2[:], K[5][:].rearrange("p (h t) -> p h t", h=4), m1b, ALU.mult),
                 reads=[b_K[5], b_cst], writes=[b_AM2])
            yield
            m3b = mask3.unsqueeze(1).to_broadcast([128, 4, 64])
            S.op("dve", _TT(Lm[0][:, :, 1, :], K[4][:, 0:256].rearrange("p (h t) -> p h t", h=4), m3b, ALU.mult),
                 reads=[b_K[4], b_cst], writes=[b_Lm[0]])
            yield
            cur = 0
            for rnd in range(6):
                nxt = 1 - cur
                do_sq = rnd < 5
                do_p = rnd >= 1
                for c in range(2):
                    rows = slice(64 * c, 64 * c + 64)
                    for h in range(4):
                        last = (c == 1 and h == 3)
                        Lc, LTc = Lm[cur][rows, h, 0, :], Lm[cur][rows, h, 1, :]
                        if do_sq and rnd < 4:
                            S.mm(K[5][rows, (h * 2) * 64:(h * 2 + 1) * 64], LTc, Lc, reads=[b_Lm[cur]],
                                 writes=[b_K[5]], signal=False)
                        if do_sq:
                            S.mm(K[5][rows, (h * 2 + 1) * 64:(h * 2 + 2) * 64], Lc, LTc, reads=[b_Lm[cur]],
                                 writes=[b_K[5]], signal=(last and not do_p))
                        if do_p:
                            S.mm(K[4][rows, 256 + h * 64:256 + (h + 1) * 64], LTc, Pm[rows, h, :],
                                 reads=[b_Lm[cur], b_Pm], writes=[b_K[4]], signal=last)
                yield
                src = K[5][:].rearrange("p (h a t) -> p h a t", h=4, a=2)
                if do_sq:
                    if rnd < 4:
                        S.op("act", _CP(Lm[nxt][:], src), reads=[b_K[5]], writes=[b_Lm[nxt]])
                    else:
                        S.op("act", _CP(Lm[nxt][:, :, 1, :], src[:, :, 1, :]), reads=[b_K[5]], writes=[b_Lm[nxt]])
                if do_p:
                    S.op("dve", _TT(Pm[:], K[4][:, 256:512].rearrange("p (h t) -> p h t", h=4), Pm[:], ALU.add),
                         reads=[b_K[4], b_Pm], writes=[b_Pm])
                cur = nxt
                yield

        def back(n):
            st, j = n // TPS, n % TPS
            d = n % 2
            tsl = slice(j * 128, (j + 1) * 128)
            t3 = n % 3
            AM1, b_AM1, AM2, b_AM2 = AM1d[d], b_AM1d[d], AM2d[d], b_AM2d[d]
            Pm, b_Pm, XT, b_XT, BK, b_BK = Pmd[d], b_Pmd[d], XTd[d], b_XTd[d], BKd[t3], b_BKd[t3]
            vs, b_vs, dgP, b_dgP, gs, b_gs, bs, b_bs = vsd[t3], b_vsd[t3], dgPd[d], b_dgPd[d], gsd[t3], b_gsd[t3], bsd[t3], b_bsd[t3]
            pO_ = K[7]
            for c in range(2):
                rows = slice(64 * c, 64 * c + 64)
                orow = slice(64 * (1 - c), 64 * (1 - c) + 64)
                for h in range(4):
                    hc = slice(256 + h * 64, 256 + (h + 1) * 64)
                    vh = vs[rows, h * 64:(h + 1) * 64]
                    S.mm(K[6][rows, hc], AM2[rows, h, 0:64], vh, reads=[b_AM2, b_vs], writes=[b_K[6]],
                         start=True, stop=False, signal=False)
                    S.mm(K[6][rows, hc], XT[rows, 0, h, :], Hs[rows, h, :], reads=[b_XT, b_H], writes=[b_K[6]],
                         start=False, stop=True, signal=(h == 3))
                yield
                S.op("dve", _TC(Ws[rows, :], K[6][rows, 256:512]), reads=[b_K[6]], writes=[b_Ws])
                yield
                for h in range(4):
                    hc = slice(h * 64, (h + 1) * 64)
                    S.mm(K[6][rows, hc], Pm[rows, h, :], Ws[rows, hc], reads=[b_Pm, b_Ws], writes=[b_K[6]],
                         signal=(h == 3))
                yield
                S.op("act", _CP(Us[rows, :], K[6][rows, 0:256]), reads=[b_K[6]], writes=[b_Us])
                yield
                for h in range(4):
                    hc = slice(256 + h * 64, 256 + (h + 1) * 64)
                    vh = vs[rows, h * 64:(h + 1) * 64]
                    uh = Us[rows, h * 64:(h + 1) * 64]
                    S.mm(pO_[rows, hc], AM2[rows, h, 64:128], vh, reads=[b_AM2, b_vs], writes=[b_K[7]],
                         start=True, stop=False, signal=False)
                    S.mm(pO_[rows, hc], XT[rows, 1, h, :], Hs[rows, h, :], reads=[b_XT, b_H], writes=[b_K[7]],
                         start=False, stop=False, signal=False)
                    S.mm(pO_[rows, hc], AM1[rows, h, 64:128], uh, reads=[b_AM1, b_Us], writes=[b_K[7]],
                         start=False, stop=True, signal=False)
                for h in range(4):
                    oc_ = slice(256 + h * 64, 256 + (h + 1) * 64)
                    vh = vs[rows, h * 64:(h + 1) * 64]
                    uh = Us[rows, h * 64:(h + 1) * 64]
                    S.mm(K[6][orow, oc_], BK[rows, 1, h * 64:(h + 1) * 64], vh, reads=[b_BK, b_vs], writes=[b_K[6]],
                         start=True, stop=False, signal=False)
                    S.mm(K[6][orow, oc_], dgP[rows, h, :], Hs[rows, h, :], reads=[b_dgP, b_H], writes=[b_K[6]],
                         start=False, stop=False, signal=False)
                    S.mm(K[6][orow, oc_], BK[rows, 0, h * 64:(h + 1) * 64], uh, reads=[b_BK, b_Us], writes=[b_K[6]],
                         start=False, stop=True, signal=(h == 3))
                yield
                S.op("act", _CP(Hs[orow, :, :], K[6][orow, 256:512].rearrange("p (h v) -> p h v", h=4)),
                     reads=[b_K[6], b_K[7]], writes=[b_H])
                yield
            o3 = pO_[:, 256:512].rearrange("p (h d) -> p h d", d=64)
            S.op("act", _CP(on[:], pO_[:, 256:512]), reads=[b_K[7]], writes=[b_on])
            yield
            S.op("act", _ACT(tG[:], on[:], AF.Square), reads=[b_on], writes=[b_tG])
            S.op("dve", lambda e: e.tensor_reduce(smG[:, 0:4], v3(on[:]), AX.X, ALU.add), reads=[b_on], writes=[b_smG])
            yield
            S.op("dve", lambda e: e.tensor_reduce(smG[:, 4:8], v3(tG[:]), AX.X, ALU.add), reads=[b_tG], writes=[b_smG])
            S.op("dve", _TS(gmv[:, 0:4], smG[:, 0:4], 1.0 / 64.0, None, ALU.mult), reads=[b_smG], writes=[b_gmv])
            S.op("dve", _TT(gmv[:, 4:8], gmv[:, 0:4], gmv[:, 0:4], ALU.mult), reads=[b_gmv], writes=[b_gmv])
            S.op("dve", _STT(gmv[:, 8:12], smG[:, 4:8], 1.0 / 64.0, gmv[:, 4:8], ALU.mult, ALU.subtract),
                 reads=[b_smG, b_gmv], writes=[b_gmv])
            yield
            S.op("act", _ACT(smG[:, 8:12], gmv[:, 8:12], AF.Sqrt, bias=epsg[:, 0:1], scale=1.0),
                 reads=[b_gmv, b_epsg], writes=[b_smG])
            S.op("dve", lambda e: e.reciprocal(smG[:, 12:16], smG[:, 8:12]), reads=[b_smG], writes=[b_smG])
            S.op("dve", _TT(v3(on[:]), v3(on[:]), bc(gmv[:, 0:4]), ALU.subtract), reads=[b_on, b_gmv], writes=[b_on])
            yield
            S.op("pool", _TT(v3(on[:]), v3(on[:]), bc(smG[:, 12:16]), ALU.mult), reads=[b_on, b_smG], writes=[b_on])
            S.op("pool", _TT(on[:], on[:], vec[:, 6, :], ALU.mult), reads=[b_on, b_vec], writes=[b_on])
            S.op("pool", _TT(on[:], on[:], vec[:, 7, :], ALU.add), reads=[b_on, b_vec], writes=[b_on])
            S.op("dve", _TT(v3(tG[:]), v3(vs[:]), bc(bs[:]), ALU.mult), reads=[b_vs, b_bs, b_tG], writes=[b_tG])
            yield
            S.op("pool", _TT(on[:], on[:], tG[:], ALU.add), reads=[b_on, b_tG], writes=[b_on])
            S.op("pool", _TT(on[:], on[:], gs[:], ALU.mult), reads=[b_on, b_gs], writes=[b_on])
            yield
            for hp in range(2):
                S.op("pe", _TR(K[7][:, hp * 128:(hp + 1) * 128], on[:, hp * 128:(hp + 1) * 128], identf[:]),
                     reads=[b_on, b_identf], writes=[b_K[7]], signal=(hp == 1))
            yield
            S.op("act", _CP(orT[:, :, tsl], K[7][:, 0:256].rearrange("p (a t) -> p a t", a=2)),
                 reads=[b_K[7]], writes=[b_orT])
            if j == TPS - 1 or n == B_TILES - 1:
                for hp in range(2):
                    S.dma("sp", D["omix"][st][256 + hp * 128:256 + (hp + 1) * 128, :],
                          orT[:, hp, :], reads=[b_orT])

        for n in range(B_TILES + 2):
            gens = []
            if n < B_TILES:
                gens.append(front1(n))
            if 0 <= n - 1 < B_TILES:
                gens.append(front2(n - 1))
            if 0 <= n - 2 < B_TILES:
                gens.append(back(n - 2))
            while gens:
                for g_ in list(gens):
                    try:
                        next(g_)
                    except StopIteration:
                        gens.remove(g_)
        S.barrier()
        S.run()


def exchange(nc, S, D, stack):
    ccs = stack.enter_context(nc.semaphore("s_cc"))
    groups = [[0, 1], [2, 3], [4, 5], [6, 7]]
    for k in range(NXCH):
        S.q["pool"].append(["x", lambda e, k=k: e.collective_compute(
            "AllGather", ALU.bypass, replica_groups=groups, ins=[D["omix"][k]], outs=[D["G"][k]]).then_inc(ccs, 1)])
    S.extra.append((ccs, NXCH, "s_cc", "cc"))
    S.barrier()
    S.run()


def build_program(phases="ABC", exch=True):
    nc = bass.Bass("TRN2", target_bir_lowering=False)
    D = {}

    def din(name, shape, dt=F32):
        D[name] = nc.dram_tensor(name, list(shape), dt, kind="ExternalInput").ap()

    din("ident", [128, 128])
    if "A" in phases or "B" in phases:
        din("xT", [D_MODEL, SEQ])
    if "A" in phases:
        din("pos", [128, 64], I32)
        din("w_att", [D_MODEL, 768])
        din("maskT", [128, 256])
    if "B" in phases:
        din("w_rw", [D_MODEL, 1024])
        din("mu", [1, 1024])
        for nm in ("w0", "a0", "k_k", "k_a", "r_k", "gn_g", "gn_b"):
            din(nm, [1, 256])
        din("w_dec", [64, 256])
        din("w_aaa", [64, 256])
        din("w_gate", [128, 256])
        din("cstB", [128, 642])
    if "C" in phases:
        din("xres", [TOKH, D_MODEL])
        din("w_out", [1024, 1024])
        din("wg", [1024, FFN])
        din("wu", [1024, FFN])
        din("wd", [FFN, 1024])
        for nm in ("ln1g", "ln1b", "ln2g", "ln2b"):
            din(nm, [1, 1024])
        din("sel", [128, 2])
        D["out"] = nc.dram_tensor("out", [TOKH, D_MODEL], F32, kind="ExternalOutput").ap()
    full = exch
    if full:
        D["omix"] = [nc.dram_tensor("omix%d" % k, [512, XCH], BF16, kind="Internal").ap() for k in range(NXCH)]
        D["G"] = [nc.dram_tensor("G%d" % k, [1024, XCH], BF16, kind="Internal").ap() for k in range(NXCH)]
    else:
        if "C" in phases:
            din("G", [NXCH, 1024, XCH], BF16)
            D["G"] = [D["G"][k] for k in range(NXCH)]
        if "A" in phases or "B" in phases:
            om = nc.dram_tensor("omix", [NXCH, 512, XCH], BF16, kind="ExternalOutput").ap()
            D["omix"] = [om[k] for k in range(NXCH)]
    with ExitStack() as st:
        S = Sched(nc, st)
        if "C" in phases:
            D["wgs"] = nc.dram_tensor("wgs", [HC, 128, 2, 8, 128], BF16, kind="Internal").ap()
            prep_ffn_weights(nc, S, D)
        if "A" in phases:
            phase_a(nc, S, D)
        if "B" in phases:
            phase_b(nc, S, D)
        if full:
            exchange(nc, S, D, st)
        if "C" in phases:
            phase_c(nc, S, D)
    return nc


def att_mask():
    i_k = np.arange(128)[:, None]
    i_q = np.arange(128)[None, :]
    m = np.zeros((128, 256), np.float32)
    m[:, 0:128] = np.where(i_k >= i_q, 0.0, NEG)
    m[:, 128:256] = np.where(i_k <= i_q, 0.0, NEG)
    return m


def rwkv_consts():
    j = np.arange(128)[:, None]
    t = np.arange(128)[None, :]
    same = (j // 64) == (t // 64)
    c = np.zeros((128, 642), np.float32)
    c[:, 0:128] = same & (j <= t)
    c[:, 128:256] = same & (j < t)
    c[:, 256:384] = same & (j > t)
    c[:, 384] = (np.arange(128) < 64)
    c[:, 385] = (np.arange(128) >= 64)
    jj = (np.arange(128) % 64)[:, None]
    tt = np.arange(64)[None, :]
    c[:, 386:450] = jj < tt
    c[:, 450:514] = jj <= tt
    c[:, 514:578] = tt < jj
    c[:, 578:642] = jj == tt
    return c


def core_inputs(inp, c, phases="ABC"):
    b, g = c // 2, c % 2
    m = {"ident": np.eye(128, dtype=np.float32)}
    w_in = inp["w_in"][0]
    if "A" in phases or "B" in phases:
        m["xT"] = np.ascontiguousarray(inp["x"][b].T)
    if "A" in phases:
        m["pos"] = np.ascontiguousarray(inp["positions"][b].reshape(64, 128).T).astype(np.int32)
        cols = np.concatenate([np.arange(256 * g, 256 * g + 256) + off for off in (0, 512, 1024)])
        m["w_att"] = np.ascontiguousarray(w_in[:, cols])
        m["maskT"] = att_mask()
    if "B" in phases:
        hs = slice(256 * g, 256 * g + 256)
        rcols = np.concatenate([1536 + off + np.arange(256 * g, 256 * g + 256) for off in (0, 512, 1024)]
                               + [1536 + 1536 + np.arange(256)])
        m["w_rw"] = np.ascontiguousarray(w_in[:, rcols])
        m["mu"] = np.ascontiguousarray(inp["mu_shift"][0][rcols - 1536][None, :])
        for nm in ("w0", "a0", "k_k", "k_a", "gn_g", "gn_b"):
            m[nm] = np.ascontiguousarray(inp[nm][0][hs][None, :])
        m["r_k"] = np.ascontiguousarray(inp["r_k"][0][4 * g:4 * g + 4].reshape(1, 256))
        m["w_dec"] = np.ascontiguousarray(inp["w_decay_up"][0][:, hs])
        m["w_aaa"] = np.ascontiguousarray(inp["w_aaa_up"][0][:, hs])
        m["w_gate"] = np.ascontiguousarray(inp["w_gate_up"][0][:, hs])
        m["cstB"] = rwkv_consts()
    if "C" in phases:
        fi = lambda r: np.concatenate([np.arange(256 * r, 256 * r + 256), 512 + np.arange(256 * r, 256 * r + 256)])
        perm = np.concatenate([fi(0), fi(1)])
        sel = np.zeros((128, 2), np.float32)
        sel[:, g] = 1.0
        m.update(xres=np.ascontiguousarray(inp["x"][b, g * TOKH:(g + 1) * TOKH]),
                 w_out=np.ascontiguousarray(inp["w_out"][0][perm]),
                 wg=inp["w_ffn_gate"][0], wu=inp["w_ffn_up"][0], wd=inp["w_ffn_down"][0],
                 ln1g=inp["ln_mix_g"], ln1b=inp["ln_mix_b"], ln2g=inp["ln_ffn_g"], ln2b=inp["ln_ffn_b"],
                 sel=sel)
    return m


_NC_CACHE = {}


def kernel(**inputs):
    inp = {k: np.asarray(v) for k, v in inputs.items()}
    if "nc" not in _NC_CACHE:
        _NC_CACHE["nc"] = build_program("ABC")
    nc = _NC_CACHE["nc"]
    in_maps = [core_inputs(inp, c, "ABC") for c in range(8)]
    res = run_bass_kernel_spmd(nc, in_maps, core_ids=list(range(8)))
    out = np.empty((BATCH, SEQ, D_MODEL), np.float32)
    for c in range(8):
        b, g = c // 2, c % 2
        out[b, g * TOKH:(g + 1) * TOKH] = np.asarray(res.results[c]["out"], dtype=np.float32)
    return out
```

```python
import math
from contextlib import ExitStack

import numpy as np
import concourse.bass as bass
import concourse.mybir as mybir
from concourse.bass_utils import run_bass_kernel_spmd

F32 = mybir.dt.float32
BF16 = mybir.dt.bfloat16
I32 = mybir.dt.int32
AF = mybir.ActivationFunctionType
ALU = mybir.AluOpType
AX = mybir.AxisListType

D_MODEL = 1024
SEQ = 8192
BATCH = 4
HD = 64
FFN = 2816
HC = FFN // 128
ALPHA = 2.0 ** 0.25
LN_EPS = 1e-5
GN_EPS = 64e-5
L2_EPS = 1e-6
DECAY_SCALE = math.exp(-0.5)
NEG = -30000.0
ROPE_THETA = 500000.0
TOKH = SEQ // 2
XCH = 1024
NXCH = SEQ // XCH
XGROUPS = [[0, 1], [2, 3], [4, 5], [6, 7]]
B_TILES = SEQ // 128


class Buf:
    __slots__ = ("name", "w", "r", "psum")

    def __init__(self, name="", psum=False):
        self.name = name
        self.w = []
        self.r = {}
        self.psum = psum


class Sched:
    ENG = ("pe", "act", "dve", "pool", "sp")
    NDS = 8

    def __init__(self, nc, stack):
        self.nc = nc
        self.q = {e: [] for e in self.ENG}
        self.sem = {e: stack.enter_context(nc.semaphore("s_" + e)) for e in self.ENG}
        self.cnt = {e: 0 for e in self.ENG}
        self.seen = {e: {} for e in self.ENG}
        self.lastop = {e: None for e in self.ENG}
        self.dq = ("sp", "act", "pool")
        self.dsem = {e: [stack.enter_context(nc.semaphore("d_%s%d" % (e, i)))
                         for i in range(self.NDS)] for e in self.dq}
        self.dcnt = {e: 0 for e in self.dq}
        self.dlast = {e: [None] * self.NDS for e in self.dq}
        self.extra = []

    def _force(self, prod):
        rec = self.lastop[prod]
        assert rec is not None and not rec[2]
        rec[2] = True
        self.cnt[prod] += 1

    def _wait(self, eng, tok):
        if tok is None:
            return
        sem, val, key, prod = tok
        if prod in self.cnt and val > self.cnt[prod]:
            self._force(prod)
            assert val <= self.cnt[prod]
        if self.seen[eng].get(key, 0) >= val:
            return
        self.seen[eng][key] = val
        self.q[eng].append(["w", sem, val])

    def _deps(self, eng, reads, writes):
        for b in reads:
            for t in b.w:
                self._wait(eng, t)
            if b.psum:
                for t in b.r.values():
                    if t[3] != eng:
                        self._wait(eng, t)
        for b in writes:
            for t in b.w:
                if t[3] != eng:
                    self._wait(eng, t)
            for t in b.r.values():
                if t[3] != eng:
                    self._wait(eng, t)

    def _mark(self, tok, reads, writes, acc=False):
        for b in reads:
            b.r[tok[2]] = tok
        for b in writes:
            if acc:
                b.w.append(tok)
            else:
                b.w = [tok]
            b.r = {}

    def op(self, eng, fn, reads=(), writes=(), signal=True):
        self._deps(eng, reads, writes)
        sem = self.sem[eng]
        rec = ["o", fn, bool(signal), sem]
        self.q[eng].append(rec)
        self.lastop[eng] = rec
        if signal:
            self.cnt[eng] += 1
            tok = (sem, self.cnt[eng], eng, eng)
        else:
            tok = (sem, self.cnt[eng] + 1, eng, eng)
        self._mark(tok, reads, writes)
        return tok

    def mm(self, out, lhsT, rhs, reads, writes, start=True, stop=True, signal=True, **kw):
        return self.op("pe", lambda e: e.matmul(out, lhsT, rhs, start=start, stop=stop, **kw),
                       reads, writes, signal)

    def dma(self, queue, out, in_, reads=(), writes=(), acc=False, **kw):
        i = self.dcnt[queue]
        self.dcnt[queue] += 1
        slot = i % self.NDS
        self._wait(queue, self.dlast[queue][slot])
        self._deps(queue, reads, writes)
        sem = self.dsem[queue][slot]
        val = 16 * (i // self.NDS + 1)
        tok = (sem, val, "d_%s%d" % (queue, slot), "dma")
        self.dlast[queue][slot] = tok
        self.q[queue].append(["d", out, in_, sem, kw])
        self._mark(tok, reads, writes, acc=acc)
        return tok

    def raw(self, eng, fn, reads=(), writes=()):
        self._deps(eng, reads, writes)
        self.q[eng].append(["x", fn])

    def barrier(self):
        for o in self.ENG:
            rec = self.lastop[o]
            if rec is not None and not rec[2]:
                self._force(o)
        toks = [(self.sem[o], self.cnt[o], o, o) for o in self.ENG if self.cnt[o] > 0]
        for qn in self.dq:
            toks += [t for t in self.dlast[qn] if t is not None]
        toks += self.extra
        for e in self.ENG:
            for t in toks:
                if t[3] != e:
                    self._wait(e, t)

    @staticmethod
    def _replay(e, recs):
        for r in recs:
            k = r[0]
            if k == "w":
                e.wait_ge(r[1], r[2])
            elif k == "o":
                ins = r[1](e)
                if r[2]:
                    ins.then_inc(r[3], 1)
            elif k == "d":
                e.dma_start(out=r[1], in_=r[2], **r[4]).then_inc(r[3], 16)
            else:
                r[1](e)

    def run(self):
        nc = self.nc
        q = self.q
        rp = self._replay
        with nc.Block() as block:
            @block.tensor
            def _(e):
                rp(e, q["pe"])

            @block.scalar
            def _(e):
                rp(e, q["act"])

            @block.vector
            def _(e):
                rp(e, q["dve"])

            @block.gpsimd
            def _(e):
                rp(e, q["pool"])

            @block.sync
            def _(e):
                rp(e, q["sp"])
        self.q = {e: [] for e in self.ENG}
        self.lastop = {e: None for e in self.ENG}


def bcast_rows(ap, n=128):
    return bass.AP(ap.tensor, ap.offset, [[0, n], [1, ap.shape[-1]]])


def prep_ffn_weights(nc, S, D):
    wgv = D["wg"].rearrange("(c p) n -> p c n", p=128)
    wuv = D["wu"].rearrange("(c p) n -> p c n", p=128)
    D["b_wgs"] = [Buf() for _ in range(HC)]
    for h in range(HC):
        S.dma("pool", D["wgs"][h, :, 0, :, :], wgv[:, :, h * 128:(h + 1) * 128], writes=[D["b_wgs"][h]])
        S.dma("pool", D["wgs"][h, :, 1, :, :], wuv[:, :, h * 128:(h + 1) * 128], writes=[D["b_wgs"][h]], acc=True)


def phase_c(nc, S, D):
    GT = 512
    NG = TOKH // GT
    TPG = GT // 128
    OCW = 256
    with ExitStack() as ph:
        def sb(n, shp, dt):
            return ph.enter_context(nc.sbuf_tensor(n, shp, dt))

        def ps(n, shp, dt):
            return ph.enter_context(nc.psum_tensor(n, shp, dt))

        Gv = [g_.rearrange("(c p) t -> p c t", p=128) for g_ in D["G"]]
        ident = sb("c_ident", [128, 128], BF16)
        b_ident = Buf()
        S.dma("pool", ident[:], D["ident"], writes=[b_ident])
        sel = sb("c_sel", [128, 2], F32)
        b_sel = Buf()
        S.dma("sp", sel[:], D["sel"], writes=[b_sel])
        woutA = sb("c_woutA", [128, 8, 1024], BF16)
        woutB = sb("c_woutB", [128, 8, 1024], BF16)
        b_wout = Buf()
        b_woutB = Buf()
        S.dma("pool", woutA[:], D["w_out"].rearrange("(c p) n -> p c n", p=128), writes=[b_wout])
        S.op("dve", lambda e: e.tensor_scalar(woutB[:], woutA[:], sel[:, 1:2], None, ALU.mult),
             reads=[b_wout, b_sel], writes=[b_woutB])
        S.op("dve", lambda e: e.tensor_scalar(woutA[:], woutA[:], sel[:, 0:1], None, ALU.mult),
             reads=[b_wout, b_sel, b_woutB], writes=[b_wout])
        wd = sb("c_wd", [128, HC, 1024], BF16)
        b_wd = [Buf() for _ in range(HC)]
        for h in range(HC):
            S.dma("pool", wd[:, h, :], D["wd"][h * 128:(h + 1) * 128, :], writes=[b_wd[h]])
        lnp = sb("c_lnp", [128, 4, 1024], F32)
        b_lnp = [Buf() for _ in range(4)]
        for i, nm in enumerate(("ln1g", "ln1b", "ln2g", "ln2b")):
            S.dma("sp", lnp[:, i, :], bcast_rows(D[nm]), writes=[b_lnp[i]])

        NW = 4
        wgu = [sb("c_wgu%d" % i, [128, 2, 8, 128], BF16) for i in range(NW)]
        b_wgu = [Buf() for _ in range(NW)]

        h1g = sb("c_h1g", [128, TPG, 1024], F32)
        b_h1g = [Buf() for _ in range(TPG)]
        h1T = sb("c_h1T", [128, 8, GT], BF16)
        b_h1T = [Buf() for _ in range(TPG)]
        actT = sb("c_actT", [128, HC, GT], BF16)
        b_actT = [Buf() for _ in range(HC)]
        NB = 2
        ocA = [sb("c_ocA%d" % i, [128, 8, OCW], BF16) for i in range(NB)]
        ocB = [sb("c_ocB%d" % i, [128, 8, OCW], BF16) for i in range(NB)]
        b_ocA = [Buf() for _ in range(NB)]
        b_ocB = [Buf() for _ in range(NB)]
        xt = [sb("c_xt%d" % i, [128, 1024], F32) for i in range(NB)]
        b_xt = [Buf() for _ in range(NB)]
        hpre = sb("c_hpre", [128, 1024], F32)
        b_hpre = Buf()
        hn = sb("c_hn", [128, 1024], F32)
        b_hn = Buf()
        h1b = sb("c_h1b", [128, 1024], BF16)
        b_h1b = Buf()
        stats = sb("c_stats", [128, 2, 6], F32)
        b_stats = Buf()
        mv = sb("c_mv", [128, 4], F32)
        b_mv = Buf()
        sg = [sb("c_sg%d" % i, [128, 512], BF16) for i in range(2)]
        b_sg = [Buf() for _ in range(2)]
        outt = [sb("c_outt%d" % i, [128, 1024], F32) for i in range(NB)]
        b_outt = [Buf() for _ in range(NB)]

        epsc = sb("c_eps", [128, 1], F32)
        b_epsc = Buf()
        S.op("dve", lambda e: e.memset(epsc[:], LN_EPS), writes=[b_epsc])
        pmix = ps("c_pmix", [128, 1024], F32)
        b_pmix = Buf(psum=True)
        pT = ps("c_pT", [128, 1024], BF16)
        b_pT = Buf(psum=True)
        pG = [ps("c_pG%d" % i, [128, 512], F32) for i in range(2)]
        pU = [ps("c_pU%d" % i, [128, 512], F32) for i in range(2)]
        b_pG = [Buf(psum=True) for _ in range(2)]
        b_pU = [Buf(psum=True) for _ in range(2)]

        def layer_norm(src_b, gi_, bi_, dst, b_dst):
            for hf in range(2):
                S.op("dve", lambda e, hf=hf: e.bn_stats(stats[:, hf, :], hpre[:, hf * 512:(hf + 1) * 512]),
                     reads=[src_b], writes=[b_stats])
            S.op("dve", lambda e: e.bn_aggr(mv[:, 0:2], stats[:].rearrange("p a b -> p (a b)")),
                 reads=[b_stats], writes=[b_mv])
            S.op("act", lambda e: e.activation(mv[:, 3:4], mv[:, 1:2], AF.Sqrt, bias=epsc[:, 0:1], scale=1.0),
                 reads=[b_mv, b_epsc], writes=[b_mv])
            S.op("dve", lambda e: e.reciprocal(mv[:, 2:3], mv[:, 3:4]), reads=[b_mv], writes=[b_mv])
            S.op("dve", lambda e: e.tensor_scalar(hn[:], hpre[:], mv[:, 0:1], mv[:, 2:3],
                                                  ALU.subtract, ALU.mult),
                 reads=[src_b, b_mv], writes=[b_hn])
            S.op("pool", lambda e: e.tensor_tensor(hn[:], hn[:], lnp[:, gi_, :], ALU.mult),
                 reads=[b_hn, b_lnp[gi_]], writes=[b_hn])
            S.op("dve", lambda e: e.tensor_tensor(dst, hn[:], lnp[:, bi_, :], ALU.add),
                 reads=[b_hn, b_lnp[bi_]], writes=[b_dst])

        nwl = 0
        pend_tr = []
        for gi in range(NG):
            for ti in range(TPG):
                it = gi * TPG + ti
                sl = it % NB
                tok0 = it * 128
                oi = (tok0 // OCW) % NB
                oo = tok0 % OCW
                if oo == 0:
                    ta, tb = tok0, TOKH + tok0
                    S.dma("sp", ocA[oi][:], Gv[ta // XCH][:, :, ta % XCH:ta % XCH + OCW], writes=[b_ocA[oi]])
                    S.dma("sp", ocB[oi][:], Gv[tb // XCH][:, :, tb % XCH:tb % XCH + OCW], writes=[b_ocB[oi]])
                S.dma("sp", xt[sl][:], D["xres"][tok0:tok0 + 128, :], writes=[b_xt[sl]])
                if len(pend_tr) > 1:
                    pend_tr.pop(0)()
                for hf in range(2):
                    cs = slice(hf * 512, (hf + 1) * 512)
                    for c in range(8):
                        S.mm(pmix[:, cs], ocA[oi][:, c, oo:oo + 128], woutA[:, c, cs],
                             reads=[b_ocA[oi], b_wout], writes=[b_pmix], start=(c == 0), stop=False, signal=False)
                    for c in range(8):
                        S.mm(pmix[:, cs], ocB[oi][:, c, oo:oo + 128], woutB[:, c, cs],
                             reads=[b_ocB[oi], b_woutB], writes=[b_pmix], start=False, stop=(c == 7),
                             signal=(c == 7))
                for hf in range(2):
                    S.op("dve", lambda e, hf=hf, sl=sl: e.scalar_tensor_tensor(
                        hpre[:, hf * 512:(hf + 1) * 512], xt[sl][:, hf * 512:(hf + 1) * 512], ALPHA,
                        pmix[:, hf * 512:(hf + 1) * 512], ALU.mult, ALU.add),
                        reads=[b_xt[sl], b_pmix], writes=[b_hpre])
                layer_norm(b_hpre, 0, 1, h1g[:, ti, :], b_h1g[ti])
                def tr_part(ti=ti):
                    S.op("act", lambda e: e.copy(h1b[:], h1g[:, ti, :]), reads=[b_h1g[ti]], writes=[b_h1b])
                    for c in range(8):
                        S.op("pe", lambda e, c=c: e.transpose(pT[:, c * 128:(c + 1) * 128],
                                                              h1b[:, c * 128:(c + 1) * 128], ident[:]),
                             reads=[b_h1b, b_ident], writes=[b_pT], signal=(c == 7))
                    S.op("act", lambda e: e.copy(h1T[:, :, ti * 128:(ti + 1) * 128],
                                                 pT[:].rearrange("p (c t) -> p c t", c=8)),
                         reads=[b_pT], writes=[b_h1T[ti]])
                pend_tr.append(tr_part)
            while pend_tr:
                pend_tr.pop(0)()
            for h in range(HC):
                ws = nwl % NW
                nwl += 1
                S.dma("sp", wgu[ws][:].rearrange("p a c n -> p (a c n)"),
                      D["wgs"][h].rearrange("p a c n -> p (a c n)"), reads=[D["b_wgs"][h]], writes=[b_wgu[ws]])
                pb = h % 2
                for c in range(8):
                    S.mm(pG[pb][:], wgu[ws][:, 0, c, :], h1T[:, c, :], reads=[b_wgu[ws]] + b_h1T,
                         writes=[b_pG[pb]], start=(c == 0), stop=(c == 7), signal=(c == 7))
                for c in range(8):
                    S.mm(pU[pb][:], wgu[ws][:, 1, c, :], h1T[:, c, :], reads=[b_wgu[ws]] + b_h1T,
                         writes=[b_pU[pb]], start=(c == 0), stop=(c == 7), signal=(c == 7))
                S.op("act", lambda e, pb=pb: e.activation(sg[pb][:], pG[pb][:], AF.Silu),
                     reads=[b_pG[pb]], writes=[b_sg[pb]])
                S.op("dve", lambda e, pb=pb, h=h: e.tensor_tensor(actT[:, h, :], pU[pb][:], sg[pb][:], ALU.mult),
                     reads=[b_pU[pb], b_sg[pb]], writes=[b_actT[h]])
            for ti in range(TPG):
                it = gi * TPG + ti
                sl = it % NB
                tok0 = it * 128
                for hf in range(2):
                    for h in range(HC):
                        S.mm(pmix[:, hf * 512:(hf + 1) * 512], actT[:, h, ti * 128:(ti + 1) * 128],
                             wd[:, h, hf * 512:(hf + 1) * 512], reads=[b_actT[h], b_wd[h]], writes=[b_pmix],
                             start=(h == 0), stop=(h == HC - 1), signal=(h == HC - 1))
                for hf in range(2):
                    S.op("dve", lambda e, hf=hf, ti=ti: e.scalar_tensor_tensor(
                        hpre[:, hf * 512:(hf + 1) * 512], h1g[:, ti, hf * 512:(hf + 1) * 512], ALPHA,
                        pmix[:, hf * 512:(hf + 1) * 512], ALU.mult, ALU.add),
                        reads=[b_h1g[ti], b_pmix], writes=[b_hpre])
                layer_norm(b_hpre, 2, 3, outt[sl][:], b_outt[sl])
                S.dma("sp", D["out"][tok0:tok0 + 128, :], outt[sl][:], reads=[b_outt[sl]])
        S.barrier()
        S.run()


def phase_a(nc, S, D):
    ST = 2048
    NST = SEQ // ST
    inv_freq = [float(np.float32(ROPE_THETA) ** np.float32(-i / 8.0)) for i in range(8)]
    TWO_PI = 2.0 * math.pi
    C1 = 6.28125
    C2 = TWO_PI - C1
    with ExitStack() as ph:
        def sb(n, shp, dt):
            return ph.enter_context(nc.sbuf_tensor(n, shp, dt))

        def ps(n, shp, dt):
            return ph.enter_context(nc.psum_tensor(n, shp, dt))

        identf = sb("a_identf", [128, 128], F32)
        b_identf = Buf()
        S.dma("sp", identf[:], D["ident"], writes=[b_identf])
        identb = sb("a_identb", [128, 128], BF16)
        b_identb = Buf()
        S.dma("pool", identb[:], D["ident"], writes=[b_identb])
        maskb = sb("a_maskb", [128, 256], BF16)
        b_maskb = Buf()
        S.dma("pool", maskb[:], D["maskT"], writes=[b_maskb])
        ones = sb("a_ones", [128, 128], F32)
        b_ones = Buf()
        S.op("dve", lambda e: e.memset(ones[:], 1.0), writes=[b_ones])
        watt = sb("a_watt", [128, 8, 768], BF16)
        b_watt = Buf()
        S.dma("pool", watt[:], D["w_att"].rearrange("(c p) n -> p c n", p=128), writes=[b_watt])

        posi = sb("a_posi", [128, 64], I32)
        b_posi = Buf()
        S.dma("sp", posi[:], D["pos"], writes=[b_posi])
        posf = sb("a_posf", [128, 64], F32)
        b_posf = Buf()
        S.op("dve", lambda e: e.tensor_copy(posf[:], posi[:]), reads=[b_posi], writes=[b_posf])
        ang = sb("a_ang", [128, 64, 8], F32)
        b_ang = Buf()
        for i in range(8):
            S.op("dve", lambda e, i=i: e.tensor_scalar(ang[:, :, i], posf[:], inv_freq[i], None, ALU.mult),
                 reads=[b_posf], writes=[b_ang])
        sinT = sb("a_sinT", [128, 64, 8], F32)
        cosT = sb("a_cosT", [128, 64, 8], F32)
        b_sinT = Buf()
        b_cosT = Buf()
        kq = sb("a_kq", [128, 512], I32)
        kf = sb("a_kf", [128, 512], F32)
        red = sb("a_red", [128, 512], F32)
        msk = sb("a_msk", [128, 512], F32)
        b_tmp = Buf()
        angf = ang[:].rearrange("p a b -> p (a b)")

        def wrap(dst):
            S.op("dve", lambda e: e.tensor_scalar(msk[:], dst, math.pi, -TWO_PI, ALU.is_gt, ALU.mult),
                 reads=[b_tmp], writes=[b_tmp])
            S.op("dve", lambda e: e.tensor_tensor(dst, dst, msk[:], ALU.add), reads=[b_tmp], writes=[b_tmp])

        S.op("dve", lambda e: e.tensor_scalar(kq[:], angf, 1.0 / TWO_PI, None, ALU.mult),
             reads=[b_ang], writes=[b_tmp])
        S.op("dve", lambda e: e.tensor_copy(kf[:], kq[:]), reads=[b_tmp], writes=[b_tmp])
        S.op("dve", lambda e: e.scalar_tensor_tensor(red[:], kf[:], -C1, angf, ALU.mult, ALU.add),
             reads=[b_tmp, b_ang], writes=[b_tmp])
        S.op("dve", lambda e: e.scalar_tensor_tensor(red[:], kf[:], -C2, red[:], ALU.mult, ALU.add),
             reads=[b_tmp], writes=[b_tmp])
        wrap(red[:])
        S.op("act", lambda e: e.activation(sinT[:].rearrange("p a b -> p (a b)"), red[:], AF.Sin),
             reads=[b_tmp], writes=[b_sinT])
        S.op("dve", lambda e: e.tensor_scalar(red[:], red[:], math.pi / 2.0, None, ALU.add),
             reads=[b_tmp, b_sinT], writes=[b_tmp])
        wrap(red[:])
        S.op("act", lambda e: e.activation(cosT[:].rearrange("p a b -> p (a b)"), red[:], AF.Sin),
             reads=[b_tmp], writes=[b_cosT])

        xTs = [sb("a_xT%d" % i, [128, 8, ST], BF16) for i in range(2)]
        b_xTs = [Buf() for _ in range(2)]
        xTv = D["xT"].rearrange("(c p) t -> p c t", p=128)
        qT = sb("a_qT", [128, 2, ST], BF16)
        b_qT = Buf()
        kT = [sb("a_kT%d" % i, [128, 2, ST], BF16) for i in range(2)]
        b_kT = [Buf() for _ in range(2)]
        V = [[sb("a_V%d_%d" % (l, i), [128, 16, 4, 65], BF16) for i in range(2)] for l in range(3)]
        b_V = [[Buf() for _ in range(2)] for l in range(3)]
        for l in range(3):
            for i in range(2):
                S.op("pool", lambda e, l=l, i=i: e.memset(V[l][i][:], 1.0), writes=[b_V[l][i]])
        ysbs = [sb("a_ysb%d" % i, [128, 512], F32) for i in range(2)]
        b_ysbs = [Buf() for _ in range(2)]
        rts = [sb("a_rt%d" % i, [128, 4, 8, 8], F32) for i in range(2)]
        b_rts = [Buf() for _ in range(2)]
        sqs = [sb("a_sq%d" % i, [128, 512], F32) for i in range(2)]
        b_sqs = [Buf() for _ in range(2)]
        ssqs = [sb("a_ssq%d" % i, [128, 8], F32) for i in range(2)]
        b_ssqs = [Buf() for _ in range(2)]
        rm = sb("a_rm", [128, 8], F32)
        b_rm = Buf()
        st4 = sb("a_st4", [4, 8], F32)
        b_st4 = Buf()
        dg4 = sb("a_dg4", [4, 4], F32)
        b_dg4 = Buf()
        negM = sb("a_negM", [128, 4], F32)
        b_negM = Buf()
        NPT = 3
        PT = [sb("a_PT%d" % i, [128, 256], BF16) for i in range(NPT)]
        b_PT = [Buf() for _ in range(NPT)]
        oacc = sb("a_oacc", [128, ST], F32)
        b_oacc = Buf()
        oT = [sb("a_oT%d" % i, [64, ST], BF16) for i in range(2)]
        b_oT = [Buf() for _ in range(2)]

        pqk = ps("a_pqk", [128, 512], F32)
        b_pqk = Buf(psum=True)
        pX = ps("a_pX", [128, 512], F32)
        b_pX = Buf(psum=True)
        pS = [ps("a_pS%d" % i, [128, 512], F32) for i in range(2)]
        b_pS = [Buf(psum=True) for _ in range(2)]
        pO = ps("a_pO", [128, ST], F32)
        b_pO = Buf(psum=True)

        S.op("dve", lambda e: e.memset(st4[:], 0.0), writes=[b_st4])

        nblk = 0
        for st in range(NST):
            xs = st % 2
            ks = st % 2
            kp = 1 - ks
            xT = xTs[xs]
            S.dma("pool", xT[:], xTv[:, :, st * ST:(st + 1) * ST], writes=[b_xTs[xs]])
            S.op("dve", lambda e: e.memset(rm[:], 0.0), reads=[], writes=[b_rm])
            def qk_tile(j, par):
                n = st * 16 + j
                pq_, bq_ = (pqk, b_pqk) if par == 0 else (pS[0], b_pS[0])
                px_, bx_ = (pX, b_pX) if par == 0 else (pS[1], b_pS[1])
                ys, b_ys, rt_, b_rt_, sq_, b_sq_, ssq_, b_ssq_ = ysbs[par], b_ysbs[par], rts[par], b_rts[par], \
                    sqs[par], b_sqs[par], ssqs[par], b_ssqs[par]
                ys3 = ys[:].rearrange("p (h d) -> p h d", d=64)
                for c in range(8):
                    S.mm(pq_[:], xT[:, c, j * 128:(j + 1) * 128], watt[:, c, 0:512],
                         reads=[b_xTs[xs], b_watt], writes=[bq_], start=(c == 0), stop=(c == 7), signal=(c == 7))
                yield
                S.op("act", _mul(ys[:, 0:256], pq_[:, 0:256], 0.125), reads=[bq_], writes=[b_ys])
                S.op("act", _CP(ys[:, 256:512], pq_[:, 256:512]), reads=[bq_], writes=[b_ys])
                yield
                cb = cosT[:, n:n + 1, :].to_broadcast([128, 8, 8])
                sbb = sinT[:, n:n + 1, :].to_broadcast([128, 8, 8])
                x1 = ys3[:, :, 0:8]
                x2 = ys3[:, :, 8:16]
                for a, (xx, tt) in enumerate(((x1, cb), (x2, sbb), (x2, cb), (x1, sbb))):
                    S.op("pool", _TT(rt_[:, a, :, :], xx, tt, ALU.mult), reads=[b_ys, b_cosT, b_sinT], writes=[b_rt_])
                yield
                S.op("pool", _TT(x1, rt_[:, 0, :, :], rt_[:, 1, :, :], ALU.subtract), reads=[b_rt_, b_ys], writes=[b_ys])
                S.op("pool", _TT(x2, rt_[:, 2, :, :], rt_[:, 3, :, :], ALU.add), reads=[b_rt_, b_ys], writes=[b_ys])
                yield
                S.op("dve", _TT(sq_[:], ys[:], ys[:], ALU.mult), reads=[b_ys], writes=[b_sq_])
                for blk in range(4):
                    S.op("pe", _TR(px_[:, blk * 128:(blk + 1) * 128], ys[:, blk * 128:(blk + 1) * 128], identf[:]),
                         reads=[b_ys, b_identf], writes=[bx_], signal=(blk == 3))
                yield
                S.op("dve", _RED(ssq_[:], sq_[:].rearrange("p (h d) -> p h d", d=64)), reads=[b_sq_], writes=[b_ssq_])
                S.op("act", _CP(qT[:, :, j * 128:(j + 1) * 128], px_[:, 0:256].rearrange("p (a t) -> p a t", a=2)),
                     reads=[bx_], writes=[b_qT])
                S.op("act", _CP(kT[ks][:, :, j * 128:(j + 1) * 128], px_[:, 256:512].rearrange("p (a t) -> p a t", a=2)),
                     reads=[bx_], writes=[b_kT[ks]])
                yield
                S.op("dve", _TT(rm[:], rm[:], ssq_[:], ALU.max), reads=[b_ssq_, b_rm], writes=[b_rm])

            for j0 in range(0, 16, 2):
                gens = [qk_tile(j0, 0), qk_tile(j0 + 1, 1)]
                while gens:
                    for g_ in list(gens):
                        try:
                            next(g_)
                        except StopIteration:
                            gens.remove(g_)
            S.op("pe", lambda e: e.transpose(pX[0:4, 0:128], rm[:, 0:4], identf[:]), reads=[b_rm, b_identf],
                 writes=[b_pX], signal=False)
            S.op("pe", lambda e: e.transpose(pX[0:4, 128:256], rm[:, 4:8], identf[:]), reads=[b_rm, b_identf],
                 writes=[b_pX])
            S.op("dve", lambda e: e.tensor_copy(st4[:, 2:3], st4[:, 1:2]), reads=[b_st4], writes=[b_st4])
            S.op("dve", lambda e: e.tensor_reduce(st4[:, 0:2], pX[0:4, 0:256].rearrange("p (a t) -> p a t", a=2),
                                                  AX.X, ALU.max), reads=[b_pX, b_st4], writes=[b_st4])
            S.op("dve", lambda e: e.tensor_tensor(st4[:, 3:4], st4[:, 1:2], st4[:, 2:3], ALU.max),
                 reads=[b_st4], writes=[b_st4])
            S.op("dve", lambda e: e.tensor_tensor(st4[:, 3:4], st4[:, 3:4], st4[:, 0:1], ALU.mult),
                 reads=[b_st4], writes=[b_st4])
            S.op("dve", lambda e: e.tensor_scalar(dg4[:], identf[0:4, 0:4], st4[:, 3:4], None, ALU.mult),
                 reads=[b_st4, b_identf], writes=[b_dg4])
            S.mm(pX[:, 256:260], ones[0:4, :], dg4[:], reads=[b_ones, b_dg4], writes=[b_pX])
            S.op("act", lambda e: e.activation(negM[:], pX[:, 256:260], AF.Sqrt), reads=[b_pX], writes=[b_negM])
            S.op("dve", lambda e: e.tensor_scalar(negM[:], negM[:], -1.0, None, ALU.mult), reads=[b_negM],
                 writes=[b_negM])
            for l, dil in enumerate((1, 4, 16)):
                for t16 in range(16):
                    if dil == 1:
                        a0, a1 = t16 * 128, (t16 + 1) * 128
                    elif dil == 4:
                        n4, r = t16 // 4, t16 % 4
                        a0, a1 = n4 * 512 + r, (n4 + 1) * 512
                    else:
                        a0, a1 = t16, ST
                    pv_, bv_ = (pX, b_pX) if t16 % 2 == 0 else (pqk, b_pqk)
                    for c in range(8):
                        S.mm(pv_[:, 0:256], xT[:, c, a0:a1:dil], watt[:, c, 512:768],
                             reads=[b_xTs[xs], b_watt], writes=[bv_], start=(c == 0), stop=(c == 7), signal=(c == 7))
                    S.op("act", _CP(V[l][ks][:, t16, :, 0:64], pv_[:, 0:256].rearrange("p (h d) -> p h d", d=64)),
                         reads=[bv_], writes=[b_V[l][ks]])
            for h in range(4):
                hp, p0 = h // 2, 64 * (h % 2)
                first = [True] * 4
                pend = []

                def block(qsl, cur, prev, outs):
                    nonlocal nblk
                    pb = nblk % 2
                    pt = nblk % NPT
                    nblk += 1
                    lo = 0 if prev is not None else 128
                    qap = qT[p0:p0 + 64, hp, qsl]
                    if prev is not None:
                        pslot, psl, pv_ap = prev
                        S.mm(pS[pb][:, 0:128], kT[pslot][p0:p0 + 64, hp, psl], qap,
                             reads=[b_kT[pslot], b_qT], writes=[b_pS[pb]], start=True, stop=False, signal=False)
                        S.mm(pS[pb][:, 0:128], identb[:], maskb[:, 0:128], reads=[b_identb, b_maskb],
                             writes=[b_pS[pb]], start=False, stop=True, signal=False)
                    S.mm(pS[pb][:, 128:256], kT[ks][p0:p0 + 64, hp, qsl], qap,
                         reads=[b_kT[ks], b_qT], writes=[b_pS[pb]], start=True, stop=False, signal=False)
                    S.mm(pS[pb][:, 128:256], identb[:], maskb[:, 128:256], reads=[b_identb, b_maskb],
                         writes=[b_pS[pb]], start=False, stop=True, signal=True)
                    S.op("act", lambda e, hh=h: e.activation(PT[pt][:, lo:256], pS[pb][:, lo:256], AF.Exp,
                                                             bias=negM[:, hh:hh + 1], scale=1.0),
                         reads=[b_pS[pb], b_negM], writes=[b_PT[pt]])
                    kts = ([(0, pv_ap, b_V_prev)] if prev is not None else []) + [(1, cur, b_V_cur)]

                    def pv_part():
                        nmm = len(kts) * len(outs)
                        i = 0
                        for (kt, vap, vb) in kts:
                            for (ocol, pcol) in outs:
                                bank = ocol.start // 512
                                i += 1
                                S.mm(pO[0:65, ocol], vap, PT[pt][:, kt * 128 + pcol.start:kt * 128 + pcol.stop],
                                     reads=[b_PT[pt], vb], writes=[b_pO], start=first[bank], stop=False,
                                     signal=(i == nmm), skip_group_check=True)
                                first[bank] = False
                    if pend:
                        pend.pop()()
                    pend.append(pv_part)

                full = [(None, slice(0, 128))]
                for jb in range(16):
                    qsl = slice(jb * 128, (jb + 1) * 128)
                    b_V_cur = b_V[0][ks]
                    cur = V[0][ks][:, jb, h, :]
                    prev = None
                    if jb > 0:
                        prev = (ks, slice((jb - 1) * 128, jb * 128), V[0][ks][:, jb - 1, h, :])
                        b_V_prev = b_V[0][ks]
                    elif st > 0:
                        prev = (kp, slice(15 * 128, 16 * 128), V[0][kp][:, 15, h, :])
                        b_V_prev = b_V[0][kp]
                    block(qsl, cur, prev, [(slice(jb * 128, (jb + 1) * 128), slice(0, 128))])
                for n4 in range(4):
                    for r in range(4):
                        qsl = slice(n4 * 512 + r, (n4 + 1) * 512, 4)
                        b_V_cur = b_V[1][ks]
                        cur = V[1][ks][:, n4 * 4 + r, h, :]
                        prev = None
                        if n4 > 0:
                            prev = (ks, slice((n4 - 1) * 512 + r, n4 * 512, 4), V[1][ks][:, (n4 - 1) * 4 + r, h, :])
                            b_V_prev = b_V[1][ks]
                        elif st > 0:
                            prev = (kp, slice(3 * 512 + r, ST, 4), V[1][kp][:, 12 + r, h, :])
                            b_V_prev = b_V[1][kp]
                        block(qsl, cur, prev, [(qsl, slice(0, 128))])
                for r in range(16):
                    qsl = slice(r, ST, 16)
                    b_V_cur = b_V[2][ks]
                    cur = V[2][ks][:, r, h, :]
                    prev = None
                    if st > 0:
                        prev = (kp, qsl, V[2][kp][:, r, h, :])
                        b_V_prev = b_V[2][kp]
                    block(qsl, cur, prev, [(slice(b4 * 512 + r, (b4 + 1) * 512, 16), slice(32 * b4, 32 * b4 + 32))
                                           for b4 in range(4)])
                if pend:
                    pend.pop()()
                for b4 in range(4):
                    cs = slice(b4 * 512, (b4 + 1) * 512)
                    eng = "act" if b4 % 2 == 0 else "dve"
                    if eng == "act":
                        S.op("act", lambda e, cs=cs: e.copy(oacc[0:65, cs], pO[0:65, cs]), reads=[b_pO], writes=[b_oacc])
                    else:
                        S.op("dve", lambda e, cs=cs: e.tensor_copy(oacc[0:65, cs], pO[0:65, cs]), reads=[b_pO],
                             writes=[b_oacc])
                S.op("dve", lambda e: e.reciprocal(oacc[64:65, :], oacc[64:65, :]), reads=[b_oacc], writes=[b_oacc])
                for b4 in range(4):
                    cs = slice(b4 * 512, (b4 + 1) * 512)
                    S.mm(pO[0:64, cs], ones[64:65, 0:64], oacc[64:65, cs], reads=[b_ones, b_oacc], writes=[b_pO],
                         signal=(b4 == 3))
                os_ = (st * 4 + h) % 2
                for b4 in range(4):
                    cs = slice(b4 * 512, (b4 + 1) * 512)
                    S.op("dve", lambda e, cs=cs, os_=os_: e.tensor_tensor(oT[os_][:, cs], pO[0:64, cs], oacc[0:64, cs],
                                                                       ALU.mult),
                         reads=[b_pO, b_oacc], writes=[b_oT[os_]])
                for xk in range(ST // XCH):
                    S.dma("sp", D["omix"][st * (ST // XCH) + xk][h * 64:(h + 1) * 64, :],
                          oT[os_][:, xk * XCH:(xk + 1) * XCH], reads=[b_oT[os_]])
        S.barrier()
        S.run()


def _TT(o, a, b, op):
    return lambda e: e.tensor_tensor(o, a, b, op)


def _TS(o, a, s1, s2, op0, op1=None):
    if op1 is None:
        return lambda e: e.tensor_scalar(o, a, s1, s2, op0)
    return lambda e: e.tensor_scalar(o, a, s1, s2, op0, op1)


def _STT(o, a, s, b, op0, op1):
    return lambda e: e.scalar_tensor_tensor(o, a, s, b, op0, op1)


def _ACT(o, i, f, **kw):
    return lambda e: e.activation(o, i, f, **kw)


def _CP(o, i):
    return lambda e: e.copy(o, i)


def _TC(o, i):
    return lambda e: e.tensor_copy(o, i)


def _TR(o, i, ident):
    return lambda e: e.transpose(o, i, ident)


def _mul(o, i, m):
    return lambda e: e.mul(o, i, m)


def _RED(o, i):
    return lambda e: e.tensor_reduce(o, i, AX.X, ALU.add)


def phase_b(nc, S, D):
    STB = 1024
    NSTB = SEQ // STB
    TPS = STB // 128
    DS = DECAY_SCALE
    with ExitStack() as ph:
        def sb(n, shp, dt=F32):
            return ph.enter_context(nc.sbuf_tensor(n, shp, dt))

        def ps(n, shp, dt=F32):
            return ph.enter_context(nc.psum_tensor(n, shp, dt))

        identf = sb("b_identf", [128, 128])
        b_identf = Buf()
        S.dma("sp", identf[:], D["ident"], writes=[b_identf])
        identb = sb("b_identb", [128, 128], BF16)
        b_identb = Buf()
        S.dma("pool", identb[:], D["ident"], writes=[b_identb])
        cst = sb("b_cst", [128, 642])
        b_cst = Buf()
        S.dma("sp", cst[:], D["cstB"], writes=[b_cst])
        triI, triS, triA = cst[:, 0:128], cst[:, 128:256], cst[:, 256:384]
        chunkind = cst[:, 384:386]
        mask1, mask3, eye2 = cst[:, 386:514], cst[:, 514:578], cst[:, 578:642]
        vec = sb("b_vec", [128, 8, 256])
        b_vec = Buf()
        for i, nm in enumerate(("w0", "a0", "k_k", "k_a", "k_a", "r_k", "gn_g", "gn_b")):
            S.dma("sp", vec[:, i, :], bcast_rows(D[nm]), writes=[b_vec], acc=(i > 0))
        S.op("dve", _TS(vec[:, 4, :], vec[:, 4, :], -1.0, 1.0, ALU.mult, ALU.add), reads=[b_vec], writes=[b_vec])
        bias_wa = vec[:, 0:2, :].rearrange("p a b -> p (a b)")
        wdec = sb("b_wdec", [128, 512])
        b_wdec = Buf()
        S.op("dve", lambda e: e.memset(wdec[:], 0.0), writes=[b_wdec])
        S.dma("sp", wdec[0:64, 0:256], D["w_dec"], writes=[b_wdec], acc=True)
        S.dma("sp", wdec[64:128, 256:512], D["w_aaa"], writes=[b_wdec], acc=True)
        wgate = sb("b_wgate", [128, 256])
        b_wgate = Buf()
        S.dma("sp", wgate[:], D["w_gate"], writes=[b_wgate])
        epsg = sb("b_epsg", [128, 1])
        b_epsg = Buf()
        S.op("dve", lambda e: e.memset(epsg[:], GN_EPS), writes=[b_epsg])
        mub = sb("b_mub", [128, 1024])
        b_mub = Buf()
        S.dma("sp", mub[:], bcast_rows(D["mu"]), writes=[b_mub])
        omub = sb("b_omub", [128, 1024])
        b_omub = Buf()
        S.op("dve", _TS(omub[:], mub[:], -1.0, 1.0, ALU.mult, ALU.add), reads=[b_mub], writes=[b_omub])
        W1 = sb("b_W1", [128, 8, 1024], BF16)
        W2 = sb("b_W2", [128, 8, 1024], BF16)
        b_W = Buf()
        wst = [sb("b_wst%d" % i, [128, 1024]) for i in range(2)]
        b_wst = [Buf() for _ in range(2)]
        for c in range(8):
            S.dma("sp", wst[c % 2][:], D["w_rw"][c * 128:(c + 1) * 128, :], writes=[b_wst[c % 2]])
            S.op("dve", _TT(W1[:, c, :], wst[c % 2][:], omub[:], ALU.mult), reads=[b_wst[c % 2], b_omub], writes=[b_W])
            S.op("pool", _TT(W2[:, c, :], wst[c % 2][:], mub[:], ALU.mult), reads=[b_wst[c % 2], b_mub], writes=[b_W])

        xTv = D["xT"].rearrange("(c p) t -> p c t", p=128)
        xc = sb("b_xc", [128, 8, STB], BF16)
        xp = sb("b_xp", [128, 8, STB], BF16)
        b_xc = Buf()
        b_xp = Buf()

        Hs = sb("b_Hs", [128, 4, 64], BF16)
        b_H = Buf()
        S.op("dve", lambda e: e.memset(Hs[:], 0.0), writes=[b_H])

        def t256(n):
            return sb(n, [128, 256]), Buf()
        tl, b_tl = sb("b_tl", [128, 256]), Buf()
        lT, b_lT = sb("b_lT", [128, 256]), Buf()
        lg, b_lg = sb("b_lg", [128, 512]), Buf()
        sg, b_sg = sb("b_sg", [128, 512]), Buf()
        gs, b_gs = t256("b_gs")
        yA = sb("b_yA", [128, 512])
        b_rs = Buf()
        rs = yA[:, 0:256]
        krs = yA[:, 256:512]
        vs, b_vs = sb("b_vs", [128, 256], BF16), Buf()
        kk, b_kk = t256("b_kk")
        t1, b_t1 = t256("b_t1")
        t2, b_t2 = t256("b_t2")
        km, b_km = t256("b_km")
        bb, b_bb = t256("b_bb")
        E, b_E = sb("b_E", [128, 4, 256]), [Buf() for _ in range(4)]
        X4, b_X4 = sb("b_X4", [128, 4, 256], BF16), Buf()
        BK, b_BK = sb("b_BK", [128, 2, 256], BF16), Buf()
        XT, b_XT = sb("b_XT", [128, 4, 4, 64], BF16), Buf()
        dgP, b_dgP = sb("b_dgP", [128, 4, 64], BF16), Buf()
        AM1, b_AM1 = sb("b_AM1", [128, 4, 128], BF16), Buf()
        AM2, b_AM2 = sb("b_AM2", [128, 4, 128], BF16), Buf()
        Lm = [sb("b_L%d" % i, [128, 4, 2, 64], BF16) for i in range(2)]
        b_Lm = [Buf() for _ in range(2)]
        Pm, b_Pm = sb("b_Pm", [128, 4, 64], BF16), Buf()
        sm, b_sm = sb("b_sm", [128, 32]), Buf()
        PC, b_PC = sb("b_PC", [128, 4, 2]), Buf()
        Ws, b_Ws = sb("b_Ws", [128, 256], BF16), Buf()
        Us, b_Us = sb("b_Us", [128, 256], BF16), Buf()
        on, b_on = t256("b_on")
        gst, b_gst = sb("b_gst", [128, 4, 6]), Buf()
        gmv, b_gmv = sb("b_gmv", [128, 12]), Buf()
        orT = sb("b_orT", [128, 2, STB], BF16)
        b_orT = Buf()

        K = [ps("b_K%d" % i, [128, 512]) for i in range(8)]
        b_K = [Buf(psum=True) for _ in range(8)]

        def v3(ap):
            return ap.rearrange("p (h d) -> p h d", d=64)

        def bc(ap4):
            return ap4.unsqueeze(2).to_broadcast([128, 4, 64])

        def dbl(name, shp, dt=F32):
            return [sb("%s_%d" % (name, i), shp, dt) for i in range(2)], [Buf() for _ in range(2)]
        AM1d, b_AM1d = dbl("b_AM1d", [128, 4, 128], BF16)
        AM2d, b_AM2d = dbl("b_AM2d", [128, 4, 128], BF16)
        Pmd, b_Pmd = dbl("b_Pmd", [128, 4, 64], BF16)
        XTd, b_XTd = dbl("b_XTd", [128, 4, 4, 64], BF16)
        def trp(name, shp, dt=F32):
            return [sb("%s_%d" % (name, i), shp, dt) for i in range(3)], [Buf() for _ in range(3)]
        BKd, b_BKd = trp("b_BKd", [128, 2, 256], BF16)
        vsd, b_vsd = trp("b_vsd", [128, 256], BF16)
        dgPd, b_dgPd = dbl("b_dgPd", [128, 4, 64], BF16)
        gsd, b_gsd = trp("b_gsd", [128, 256])
        bsd, b_bsd = trp("b_bsd", [128, 4])
        X4d, b_X4d = dbl("b_X4d", [128, 4, 256], BF16)
        PCd, b_PCd = dbl("b_PCd", [128, 4, 2])
        smG, b_smG = sb("b_smG", [128, 16]), Buf()
        tG, b_tG = sb("b_tG", [128, 256]), Buf()

        def front1(n):
            st, j = n // TPS, n % TPS
            d, t3 = n % 2, n % 3
            BK, b_BK, vs, b_vs, gs, b_gs, bs, b_bs = BKd[t3], b_BKd[t3], vsd[t3], b_vsd[t3], gsd[t3], b_gsd[t3], bsd[t3], b_bsd[t3]
            X4, b_X4, PC, b_PC = X4d[d], b_X4d[d], PCd[d], b_PCd[d]
            if j == 0:
                S.dma("pool", xc[:], xTv[:, :, st * STB:(st + 1) * STB], writes=[b_xc])
                if st == 0:
                    S.op("dve", lambda e: e.memset(xp[:, :, 0:1], 0.0), writes=[b_xp])
                    S.dma("pool", xp[:, :, 1:STB], xTv[:, :, 0:STB - 1], writes=[b_xp], acc=True)
                else:
                    S.dma("pool", xp[:], xTv[:, :, st * STB - 1:(st + 1) * STB - 1], writes=[b_xp])
            tsl = slice(j * 128, (j + 1) * 128)
            for bk in range(2):
                cs = slice(bk * 512, (bk + 1) * 512)
                for c in range(8):
                    S.mm(K[bk][:], xc[:, c, tsl], W1[:, c, cs], reads=[b_xc, b_W], writes=[b_K[bk]],
                         start=(c == 0), stop=False, signal=False)
                for c in range(8):
                    S.mm(K[bk][:], xp[:, c, tsl], W2[:, c, cs], reads=[b_xp, b_W], writes=[b_K[bk]],
                         start=False, stop=(c == 7), signal=(c == 7))
            yield
            pA, pB = K[0], K[1]
            S.op("act", _ACT(tl[:, 0:64], pB[:, 256:320], AF.Tanh), reads=[b_K[1]], writes=[b_tl])
            S.op("act", _CP(tl[:, 64:128], pB[:, 320:384]), reads=[b_K[1]], writes=[b_tl])
            S.op("act", _ACT(tl[:, 128:256], pB[:, 384:512], AF.Sigmoid), reads=[b_K[1]], writes=[b_tl])
            S.op("act", _CP(vs[:], pB[:, 0:256]), reads=[b_K[1]], writes=[b_vs])
            S.op("act", _CP(yA[:], pA[:]), reads=[b_K[0]], writes=[b_rs])
            yield
            S.op("pe", _TR(K[3][:, 0:128], tl[:, 0:128], identf[:]), reads=[b_tl, b_identf], writes=[b_K[3]], signal=False)
            S.op("pe", _TR(K[3][:, 128:256], tl[:, 128:256], identf[:]), reads=[b_tl, b_identf], writes=[b_K[3]])
            yield
            S.op("dve", _TC(lT[:], K[3][:, 0:256]), reads=[b_K[3]], writes=[b_lT])
            yield
            S.mm(K[2][:], lT[:, 0:128], wdec[:], reads=[b_lT, b_wdec], writes=[b_K[2]], signal=False)
            S.mm(K[3][:, 256:512], lT[:, 128:256], wgate[:], reads=[b_lT, b_wgate], writes=[b_K[3]])
            yield
            S.op("dve", _TT(lg[:], K[2][:], bias_wa, ALU.add), reads=[b_K[2], b_vec], writes=[b_lg])
            S.op("act", _ACT(sg[:], lg[:], AF.Sigmoid), reads=[b_lg], writes=[b_sg])
            S.op("act", _CP(gs[:], K[3][:, 256:512]), reads=[b_K[3]], writes=[b_gs])
            sw, aa = sg[:, 0:256], sg[:, 256:512]
            yield
            S.mm(K[2][:, 0:256], triS, sw, reads=[b_cst, b_sg], writes=[b_K[2]], signal=False)
            S.mm(K[2][:, 256:512], triI, sw, reads=[b_cst, b_sg], writes=[b_K[2]], signal=False)
            S.mm(K[3][:, 0:256], triA, sw, reads=[b_cst, b_sg], writes=[b_K[3]], signal=False)
            for c in range(2):
                rows = slice(64 * c, 64 * c + 64)
                for h in range(4):
                    S.mm(K[3][rows, 256 + 2 * h:258 + 2 * h], sg[rows, h * 64:(h + 1) * 64], cst[rows, 384:386],
                         reads=[b_cst, b_sg], writes=[b_K[3]], signal=(c == 1 and h == 3))
            yield
            S.op("act", _ACT(E[:, 0, :], K[2][:, 0:256], AF.Exp, scale=-DS), reads=[b_K[2]], writes=[b_E[0]])
            S.op("act", _ACT(E[:, 1, :], K[2][:, 256:512], AF.Exp, scale=-DS), reads=[b_K[2]], writes=[b_E[1]])
            S.op("act", _ACT(E[:, 2, :], K[2][:, 256:512], AF.Exp, scale=DS), reads=[b_K[2]], writes=[b_E[2]])
            S.op("act", _ACT(E[:, 3, :], K[3][:, 0:256], AF.Exp, scale=-DS), reads=[b_K[3]], writes=[b_E[3]])
            S.op("act", _ACT(PC[:], K[3][:, 256:264], AF.Exp, scale=-DS), reads=[b_K[3]], writes=[b_PC])
            S.op("dve", _TT(kk[:], krs, vec[:, 2, :], ALU.mult), reads=[b_rs, b_vec], writes=[b_kk])
            S.op("pool", _TT(t1[:], kk[:], kk[:], ALU.mult), reads=[b_kk], writes=[b_t1])
            S.op("pool", _TT(t2[:], aa, vec[:, 3, :], ALU.mult), reads=[b_sg, b_vec], writes=[b_t2])
            S.op("pool", _TT(t2[:], t2[:], vec[:, 4, :], ALU.add), reads=[b_t2, b_vec], writes=[b_t2])
            yield
            S.op("dve", lambda e: e.tensor_reduce(sm[:, 0:4], v3(t1[:]), AX.X, ALU.add), reads=[b_t1], writes=[b_sm])
            S.op("act", _ACT(sm[:, 4:8], sm[:, 0:4], AF.Sqrt), reads=[b_sm], writes=[b_sm])
            S.op("dve", _TT(km[:], krs, t2[:], ALU.mult), reads=[b_rs, b_t2], writes=[b_km])
            yield
            S.op("dve", _TS(sm[:, 4:8], sm[:, 4:8], L2_EPS, None, ALU.max), reads=[b_sm], writes=[b_sm])
            S.op("dve", lambda e: e.reciprocal(sm[:, 8:12], sm[:, 4:8]), reads=[b_sm], writes=[b_sm])
            S.op("dve", _TT(v3(kk[:]), v3(kk[:]), bc(sm[:, 8:12]), ALU.mult), reads=[b_kk, b_sm], writes=[b_kk])
            S.op("pool", _TT(t1[:], rs, km[:], ALU.mult), reads=[b_rs, b_km, b_sm], writes=[b_t1])
            S.op("pool", _TT(t1[:], t1[:], vec[:, 5, :], ALU.mult), reads=[b_t1, b_vec], writes=[b_t1])
            yield
            S.op("pool", _TT(bb[:], kk[:], aa, ALU.mult), reads=[b_kk, b_sg], writes=[b_bb])
            S.op("dve", lambda e: e.tensor_reduce(bs[:], v3(t1[:]), AX.X, ALU.add), reads=[b_t1], writes=[b_bs])
            S.op("dve", _STT(X4[:, 0, :], kk[:], -1.0, E[:, 0, :], ALU.mult, ALU.mult), reads=[b_kk, b_E[0]], writes=[b_X4])
            S.op("pool", _TT(X4[:, 1, :], rs, E[:, 1, :], ALU.mult), reads=[b_rs, b_E[1]], writes=[b_X4])
            yield
            S.op("dve", _TT(X4[:, 2, :], bb[:], E[:, 2, :], ALU.mult), reads=[b_bb, b_E[2]], writes=[b_X4])
            S.op("pool", _TT(X4[:, 3, :], km[:], E[:, 2, :], ALU.mult), reads=[b_km, b_E[2]], writes=[b_X4])
            S.op("dve", _TT(BK[:, 0, :], bb[:], E[:, 3, :], ALU.mult), reads=[b_bb, b_E[3]], writes=[b_BK])
            S.op("pool", _TT(BK[:, 1, :], km[:], E[:, 3, :], ALU.mult), reads=[b_km, b_E[3]], writes=[b_BK])

        def front2(n):
            d = n % 2
            AM1, b_AM1, AM2, b_AM2 = AM1d[d], b_AM1d[d], AM2d[d], b_AM2d[d]
            Pm, b_Pm, XT, b_XT, dgP, b_dgP = Pmd[d], b_Pmd[d], XTd[d], b_XTd[d], dgPd[d], b_dgPd[d]
            X4, b_X4, PC, b_PC = X4d[d], b_X4d[d], PCd[d], b_PCd[d]
            for c in range(2):
                rows = slice(64 * c, 64 * c + 64)
                for q in range(4):
                    for h in range(4):
                        blk = q * 4 + h
                        S.mm(K[4 + blk // 8][rows, (blk % 8) * 64:(blk % 8 + 1) * 64],
                             X4[rows, q, h * 64:(h + 1) * 64], identb[rows, rows],
                             reads=[b_X4, b_identb], writes=[b_K[4 + blk // 8]], signal=(c == 1 and blk % 8 == 7))
            yield
            XTf = XT[:].rearrange("p q h t -> p (q h t)")
            S.op("act", _CP(XTf[:, 0:512], K[4][:]), reads=[b_K[4]], writes=[b_XT])
            S.op("dve", _TC(XTf[:, 512:1024], K[5][:]), reads=[b_K[5]], writes=[b_XT])
            for c in range(2):
                rows = slice(64 * c, 64 * c + 64)
                S.op("pool", _TT(dgP[rows, :, :], cst[rows, 578:642].unsqueeze(1).to_broadcast([64, 4, 64]),
                                 PC[rows, :, c:c + 1].to_broadcast([64, 4, 64]), ALU.mult),
                     reads=[b_PC, b_cst], writes=[b_dgP])
            yield
            for c in range(2):
                rows = slice(64 * c, 64 * c + 64)
                for h in range(4):
                    last = (c == 1 and h == 3)
                    S.mm(K[4][rows, h * 128:(h + 1) * 128], XT[rows, 2, h, :], XT[rows, 0:2, h, :],
                         reads=[b_XT], writes=[b_K[4]], signal=False)
                    S.mm(K[5][rows, h * 128:(h + 1) * 128], XT[rows, 3, h, :], XT[rows, 0:2, h, :],
                         reads=[b_XT], writes=[b_K[5]], signal=last)
            yield
            m1b = mask1.unsqueeze(1).to_broadcast([128, 4, 128])
            S.op("dve", _TT(AM1[:], K[4][:].rearrange("p (h t) -> p h t", h=4), m1b, ALU.mult),
                 reads=[b_K[4], b_cst], writes=[b_AM1])
            S.op("pool", _TC(Lm[0][:, :, 0, :], AM1[:, :, 0:64]), reads=[b_AM1], writes=[b_Lm[0]])
            S.op("pool", _TT(Pm[:], AM1[:, :, 0:64], eye2.unsqueeze(1).to_broadcast([128, 4, 64]), ALU.add),
                 reads=[b_AM1, b_cst], writes=[b_Pm])
            yield
            for c in range(2):
                rows = slice(64 * c, 64 * c + 64)
                for h in range(4):
                    S.mm(K[4][rows, h * 64:(h + 1) * 64], XT[rows, 0, h, :], XT[rows, 2, h, :],
                         reads=[b_XT], writes=[b_K[4]], signal=(c == 1 and h == 3))
            S.op("dve", _TT(AM2[:], K[5][:].rearrange("p (h t) -> p h t", h=4), m1b, ALU.mult),
                 reads=[b_K[5], b_cst], writes=[b_AM2])
            yield
            m3b = mask3.unsqueeze(1).to_broadcast([128, 4, 64])
            S.op("dve", _TT(Lm[0][:, :, 1, :], K[4][:, 0:256].rearrange("p (h t) -> p h t", h=4), m3b, ALU.mult),
                 reads=[b_K[4], b_cst], writes=[b_Lm[0]])
            yield
            cur = 0
            for rnd in range(6):
                nxt = 1 - cur
                do_sq = rnd < 5
                do_p = rnd >= 1
                for c in range(2):
                    rows = slice(64 * c, 64 * c + 64)
                    for h in range(4):
                        last = (c == 1 and h == 3)
                        Lc, LTc = Lm[cur][rows, h, 0, :], Lm[cur][rows, h, 1, :]
                        if do_sq and rnd < 4:
                            S.mm(K[5][rows, (h * 2) * 64:(h * 2 + 1) * 64], LTc, Lc, reads=[b_Lm[cur]],
                                 writes=[b_K[5]], signal=False)
                        if do_sq:
                            S.mm(K[5][rows, (h * 2 + 1) * 64:(h * 2 + 2) * 64], Lc, LTc, reads=[b_Lm[cur]],
                                 writes=[b_K[5]], signal=(last and not do_p))
                        if do_p:
                            S.mm(K[4][rows, 256 + h * 64:256 + (h + 1) * 64], LTc, Pm[rows, h, :],
                                 reads=[b_Lm[cur], b_Pm], writes=[b_K[4]], signal=last)
                yield
                src = K[5][:].rearrange("p (h a t) -> p h a t", h=4, a=2)
                if do_sq:
                    if rnd < 4:
                        S.op("act", _CP(Lm[nxt][:], src), reads=[b_K[5]], writes=[b_Lm[nxt]])
                    else:
                        S.op("act", _CP(Lm[nxt][:, :, 1, :], src[:, :, 1, :]), reads=[b_K[5]], writes=[b_Lm[nxt]])
                if do_p:
                    S.op("dve", _TT(Pm[:], K[4][:, 256:512].rearrange("p (h t) -> p h t", h=4), Pm[:], ALU.add),
                         reads=[b_K[4], b_Pm], writes=[b_Pm])
                cur = nxt
                yield

        def back(n):
            st, j = n // TPS, n % TPS
            d = n % 2
            tsl = slice(j * 128, (j + 1) * 128)
            t3 = n % 3
            AM1, b_AM1, AM2, b_AM2 = AM1d[d], b_AM1d[d], AM2d[d], b_AM2d[d]
            Pm, b_Pm, XT, b_XT, BK, b_BK = Pmd[d], b_Pmd[d], XTd[d], b_XTd[d], BKd[t3], b_BKd[t3]
            vs, b_vs, dgP, b_dgP, gs, b_gs, bs, b_bs = vsd[t3], b_vsd[t3], dgPd[d], b_dgPd[d], gsd[t3], b_gsd[t3], bsd[t3], b_bsd[t3]
            pO_ = K[7]
            for c in range(2):
                rows = slice(64 * c, 64 * c + 64)
                orow = slice(64 * (1 - c), 64 * (1 - c) + 64)
                for h in range(4):
                    hc = slice(256 + h * 64, 256 + (h + 1) * 64)
                    vh = vs[rows, h * 64:(h + 1) * 64]
                    S.mm(K[6][rows, hc], AM2[rows, h, 0:64], vh, reads=[b_AM2, b_vs], writes=[b_K[6]],
                         start=True, stop=False, signal=False)
                    S.mm(K[6][rows, hc], XT[rows, 0, h, :], Hs[rows, h, :], reads=[b_XT, b_H], writes=[b_K[6]],
                         start=False, stop=True, signal=(h == 3))
                yield
                S.op("dve", _TC(Ws[rows, :], K[6][rows, 256:512]), reads=[b_K[6]], writes=[b_Ws])
                yield
                for h in range(4):
                    hc = slice(h * 64, (h + 1) * 64)
                    S.mm(K[6][rows, hc], Pm[rows, h, :], Ws[rows, hc], reads=[b_Pm, b_Ws], writes=[b_K[6]],
                         signal=(h == 3))
                yield
                S.op("act", _CP(Us[rows, :], K[6][rows, 0:256]), reads=[b_K[6]], writes=[b_Us])
                yield
                for h in range(4):
                    hc = slice(256 + h * 64, 256 + (h + 1) * 64)
                    vh = vs[rows, h * 64:(h + 1) * 64]
                    uh = Us[rows, h * 64:(h + 1) * 64]
                    S.mm(pO_[rows, hc], AM2[rows, h, 64:128], vh, reads=[b_AM2, b_vs], writes=[b_K[7]],
                         start=True, stop=False, signal=False)
                    S.mm(pO_[rows, hc], XT[rows, 1, h, :], Hs[rows, h, :], reads=[b_XT, b_H], writes=[b_K[7]],
                         start=False, stop=False, signal=False)
                    S.mm(pO_[rows, hc], AM1[rows, h, 64:128], uh, reads=[b_AM1, b_Us], writes=[b_K[7]],
                         start=False, stop=True, signal=False)
                for h in range(4):
                    oc_ = slice(256 + h * 64, 256 + (h + 1) * 64)
                    vh = vs[rows, h * 64:(h + 1) * 64]
                    uh = Us[rows, h * 64:(h + 1) * 64]
                    S.mm(K[6][orow, oc_], BK[rows, 1, h * 64:(h + 1) * 64], vh, reads=[b_BK, b_vs], writes=[b_K[6]],
                         start=True, stop=False, signal=False)
                    S.mm(K[6][orow, oc_], dgP[rows, h, :], Hs[rows, h, :], reads=[b_dgP, b_H], writes=[b_K[6]],
                         start=False, stop=False, signal=False)
                    S.mm(K[6][orow, oc_], BK[rows, 0, h * 64:(h + 1) * 64], uh, reads=[b_BK, b_Us], writes=[b_K[6]],
                         start=False, stop=True, signal=(h == 3))
                yield
                S.op("act", _CP(Hs[orow, :, :], K[6][orow, 256:512].rearrange("p (h v) -> p h v", h=4)),
                     reads=[b_K[6], b_K[7]], writes=[b_H])
                yield
            o3 = pO_[:, 256:512].rearrange("p (h d) -> p h d", d=64)
            S.op("act", _CP(on[:], pO_[:, 256:512]), reads=[b_K[7]], writes=[b_on])
            yield
            S.op("act", _ACT(tG[:], on[:], AF.Square), reads=[b_on], writes=[b_tG])
            S.op("dve", lambda e: e.tensor_reduce(smG[:, 0:4], v3(on[:]), AX.X, ALU.add), reads=[b_on], writes=[b_smG])
            yield
            S.op("dve", lambda e: e.tensor_reduce(smG[:, 4:8], v3(tG[:]), AX.X, ALU.add), reads=[b_tG], writes=[b_smG])
            S.op("dve", _TS(gmv[:, 0:4], smG[:, 0:4], 1.0 / 64.0, None, ALU.mult), reads=[b_smG], writes=[b_gmv])
            S.op("dve", _TT(gmv[:, 4:8], gmv[:, 0:4], gmv[:, 0:4], ALU.mult), reads=[b_gmv], writes=[b_gmv])
            S.op("dve", _STT(gmv[:, 8:12], smG[:, 4:8], 1.0 / 64.0, gmv[:, 4:8], ALU.mult, ALU.subtract),
                 reads=[b_smG, b_gmv], writes=[b_gmv])
            yield
            S.op("act", _ACT(smG[:, 8:12], gmv[:, 8:12], AF.Sqrt, bias=epsg[:, 0:1], scale=1.0),
                 reads=[b_gmv, b_epsg], writes=[b_smG])
            S.op("dve", lambda e: e.reciprocal(smG[:, 12:16], smG[:, 8:12]), reads=[b_smG], writes=[b_smG])
            S.op("dve", _TT(v3(on[:]), v3(on[:]), bc(gmv[:, 0:4]), ALU.subtract), reads=[b_on, b_gmv], writes=[b_on])
            yield
            S.op("pool", _TT(v3(on[:]), v3(on[:]), bc(smG[:, 12:16]), ALU.mult), reads=[b_on, b_smG], writes=[b_on])
            S.op("pool", _TT(on[:], on[:], vec[:, 6, :], ALU.mult), reads=[b_on, b_vec], writes=[b_on])
            S.op("pool", _TT(on[:], on[:], vec[:, 7, :], ALU.add), reads=[b_on, b_vec], writes=[b_on])
            S.op("dve", _TT(v3(tG[:]), v3(vs[:]), bc(bs[:]), ALU.mult), reads=[b_vs, b_bs, b_tG], writes=[b_tG])
            yield
            S.op("pool", _TT(on[:], on[:], tG[:], ALU.add), reads=[b_on, b_tG], writes=[b_on])
            S.op("pool", _TT(on[:], on[:], gs[:], ALU.mult), reads=[b_on, b_gs], writes=[b_on])
            yield
            for hp in range(2):
                S.op("pe", _TR(K[7][:, hp * 128:(hp + 1) * 128], on[:, hp * 128:(hp + 1) * 128], identf[:]),
                     reads=[b_on, b_identf], writes=[b_K[7]], signal=(hp == 1))
            yield
            S.op("act", _CP(orT[:, :, tsl], K[7][:, 0:256].rearrange("p (a t) -> p a t", a=2)),
                 reads=[b_K[7]], writes=[b_orT])
            if j == TPS - 1 or n == B_TILES - 1:
                b_om = Buf()
                for hp in range(2):
                    S.dma("sp", D["omix"][st][256 + hp * 128:256 + (hp + 1) * 128, :],
                          orT[:, hp, :], reads=[b_orT], writes=[b_om], acc=(hp == 1))
                if D.get("ccs") is not None:
                    S.raw("pool", lambda e, k=st: e.collective_compute(
                        "AllGather", ALU.bypass, replica_groups=XGROUPS, ins=[D["omix"][k]],
                        outs=[D["G"][k]]).then_inc(D["ccs"], 1), reads=[b_om])

        for n in range(B_TILES + 2):
            gens = []
            if n < B_TILES:
                gens.append(front1(n))
            if 0 <= n - 1 < B_TILES:
                gens.append(front2(n - 1))
            if 0 <= n - 2 < B_TILES:
                gens.append(back(n - 2))
            while gens:
                for g_ in list(gens):
                    try:
                        next(g_)
                    except StopIteration:
                        gens.remove(g_)
        S.barrier()
        S.run()


def exchange(nc, S, D, stack):
    S.extra.append((D["ccs"], NXCH, "s_cc", "cc"))
    S.barrier()
    S.run()


def build_program(phases="ABC", exch=True):
    nc = bass.Bass("TRN2", target_bir_lowering=False)
    D = {}

    def din(name, shape, dt=F32):
        D[name] = nc.dram_tensor(name, list(shape), dt, kind="ExternalInput").ap()

    din("ident", [128, 128])
    if "A" in phases or "B" in phases:
        din("xT", [D_MODEL, SEQ])
    if "A" in phases:
        din("pos", [128, 64], I32)
        din("w_att", [D_MODEL, 768])
        din("maskT", [128, 256])
    if "B" in phases:
        din("w_rw", [D_MODEL, 1024])
        din("mu", [1, 1024])
        for nm in ("w0", "a0", "k_k", "k_a", "r_k", "gn_g", "gn_b"):
            din(nm, [1, 256])
        din("w_dec", [64, 256])
        din("w_aaa", [64, 256])
        din("w_gate", [128, 256])
        din("cstB", [128, 642])
    if "C" in phases:
        din("xres", [TOKH, D_MODEL])
        din("w_out", [1024, 1024])
        din("wg", [1024, FFN])
        din("wu", [1024, FFN])
        din("wd", [FFN, 1024])
        for nm in ("ln1g", "ln1b", "ln2g", "ln2b"):
            din(nm, [1, 1024])
        din("sel", [128, 2])
        D["out"] = nc.dram_tensor("out", [TOKH, D_MODEL], F32, kind="ExternalOutput").ap()
    full = exch
    if full:
        D["omix"] = [nc.dram_tensor("omix%d" % k, [512, XCH], BF16, kind="Internal").ap() for k in range(NXCH)]
        D["G"] = [nc.dram_tensor("G%d" % k, [1024, XCH], BF16, kind="Internal").ap() for k in range(NXCH)]
    else:
        if "C" in phases:
            din("G", [NXCH, 1024, XCH], BF16)
            D["G"] = [D["G"][k] for k in range(NXCH)]
        if "A" in phases or "B" in phases:
            om = nc.dram_tensor("omix", [NXCH, 512, XCH], BF16, kind="ExternalOutput").ap()
            D["omix"] = [om[k] for k in range(NXCH)]
    with ExitStack() as st:
        S = Sched(nc, st)
        D["ccs"] = st.enter_context(nc.semaphore("s_cc")) if full else None
        if "C" in phases:
            D["wgs"] = nc.dram_tensor("wgs", [HC, 128, 2, 8, 128], BF16, kind="Internal").ap()
            prep_ffn_weights(nc, S, D)
        if "A" in phases:
            phase_a(nc, S, D)
        if "B" in phases:
            phase_b(nc, S, D)
        if full:
            exchange(nc, S, D, st)
        if "C" in phases:
            phase_c(nc, S, D)
    return nc


def att_mask():
    i_k = np.arange(128)[:, None]
    i_q = np.arange(128)[None, :]
    m = np.zeros((128, 256), np.float32)
    m[:, 0:128] = np.where(i_k >= i_q, 0.0, NEG)
    m[:, 128:256] = np.where(i_k <= i_q, 0.0, NEG)
    return m


def rwkv_consts():
    j = np.arange(128)[:, None]
    t = np.arange(128)[None, :]
    same = (j // 64) == (t // 64)
    c = np.zeros((128, 642), np.float32)
    c[:, 0:128] = same & (j <= t)
    c[:, 128:256] = same & (j < t)
    c[:, 256:384] = same & (j > t)
    c[:, 384] = (np.arange(128) < 64)
    c[:, 385] = (np.arange(128) >= 64)
    jj = (np.arange(128) % 64)[:, None]
    tt = np.arange(64)[None, :]
    c[:, 386:450] = jj < tt
    c[:, 450:514] = jj <= tt
    c[:, 514:578] = tt < jj
    c[:, 578:642] = jj == tt
    return c


def core_inputs(inp, c, phases="ABC"):
    b, g = c // 2, c % 2
    m = {"ident": np.eye(128, dtype=np.float32)}
    w_in = inp["w_in"][0]
    if "A" in phases or "B" in phases:
        m["xT"] = np.ascontiguousarray(inp["x"][b].T)
    if "A" in phases:
        m["pos"] = np.ascontiguousarray(inp["positions"][b].reshape(64, 128).T).astype(np.int32)
        cols = np.concatenate([np.arange(256 * g, 256 * g + 256) + off for off in (0, 512, 1024)])
        m["w_att"] = np.ascontiguousarray(w_in[:, cols])
        m["maskT"] = att_mask()
    if "B" in phases:
        hs = slice(256 * g, 256 * g + 256)
        rcols = np.concatenate([1536 + off + np.arange(256 * g, 256 * g + 256) for off in (0, 512, 1024)]
                               + [1536 + 1536 + np.arange(256)])
        m["w_rw"] = np.ascontiguousarray(w_in[:, rcols])
        m["mu"] = np.ascontiguousarray(inp["mu_shift"][0][rcols - 1536][None, :])
        for nm in ("w0", "a0", "k_k", "k_a", "gn_g", "gn_b"):
            m[nm] = np.ascontiguousarray(inp[nm][0][hs][None, :])
        m["r_k"] = np.ascontiguousarray(inp["r_k"][0][4 * g:4 * g + 4].reshape(1, 256))
        m["w_dec"] = np.ascontiguousarray(inp["w_decay_up"][0][:, hs])
        m["w_aaa"] = np.ascontiguousarray(inp["w_aaa_up"][0][:, hs])
        m["w_gate"] = np.ascontiguousarray(inp["w_gate_up"][0][:, hs])
        m["cstB"] = rwkv_consts()
    if "C" in phases:
        fi = lambda r: np.concatenate([np.arange(256 * r, 256 * r + 256), 512 + np.arange(256 * r, 256 * r + 256)])
        perm = np.concatenate([fi(0), fi(1)])
        sel = np.zeros((128, 2), np.float32)
        sel[:, g] = 1.0
        m.update(xres=np.ascontiguousarray(inp["x"][b, g * TOKH:(g + 1) * TOKH]),
                 w_out=np.ascontiguousarray(inp["w_out"][0][perm]),
                 wg=inp["w_ffn_gate"][0], wu=inp["w_ffn_up"][0], wd=inp["w_ffn_down"][0],
                 ln1g=inp["ln_mix_g"], ln1b=inp["ln_mix_b"], ln2g=inp["ln_ffn_g"], ln2b=inp["ln_ffn_b"],
                 sel=sel)
    return m


_NC_CACHE = {}


def kernel(**inputs):
    inp = {k: np.asarray(v) for k, v in inputs.items()}
    if "nc" not in _NC_CACHE:
        _NC_CACHE["nc"] = build_program("ABC")
    nc = _NC_CACHE["nc"]
    in_maps = [core_inputs(inp, c, "ABC") for c in range(8)]
    res = run_bass_kernel_spmd(nc, in_maps, core_ids=list(range(8)))
    out = np.empty((BATCH, SEQ, D_MODEL), np.float32)
    for c in range(8):
        b, g = c // 2, c % 2
        out[b, g * TOKH:(g + 1) * TOKH] = np.asarray(res.results[c]["out"], dtype=np.float32)
    return out
```

```python
import math
from contextlib import ExitStack

import numpy as np
import concourse.bass as bass
import concourse.mybir as mybir
from concourse.bass_utils import run_bass_kernel_spmd

F32 = mybir.dt.float32
BF16 = mybir.dt.bfloat16
I32 = mybir.dt.int32
AF = mybir.ActivationFunctionType
ALU = mybir.AluOpType
AX = mybir.AxisListType

D_MODEL = 1024
SEQ = 8192
BATCH = 4
HD = 64
FFN = 2816
HC = FFN // 128
ALPHA = 2.0 ** 0.25
LN_EPS = 1e-5
GN_EPS = 64e-5
L2_EPS = 1e-6
DECAY_SCALE = math.exp(-0.5)
NEG = -30000.0
ROPE_THETA = 500000.0
TOKH = SEQ // 2
XCH = 1024
NXCH = SEQ // XCH
XGROUPS = [[0, 1], [2, 3], [4, 5], [6, 7]]
B_TILES = SEQ // 128


class Buf:
    __slots__ = ("name", "w", "r", "psum")

    def __init__(self, name="", psum=False):
        self.name = name
        self.w = []
        self.r = {}
        self.psum = psum


class Sched:
    ENG = ("pe", "act", "dve", "pool", "sp")
    NDS = 8

    def __init__(self, nc, stack):
        self.nc = nc
        self.q = {e: [] for e in self.ENG}
        self.sem = {e: stack.enter_context(nc.semaphore("s_" + e)) for e in self.ENG}
        self.cnt = {e: 0 for e in self.ENG}
        self.seen = {e: {} for e in self.ENG}
        self.lastop = {e: None for e in self.ENG}
        self.dq = ("sp", "act", "pool")
        self.dsem = {e: [stack.enter_context(nc.semaphore("d_%s%d" % (e, i)))
                         for i in range(self.NDS)] for e in self.dq}
        self.dcnt = {e: 0 for e in self.dq}
        self.dlast = {e: [None] * self.NDS for e in self.dq}
        self.extra = []

    def _force(self, prod):
        rec = self.lastop[prod]
        assert rec is not None and not rec[2]
        rec[2] = True
        self.cnt[prod] += 1

    def _wait(self, eng, tok):
        if tok is None:
            return
        sem, val, key, prod = tok
        if prod in self.cnt and val > self.cnt[prod]:
            self._force(prod)
            assert val <= self.cnt[prod]
        if self.seen[eng].get(key, 0) >= val:
            return
        self.seen[eng][key] = val
        self.q[eng].append(["w", sem, val])

    def _deps(self, eng, reads, writes):
        for b in reads:
            for t in b.w:
                self._wait(eng, t)
            if b.psum:
                for t in b.r.values():
                    if t[3] != eng:
                        self._wait(eng, t)
        for b in writes:
            for t in b.w:
                if t[3] != eng:
                    self._wait(eng, t)
            for t in b.r.values():
                if t[3] != eng:
                    self._wait(eng, t)

    def _mark(self, tok, reads, writes, acc=False):
        for b in reads:
            b.r[tok[2]] = tok
        for b in writes:
            if acc:
                b.w.append(tok)
            else:
                b.w = [tok]
            b.r = {}

    def op(self, eng, fn, reads=(), writes=(), signal=True):
        self._deps(eng, reads, writes)
        sem = self.sem[eng]
        rec = ["o", fn, bool(signal), sem]
        self.q[eng].append(rec)
        self.lastop[eng] = rec
        if signal:
            self.cnt[eng] += 1
            tok = (sem, self.cnt[eng], eng, eng)
        else:
            tok = (sem, self.cnt[eng] + 1, eng, eng)
        self._mark(tok, reads, writes)
        return tok

    def mm(self, out, lhsT, rhs, reads, writes, start=True, stop=True, signal=True, **kw):
        return self.op("pe", lambda e: e.matmul(out, lhsT, rhs, start=start, stop=stop, **kw),
                       reads, writes, signal)

    def dma(self, queue, out, in_, reads=(), writes=(), acc=False, **kw):
        i = self.dcnt[queue]
        self.dcnt[queue] += 1
        slot = i % self.NDS
        self._wait(queue, self.dlast[queue][slot])
        self._deps(queue, reads, writes)
        sem = self.dsem[queue][slot]
        val = 16 * (i // self.NDS + 1)
        tok = (sem, val, "d_%s%d" % (queue, slot), "dma")
        self.dlast[queue][slot] = tok
        self.q[queue].append(["d", out, in_, sem, kw])
        self._mark(tok, reads, writes, acc=acc)
        return tok

    def raw(self, eng, fn, reads=(), writes=()):
        self._deps(eng, reads, writes)
        self.q[eng].append(["x", fn])

    def barrier(self):
        for o in self.ENG:
            rec = self.lastop[o]
            if rec is not None and not rec[2]:
                self._force(o)
        toks = [(self.sem[o], self.cnt[o], o, o) for o in self.ENG if self.cnt[o] > 0]
        for qn in self.dq:
            toks += [t for t in self.dlast[qn] if t is not None]
        toks += self.extra
        for e in self.ENG:
            for t in toks:
                if t[3] != e:
                    self._wait(e, t)

    @staticmethod
    def _replay(e, recs):
        for r in recs:
            k = r[0]
            if k == "w":
                e.wait_ge(r[1], r[2])
            elif k == "o":
                ins = r[1](e)
                if r[2]:
                    ins.then_inc(r[3], 1)
            elif k == "d":
                e.dma_start(out=r[1], in_=r[2], **r[4]).then_inc(r[3], 16)
            else:
                r[1](e)

    def run(self):
        nc = self.nc
        q = self.q
        rp = self._replay
        with nc.Block() as block:
            @block.tensor
            def _(e):
                rp(e, q["pe"])

            @block.scalar
            def _(e):
                rp(e, q["act"])

            @block.vector
            def _(e):
                rp(e, q["dve"])

            @block.gpsimd
            def _(e):
                rp(e, q["pool"])

            @block.sync
            def _(e):
                rp(e, q["sp"])
        self.q = {e: [] for e in self.ENG}
        self.lastop = {e: None for e in self.ENG}


def bcast_rows(ap, n=128):
    return bass.AP(ap.tensor, ap.offset, [[0, n], [1, ap.shape[-1]]])


def prep_ffn_weights(nc, S, D):
    wgv = D["wg"].rearrange("(c p) n -> p c n", p=128)
    wuv = D["wu"].rearrange("(c p) n -> p c n", p=128)
    D["b_wgs"] = [Buf() for _ in range(HC)]
    for h in range(HC):
        S.dma("pool", D["wgs"][h, :, 0, :, :], wgv[:, :, h * 128:(h + 1) * 128], writes=[D["b_wgs"][h]])
        S.dma("pool", D["wgs"][h, :, 1, :, :], wuv[:, :, h * 128:(h + 1) * 128], writes=[D["b_wgs"][h]], acc=True)


def phase_c(nc, S, D):
    GT = 512
    NG = TOKH // GT
    TPG = GT // 128
    OCW = 256
    with ExitStack() as ph:
        def sb(n, shp, dt):
            return ph.enter_context(nc.sbuf_tensor(n, shp, dt))

        def ps(n, shp, dt):
            return ph.enter_context(nc.psum_tensor(n, shp, dt))

        Gv = [g_.rearrange("(c p) t -> p c t", p=128) for g_ in D["G"]]
        ident = sb("c_ident", [128, 128], BF16)
        b_ident = Buf()
        S.dma("pool", ident[:], D["ident"], writes=[b_ident])
        sel = sb("c_sel", [128, 2], F32)
        b_sel = Buf()
        S.dma("sp", sel[:], D["sel"], writes=[b_sel])
        woutA = sb("c_woutA", [128, 8, 1024], BF16)
        woutB = sb("c_woutB", [128, 8, 1024], BF16)
        b_wout = Buf()
        b_woutB = Buf()
        S.dma("pool", woutA[:], D["w_out"].rearrange("(c p) n -> p c n", p=128), writes=[b_wout])
        S.op("dve", lambda e: e.tensor_scalar(woutB[:], woutA[:], sel[:, 1:2], None, ALU.mult),
             reads=[b_wout, b_sel], writes=[b_woutB])
        S.op("dve", lambda e: e.tensor_scalar(woutA[:], woutA[:], sel[:, 0:1], None, ALU.mult),
             reads=[b_wout, b_sel, b_woutB], writes=[b_wout])
        wd = sb("c_wd", [128, HC, 1024], BF16)
        b_wd = [Buf() for _ in range(HC)]
        for h in range(HC):
            S.dma("pool", wd[:, h, :], D["wd"][h * 128:(h + 1) * 128, :], writes=[b_wd[h]])
        lnp = sb("c_lnp", [128, 4, 1024], F32)
        b_lnp = [Buf() for _ in range(4)]
        for i, nm in enumerate(("ln1g", "ln1b", "ln2g", "ln2b")):
            S.dma("sp", lnp[:, i, :], bcast_rows(D[nm]), writes=[b_lnp[i]])

        NW = 4
        wgu = [sb("c_wgu%d" % i, [128, 2, 8, 128], BF16) for i in range(NW)]
        b_wgu = [Buf() for _ in range(NW)]

        h1g = sb("c_h1g", [128, TPG, 1024], F32)
        b_h1g = [Buf() for _ in range(TPG)]
        h1T = sb("c_h1T", [128, 8, GT], BF16)
        b_h1T = [Buf() for _ in range(TPG)]
        actT = sb("c_actT", [128, HC, GT], BF16)
        b_actT = [Buf() for _ in range(HC)]
        NB = 2
        ocA = [sb("c_ocA%d" % i, [128, 8, OCW], BF16) for i in range(NB)]
        ocB = [sb("c_ocB%d" % i, [128, 8, OCW], BF16) for i in range(NB)]
        b_ocA = [Buf() for _ in range(NB)]
        b_ocB = [Buf() for _ in range(NB)]
        xt = [sb("c_xt%d" % i, [128, 1024], F32) for i in range(NB)]
        b_xt = [Buf() for _ in range(NB)]
        hpre = sb("c_hpre", [128, 1024], F32)
        b_hpre = Buf()
        hn = sb("c_hn", [128, 1024], F32)
        b_hn = Buf()
        h1b = sb("c_h1b", [128, 1024], BF16)
        b_h1b = Buf()
        stats = sb("c_stats", [128, 2, 6], F32)
        b_stats = Buf()
        mv = sb("c_mv", [128, 4], F32)
        b_mv = Buf()
        sg = [sb("c_sg%d" % i, [128, 512], BF16) for i in range(2)]
        b_sg = [Buf() for _ in range(2)]
        outt = [sb("c_outt%d" % i, [128, 1024], F32) for i in range(NB)]
        b_outt = [Buf() for _ in range(NB)]

        epsc = sb("c_eps", [128, 1], F32)
        b_epsc = Buf()
        S.op("dve", lambda e: e.memset(epsc[:], LN_EPS), writes=[b_epsc])
        pmix = ps("c_pmix", [128, 1024], F32)
        b_pmix = Buf(psum=True)
        pT = ps("c_pT", [128, 1024], BF16)
        b_pT = Buf(psum=True)
        pG = [ps("c_pG%d" % i, [128, 512], F32) for i in range(2)]
        pU = [ps("c_pU%d" % i, [128, 512], F32) for i in range(2)]
        b_pG = [Buf(psum=True) for _ in range(2)]
        b_pU = [Buf(psum=True) for _ in range(2)]

        def layer_norm(src_b, gi_, bi_, dst, b_dst):
            for hf in range(2):
                S.op("dve", lambda e, hf=hf: e.bn_stats(stats[:, hf, :], hpre[:, hf * 512:(hf + 1) * 512]),
                     reads=[src_b], writes=[b_stats])
            S.op("dve", lambda e: e.bn_aggr(mv[:, 0:2], stats[:].rearrange("p a b -> p (a b)")),
                 reads=[b_stats], writes=[b_mv])
            S.op("act", lambda e: e.activation(mv[:, 3:4], mv[:, 1:2], AF.Sqrt, bias=epsc[:, 0:1], scale=1.0),
                 reads=[b_mv, b_epsc], writes=[b_mv])
            S.op("dve", lambda e: e.reciprocal(mv[:, 2:3], mv[:, 3:4]), reads=[b_mv], writes=[b_mv])
            S.op("dve", lambda e: e.tensor_scalar(hn[:], hpre[:], mv[:, 0:1], mv[:, 2:3],
                                                  ALU.subtract, ALU.mult),
                 reads=[src_b, b_mv], writes=[b_hn])
            S.op("pool", lambda e: e.tensor_tensor(hn[:], hn[:], lnp[:, gi_, :], ALU.mult),
                 reads=[b_hn, b_lnp[gi_]], writes=[b_hn])
            S.op("dve", lambda e: e.tensor_tensor(dst, hn[:], lnp[:, bi_, :], ALU.add),
                 reads=[b_hn, b_lnp[bi_]], writes=[b_dst])

        nwl = 0
        pend_tr = []
        for gi in range(NG):
            for ti in range(TPG):
                it = gi * TPG + ti
                sl = it % NB
                tok0 = it * 128
                oi = (tok0 // OCW) % NB
                oo = tok0 % OCW
                if oo == 0:
                    ta, tb = tok0, TOKH + tok0
                    S.dma("sp", ocA[oi][:], Gv[ta // XCH][:, :, ta % XCH:ta % XCH + OCW], writes=[b_ocA[oi]])
                    S.dma("sp", ocB[oi][:], Gv[tb // XCH][:, :, tb % XCH:tb % XCH + OCW], writes=[b_ocB[oi]])
                S.dma("sp", xt[sl][:], D["xres"][tok0:tok0 + 128, :], writes=[b_xt[sl]])
                if len(pend_tr) > 1:
                    pend_tr.pop(0)()
                for hf in range(2):
                    cs = slice(hf * 512, (hf + 1) * 512)
                    for c in range(8):
                        S.mm(pmix[:, cs], ocA[oi][:, c, oo:oo + 128], woutA[:, c, cs],
                             reads=[b_ocA[oi], b_wout], writes=[b_pmix], start=(c == 0), stop=False, signal=False)
                    for c in range(8):
                        S.mm(pmix[:, cs], ocB[oi][:, c, oo:oo + 128], woutB[:, c, cs],
                             reads=[b_ocB[oi], b_woutB], writes=[b_pmix], start=False, stop=(c == 7),
                             signal=(c == 7))
                for hf in range(2):
                    S.op("dve", lambda e, hf=hf, sl=sl: e.scalar_tensor_tensor(
                        hpre[:, hf * 512:(hf + 1) * 512], xt[sl][:, hf * 512:(hf + 1) * 512], ALPHA,
                        pmix[:, hf * 512:(hf + 1) * 512], ALU.mult, ALU.add),
                        reads=[b_xt[sl], b_pmix], writes=[b_hpre])
                layer_norm(b_hpre, 0, 1, h1g[:, ti, :], b_h1g[ti])
                def tr_part(ti=ti):
                    S.op("act", lambda e: e.copy(h1b[:], h1g[:, ti, :]), reads=[b_h1g[ti]], writes=[b_h1b])
                    for c in range(8):
                        S.op("pe", lambda e, c=c: e.transpose(pT[:, c * 128:(c + 1) * 128],
                                                              h1b[:, c * 128:(c + 1) * 128], ident[:]),
                             reads=[b_h1b, b_ident], writes=[b_pT], signal=(c == 7))
                    S.op("act", lambda e: e.copy(h1T[:, :, ti * 128:(ti + 1) * 128],
                                                 pT[:].rearrange("p (c t) -> p c t", c=8)),
                         reads=[b_pT], writes=[b_h1T[ti]])
                pend_tr.append(tr_part)
            while pend_tr:
                pend_tr.pop(0)()
            for h in range(HC):
                ws = nwl % NW
                nwl += 1
                S.dma("sp", wgu[ws][:].rearrange("p a c n -> p (a c n)"),
                      D["wgs"][h].rearrange("p a c n -> p (a c n)"), reads=[D["b_wgs"][h]], writes=[b_wgu[ws]])
                pb = h % 2
                for c in range(8):
                    S.mm(pG[pb][:], wgu[ws][:, 0, c, :], h1T[:, c, :], reads=[b_wgu[ws]] + b_h1T,
                         writes=[b_pG[pb]], start=(c == 0), stop=(c == 7), signal=(c == 7))
                for c in range(8):
                    S.mm(pU[pb][:], wgu[ws][:, 1, c, :], h1T[:, c, :], reads=[b_wgu[ws]] + b_h1T,
                         writes=[b_pU[pb]], start=(c == 0), stop=(c == 7), signal=(c == 7))
                S.op("act", lambda e, pb=pb: e.activation(sg[pb][:], pG[pb][:], AF.Silu),
                     reads=[b_pG[pb]], writes=[b_sg[pb]])
                S.op("dve", lambda e, pb=pb, h=h: e.tensor_tensor(actT[:, h, :], pU[pb][:], sg[pb][:], ALU.mult),
                     reads=[b_pU[pb], b_sg[pb]], writes=[b_actT[h]])
            for ti in range(TPG):
                it = gi * TPG + ti
                sl = it % NB
                tok0 = it * 128
                for hf in range(2):
                    for h in range(HC):
                        S.mm(pmix[:, hf * 512:(hf + 1) * 512], actT[:, h, ti * 128:(ti + 1) * 128],
                             wd[:, h, hf * 512:(hf + 1) * 512], reads=[b_actT[h], b_wd[h]], writes=[b_pmix],
                             start=(h == 0), stop=(h == HC - 1), signal=(h == HC - 1))
                for hf in range(2):
                    S.op("dve", lambda e, hf=hf, ti=ti: e.scalar_tensor_tensor(
                        hpre[:, hf * 512:(hf + 1) * 512], h1g[:, ti, hf * 512:(hf + 1) * 512], ALPHA,
                        pmix[:, hf * 512:(hf + 1) * 512], ALU.mult, ALU.add),
                        reads=[b_h1g[ti], b_pmix], writes=[b_hpre])
                layer_norm(b_hpre, 2, 3, outt[sl][:], b_outt[sl])
                S.dma("sp", D["out"][tok0:tok0 + 128, :], outt[sl][:], reads=[b_outt[sl]])
        S.barrier()
        S.run()


def phase_a(nc, S, D):
    ST = 2048
    NST = SEQ // ST
    inv_freq = [float(np.float32(ROPE_THETA) ** np.float32(-i / 8.0)) for i in range(8)]
    TWO_PI = 2.0 * math.pi
    C1 = 6.28125
    C2 = TWO_PI - C1
    with ExitStack() as ph:
        def sb(n, shp, dt):
            return ph.enter_context(nc.sbuf_tensor(n, shp, dt))

        def ps(n, shp, dt):
            return ph.enter_context(nc.psum_tensor(n, shp, dt))

        identf = sb("a_identf", [128, 128], F32)
        b_identf = Buf()
        S.dma("sp", identf[:], D["ident"], writes=[b_identf])
        identb = sb("a_identb", [128, 128], BF16)
        b_identb = Buf()
        S.dma("pool", identb[:], D["ident"], writes=[b_identb])
        maskb = sb("a_maskb", [128, 256], BF16)
        b_maskb = Buf()
        S.dma("pool", maskb[:], D["maskT"], writes=[b_maskb])
        ones = sb("a_ones", [128, 128], F32)
        b_ones = Buf()
        S.op("dve", lambda e: e.memset(ones[:], 1.0), writes=[b_ones])
        watt = sb("a_watt", [128, 8, 768], BF16)
        b_watt = Buf()
        S.dma("pool", watt[:], D["w_att"].rearrange("(c p) n -> p c n", p=128), writes=[b_watt])

        posi = sb("a_posi", [128, 64], I32)
        b_posi = Buf()
        S.dma("sp", posi[:], D["pos"], writes=[b_posi])
        posf = sb("a_posf", [128, 64], F32)
        b_posf = Buf()
        S.op("dve", lambda e: e.tensor_copy(posf[:], posi[:]), reads=[b_posi], writes=[b_posf])
        ang = sb("a_ang", [128, 64, 8], F32)
        b_ang = Buf()
        for i in range(8):
            S.op("dve", lambda e, i=i: e.tensor_scalar(ang[:, :, i], posf[:], inv_freq[i], None, ALU.mult),
                 reads=[b_posf], writes=[b_ang])
        sinT = sb("a_sinT", [128, 64, 8], F32)
        cosT = sb("a_cosT", [128, 64, 8], F32)
        b_sinT = Buf()
        b_cosT = Buf()
        kq = sb("a_kq", [128, 512], I32)
        red = sb("a_red", [128, 512], F32)
        msk = sb("a_msk", [128, 512], F32)
        b_tmp = Buf()
        angf = ang[:].rearrange("p a b -> p (a b)")

        def wrap(dst):
            S.op("dve", lambda e: e.tensor_scalar(msk[:], dst, math.pi, -TWO_PI, ALU.is_gt, ALU.mult),
                 reads=[b_tmp], writes=[b_tmp])
            S.op("dve", lambda e: e.tensor_tensor(dst, dst, msk[:], ALU.add), reads=[b_tmp], writes=[b_tmp])

        S.op("dve", lambda e: e.tensor_scalar(kq[:], angf, 1.0 / TWO_PI, None, ALU.mult),
             reads=[b_ang], writes=[b_tmp])
        S.op("dve", lambda e: e.tensor_copy(msk[:], kq[:]), reads=[b_tmp], writes=[b_tmp])
        S.op("dve", lambda e: e.scalar_tensor_tensor(red[:], msk[:], -C1, angf, ALU.mult, ALU.add),
             reads=[b_tmp, b_ang], writes=[b_tmp])
        S.op("dve", lambda e: e.scalar_tensor_tensor(red[:], msk[:], -C2, red[:], ALU.mult, ALU.add),
             reads=[b_tmp], writes=[b_tmp])
        wrap(red[:])
        S.op("act", lambda e: e.activation(sinT[:].rearrange("p a b -> p (a b)"), red[:], AF.Sin),
             reads=[b_tmp], writes=[b_sinT])
        S.op("dve", lambda e: e.tensor_scalar(red[:], red[:], math.pi / 2.0, None, ALU.add),
             reads=[b_tmp, b_sinT], writes=[b_tmp])
        wrap(red[:])
        S.op("act", lambda e: e.activation(cosT[:].rearrange("p a b -> p (a b)"), red[:], AF.Sin),
             reads=[b_tmp], writes=[b_cosT])

        xTs = [sb("a_xT%d" % i, [128, 8, ST], BF16) for i in range(2)]
        b_xTs = [Buf() for _ in range(2)]
        xTv = D["xT"].rearrange("(c p) t -> p c t", p=128)
        qT = sb("a_qT", [128, 2, ST], BF16)
        b_qT = Buf()
        kT = [sb("a_kT%d" % i, [128, 2, ST], BF16) for i in range(2)]
        b_kT = [Buf() for _ in range(2)]
        V = [[sb("a_V%d_%d" % (l, i), [128, 16, 4, 65], BF16) for i in range(2)] for l in range(3)]
        b_V = [[Buf() for _ in range(2)] for l in range(3)]
        for l in range(3):
            for i in range(2):
                S.op("pool", lambda e, l=l, i=i: e.memset(V[l][i][:], 1.0), writes=[b_V[l][i]])
        ysbs = [sb("a_ysb%d" % i, [128, 512], F32) for i in range(4)]
        b_ysbs = [Buf() for _ in range(4)]
        rts = [sb("a_rt%d" % i, [128, 4, 8, 8], F32) for i in range(4)]
        b_rts = [Buf() for _ in range(4)]
        sqs = [sb("a_sq%d" % i, [128, 512], BF16) for i in range(4)]
        b_sqs = [Buf() for _ in range(4)]
        ssqs = [sb("a_ssq%d" % i, [128, 8], F32) for i in range(4)]
        b_ssqs = [Buf() for _ in range(4)]
        jn = sb("a_jn", [128, 1], F32)
        b_pOq = [Buf(psum=True) for _ in range(4)]
        rm = sb("a_rm", [128, 8], F32)
        b_rm = Buf()
        st4 = sb("a_st4", [4, 8], F32)
        b_st4 = Buf()
        dg4 = sb("a_dg4", [4, 4], F32)
        b_dg4 = Buf()
        negM = sb("a_negM", [128, 4], F32)
        b_negM = Buf()
        NPT = 3
        PT = [sb("a_PT%d" % i, [128, 256], BF16) for i in range(NPT)]
        b_PT = [Buf() for _ in range(NPT)]
        oacc = sb("a_oacc", [128, ST], F32)
        b_oacc = Buf()
        oT = [sb("a_oT%d" % i, [64, ST], BF16) for i in range(2)]
        b_oT = [Buf() for _ in range(2)]

        pqk = ps("a_pqk", [128, 512], F32)
        b_pqk = Buf(psum=True)
        pX = ps("a_pX", [128, 512], F32)
        b_pX = Buf(psum=True)
        pS = [ps("a_pS%d" % i, [128, 512], F32) for i in range(2)]
        b_pS = [Buf(psum=True) for _ in range(2)]
        pO = ps("a_pO", [128, ST], F32)
        b_pO = Buf(psum=True)

        S.op("dve", lambda e: e.memset(st4[:], 0.0), writes=[b_st4])

        nblk = 0
        for st in range(NST):
            xs = st % 2
            ks = st % 2
            kp = 1 - ks
            xT = xTs[xs]
            S.dma("pool", xT[:], xTv[:, :, st * ST:(st + 1) * ST], writes=[b_xTs[xs]])
            S.op("dve", lambda e: e.memset(rm[:], 0.0), reads=[], writes=[b_rm])
            def qk_tile(j, par):
                n = st * 16 + j
                pq_, bq_ = [(pqk, b_pqk), (pS[0], b_pS[0]), (pO[:, 0:512], b_pOq[0]), (pO[:, 1024:1536], b_pOq[2])][par]
                px_, bx_ = [(pX, b_pX), (pS[1], b_pS[1]), (pO[:, 512:1024], b_pOq[1]), (pO[:, 1536:2048], b_pOq[3])][par]
                ys, b_ys, rt_, b_rt_, sq_, b_sq_, ssq_, b_ssq_ = ysbs[par], b_ysbs[par], rts[par], b_rts[par], \
                    sqs[par], b_sqs[par], ssqs[par], b_ssqs[par]
                ys3 = ys[:].rearrange("p (h d) -> p h d", d=64)
                for c in range(8):
                    S.mm(pq_[:, 0:512], xT[:, c, j * 128:(j + 1) * 128], watt[:, c, 0:512],
                         reads=[b_xTs[xs], b_watt], writes=[bq_], start=(c == 0), stop=(c == 7), signal=(c == 7))
                yield
                S.op("act", _mul(ys[:, 0:256], pq_[:, 0:256], 0.125), reads=[bq_], writes=[b_ys])
                S.op("act", _CP(ys[:, 256:512], pq_[:, 256:512]), reads=[bq_], writes=[b_ys])
                yield
                cb = cosT[:, n:n + 1, :].to_broadcast([128, 8, 8])
                sbb = sinT[:, n:n + 1, :].to_broadcast([128, 8, 8])
                x1 = ys3[:, :, 0:8]
                x2 = ys3[:, :, 8:16]
                for a, (xx, tt) in enumerate(((x1, cb), (x2, sbb), (x2, cb), (x1, sbb))):
                    S.op("pool", _TT(rt_[:, a, :, :], xx, tt, ALU.mult), reads=[b_ys, b_cosT, b_sinT], writes=[b_rt_])
                yield
                S.op("pool", _TT(x1, rt_[:, 0, :, :], rt_[:, 1, :, :], ALU.subtract), reads=[b_rt_, b_ys], writes=[b_ys])
                S.op("pool", _TT(x2, rt_[:, 2, :, :], rt_[:, 3, :, :], ALU.add), reads=[b_rt_, b_ys], writes=[b_ys])
                yield
                S.op("dve", _TT(sq_[:], ys[:], ys[:], ALU.mult), reads=[b_ys], writes=[b_sq_])
                for blk in range(4):
                    S.op("pe", _TR(px_[:, blk * 128:(blk + 1) * 128], ys[:, blk * 128:(blk + 1) * 128], identf[:]),
                         reads=[b_ys, b_identf], writes=[bx_], signal=(blk == 3))
                yield
                S.op("dve", _RED(ssq_[:], sq_[:].rearrange("p (h d) -> p h d", d=64)), reads=[b_sq_], writes=[b_ssq_])
                S.op("act", _CP(qT[:, :, j * 128:(j + 1) * 128], px_[:, 0:256].rearrange("p (a t) -> p a t", a=2)),
                     reads=[bx_], writes=[b_qT])
                S.op("act", _CP(kT[ks][:, :, j * 128:(j + 1) * 128], px_[:, 256:512].rearrange("p (a t) -> p a t", a=2)),
                     reads=[bx_], writes=[b_kT[ks]])
                yield
                S.op("dve", _TT(rm[:], rm[:], ssq_[:], ALU.max), reads=[b_ssq_, b_rm], writes=[b_rm])

            S.op("dve", lambda e: e.memset(jn[:], 0.0), writes=[b_pO] + b_pOq)
            for j0 in range(0, 16, 4):
                gens = [qk_tile(j0 + i, i) for i in range(4)]
                while gens:
                    for g_ in list(gens):
                        try:
                            next(g_)
                        except StopIteration:
                            gens.remove(g_)
            S.op("dve", lambda e: e.memset(jn[:], 0.0), writes=[b_pO] + b_pOq)
            S.op("pe", lambda e: e.transpose(pX[0:4, 0:128], rm[:, 0:4], identf[:]), reads=[b_rm, b_identf],
                 writes=[b_pX], signal=False)
            S.op("pe", lambda e: e.transpose(pX[0:4, 128:256], rm[:, 4:8], identf[:]), reads=[b_rm, b_identf],
                 writes=[b_pX])
            S.op("dve", lambda e: e.tensor_copy(st4[:, 2:3], st4[:, 1:2]), reads=[b_st4], writes=[b_st4])
            S.op("dve", lambda e: e.tensor_reduce(st4[:, 0:2], pX[0:4, 0:256].rearrange("p (a t) -> p a t", a=2),
                                                  AX.X, ALU.max), reads=[b_pX, b_st4], writes=[b_st4])
            S.op("dve", lambda e: e.tensor_tensor(st4[:, 3:4], st4[:, 1:2], st4[:, 2:3], ALU.max),
                 reads=[b_st4], writes=[b_st4])
            S.op("dve", lambda e: e.tensor_tensor(st4[:, 3:4], st4[:, 3:4], st4[:, 0:1], ALU.mult),
                 reads=[b_st4], writes=[b_st4])
            S.op("dve", lambda e: e.tensor_scalar(dg4[:], identf[0:4, 0:4], st4[:, 3:4], None, ALU.mult),
                 reads=[b_st4, b_identf], writes=[b_dg4])
            S.mm(pX[:, 256:260], ones[0:4, :], dg4[:], reads=[b_ones, b_dg4], writes=[b_pX])
            S.op("act", lambda e: e.activation(negM[:], pX[:, 256:260], AF.Sqrt), reads=[b_pX], writes=[b_negM])
            S.op("dve", lambda e: e.tensor_scalar(negM[:], negM[:], -1.0, None, ALU.mult), reads=[b_negM],
                 writes=[b_negM])
            for l, dil in enumerate((1, 4, 16)):
                for t16 in range(16):
                    if dil == 1:
                        a0, a1 = t16 * 128, (t16 + 1) * 128
                    elif dil == 4:
                        n4, r = t16 // 4, t16 % 4
                        a0, a1 = n4 * 512 + r, (n4 + 1) * 512
                    else:
                        a0, a1 = t16, ST
                    pv_, bv_ = (pX, b_pX) if t16 % 2 == 0 else (pqk, b_pqk)
                    for c in range(8):
                        S.mm(pv_[:, 0:256], xT[:, c, a0:a1:dil], watt[:, c, 512:768],
                             reads=[b_xTs[xs], b_watt], writes=[bv_], start=(c == 0), stop=(c == 7), signal=(c == 7))
                    S.op("act", _CP(V[l][ks][:, t16, :, 0:64], pv_[:, 0:256].rearrange("p (h d) -> p h d", d=64)),
                         reads=[bv_], writes=[b_V[l][ks]])
            for h in range(4):
                hp, p0 = h // 2, 64 * (h % 2)
                first = [True] * 4
                pend = []

                def block(qsl, cur, prev, outs):
                    nonlocal nblk
                    pb = nblk % 2
                    pt = nblk % NPT
                    nblk += 1
                    lo = 0 if prev is not None else 128
                    qap = qT[p0:p0 + 64, hp, qsl]
                    if prev is not None:
                        pslot, psl, pv_ap = prev
                        S.mm(pS[pb][:, 0:128], kT[pslot][p0:p0 + 64, hp, psl], qap,
                             reads=[b_kT[pslot], b_qT], writes=[b_pS[pb]], start=True, stop=False, signal=False)
                        S.mm(pS[pb][:, 0:128], identb[:], maskb[:, 0:128], reads=[b_identb, b_maskb],
                             writes=[b_pS[pb]], start=False, stop=True, signal=False)
                    S.mm(pS[pb][:, 128:256], kT[ks][p0:p0 + 64, hp, qsl], qap,
                         reads=[b_kT[ks], b_qT], writes=[b_pS[pb]], start=True, stop=False, signal=False)
                    S.mm(pS[pb][:, 128:256], identb[:], maskb[:, 128:256], reads=[b_identb, b_maskb],
                         writes=[b_pS[pb]], start=False, stop=True, signal=True)
                    S.op("act", lambda e, hh=h: e.activation(PT[pt][:, lo:256], pS[pb][:, lo:256], AF.Exp,
                                                             bias=negM[:, hh:hh + 1], scale=1.0),
                         reads=[b_pS[pb], b_negM], writes=[b_PT[pt]])
                    kts = ([(0, pv_ap, b_V_prev)] if prev is not None else []) + [(1, cur, b_V_cur)]

                    def pv_part():
                        nmm = len(kts) * len(outs)
                        i = 0
                        for (kt, vap, vb) in kts:
                            for (ocol, pcol) in outs:
                                bank = ocol.start // 512
                                i += 1
                                S.mm(pO[0:65, ocol], vap, PT[pt][:, kt * 128 + pcol.start:kt * 128 + pcol.stop],
                                     reads=[b_PT[pt], vb], writes=[b_pO], start=first[bank], stop=False,
                                     signal=(i == nmm), skip_group_check=True)
                                first[bank] = False
                    if pend:
                        pend.pop()()
                    pend.append(pv_part)

                full = [(None, slice(0, 128))]
                for jb in range(16):
                    qsl = slice(jb * 128, (jb + 1) * 128)
                    b_V_cur = b_V[0][ks]
                    cur = V[0][ks][:, jb, h, :]
                    prev = None
                    if jb > 0:
                        prev = (ks, slice((jb - 1) * 128, jb * 128), V[0][ks][:, jb - 1, h, :])
                        b_V_prev = b_V[0][ks]
                    elif st > 0:
                        prev = (kp, slice(15 * 128, 16 * 128), V[0][kp][:, 15, h, :])
                        b_V_prev = b_V[0][kp]
                    block(qsl, cur, prev, [(slice(jb * 128, (jb + 1) * 128), slice(0, 128))])
                for n4 in range(4):
                    for r in range(4):
                        qsl = slice(n4 * 512 + r, (n4 + 1) * 512, 4)
                        b_V_cur = b_V[1][ks]
                        cur = V[1][ks][:, n4 * 4 + r, h, :]
                        prev = None
                        if n4 > 0:
                            prev = (ks, slice((n4 - 1) * 512 + r, n4 * 512, 4), V[1][ks][:, (n4 - 1) * 4 + r, h, :])
                            b_V_prev = b_V[1][ks]
                        elif st > 0:
                            prev = (kp, slice(3 * 512 + r, ST, 4), V[1][kp][:, 12 + r, h, :])
                            b_V_prev = b_V[1][kp]
                        block(qsl, cur, prev, [(qsl, slice(0, 128))])
                for r in range(16):
                    qsl = slice(r, ST, 16)
                    b_V_cur = b_V[2][ks]
                    cur = V[2][ks][:, r, h, :]
                    prev = None
                    if st > 0:
                        prev = (kp, qsl, V[2][kp][:, r, h, :])
                        b_V_prev = b_V[2][kp]
                    block(qsl, cur, prev, [(slice(b4 * 512 + r, (b4 + 1) * 512, 16), slice(32 * b4, 32 * b4 + 32))
                                           for b4 in range(4)])
                if pend:
                    pend.pop()()
                for b4 in range(4):
                    cs = slice(b4 * 512, (b4 + 1) * 512)
                    eng = "act" if b4 % 2 == 0 else "dve"
                    if eng == "act":
                        S.op("act", lambda e, cs=cs: e.copy(oacc[0:65, cs], pO[0:65, cs]), reads=[b_pO], writes=[b_oacc])
                    else:
                        S.op("dve", lambda e, cs=cs: e.tensor_copy(oacc[0:65, cs], pO[0:65, cs]), reads=[b_pO],
                             writes=[b_oacc])
                S.op("dve", lambda e: e.reciprocal(oacc[64:65, :], oacc[64:65, :]), reads=[b_oacc], writes=[b_oacc])
                for b4 in range(4):
                    cs = slice(b4 * 512, (b4 + 1) * 512)
                    S.mm(pO[0:64, cs], ones[64:65, 0:64], oacc[64:65, cs], reads=[b_ones, b_oacc], writes=[b_pO],
                         signal=(b4 == 3))
                os_ = (st * 4 + h) % 2
                for b4 in range(4):
                    cs = slice(b4 * 512, (b4 + 1) * 512)
                    S.op("dve", lambda e, cs=cs, os_=os_: e.tensor_tensor(oT[os_][:, cs], pO[0:64, cs], oacc[0:64, cs],
                                                                       ALU.mult),
                         reads=[b_pO, b_oacc], writes=[b_oT[os_]])
                for xk in range(ST // XCH):
                    S.dma("sp", D["omix"][st * (ST // XCH) + xk][h * 64:(h + 1) * 64, :],
                          oT[os_][:, xk * XCH:(xk + 1) * XCH], reads=[b_oT[os_]])
        S.barrier()
        S.run()


def _TT(o, a, b, op):
    return lambda e: e.tensor_tensor(o, a, b, op)


def _TS(o, a, s1, s2, op0, op1=None):
    if op1 is None:
        return lambda e: e.tensor_scalar(o, a, s1, s2, op0)
    return lambda e: e.tensor_scalar(o, a, s1, s2, op0, op1)


def _STT(o, a, s, b, op0, op1):
    return lambda e: e.scalar_tensor_tensor(o, a, s, b, op0, op1)


def _ACT(o, i, f, **kw):
    return lambda e: e.activation(o, i, f, **kw)


def _CP(o, i):
    return lambda e: e.copy(o, i)


def _TC(o, i):
    return lambda e: e.tensor_copy(o, i)


def _TR(o, i, ident):
    return lambda e: e.transpose(o, i, ident)


def _mul(o, i, m):
    return lambda e: e.mul(o, i, m)


def _RED(o, i):
    return lambda e: e.tensor_reduce(o, i, AX.X, ALU.add)


def phase_b(nc, S, D):
    STB = 1024
    NSTB = SEQ // STB
    TPS = STB // 128
    DS = DECAY_SCALE
    with ExitStack() as ph:
        def sb(n, shp, dt=F32):
            return ph.enter_context(nc.sbuf_tensor(n, shp, dt))

        def ps(n, shp, dt=F32):
            return ph.enter_context(nc.psum_tensor(n, shp, dt))

        identf = sb("b_identf", [128, 128])
        b_identf = Buf()
        S.dma("sp", identf[:], D["ident"], writes=[b_identf])
        identb = sb("b_identb", [128, 128], BF16)
        b_identb = Buf()
        S.dma("pool", identb[:], D["ident"], writes=[b_identb])
        cst = sb("b_cst", [128, 642])
        b_cst = Buf()
        S.dma("sp", cst[:], D["cstB"], writes=[b_cst])
        triI, triS, triA = cst[:, 0:128], cst[:, 128:256], cst[:, 256:384]
        chunkind = cst[:, 384:386]
        mask1, mask3, eye2 = cst[:, 386:514], cst[:, 514:578], cst[:, 578:642]
        vec = sb("b_vec", [128, 8, 256])
        b_vec = Buf()
        for i, nm in enumerate(("w0", "a0", "k_k", "k_a", "k_a", "r_k", "gn_g", "gn_b")):
            S.dma("sp", vec[:, i, :], bcast_rows(D[nm]), writes=[b_vec], acc=(i > 0))
        S.op("dve", _TS(vec[:, 4, :], vec[:, 4, :], -1.0, 1.0, ALU.mult, ALU.add), reads=[b_vec], writes=[b_vec])
        bias_wa = vec[:, 0:2, :].rearrange("p a b -> p (a b)")
        wdec = sb("b_wdec", [128, 512])
        b_wdec = Buf()
        S.op("dve", lambda e: e.memset(wdec[:], 0.0), writes=[b_wdec])
        S.dma("sp", wdec[0:64, 0:256], D["w_dec"], writes=[b_wdec], acc=True)
        S.dma("sp", wdec[64:128, 256:512], D["w_aaa"], writes=[b_wdec], acc=True)
        wgate = sb("b_wgate", [128, 256])
        b_wgate = Buf()
        S.dma("sp", wgate[:], D["w_gate"], writes=[b_wgate])
        epsg = sb("b_epsg", [128, 1])
        b_epsg = Buf()
        S.op("dve", lambda e: e.memset(epsg[:], GN_EPS), writes=[b_epsg])
        mub = sb("b_mub", [128, 1024])
        b_mub = Buf()
        S.dma("sp", mub[:], bcast_rows(D["mu"]), writes=[b_mub])
        omub = sb("b_omub", [128, 1024])
        b_omub = Buf()
        S.op("dve", _TS(omub[:], mub[:], -1.0, 1.0, ALU.mult, ALU.add), reads=[b_mub], writes=[b_omub])
        W1 = sb("b_W1", [128, 8, 1024], BF16)
        W2 = sb("b_W2", [128, 8, 1024], BF16)
        b_W = Buf()
        wst = [sb("b_wst%d" % i, [128, 1024]) for i in range(2)]
        b_wst = [Buf() for _ in range(2)]
        for c in range(8):
            S.dma("sp", wst[c % 2][:], D["w_rw"][c * 128:(c + 1) * 128, :], writes=[b_wst[c % 2]])
            S.op("dve", _TT(W1[:, c, :], wst[c % 2][:], omub[:], ALU.mult), reads=[b_wst[c % 2], b_omub], writes=[b_W])
            S.op("pool", _TT(W2[:, c, :], wst[c % 2][:], mub[:], ALU.mult), reads=[b_wst[c % 2], b_mub], writes=[b_W])

        xTv = D["xT"].rearrange("(c p) t -> p c t", p=128)
        xc = sb("b_xc", [128, 8, STB], BF16)
        xp = sb("b_xp", [128, 8, STB], BF16)
        b_xc = Buf()
        b_xp = Buf()

        Hs = sb("b_Hs", [128, 4, 64], BF16)
        b_H = Buf()
        S.op("dve", lambda e: e.memset(Hs[:], 0.0), writes=[b_H])

        def t256(n):
            return sb(n, [128, 256]), Buf()
        tl, b_tl = sb("b_tl", [128, 256]), Buf()
        lT, b_lT = sb("b_lT", [128, 256]), Buf()
        lg, b_lg = sb("b_lg", [128, 512]), Buf()
        sg, b_sg = sb("b_sg", [128, 512]), Buf()
        gs, b_gs = t256("b_gs")
        yA = sb("b_yA", [128, 512])
        b_rs = Buf()
        rs = yA[:, 0:256]
        krs = yA[:, 256:512]
        vs, b_vs = sb("b_vs", [128, 256], BF16), Buf()
        kk, b_kk = t256("b_kk")
        t1, b_t1 = t256("b_t1")
        t2, b_t2 = t256("b_t2")
        km, b_km = t256("b_km")
        bb, b_bb = t256("b_bb")
        E, b_E = sb("b_E", [128, 4, 256]), [Buf() for _ in range(4)]
        X4, b_X4 = sb("b_X4", [128, 4, 256], BF16), Buf()
        BK, b_BK = sb("b_BK", [128, 2, 256], BF16), Buf()
        XT, b_XT = sb("b_XT", [128, 4, 4, 64], BF16), Buf()
        dgP, b_dgP = sb("b_dgP", [128, 4, 64], BF16), Buf()
        AM1, b_AM1 = sb("b_AM1", [128, 4, 128], BF16), Buf()
        AM2, b_AM2 = sb("b_AM2", [128, 4, 128], BF16), Buf()
        Lm = [sb("b_L%d" % i, [128, 4, 2, 64], BF16) for i in range(2)]
        b_Lm = [Buf() for _ in range(2)]
        Pm, b_Pm = sb("b_Pm", [128, 4, 64], BF16), Buf()
        sm, b_sm = sb("b_sm", [128, 32]), Buf()
        PC, b_PC = sb("b_PC", [128, 4, 2]), Buf()
        Ws, b_Ws = sb("b_Ws", [128, 256], BF16), Buf()
        Us, b_Us = sb("b_Us", [128, 256], BF16), Buf()
        on, b_on = t256("b_on")
        gst, b_gst = sb("b_gst", [128, 4, 6]), Buf()
        gmv, b_gmv = sb("b_gmv", [128, 12]), Buf()
        orT = sb("b_orT", [128, 2, STB], BF16)
        b_orT = Buf()

        K = [ps("b_K%d" % i, [128, 512]) for i in range(8)]
        b_K = [Buf(psum=True) for _ in range(8)]

        def v3(ap):
            return ap.rearrange("p (h d) -> p h d", d=64)

        def bc(ap4):
            return ap4.unsqueeze(2).to_broadcast([128, 4, 64])

        def dbl(name, shp, dt=F32):
            return [sb("%s_%d" % (name, i), shp, dt) for i in range(2)], [Buf() for _ in range(2)]
        AM1d, b_AM1d = dbl("b_AM1d", [128, 4, 128], BF16)
        AM2d, b_AM2d = dbl("b_AM2d", [128, 4, 128], BF16)
        Pmd, b_Pmd = dbl("b_Pmd", [128, 4, 64], BF16)
        XTd, b_XTd = dbl("b_XTd", [128, 4, 4, 64], BF16)
        def trp(name, shp, dt=F32):
            return [sb("%s_%d" % (name, i), shp, dt) for i in range(3)], [Buf() for _ in range(3)]
        BKd, b_BKd = trp("b_BKd", [128, 2, 256], BF16)
        vsd, b_vsd = trp("b_vsd", [128, 256], BF16)
        dgPd, b_dgPd = dbl("b_dgPd", [128, 4, 64], BF16)
        gsd, b_gsd = trp("b_gsd", [128, 256])
        bsd, b_bsd = trp("b_bsd", [128, 4])
        X4d, b_X4d = dbl("b_X4d", [128, 4, 256], BF16)
        PCd, b_PCd = dbl("b_PCd", [128, 4, 2])
        smG, b_smG = sb("b_smG", [128, 16]), Buf()
        tG, b_tG = sb("b_tG", [128, 256]), Buf()

        def front1(n):
            st, j = n // TPS, n % TPS
            d, t3 = n % 2, n % 3
            BK, b_BK, vs, b_vs, gs, b_gs, bs, b_bs = BKd[t3], b_BKd[t3], vsd[t3], b_vsd[t3], gsd[t3], b_gsd[t3], bsd[t3], b_bsd[t3]
            X4, b_X4, PC, b_PC = X4d[d], b_X4d[d], PCd[d], b_PCd[d]
            if j == 0:
                S.dma("pool", xc[:], xTv[:, :, st * STB:(st + 1) * STB], writes=[b_xc])
                if st == 0:
                    S.op("dve", lambda e: e.memset(xp[:, :, 0:1], 0.0), writes=[b_xp])
                    S.dma("pool", xp[:, :, 1:STB], xTv[:, :, 0:STB - 1], writes=[b_xp], acc=True)
                else:
                    S.dma("pool", xp[:], xTv[:, :, st * STB - 1:(st + 1) * STB - 1], writes=[b_xp])
            tsl = slice(j * 128, (j + 1) * 128)
            for bk in range(2):
                cs = slice(bk * 512, (bk + 1) * 512)
                for c in range(8):
                    S.mm(K[bk][:], xc[:, c, tsl], W1[:, c, cs], reads=[b_xc, b_W], writes=[b_K[bk]],
                         start=(c == 0), stop=False, signal=False)
                for c in range(8):
                    S.mm(K[bk][:], xp[:, c, tsl], W2[:, c, cs], reads=[b_xp, b_W], writes=[b_K[bk]],
                         start=False, stop=(c == 7), signal=(c == 7))
            yield
            pA, pB = K[0], K[1]
            S.op("act", _ACT(tl[:, 0:64], pB[:, 256:320], AF.Tanh), reads=[b_K[1]], writes=[b_tl])
            S.op("act", _CP(tl[:, 64:128], pB[:, 320:384]), reads=[b_K[1]], writes=[b_tl])
            S.op("act", _ACT(tl[:, 128:256], pB[:, 384:512], AF.Sigmoid), reads=[b_K[1]], writes=[b_tl])
            S.op("act", _CP(vs[:], pB[:, 0:256]), reads=[b_K[1]], writes=[b_vs])
            S.op("act", _CP(yA[:], pA[:]), reads=[b_K[0]], writes=[b_rs])
            yield
            S.op("pe", _TR(K[3][:, 0:128], tl[:, 0:128], identf[:]), reads=[b_tl, b_identf], writes=[b_K[3]], signal=False)
            S.op("pe", _TR(K[3][:, 128:256], tl[:, 128:256], identf[:]), reads=[b_tl, b_identf], writes=[b_K[3]])
            yield
            S.op("dve", _TC(lT[:], K[3][:, 0:256]), reads=[b_K[3]], writes=[b_lT])
            yield
            S.mm(K[2][:], lT[:, 0:128], wdec[:], reads=[b_lT, b_wdec], writes=[b_K[2]], signal=False)
            S.mm(K[3][:, 256:512], lT[:, 128:256], wgate[:], reads=[b_lT, b_wgate], writes=[b_K[3]])
            yield
            S.op("dve", _TT(lg[:], K[2][:], bias_wa, ALU.add), reads=[b_K[2], b_vec], writes=[b_lg])
            S.op("act", _ACT(sg[:], lg[:], AF.Sigmoid), reads=[b_lg], writes=[b_sg])
            S.op("act", _CP(gs[:], K[3][:, 256:512]), reads=[b_K[3]], writes=[b_gs])
            sw, aa = sg[:, 0:256], sg[:, 256:512]
            yield
            S.mm(K[2][:, 0:256], triS, sw, reads=[b_cst, b_sg], writes=[b_K[2]], signal=False)
            S.mm(K[2][:, 256:512], triI, sw, reads=[b_cst, b_sg], writes=[b_K[2]], signal=False)
            S.mm(K[3][:, 0:256], triA, sw, reads=[b_cst, b_sg], writes=[b_K[3]], signal=False)
            for c in range(2):
                rows = slice(64 * c, 64 * c + 64)
                for h in range(4):
                    S.mm(K[3][rows, 256 + 2 * h:258 + 2 * h], sg[rows, h * 64:(h + 1) * 64], cst[rows, 384:386],
                         reads=[b_cst, b_sg], writes=[b_K[3]], signal=(c == 1 and h == 3))
            yield
            S.op("act", _ACT(E[:, 0, :], K[2][:, 0:256], AF.Exp, scale=-DS), reads=[b_K[2]], writes=[b_E[0]])
            S.op("act", _ACT(E[:, 1, :], K[2][:, 256:512], AF.Exp, scale=-DS), reads=[b_K[2]], writes=[b_E[1]])
            S.op("act", _ACT(E[:, 2, :], K[2][:, 256:512], AF.Exp, scale=DS), reads=[b_K[2]], writes=[b_E[2]])
            S.op("act", _ACT(E[:, 3, :], K[3][:, 0:256], AF.Exp, scale=-DS), reads=[b_K[3]], writes=[b_E[3]])
            S.op("act", _ACT(PC[:], K[3][:, 256:264], AF.Exp, scale=-DS), reads=[b_K[3]], writes=[b_PC])
            S.op("dve", _TT(kk[:], krs, vec[:, 2, :], ALU.mult), reads=[b_rs, b_vec], writes=[b_kk])
            S.op("pool", _TT(t1[:], kk[:], kk[:], ALU.mult), reads=[b_kk], writes=[b_t1])
            S.op("pool", _TT(t2[:], aa, vec[:, 3, :], ALU.mult), reads=[b_sg, b_vec], writes=[b_t2])
            S.op("pool", _TT(t2[:], t2[:], vec[:, 4, :], ALU.add), reads=[b_t2, b_vec], writes=[b_t2])
            yield
            S.op("dve", lambda e: e.tensor_reduce(sm[:, 0:4], v3(t1[:]), AX.X, ALU.add), reads=[b_t1], writes=[b_sm])
            S.op("act", _ACT(sm[:, 4:8], sm[:, 0:4], AF.Sqrt), reads=[b_sm], writes=[b_sm])
            S.op("dve", _TT(km[:], krs, t2[:], ALU.mult), reads=[b_rs, b_t2], writes=[b_km])
            yield
            S.op("dve", _TS(sm[:, 4:8], sm[:, 4:8], L2_EPS, None, ALU.max), reads=[b_sm], writes=[b_sm])
            S.op("dve", lambda e: e.reciprocal(sm[:, 8:12], sm[:, 4:8]), reads=[b_sm], writes=[b_sm])
            S.op("dve", _TT(v3(kk[:]), v3(kk[:]), bc(sm[:, 8:12]), ALU.mult), reads=[b_kk, b_sm], writes=[b_kk])
            S.op("pool", _TT(t1[:], rs, km[:], ALU.mult), reads=[b_rs, b_km, b_sm], writes=[b_t1])
            S.op("pool", _TT(t1[:], t1[:], vec[:, 5, :], ALU.mult), reads=[b_t1, b_vec], writes=[b_t1])
            yield
            S.op("pool", _TT(bb[:], kk[:], aa, ALU.mult), reads=[b_kk, b_sg], writes=[b_bb])
            S.op("dve", lambda e: e.tensor_reduce(bs[:], v3(t1[:]), AX.X, ALU.add), reads=[b_t1], writes=[b_bs])
            S.op("dve", _STT(X4[:, 0, :], kk[:], -1.0, E[:, 0, :], ALU.mult, ALU.mult), reads=[b_kk, b_E[0]], writes=[b_X4])
            S.op("pool", _TT(X4[:, 1, :], rs, E[:, 1, :], ALU.mult), reads=[b_rs, b_E[1]], writes=[b_X4])
            yield
            S.op("dve", _TT(X4[:, 2, :], bb[:], E[:, 2, :], ALU.mult), reads=[b_bb, b_E[2]], writes=[b_X4])
            S.op("pool", _TT(X4[:, 3, :], km[:], E[:, 2, :], ALU.mult), reads=[b_km, b_E[2]], writes=[b_X4])
            S.op("dve", _TT(BK[:, 0, :], bb[:], E[:, 3, :], ALU.mult), reads=[b_bb, b_E[3]], writes=[b_BK])
            S.op("pool", _TT(BK[:, 1, :], km[:], E[:, 3, :], ALU.mult), reads=[b_km, b_E[3]], writes=[b_BK])

        def front2(n):
            d = n % 2
            AM1, b_AM1, AM2, b_AM2 = AM1d[d], b_AM1d[d], AM2d[d], b_AM2d[d]
            Pm, b_Pm, XT, b_XT, dgP, b_dgP = Pmd[d], b_Pmd[d], XTd[d], b_XTd[d], dgPd[d], b_dgPd[d]
            X4, b_X4, PC, b_PC = X4d[d], b_X4d[d], PCd[d], b_PCd[d]
            for c in range(2):
                rows = slice(64 * c, 64 * c + 64)
                for q in range(4):
                    for h in range(4):
                        blk = q * 4 + h
                        S.mm(K[4 + blk // 8][rows, (blk % 8) * 64:(blk % 8 + 1) * 64],
                             X4[rows, q, h * 64:(h + 1) * 64], identb[rows, rows],
                             reads=[b_X4, b_identb], writes=[b_K[4 + blk // 8]], signal=(c == 1 and blk % 8 == 7))
            yield
            XTf = XT[:].rearrange("p q h t -> p (q h t)")
            S.op("act", _CP(XTf[:, 0:512], K[4][:]), reads=[b_K[4]], writes=[b_XT])
            S.op("dve", _TC(XTf[:, 512:1024], K[5][:]), reads=[b_K[5]], writes=[b_XT])
            for c in range(2):
                rows = slice(64 * c, 64 * c + 64)
                S.op("pool", _TT(dgP[rows, :, :], cst[rows, 578:642].unsqueeze(1).to_broadcast([64, 4, 64]),
                                 PC[rows, :, c:c + 1].to_broadcast([64, 4, 64]), ALU.mult),
                     reads=[b_PC, b_cst], writes=[b_dgP])
            yield
            for c in range(2):
                rows = slice(64 * c, 64 * c + 64)
                for h in range(4):
                    last = (c == 1 and h == 3)
                    S.mm(K[4][rows, h * 128:(h + 1) * 128], XT[rows, 2, h, :], XT[rows, 0:2, h, :],
                         reads=[b_XT], writes=[b_K[4]], signal=False)
                    S.mm(K[5][rows, h * 128:(h + 1) * 128], XT[rows, 3, h, :], XT[rows, 0:2, h, :],
                         reads=[b_XT], writes=[b_K[5]], signal=last)
            yield
            m1b = mask1.unsqueeze(1).to_broadcast([128, 4, 128])
            S.op("dve", _TT(AM1[:], K[4][:].rearrange("p (h t) -> p h t", h=4), m1b, ALU.mult),
                 reads=[b_K[4], b_cst], writes=[b_AM1])
            S.op("pool", _TC(Lm[0][:, :, 0, :], AM1[:, :, 0:64]), reads=[b_AM1], writes=[b_Lm[0]])
            S.op("pool", _TT(Pm[:], AM1[:, :, 0:64], eye2.unsqueeze(1).to_broadcast([128, 4, 64]), ALU.add),
                 reads=[b_AM1, b_cst], writes=[b_Pm])
            yield
            for c in range(2):
                rows = slice(64 * c, 64 * c + 64)
                for h in range(4):
                    S.mm(K[4][rows, h * 64:(h + 1) * 64], XT[rows, 0, h, :], XT[rows, 2, h, :],
                         reads=[b_XT], writes=[b_K[4]], signal=(c == 1 and h == 3))
            S.op("dve", _TT(AM2[:], K[5][:].rearrange("p (h t) -> p h t", h=4), m1b, ALU.mult),
                 reads=[b_K[5], b_cst], writes=[b_AM2])
            yield
            m3b = mask3.unsqueeze(1).to_broadcast([128, 4, 64])
            S.op("dve", _TT(Lm[0][:, :, 1, :], K[4][:, 0:256].rearrange("p (h t) -> p h t", h=4), m3b, ALU.mult),
                 reads=[b_K[4], b_cst], writes=[b_Lm[0]])
            yield
            cur = 0
            for rnd in range(6):
                nxt = 1 - cur
                do_sq = rnd < 5
                do_p = rnd >= 1
                for c in range(2):
                    rows = slice(64 * c, 64 * c + 64)
                    for h in range(4):
                        last = (c == 1 and h == 3)
                        Lc, LTc = Lm[cur][rows, h, 0, :], Lm[cur][rows, h, 1, :]
                        if do_sq and rnd < 4:
                            S.mm(K[5][rows, (h * 2) * 64:(h * 2 + 1) * 64], LTc, Lc, reads=[b_Lm[cur]],
                                 writes=[b_K[5]], signal=False)
                        if do_sq:
                            S.mm(K[5][rows, (h * 2 + 1) * 64:(h * 2 + 2) * 64], Lc, LTc, reads=[b_Lm[cur]],
                                 writes=[b_K[5]], signal=(last and not do_p))
                        if do_p:
                            S.mm(K[4][rows, 256 + h * 64:256 + (h + 1) * 64], LTc, Pm[rows, h, :],
                                 reads=[b_Lm[cur], b_Pm], writes=[b_K[4]], signal=last)
                yield
                src = K[5][:].rearrange("p (h a t) -> p h a t", h=4, a=2)
                if do_sq:
                    if rnd < 4:
                        S.op("act", _CP(Lm[nxt][:], src), reads=[b_K[5]], writes=[b_Lm[nxt]])
                    else:
                        S.op("act", _CP(Lm[nxt][:, :, 1, :], src[:, :, 1, :]), reads=[b_K[5]], writes=[b_Lm[nxt]])
                if do_p:
                    S.op("dve", _TT(Pm[:], K[4][:, 256:512].rearrange("p (h t) -> p h t", h=4), Pm[:], ALU.add),
                         reads=[b_K[4], b_Pm], writes=[b_Pm])
                cur = nxt
                yield

        def back(n):
            st, j = n // TPS, n % TPS
            d = n % 2
            tsl = slice(j * 128, (j + 1) * 128)
            t3 = n % 3
            AM1, b_AM1, AM2, b_AM2 = AM1d[d], b_AM1d[d], AM2d[d], b_AM2d[d]
            Pm, b_Pm, XT, b_XT, BK, b_BK = Pmd[d], b_Pmd[d], XTd[d], b_XTd[d], BKd[t3], b_BKd[t3]
            vs, b_vs, dgP, b_dgP, gs, b_gs, bs, b_bs = vsd[t3], b_vsd[t3], dgPd[d], b_dgPd[d], gsd[t3], b_gsd[t3], bsd[t3], b_bsd[t3]
            pO_ = K[7]
            for c in range(2):
                rows = slice(64 * c, 64 * c + 64)
                orow = slice(64 * (1 - c), 64 * (1 - c) + 64)
                for h in range(4):
                    hc = slice(256 + h * 64, 256 + (h + 1) * 64)
                    vh = vs[rows, h * 64:(h + 1) * 64]
                    S.mm(K[6][rows, hc], AM2[rows, h, 0:64], vh, reads=[b_AM2, b_vs], writes=[b_K[6]],
                         start=True, stop=False, signal=False)
                    S.mm(K[6][rows, hc], XT[rows, 0, h, :], Hs[rows, h, :], reads=[b_XT, b_H], writes=[b_K[6]],
                         start=False, stop=True, signal=(h == 3))
                yield
                S.op("dve", _TC(Ws[rows, :], K[6][rows, 256:512]), reads=[b_K[6]], writes=[b_Ws])
                yield
                for h in range(4):
                    hc = slice(h * 64, (h + 1) * 64)
                    S.mm(K[6][rows, hc], Pm[rows, h, :], Ws[rows, hc], reads=[b_Pm, b_Ws], writes=[b_K[6]],
                         signal=(h == 3))
                yield
                S.op("act", _CP(Us[rows, :], K[6][rows, 0:256]), reads=[b_K[6]], writes=[b_Us])
                yield
                for h in range(4):
                    hc = slice(256 + h * 64, 256 + (h + 1) * 64)
                    vh = vs[rows, h * 64:(h + 1) * 64]
                    uh = Us[rows, h * 64:(h + 1) * 64]
                    S.mm(pO_[rows, hc], AM2[rows, h, 64:128], vh, reads=[b_AM2, b_vs], writes=[b_K[7]],
                         start=True, stop=False, signal=False)
                    S.mm(pO_[rows, hc], XT[rows, 1, h, :], Hs[rows, h, :], reads=[b_XT, b_H], writes=[b_K[7]],
                         start=False, stop=False, signal=False)
                    S.mm(pO_[rows, hc], AM1[rows, h, 64:128], uh, reads=[b_AM1, b_Us], writes=[b_K[7]],
                         start=False, stop=True, signal=False)
                for h in range(4):
                    oc_ = slice(256 + h * 64, 256 + (h + 1) * 64)
                    vh = vs[rows, h * 64:(h + 1) * 64]
                    uh = Us[rows, h * 64:(h + 1) * 64]
                    S.mm(K[6][orow, oc_], BK[rows, 1, h * 64:(h + 1) * 64], vh, reads=[b_BK, b_vs], writes=[b_K[6]],
                         start=True, stop=False, signal=False)
                    S.mm(K[6][orow, oc_], dgP[rows, h, :], Hs[rows, h, :], reads=[b_dgP, b_H], writes=[b_K[6]],
                         start=False, stop=False, signal=False)
                    S.mm(K[6][orow, oc_], BK[rows, 0, h * 64:(h + 1) * 64], uh, reads=[b_BK, b_Us], writes=[b_K[6]],
                         start=False, stop=True, signal=(h == 3))
                yield
                S.op("act", _CP(Hs[orow, :, :], K[6][orow, 256:512].rearrange("p (h v) -> p h v", h=4)),
                     reads=[b_K[6], b_K[7]], writes=[b_H])
                yield
            o3 = pO_[:, 256:512].rearrange("p (h d) -> p h d", d=64)
            S.op("act", _CP(on[:], pO_[:, 256:512]), reads=[b_K[7]], writes=[b_on])
            yield
            S.op("act", _ACT(tG[:], on[:], AF.Square), reads=[b_on], writes=[b_tG])
            S.op("dve", lambda e: e.tensor_reduce(smG[:, 0:4], v3(on[:]), AX.X, ALU.add), reads=[b_on], writes=[b_smG])
            yield
            S.op("dve", lambda e: e.tensor_reduce(smG[:, 4:8], v3(tG[:]), AX.X, ALU.add), reads=[b_tG], writes=[b_smG])
            S.op("dve", _TS(gmv[:, 0:4], smG[:, 0:4], 1.0 / 64.0, None, ALU.mult), reads=[b_smG], writes=[b_gmv])
            S.op("dve", _TT(gmv[:, 4:8], gmv[:, 0:4], gmv[:, 0:4], ALU.mult), reads=[b_gmv], writes=[b_gmv])
            S.op("dve", _STT(gmv[:, 8:12], smG[:, 4:8], 1.0 / 64.0, gmv[:, 4:8], ALU.mult, ALU.subtract),
                 reads=[b_smG, b_gmv], writes=[b_gmv])
            yield
            S.op("act", _ACT(smG[:, 8:12], gmv[:, 8:12], AF.Sqrt, bias=epsg[:, 0:1], scale=1.0),
                 reads=[b_gmv, b_epsg], writes=[b_smG])
            S.op("dve", lambda e: e.reciprocal(smG[:, 12:16], smG[:, 8:12]), reads=[b_smG], writes=[b_smG])
            S.op("dve", _TT(v3(on[:]), v3(on[:]), bc(gmv[:, 0:4]), ALU.subtract), reads=[b_on, b_gmv], writes=[b_on])
            yield
            S.op("pool", _TT(v3(on[:]), v3(on[:]), bc(smG[:, 12:16]), ALU.mult), reads=[b_on, b_smG], writes=[b_on])
            S.op("pool", _TT(on[:], on[:], vec[:, 6, :], ALU.mult), reads=[b_on, b_vec], writes=[b_on])
            S.op("pool", _TT(on[:], on[:], vec[:, 7, :], ALU.add), reads=[b_on, b_vec], writes=[b_on])
            S.op("dve", _TT(v3(tG[:]), v3(vs[:]), bc(bs[:]), ALU.mult), reads=[b_vs, b_bs, b_tG], writes=[b_tG])
            yield
            S.op("pool", _TT(on[:], on[:], tG[:], ALU.add), reads=[b_on, b_tG], writes=[b_on])
            S.op("pool", _TT(on[:], on[:], gs[:], ALU.mult), reads=[b_on, b_gs], writes=[b_on])
            yield
            for hp in range(2):
                S.op("pe", _TR(K[7][:, hp * 128:(hp + 1) * 128], on[:, hp * 128:(hp + 1) * 128], identf[:]),
                     reads=[b_on, b_identf], writes=[b_K[7]], signal=(hp == 1))
            yield
            S.op("act", _CP(orT[:, :, tsl], K[7][:, 0:256].rearrange("p (a t) -> p a t", a=2)),
                 reads=[b_K[7]], writes=[b_orT])
            if j == TPS - 1 or n == B_TILES - 1:
                b_om = Buf()
                for hp in range(2):
                    S.dma("sp", D["omix"][st][256 + hp * 128:256 + (hp + 1) * 128, :],
                          orT[:, hp, :], reads=[b_orT], writes=[b_om], acc=(hp == 1))
                if D.get("ccs") is not None:
                    S.raw("pool", lambda e, k=st: e.collective_compute(
                        "AllGather", ALU.bypass, replica_groups=XGROUPS, ins=[D["omix"][k]],
                        outs=[D["G"][k]]).then_inc(D["ccs"], 1), reads=[b_om])

        for n in range(B_TILES + 2):
            gens = []
            if n < B_TILES:
                gens.append(front1(n))
            if 0 <= n - 1 < B_TILES:
                gens.append(front2(n - 1))
            if 0 <= n - 2 < B_TILES:
                gens.append(back(n - 2))
            while gens:
                for g_ in list(gens):
                    try:
                        next(g_)
                    except StopIteration:
                        gens.remove(g_)
        S.barrier()
        S.run()


def exchange(nc, S, D, stack):
    S.extra.append((D["ccs"], NXCH, "s_cc", "cc"))
    S.barrier()
    S.run()


def build_program(phases="ABC", exch=True):
    nc = bass.Bass("TRN2", target_bir_lowering=False)
    D = {}

    def din(name, shape, dt=F32):
        D[name] = nc.dram_tensor(name, list(shape), dt, kind="ExternalInput").ap()

    din("ident", [128, 128])
    if "A" in phases or "B" in phases:
        din("xT", [D_MODEL, SEQ])
    if "A" in phases:
        din("pos", [128, 64], I32)
        din("w_att", [D_MODEL, 768])
        din("maskT", [128, 256])
    if "B" in phases:
        din("w_rw", [D_MODEL, 1024])
        din("mu", [1, 1024])
        for nm in ("w0", "a0", "k_k", "k_a", "r_k", "gn_g", "gn_b"):
            din(nm, [1, 256])
        din("w_dec", [64, 256])
        din("w_aaa", [64, 256])
        din("w_gate", [128, 256])
        din("cstB", [128, 642])
    if "C" in phases:
        din("xres", [TOKH, D_MODEL])
        din("w_out", [1024, 1024])
        din("wg", [1024, FFN])
        din("wu", [1024, FFN])
        din("wd", [FFN, 1024])
        for nm in ("ln1g", "ln1b", "ln2g", "ln2b"):
            din(nm, [1, 1024])
        din("sel", [128, 2])
        D["out"] = nc.dram_tensor("out", [TOKH, D_MODEL], F32, kind="ExternalOutput").ap()
    full = exch
    if full:
        D["omix"] = [nc.dram_tensor("omix%d" % k, [512, XCH], BF16, kind="Internal").ap() for k in range(NXCH)]
        D["G"] = [nc.dram_tensor("G%d" % k, [1024, XCH], BF16, kind="Internal").ap() for k in range(NXCH)]
    else:
        if "C" in phases:
            din("G", [NXCH, 1024, XCH], BF16)
            D["G"] = [D["G"][k] for k in range(NXCH)]
        if "A" in phases or "B" in phases:
            om = nc.dram_tensor("omix", [NXCH, 512, XCH], BF16, kind="ExternalOutput").ap()
            D["omix"] = [om[k] for k in range(NXCH)]
    with ExitStack() as st:
        S = Sched(nc, st)
        D["ccs"] = st.enter_context(nc.semaphore("s_cc")) if full else None
        if "C" in phases:
            D["wgs"] = nc.dram_tensor("wgs", [HC, 128, 2, 8, 128], BF16, kind="Internal").ap()
            prep_ffn_weights(nc, S, D)
        if "A" in phases:
            phase_a(nc, S, D)
        if "B" in phases:
            phase_b(nc, S, D)
        if full:
            exchange(nc, S, D, st)
        if "C" in phases:
            phase_c(nc, S, D)
    return nc


def att_mask():
    i_k = np.arange(128)[:, None]
    i_q = np.arange(128)[None, :]
    m = np.zeros((128, 256), np.float32)
    m[:, 0:128] = np.where(i_k >= i_q, 0.0, NEG)
    m[:, 128:256] = np.where(i_k <= i_q, 0.0, NEG)
    return m


def rwkv_consts():
    j = np.arange(128)[:, None]
    t = np.arange(128)[None, :]
    same = (j // 64) == (t // 64)
    c = np.zeros((128, 642), np.float32)
    c[:, 0:128] = same & (j <= t)
    c[:, 128:256] = same & (j < t)
    c[:, 256:384] = same & (j > t)
    c[:, 384] = (np.arange(128) < 64)
    c[:, 385] = (np.arange(128) >= 64)
    jj = (np.arange(128) % 64)[:, None]
    tt = np.arange(64)[None, :]
    c[:, 386:450] = jj < tt
    c[:, 450:514] = jj <= tt
    c[:, 514:578] = tt < jj
    c[:, 578:642] = jj == tt
    return c


def core_inputs(inp, c, phases="ABC"):
    b, g = c // 2, c % 2
    m = {"ident": np.eye(128, dtype=np.float32)}
    w_in = inp["w_in"][0]
    if "A" in phases or "B" in phases:
        m["xT"] = np.ascontiguousarray(inp["x"][b].T)
    if "A" in phases:
        m["pos"] = np.ascontiguousarray(inp["positions"][b].reshape(64, 128).T).astype(np.int32)
        cols = np.concatenate([np.arange(256 * g, 256 * g + 256) + off for off in (0, 512, 1024)])
        m["w_att"] = np.ascontiguousarray(w_in[:, cols])
        m["maskT"] = att_mask()
    if "B" in phases:
        hs = slice(256 * g, 256 * g + 256)
        rcols = np.concatenate([1536 + off + np.arange(256 * g, 256 * g + 256) for off in (0, 512, 1024)]
                               + [1536 + 1536 + np.arange(256)])
        m["w_rw"] = np.ascontiguousarray(w_in[:, rcols])
        m["mu"] = np.ascontiguousarray(inp["mu_shift"][0][rcols - 1536][None, :])
        for nm in ("w0", "a0", "k_k", "k_a", "gn_g", "gn_b"):
            m[nm] = np.ascontiguousarray(inp[nm][0][hs][None, :])
        m["r_k"] = np.ascontiguousarray(inp["r_k"][0][4 * g:4 * g + 4].reshape(1, 256))
        m["w_dec"] = np.ascontiguousarray(inp["w_decay_up"][0][:, hs])
        m["w_aaa"] = np.ascontiguousarray(inp["w_aaa_up"][0][:, hs])
        m["w_gate"] = np.ascontiguousarray(inp["w_gate_up"][0][:, hs])
        m["cstB"] = rwkv_consts()
    if "C" in phases:
        fi = lambda r: np.concatenate([np.arange(256 * r, 256 * r + 256), 512 + np.arange(256 * r, 256 * r + 256)])
        perm = np.concatenate([fi(0), fi(1)])
        sel = np.zeros((128, 2), np.float32)
        sel[:, g] = 1.0
        m.update(xres=np.ascontiguousarray(inp["x"][b, g * TOKH:(g + 1) * TOKH]),
                 w_out=np.ascontiguousarray(inp["w_out"][0][perm]),
                 wg=inp["w_ffn_gate"][0], wu=inp["w_ffn_up"][0], wd=inp["w_ffn_down"][0],
                 ln1g=inp["ln_mix_g"], ln1b=inp["ln_mix_b"], ln2g=inp["ln_ffn_g"], ln2b=inp["ln_ffn_b"],
                 sel=sel)
    return m


_NC_CACHE = {}


def kernel(**inputs):
    inp = {k: np.asarray(v) for k, v in inputs.items()}
    if "nc" not in _NC_CACHE:
        _NC_CACHE["nc"] = build_program("ABC")
    nc = _NC_CACHE["nc"]
    in_maps = [core_inputs(inp, c, "ABC") for c in range(8)]
    res = run_bass_kernel_spmd(nc, in_maps, core_ids=list(range(8)))
    out = np.empty((BATCH, SEQ, D_MODEL), np.float32)
    for c in range(8):
        b, g = c // 2, c % 2
        out[b, g * TOKH:(g + 1) * TOKH] = np.asarray(res.results[c]["out"], dtype=np.float32)
    return out
```
